# Optimizing a Trainium2 kernel written in Bass

```python
import math
import jax
import jax.numpy as jnp
from jax import lax
import numpy as np

D_MODEL = 2048
BATCH = 4
SEQ = 4096
DEPTH = 2

CTX_LEN = 256
GRID_W = 64
EPS = 1e-6

MIX_WIDTH = D_MODEL

S5_GROUP = 16
S5_WIDTH = MIX_WIDTH // 4
S5_GROUPS = S5_WIDTH // S5_GROUP
S5_STATE = 64

SSD_HEAD_DIM = 64
SSD_WIDTH = MIX_WIDTH - S5_WIDTH
SSD_HEADS = SSD_WIDTH // SSD_HEAD_DIM
SSD_GROUPS = 4
SSD_STATE = 128
SSD_CHUNK = 128
SSD_CONV_DIM = SSD_WIDTH + 2 * SSD_GROUPS * SSD_STATE

SHORT_CONV_W = 3

GDN_HEAD_DIM = 128
GDN_WIDTH = MIX_WIDTH // 2
GDN_HEADS = GDN_WIDTH // GDN_HEAD_DIM
GDN_CHUNK = 64

MLSTM_V_DIM = 256
MLSTM_QK_DIM = 128
MLSTM_WIDTH = MIX_WIDTH - GDN_WIDTH
MLSTM_HEADS = MLSTM_WIDTH // MLSTM_V_DIM
MLSTM_CHUNK = 64

FFN_DIM = 5504
FFN_CONV_W = 3

EVEN_SPLITS = [S5_WIDTH, S5_WIDTH + SSD_WIDTH, S5_WIDTH + SSD_WIDTH + SSD_CONV_DIM]
EVEN_IN = S5_WIDTH + SSD_WIDTH + SSD_CONV_DIM + 2 * SSD_HEADS
_ODD_SIZES = [3 * GDN_WIDTH, GDN_WIDTH, 2 * GDN_HEADS, 2 * GDN_HEADS,
              MLSTM_HEADS * MLSTM_QK_DIM, MLSTM_HEADS * MLSTM_QK_DIM, MLSTM_WIDTH, MLSTM_WIDTH,
              2 * MLSTM_HEADS, 2 * MLSTM_HEADS]
ODD_SPLITS = [sum(_ODD_SIZES[:i + 1]) for i in range(len(_ODD_SIZES) - 1)]
ODD_IN = sum(_ODD_SIZES)

kernel_name = 'hybrid_s5_ssd_gdn_mlstm_dit'


def _rmsnorm(x, w):
    xf = x.astype(jnp.float32)
    y = xf * lax.rsqrt(jnp.mean(xf * xf, axis=-1, keepdims=True) + EPS)
    return (y * w.astype(jnp.float32)).astype(x.dtype)


def _l2norm(x):
    return x * lax.rsqrt(jnp.sum(x * x, axis=-1, keepdims=True) + EPS)


def _modulate(h, shift, scale):
    return h * (1.0 + scale) + shift


def _short_conv(x, w):
    width = w.shape[0]
    pad = width // 2
    n = x.shape[1]
    xp = jnp.pad(x, ((0, 0), (pad, pad), (0, 0)))
    y = xp[:, 0:n] * w[0]
    for j in range(1, width):
        y = y + xp[:, j:j + n] * w[j]
    return y


def _dwconv2d(x, w):
    return lax.conv_general_dilated(
        x, w[:, :, None, :].astype(x.dtype), window_strides=(1, 1), padding='SAME',
        dimension_numbers=('NHWC', 'HWIO', 'NHWC'), feature_group_count=x.shape[-1])


def _heads_first(t, chunk):
    bsz, n = t.shape[0], t.shape[1]
    t = t.reshape((bsz, n // chunk, chunk) + t.shape[2:])
    return jnp.moveaxis(t, 3, 2)


def _seq_from_chunks(t):
    t = jnp.moveaxis(t, 2, 3)
    return t.reshape((t.shape[0], t.shape[1] * t.shape[2]) + t.shape[3:])


def _lin_combine(left, right):
    a_l, b_l = left
    a_r, b_r = right
    return a_l * a_r, a_r * b_l + b_r


def _s5_scan(u, lam_re, lam_im, log_step, b_re, b_im, c_re, c_im, x0):
    f32 = jnp.float32
    lam = lax.complex(lam_re.astype(f32), lam_im.astype(f32))
    lam_bar = jnp.exp(lam * jnp.exp(log_step.astype(f32))[:, None])
    b_bar = ((lam_bar - 1.0) / lam)[:, :, None] * lax.complex(b_re.astype(f32), b_im.astype(f32))
    bu = jnp.einsum('blgh,gph->blgp', u.astype(jnp.complex64), b_bar)
    bu = bu.at[:, 0].add(lam_bar * x0)
    a = jnp.broadcast_to(lam_bar, bu.shape)
    _, states = lax.associative_scan(_lin_combine, (a, bu), axis=1)
    c = lax.complex(c_re.astype(f32), c_im.astype(f32))
    y = jnp.real(jnp.einsum('blgp,ghp->blgh', states, c))
    return y, states[:, -1]


def _s5_branch(u, p, init):
    f32 = jnp.float32
    bsz, n, _ = u.shape
    uf = u.astype(f32).reshape(bsz, n, S5_GROUPS, S5_GROUP)
    y = p['s5_d'].astype(f32).reshape(S5_GROUPS, S5_GROUP) * uf
    finals = []
    for d in range(2):
        ud = uf if d == 0 else jnp.flip(uf, axis=1)
        x0 = jnp.zeros((bsz, S5_GROUPS, S5_STATE), jnp.complex64) if init is None else init[d]
        yd, xf = _s5_scan(ud, p['s5_lam_re'][d], p['s5_lam_im'][d], p['s5_log_step'][d],
                          p['s5_b_re'][d], p['s5_b_im'][d], p['s5_c_re'][d], p['s5_c_im'][d], x0)
        y = y + (yd if d == 0 else jnp.flip(yd, axis=1))
        finals.append(xf)
    y = jax.nn.gelu(y.reshape(bsz, n, S5_WIDTH))
    y = y * jax.nn.sigmoid(y @ p['s5_glu_w'].astype(f32))
    return y, finals


def _ssd_chunked(x, dt, a_neg, bm, cm, s0):
    bsz, n, nh, hp = x.shape
    ng, ns = bm.shape[2], bm.shape[3]
    r = nh // ng
    q = SSD_CHUNK
    nc = n // q
    la = (dt * a_neg).reshape(bsz, nc, q, ng, r)
    xdt = (x * dt[..., None]).reshape(bsz, nc, q, ng, r, hp)
    bc = bm.reshape(bsz, nc, q, ng, ns)
    cc = cm.reshape(bsz, nc, q, ng, ns)
    cum = jnp.cumsum(la, axis=2)
    tri = jnp.tril(jnp.ones((q, q), bool))[:, :, None, None]
    seg = jnp.where(tri, cum[:, :, :, None] - cum[:, :, None, :], -jnp.inf)
    w_intra = jnp.einsum('bcign,bcjgn->bcijg', cc, bc)[..., None] * jnp.exp(seg)
    y_intra = jnp.einsum('bcijgr,bcjgrp->bcigrp', w_intra, xdt)
    dec_end = jnp.exp(cum[:, :, -1:] - cum)
    st = jnp.einsum('bcjgn,bcjgr,bcjgrp->bcgrpn', bc, dec_end, xdt)
    chunk_dec = jnp.exp(cum[:, :, -1])

    def step(s, inp):
        st_c, dec_c = inp
        return dec_c[..., None, None] * s + st_c, s

    s_fin, s_in = lax.scan(step, s0.reshape(bsz, ng, r, hp, ns),
                           (jnp.moveaxis(st, 1, 0), jnp.moveaxis(chunk_dec, 1, 0)))
    s_in = jnp.moveaxis(s_in, 0, 1)
    y_inter = jnp.einsum('bcign,bcgrpn->bcigrp', cc, s_in) * jnp.exp(cum)[..., None]
    y = (y_intra + y_inter).reshape(bsz, n, nh, hp)
    return y, s_fin.reshape(bsz, nh, hp, ns)


def _ssd_branch(z, xbc, dt_raw, p, init):
    f32 = jnp.float32
    bsz, n, _ = z.shape
    xbc = jax.nn.silu(_short_conv(xbc, p['ssd_conv_w']) + p['ssd_conv_b']).astype(f32)
    xs, bm, cm = jnp.split(xbc, [SSD_WIDTH, SSD_WIDTH + SSD_GROUPS * SSD_STATE], axis=-1)
    xs = xs.reshape(bsz, n, SSD_HEADS, SSD_HEAD_DIM)
    bm = bm.reshape(bsz, n, SSD_GROUPS, SSD_STATE)
    cm = cm.reshape(bsz, n, SSD_GROUPS, SSD_STATE)
    dt_raw = dt_raw.astype(f32).reshape(bsz, n, 2, SSD_HEADS)
    ys, finals = [], []
    for d in range(2):
        dt = jax.nn.softplus(dt_raw[:, :, d] + p['ssd_dt_bias'][d].astype(f32))
        a_neg = -jnp.exp(p['ssd_a_log'][d].astype(f32))
        seqs = (xs, dt, bm, cm) if d == 0 else tuple(jnp.flip(t, axis=1) for t in (xs, dt, bm, cm))
        s0 = jnp.zeros((bsz, SSD_HEADS, SSD_HEAD_DIM, SSD_STATE), f32) if init is None else init[d]
        yd, sf = _ssd_chunked(seqs[0], seqs[1], a_neg, seqs[2], seqs[3], s0)
        ys.append(yd if d == 0 else jnp.flip(yd, axis=1))
        finals.append(sf)
    y = ys[0] + ys[1] + p['ssd_d'].astype(f32)[:, None] * xs
    y = y.reshape(bsz, n, SSD_WIDTH) * jax.nn.silu(z.astype(f32))
    gw = SSD_WIDTH // SSD_GROUPS
    y = _rmsnorm(y.reshape(bsz, n, SSD_GROUPS, gw), p['ssd_norm_w'].reshape(SSD_GROUPS, gw))
    return y.reshape(bsz, n, SSD_WIDTH), finals


def _gdn_chunked(q, k, v, g, beta, s0):
    kd = q.shape[-1]
    vd = v.shape[-1]
    qn = GDN_CHUNK
    qc = _heads_first(q * kd ** -0.5, qn)
    kc = _heads_first(k, qn)
    vc = _heads_first(v, qn)
    gc = _heads_first(g, qn)
    bc = _heads_first(beta, qn)
    gcum = jnp.cumsum(gc, axis=-1)
    incl = jnp.tril(jnp.ones((qn, qn), bool))
    strict = jnp.tril(jnp.ones((qn, qn), bool), -1)
    gam = jnp.exp(jnp.where(incl, gcum[..., :, None] - gcum[..., None, :], -jnp.inf))
    a_mat = jnp.where(strict, bc[..., :, None] * jnp.einsum('bchik,bchjk->bchij', kc, kc) * gam, 0.0)
    rhs = jnp.concatenate([vc * bc[..., None], kc * (bc * jnp.exp(gcum))[..., None]], axis=-1)
    sol = lax.linalg.triangular_solve(a_mat + jnp.eye(qn, dtype=a_mat.dtype), rhs,
                                      left_side=True, lower=True, unit_diagonal=True)
    u_c, w_c = sol[..., :vd], sol[..., vd:]
    qk = jnp.einsum('bchik,bchjk->bchij', qc, kc) * gam
    q_dec = qc * jnp.exp(gcum)[..., None]
    k_dec = kc * jnp.exp(gcum[..., -1:] - gcum)[..., None]
    g_end = jnp.exp(gcum[..., -1])

    def step(s, inp):
        u_i, w_i, qk_i, qd_i, kd_i, ge_i = inp
        v_new = u_i - jnp.einsum('bhqk,bhkv->bhqv', w_i, s)
        o = jnp.einsum('bhqk,bhkv->bhqv', qd_i, s) + jnp.einsum('bhij,bhjv->bhiv', qk_i, v_new)
        s = ge_i[..., None, None] * s + jnp.einsum('bhqk,bhqv->bhkv', kd_i, v_new)
        return s, o

    s_fin, o = lax.scan(step, s0, tuple(jnp.moveaxis(t, 1, 0) for t in (u_c, w_c, qk, q_dec, k_dec, g_end)))
    return _seq_from_chunks(jnp.moveaxis(o, 0, 1)), s_fin


def _gdn_branch(qkv, z, beta_raw, a_raw, p, init):
    f32 = jnp.float32
    bsz, n, _ = qkv.shape
    qkv = jax.nn.silu(_short_conv(qkv, p['gdn_conv_w'])).astype(f32)
    q, k, v = jnp.split(qkv, 3, axis=-1)
    shp = (bsz, n, GDN_HEADS, GDN_HEAD_DIM)
    q = _l2norm(q.reshape(shp))
    k = _l2norm(k.reshape(shp))
    v = v.reshape(shp)
    beta = jax.nn.sigmoid(beta_raw.astype(f32)).reshape(bsz, n, 2, GDN_HEADS)
    a_raw = a_raw.astype(f32).reshape(bsz, n, 2, GDN_HEADS)
    outs, finals = [], []
    for d in range(2):
        g = -jnp.exp(p['gdn_a_log'][d].astype(f32)) * jax.nn.softplus(a_raw[:, :, d] + p['gdn_dt_bias'][d].astype(f32))
        seqs = (q, k, v, g, beta[:, :, d])
        if d == 1:
            seqs = tuple(jnp.flip(t, axis=1) for t in seqs)
        s0 = jnp.zeros((bsz, GDN_HEADS, GDN_HEAD_DIM, GDN_HEAD_DIM), f32) if init is None else init[d]
        od, sf = _gdn_chunked(*seqs, s0)
        outs.append(od if d == 0 else jnp.flip(od, axis=1))
        finals.append(sf)
    o = _rmsnorm(outs[0] + outs[1], p['gdn_norm_w']) * jax.nn.silu(z.astype(f32).reshape(shp))
    return o.reshape(bsz, n, GDN_WIDTH), finals


def _mlstm_chunked(q, k, v, i_pre, logf, state0):
    kd = q.shape[-1]
    qn = MLSTM_CHUNK
    qc = _heads_first(q, qn)
    kc = _heads_first(k * kd ** -0.5, qn)
    vc = _heads_first(v, qn)
    ic = _heads_first(i_pre, qn)
    bcum = jnp.cumsum(_heads_first(logf, qn), axis=-1)
    b_end = bcum[..., -1]
    a_end = b_end[..., None] - bcum + ic
    m_loc = jnp.max(a_end, axis=-1)
    w_end = jnp.exp(a_end - m_loc[..., None])
    c_loc = jnp.einsum('bchq,bchqk,bchqv->bchkv', w_end, kc, vc)
    n_loc = jnp.einsum('bchq,bchqk->bchk', w_end, kc)

    def step(carry, inp):
        c_st, n_st, m_st = carry
        cl, nl, ml, be = inp
        m_new = jnp.maximum(be + m_st, ml)
        s_old = jnp.exp(be + m_st - m_new)
        s_new = jnp.exp(ml - m_new)
        c_next = s_old[..., None, None] * c_st + s_new[..., None, None] * cl
        n_next = s_old[..., None] * n_st + s_new[..., None] * nl
        return (c_next, n_next, m_new), (c_st, n_st, m_st)

    final, (c_in, n_in, m_in) = lax.scan(
        step, state0, tuple(jnp.moveaxis(t, 1, 0) for t in (c_loc, n_loc, m_loc, b_end)))
    c_in, n_in, m_in = (jnp.moveaxis(t, 0, 1) for t in (c_in, n_in, m_in))
    tri = jnp.tril(jnp.ones((qn, qn), bool))
    dmat = jnp.where(tri, bcum[..., :, None] - bcum[..., None, :] + ic[..., None, :], -jnp.inf)
    inter = bcum + m_in[..., None]
    m_row = jnp.maximum(inter, jnp.max(dmat, axis=-1))
    w_row = jnp.exp(dmat - m_row[..., None])
    w_int = jnp.exp(inter - m_row)
    s = jnp.einsum('bchik,bchjk->bchij', qc, kc) * w_row
    num = w_int[..., None] * jnp.einsum('bchik,bchkv->bchiv', qc, c_in) + jnp.einsum('bchij,bchjv->bchiv', s, vc)
    den = w_int * jnp.einsum('bchik,bchk->bchi', qc, n_in) + jnp.sum(s, axis=-1)
    h = num / jnp.maximum(jnp.abs(den), jnp.exp(-m_row))[..., None]
    return _seq_from_chunks(h), final


def _mlstm_branch(q, k, v, o_raw, i_raw, f_raw, p, init):
    f32 = jnp.float32
    bsz, n, _ = q.shape
    q = q.astype(f32).reshape(bsz, n, MLSTM_HEADS, MLSTM_QK_DIM)
    k = k.astype(f32).reshape(bsz, n, MLSTM_HEADS, MLSTM_QK_DIM)
    v = v.astype(f32).reshape(bsz, n, MLSTM_HEADS, MLSTM_V_DIM)
    i_raw = i_raw.astype(f32).reshape(bsz, n, 2, MLSTM_HEADS)
    f_raw = f_raw.astype(f32).reshape(bsz, n, 2, MLSTM_HEADS)
    outs, finals = [], []
    for d in range(2):
        i_pre = i_raw[:, :, d] + p['mlstm_igate_b'][d].astype(f32)
        logf = jax.nn.log_sigmoid(f_raw[:, :, d] + p['mlstm_fgate_b'][d].astype(f32))
        seqs = (q, k, v, i_pre, logf)
        if d == 1:
            seqs = tuple(jnp.flip(t, axis=1) for t in seqs)
        if init is None:
            st0 = (jnp.zeros((bsz, MLSTM_HEADS, MLSTM_QK_DIM, MLSTM_V_DIM), f32),
                   jnp.zeros((bsz, MLSTM_HEADS, MLSTM_QK_DIM), f32),
                   jnp.zeros((bsz, MLSTM_HEADS), f32))
        else:
            st0 = init[d]
        hd, sf = _mlstm_chunked(*seqs, st0)
        outs.append(hd if d == 0 else jnp.flip(hd, axis=1))
        finals.append(sf)
    h = _rmsnorm(outs[0] + outs[1], p['mlstm_norm_w'].reshape(MLSTM_HEADS, MLSTM_V_DIM))
    h = h * jax.nn.sigmoid(o_raw.astype(f32).reshape(bsz, n, MLSTM_HEADS, MLSTM_V_DIM))
    return h.reshape(bsz, n, MLSTM_WIDTH), finals


def _even_core(h, p, init):
    proj = h @ p['in_w']
    u, z, xbc, dt_raw = jnp.split(proj, EVEN_SPLITS, axis=-1)
    y_a, st_a = _s5_branch(u, p, None if init is None else init[0])
    y_b, st_b = _ssd_branch(z, xbc, dt_raw, p, None if init is None else init[1])
    return jnp.concatenate([y_a, y_b], axis=-1).astype(h.dtype), (st_a, st_b)


def _odd_core(h, p, init):
    proj = h @ p['in_w']
    qkv, zg, beta_raw, a_raw, qm, km, vm, om, ig, fg = jnp.split(proj, ODD_SPLITS, axis=-1)
    y_c, st_c = _gdn_branch(qkv, zg, beta_raw, a_raw, p, None if init is None else init[0])
    y_d, st_d = _mlstm_branch(qm, km, vm, om, ig, fg, p, None if init is None else init[1])
    return jnp.concatenate([y_c, y_d], axis=-1).astype(h.dtype), (st_c, st_d)


def _conv_ffn(h, up_w, conv_w, down_w, rows, cols):
    bsz, n, _ = h.shape
    a, v = jnp.split(h @ up_w, 2, axis=-1)
    a = _dwconv2d(a.reshape(bsz, rows, cols, FFN_DIM), conv_w).reshape(bsz, n, FFN_DIM)
    return (jax.nn.silu(a) * v) @ down_w


def setup_inputs(seed: int = 0) -> dict:
    key = jax.random.key(seed)
    keys = jax.random.split(key, 64)
    counter = [0]
    f32 = jnp.float32
    n_even = (DEPTH + 1) // 2
    n_odd = DEPTH // 2

    def nxt():
        kk = keys[counter[0]]
        counter[0] += 1
        return kk

    def normal(shape, scale):
        return scale * jax.random.normal(nxt(), shape, f32)

    def gain(shape):
        return 1.0 + normal(shape, 0.05)

    def dt_bias(shape):
        dt = jnp.exp(jax.random.uniform(nxt(), shape, f32, math.log(1e-3), math.log(1e-1)))
        return dt + jnp.log(-jnp.expm1(-dt))

    def a_log(shape):
        return jnp.log(jax.random.uniform(nxt(), shape, f32, 1.0, 16.0))

    s5_ax = (n_even, 2, S5_GROUPS)
    return {
        'x': normal((BATCH, SEQ, D_MODEL), 1.0),
        'c': normal((BATCH, D_MODEL), 1.0),
        'ctx': normal((BATCH, CTX_LEN, D_MODEL), 1.0),
        'c_ctx': normal((D_MODEL,), 1.0),
        'mod_w': normal((DEPTH, D_MODEL, 6 * D_MODEL), 0.5 * D_MODEL ** -0.5),
        'mod_b': normal((DEPTH, 6 * D_MODEL), 0.02),
        'norm_mix_w': gain((DEPTH, D_MODEL)),
        'norm_ffn_w': gain((DEPTH, D_MODEL)),
        'ffn_up_w': normal((DEPTH, D_MODEL, 2 * FFN_DIM), D_MODEL ** -0.5),
        'ffn_conv_w': normal((DEPTH, FFN_CONV_W, FFN_CONV_W, FFN_DIM), 1.0 / FFN_CONV_W),
        'ffn_down_w': normal((DEPTH, FFN_DIM, D_MODEL), FFN_DIM ** -0.5),
        'final_norm_w': gain((D_MODEL,)),
        'ev_in_w': normal((n_even, D_MODEL, EVEN_IN), D_MODEL ** -0.5),
        'ev_out_w': normal((n_even, MIX_WIDTH, D_MODEL), MIX_WIDTH ** -0.5),
        's5_lam_re': -0.5 + normal(s5_ax + (S5_STATE,), 0.01),
        's5_lam_im': math.pi * jnp.arange(S5_STATE, dtype=f32) + normal(s5_ax + (S5_STATE,), 0.01),
        's5_log_step': jax.random.uniform(nxt(), s5_ax, f32, math.log(1e-3), math.log(1e-1)),
        's5_b_re': normal(s5_ax + (S5_STATE, S5_GROUP), (2 * S5_GROUP) ** -0.5),
        's5_b_im': normal(s5_ax + (S5_STATE, S5_GROUP), (2 * S5_GROUP) ** -0.5),
        's5_c_re': normal(s5_ax + (S5_GROUP, S5_STATE), S5_STATE ** -0.5),
        's5_c_im': normal(s5_ax + (S5_GROUP, S5_STATE), S5_STATE ** -0.5),
        's5_d': normal((n_even, S5_WIDTH), 1.0),
        's5_glu_w': normal((n_even, S5_WIDTH, S5_WIDTH), S5_WIDTH ** -0.5),
        'ssd_conv_w': normal((n_even, SHORT_CONV_W, SSD_CONV_DIM), SHORT_CONV_W ** -0.5),
        'ssd_conv_b': normal((n_even, SSD_CONV_DIM), 0.02),
        'ssd_dt_bias': dt_bias((n_even, 2, SSD_HEADS)),
        'ssd_a_log': a_log((n_even, 2, SSD_HEADS)),
        'ssd_d': 1.0 + normal((n_even, SSD_HEADS), 0.1),
        'ssd_norm_w': gain((n_even, SSD_WIDTH)),
        'od_in_w': normal((n_odd, D_MODEL, ODD_IN), D_MODEL ** -0.5),
        'od_out_w': normal((n_odd, MIX_WIDTH, D_MODEL), MIX_WIDTH ** -0.5),
        'gdn_conv_w': normal((n_odd, SHORT_CONV_W, 3 * GDN_WIDTH), SHORT_CONV_W ** -0.5),
        'gdn_dt_bias': dt_bias((n_odd, 2, GDN_HEADS)),
        'gdn_a_log': a_log((n_odd, 2, GDN_HEADS)),
        'gdn_norm_w': gain((n_odd, GDN_HEAD_DIM)),
        'mlstm_igate_b': normal((n_odd, 2, MLSTM_HEADS), 0.1),
        'mlstm_fgate_b': jnp.linspace(3.0, 6.0, MLSTM_HEADS, dtype=f32) + normal((n_odd, 2, MLSTM_HEADS), 0.1),
        'mlstm_norm_w': gain((n_odd, MLSTM_WIDTH)),
    }


def reference(x, c, ctx, c_ctx, mod_w, mod_b, norm_mix_w, norm_ffn_w, ffn_up_w, ffn_conv_w, ffn_down_w,
              final_norm_w, ev_in_w, ev_out_w, s5_lam_re, s5_lam_im, s5_log_step, s5_b_re, s5_b_im,
              s5_c_re, s5_c_im, s5_d, s5_glu_w, ssd_conv_w, ssd_conv_b, ssd_dt_bias, ssd_a_log, ssd_d,
              ssd_norm_w, od_in_w, od_out_w, gdn_conv_w, gdn_dt_bias, gdn_a_log, gdn_norm_w,
              mlstm_igate_b, mlstm_fgate_b, mlstm_norm_w):
    rows = x.shape[1] // GRID_W
    ctx_len = ctx.shape[1]
    xc = ctx
    for layer in range(DEPTH):
        j = layer // 2
        if layer % 2 == 0:
            p = {'in_w': ev_in_w[j], 's5_lam_re': s5_lam_re[j], 's5_lam_im': s5_lam_im[j],
                 's5_log_step': s5_log_step[j], 's5_b_re': s5_b_re[j], 's5_b_im': s5_b_im[j],
                 's5_c_re': s5_c_re[j], 's5_c_im': s5_c_im[j], 's5_d': s5_d[j], 's5_glu_w': s5_glu_w[j],
                 'ssd_conv_w': ssd_conv_w[j], 'ssd_conv_b': ssd_conv_b[j], 'ssd_dt_bias': ssd_dt_bias[j],
                 'ssd_a_log': ssd_a_log[j], 'ssd_d': ssd_d[j], 'ssd_norm_w': ssd_norm_w[j]}
            core, out_w = _even_core, ev_out_w[j]
        else:
            p = {'in_w': od_in_w[j], 'gdn_conv_w': gdn_conv_w[j], 'gdn_dt_bias': gdn_dt_bias[j],
                 'gdn_a_log': gdn_a_log[j], 'gdn_norm_w': gdn_norm_w[j], 'mlstm_igate_b': mlstm_igate_b[j],
                 'mlstm_fgate_b': mlstm_fgate_b[j], 'mlstm_norm_w': mlstm_norm_w[j]}
            core, out_w = _odd_core, od_out_w[j]
        mod = jax.nn.silu(c) @ mod_w[layer] + mod_b[layer]
        mod_c = jax.nn.silu(c_ctx) @ mod_w[layer] + mod_b[layer]
        sh1, sc1, g1, sh2, sc2, g2 = jnp.split(mod[:, None, :], 6, axis=-1)
        csh1, csc1, cg1, csh2, csc2, cg2 = jnp.split(mod_c[None, None, :], 6, axis=-1)
        y_c, ctx_states = core(_modulate(_rmsnorm(xc, norm_mix_w[layer]), csh1, csc1), p, None)
        y_l, _ = core(_modulate(_rmsnorm(x, norm_mix_w[layer]), sh1, sc1), p, ctx_states)
        x = x + g1 * (y_l @ out_w)
        x = x + g2 * _conv_ffn(_modulate(_rmsnorm(x, norm_ffn_w[layer]), sh2, sc2),
                               ffn_up_w[layer], ffn_conv_w[layer], ffn_down_w[layer], rows, GRID_W)
        if layer < DEPTH - 1:
            xc = xc + cg1 * (y_c @ out_w)
            xc = xc + cg2 * _conv_ffn(_modulate(_rmsnorm(xc, norm_ffn_w[layer]), csh2, csc2),
                                      ffn_up_w[layer], ffn_conv_w[layer], ffn_down_w[layer], 1, ctx_len)
    return _rmsnorm(x, final_norm_w)
```

```python
import numpy as np
from contextlib import ExitStack
import concourse.bass as bass
import concourse.mybir as mybir
from concourse.bass_utils import run_bass_kernel_spmd

F32 = mybir.dt.float32
BF16 = mybir.dt.bfloat16
I32 = mybir.dt.int32
AF = mybir.ActivationFunctionType
ALU = mybir.AluOpType
AX = mybir.AxisListType

D = 2048
KC = 16
TC = 256
TL = 4096
T = TC + TL
FFN = 5504
EPS = 1e-6
EVEN_IN = 4656
ODD_IN = 7216
NEG = -30000.0

SEM_LIMIT = 4000
DMA_SLOT_LIMIT = 1200
DMA_POOL = 6


class Prog:
    ENG = ('sync', 'scalar', 'vector', 'gpsimd', 'tensor')

    def __init__(self, nc, es):
        self.nc = nc
        self.es = es
        self.e = dict(sync=nc.sync, scalar=nc.scalar, vector=nc.vector,
                      gpsimd=nc.gpsimd, tensor=nc.tensor)
        self.semh = []
        self.cur = {}
        self.cnt = {}
        self.known = {e: {} for e in self.ENG}
        self.lastw = {}
        self.readers = {}
        self.pool = {}
        self.pidx = {}
        self.nins = {e: 0 for e in self.ENG}
        for e in self.ENG:
            self._fresh(e)

    def _newsem(self, name):
        h = self.es.enter_context(self.nc.semaphore(name))
        self.semh.append(h)
        return len(self.semh) - 1

    def _fresh(self, e):
        self.cur[e] = self._newsem(f"s_{e}_{len(self.semh)}")
        self.cnt[e] = 0

    def _deps(self, reads, writes):
        d = {}

        def add(tok):
            if tok is None:
                return
            sk, val, pe = tok
            if sk not in d or d[sk][0] < val:
                d[sk] = (val, pe)
        for k in reads:
            add(self.lastw.get(k))
        for k in writes:
            add(self.lastw.get(k))
            for t in self.readers.get(k, ()):
                add(t)
        return d

    def _update(self, reads, writes, tok):
        for k in reads:
            self.readers.setdefault(k, []).append(tok)
        for k in writes:
            self.lastw[k] = tok
            self.readers[k] = []

    def _wait(self, eng, sk, val):
        if self.known[eng].get(sk, 0) >= val:
            return
        self.e[eng].wait_ge(self.semh[sk], val)
        self.known[eng][sk] = val
        self.nins[eng] += 1

    def _waits(self, eng, deps):
        for sk, (val, pe) in deps.items():
            if pe == 'tensor' and eng == 'tensor':
                continue
            self._wait(eng, sk, val)

    @staticmethod
    def _excl(reads, writes):
        ex = [k for k in reads if isinstance(k, str) and k.startswith('PS:')]
        if ex:
            reads = [k for k in reads if k not in ex]
            writes = list(writes) + ex
        return reads, writes

    def op(self, eng, fn, reads=(), writes=()):
        reads, writes = self._excl(reads, writes)
        deps = self._deps(reads, writes)
        self._waits(eng, deps)
        ins = fn(self.e[eng])
        if self.cnt[eng] >= SEM_LIMIT:
            self._fresh(eng)
        self.cnt[eng] += 1
        ins.then_inc(self.semh[self.cur[eng]], 1)
        tok = (self.cur[eng], self.cnt[eng], eng)
        self._update(reads, writes, tok)
        self.nins[eng] += 1
        return tok

    def dma(self, q, out, in_, reads=(), writes=(), **kw):
        deps = self._deps(reads, writes)
        self._waits(q, deps)
        E = self.e[q]
        if q not in self.pool:
            self.pool[q] = [[self._newsem(f"d_{q}_{i}_{len(self.semh)}"), 0] for i in range(DMA_POOL)]
            self.pidx[q] = 0
        i = self.pidx[q] % DMA_POOL
        self.pidx[q] += 1
        slot = self.pool[q][i]
        if slot[1] >= DMA_SLOT_LIMIT:
            self._wait(q, slot[0], 16 * slot[1])
            slot[0] = self._newsem(f"d_{q}_{i}_{len(self.semh)}")
            slot[1] = 0
        sk, n = slot
        if n > 0:
            self._wait(q, sk, 16 * n)
        ins = E.dma_start(out=out, in_=in_, **kw)
        ins.then_inc(self.semh[sk], 16)
        slot[1] = n + 1
        tok = (sk, 16 * (n + 1), 'dma')
        self._update(reads, writes, tok)
        self.nins[q] += 1
        return tok

    def barrier(self):
        toks = []
        for q, slots in self.pool.items():
            for sk, n in slots:
                if n > 0:
                    toks.append((sk, 16 * n))
        for e in self.ENG:
            if self.cnt[e] > 0:
                toks.append((self.cur[e], self.cnt[e]))
        for e in self.ENG:
            for sk, val in toks:
                self._wait(e, sk, val)
        self.lastw.clear()
        self.readers.clear()

    def finish(self):
        self.barrier()


class B:
    def __init__(self, stage=99, dbg=(), sub=99):
        self.sub = sub
        self.stage = stage
        self.dbg = set(dbg)
        self.nc = bass.Bass("TRN2", target_bir_lowering=False)
        self.dr = {}

    def din(self, name, shape, dt=F32):
        self.dr[name] = self.nc.dram_tensor(name, list(shape), dt, kind="ExternalInput").ap()
        return self.dr[name]

    def dout(self, name, shape, dt=F32):
        self.dr[name] = self.nc.dram_tensor(name, list(shape), dt, kind="ExternalOutput").ap()
        return self.dr[name]

    def dscr(self, name, shape, dt=F32):
        kind = "ExternalOutput" if name in self.dbg else "Internal"
        self.dr[name] = self.nc.dram_tensor(name, list(shape), dt, kind=kind).ap()
        return self.dr[name]

    def _uniq(self, name):
        self._names = getattr(self, '_names', {})
        n = self._names.get(name, 0)
        self._names[name] = n + 1
        return name if n == 0 else f"{name}_u{n}"

    def sb(self, es, name, shape, dt=F32):
        return es.enter_context(self.nc.sbuf_tensor(self._uniq(name), list(shape), dt))

    def ps(self, es, name, shape, dt=F32):
        return es.enter_context(self.nc.psum_tensor(self._uniq(name), list(shape), dt))

    def build(self):
        nc = self.nc
        din = self.din
        din("xT", [D, T])
        din("ccT", [128, KC, 2])
        din("mod_w", [2, D, 6 * D])
        din("mod_bT", [2, 128, 96])
        din("nmwT", [2, 128, KC])
        din("nfwT", [2, 128, KC])
        din("ev_in_w", [D, EVEN_IN])
        din("od_in_w", [D, ODD_IN])
        self.dscr("XS", [D, T])
        self.dscr("PT", [ODD_IN, T])
        self.dscr("WB", [D, ODD_IN], BF16)
        din("ssd_cw", [128, 20, 3])
        din("ssd_cb", [128, 20])
        din("ssd_dtb", [48, 1])
        din("ssd_alog", [48, 1])
        din("ssd_d", [1, 24])
        din("ssd_nw", [128, 12])
        self.dscr("TOK", [T, 2048], BF16)
        self.dscr("BCT", [2048, T], BF16)
        self.dscr("YF", [T, 2048])
        self.dscr("YT", [D, T], BF16)
        for nm in ("s5_lre", "s5_lim", "s5_dlt"):
            din(nm, [128, 32])
        for nm in ("s5_bre", "s5_bim", "s5_cre", "s5_cim"):
            din(nm, [128, 32, 16])
        din("s5_dsk", [128, 4])
        din("s5_glu_w", [512, 512])
        din("fnwT", [128, KC])
        din("gdn_dtb", [16, 1])
        din("gdn_alog", [16, 1])
        din("gdn_cw", [128, 24, 3])
        din("gdn_nw", [1, 128])
        din("ml_ib", [8, 1])
        din("ml_fb", [8, 1])
        din("ml_nw", [1, 1024])
        din("ev_out_w", [D, D])
        din("od_out_w", [D, D])
        din("ffn_up_w", [2, D, 2 * FFN])
        din("ffn_down_w", [2, FFN, D])
        din("ffn_cw", [2, 128, 43, 9])
        self.dscr("WO", [D, D], BF16)
        self.dscr("WU", [D, 2 * FFN], BF16)
        self.dscr("WD", [FFN, D], BF16)
        self.dscr("XS2", [D, T])
        self.dout("outT", [D, TL])
        if 'YTin' in self.dbg:
            din('YTin', [D, T])
        if 'XSin' in self.dbg:
            din('XSin', [D, T])
        dshapes = {'dbg_mod': [128, 96, 2], 'dbg_h': [D, T], 'dbg_gates': [2, 128, 34 * 48]}
        for k in self.dbg:
            if k in dshapes:
                self.dout(k, dshapes[k])
        with ExitStack() as es:
            self.P = Prog(nc, es)
            self.consts(es)
            for layer in range(2):
                if 'XSin' in self.dbg:
                    if layer == 0:
                        for r in range(0, D, 512):
                            self.P.dma('gpsimd', self.dr['XS'][r:r + 512, :], self.dr['XSin'][r:r + 512, :], writes=['xsin'])
                        self.P.barrier()
                        continue
                self.layer(layer)
                if self.stage <= layer * 10 + 9:
                    break
            self.P.finish()
        return nc

    def consts(self, es):
        P = self.P
        sb = self.sb
        self.ident_b = sb(es, "ident_b", [128, 128], BF16)
        self.ident_f = sb(es, "ident_f", [128, 128], F32)
        self.ones_b = sb(es, "ones_b", [128, 128], BF16)
        self.ones_f = sb(es, "ones_f", [128, 128], F32)
        self.negI = sb(es, "negI", [128, 128], F32)
        self.MGT = sb(es, "MGT", [128, 128], F32)
        self.MLT = sb(es, "MLT", [128, 128], F32)
        self.TLE = sb(es, "TLE", [128, 128], F32)
        self.TGE = sb(es, "TGE", [128, 128], F32)
        g = 'gpsimd'
        P.op(g, lambda e: e.memset(self.ones_f[:], 1.0), writes=['ones_f'])
        P.op(g, lambda e: e.memset(self.ones_b[:], 1.0), writes=['ones_b'])

        def sel(out, cm, step, cmp, key, fill=0.0, src=None):
            src = self.ones_f if src is None else src
            P.op(g, lambda e: e.affine_select(out=out[:], in_=src[:], pattern=[[step, 128]], base=0,
                                              channel_multiplier=cm, compare_op=cmp, fill=fill),
                 reads=['ones_f'], writes=[key])
        sel(self.ident_f, 1, -1, ALU.is_equal, 'ident_f')
        sel(self.MGT, 1, -1, ALU.is_gt, 'MGT')
        sel(self.MLT, -1, 1, ALU.is_gt, 'MLT')
        sel(self.TLE, -1, 1, ALU.is_ge, 'TLE')
        sel(self.TGE, 1, -1, ALU.is_ge, 'TGE')
        P.op(g, lambda e: e.tensor_copy(out=self.ident_b[:], in_=self.ident_f[:]), reads=['ident_f'], writes=['ident_b'])
        P.op(g, lambda e: e.tensor_scalar(out=self.negI[:], in0=self.ident_f[:], scalar1=NEG, scalar2=None, op0=ALU.mult),
             reads=['ident_f'], writes=['negI'])
        self.modT = sb(es, "modT", [128, 96, 2], F32)
        self.s1 = sb(es, "s1", [128, KC, 2], F32)
        self.s2 = sb(es, "s2", [128, KC, 2], F32)
        self.eps_col = sb(es, "eps_col", [128, 1], F32)
        P.op(g, lambda e: e.memset(self.eps_col[:], EPS), writes=['eps_col'])
        self.one_col = sb(es, "one_col", [128, 1], F32)
        P.op(g, lambda e: e.memset(self.one_col[:], 1.0), writes=['one_col'])
        P.barrier()

    def layer(self, layer):
        import os
        if os.environ.get('SKIP12'):
            self.phase_ssd()
            return
        self.phase_mod(layer)
        if self.stage <= layer * 10 + 1:
            return
        self.phase_inproj(layer)
        if self.stage <= layer * 10 + 2:
            return
        if layer == 0 and 'YTin' not in self.dbg:
            import os
            if not os.environ.get('NOS5'):
                self.phase_s5()
            if self.stage <= layer * 10 + 2 or os.environ.get('NOSSD'):
                return
            self.phase_ssd()
            if self.stage <= layer * 10 + 3:
                return
        if layer == 1 and 'YTin' not in self.dbg:
            import os
            if not os.environ.get('NOGDN'):
                self.phase_gdn()
            if not os.environ.get('NOML'):
                self.phase_mlstm()
            if self.stage <= layer * 10 + 3:
                return
        if 'YTin' in self.dbg:
            self.P.dma('gpsimd', self.dr['YT'], self.dr['YTin'], writes=['ytin'])
            self.P.barrier()
        src = self.dr['xT'] if layer == 0 else self.dr['XS']
        self.phase_outproj(layer, src, self.dr['XS2'])
        if self.stage <= layer * 10 + 4:
            return
        self.phase_ffn(layer, self.dr['XS2'], self.dr['XS'])
        if layer == 1:
            self.phase_final(self.dr['XS'])

    def phase_mod(self, layer):
        P, nc = self.P, self.nc
        dr = self.dr
        with ExitStack() as es:
            sb, ps = self.sb, self.ps
            PW = 768
            wt = [sb(es, f"modw{i}", [128, KC, PW], F32) for i in range(2)]
            cc = sb(es, "cc", [128, KC, 2], F32)
            scc = sb(es, "scc", [128, KC, 2], F32)
            mb = sb(es, "mb", [128, 96], F32)
            nmw = sb(es, "nmw", [128, KC], F32)
            nfw = sb(es, "nfw", [128, KC], F32)
            pm = ps(es, "pm", [128, 96, 2], F32)
            P.dma('sync', cc[:], dr["ccT"], writes=['cc'])
            P.dma('sync', mb[:], dr["mod_bT"][layer], writes=['mb'])
            P.dma('sync', nmw[:], dr["nmwT"][layer], writes=['nmw'])
            P.dma('sync', nfw[:], dr["nfwT"][layer], writes=['nfw'])
            P.op('scalar', lambda e: e.activation(out=scc[:], in_=cc[:], func=AF.Silu), reads=['cc'], writes=['scc'])
            wsrc = dr["mod_w"][layer].rearrange("(kc p) n -> p kc n", p=128)
            for pn in range(16):
                w = wt[pn % 2]
                wk = f'modw{pn % 2}'
                P.dma('sync' if pn % 2 == 0 else 'gpsimd', w[:], wsrc[:, :, pn * PW:(pn + 1) * PW], writes=[wk])
                for jj in range(6):
                    j = pn * 6 + jj
                    for kc in range(KC):
                        P.op('tensor', lambda e, w=w, jj=jj, kc=kc, j=j: e.matmul(
                            pm[:, j, :], lhsT=w[:, kc, jj * 128:(jj + 1) * 128], rhs=scc[:, kc, :],
                            start=(kc == 0), stop=(kc == KC - 1)), reads=[wk, 'scc'], writes=['PS:pm'])
            for m in range(2):
                P.op('vector', lambda e, m=m: e.tensor_tensor(out=self.modT[:, :, m], in0=pm[:, :, m], in1=mb[:], op=ALU.add),
                     reads=['PS:pm', 'mb'], writes=['modT'])
            for m in range(2):
                P.op('vector', lambda e, m=m: e.scalar_tensor_tensor(
                    out=self.s1[:, :, m], in0=self.modT[:, 16:32, m], scalar=1.0, in1=nmw[:], op0=ALU.add, op1=ALU.mult),
                    reads=['modT', 'nmw'], writes=['s1'])
                P.op('vector', lambda e, m=m: e.scalar_tensor_tensor(
                    out=self.s2[:, :, m], in0=self.modT[:, 64:80, m], scalar=1.0, in1=nfw[:], op0=ALU.add, op1=ALU.mult),
                    reads=['modT', 'nfw'], writes=['s2'])
            if 'dbg_mod' in self.dbg:
                P.dma('sync', dr['dbg_mod'], self.modT[:], reads=['modT'], writes=['dbg_mod'])
            P.barrier()

    def blocks(self):
        return [(0, TC, 1)] + [(TC + i * 512, 512, 0) for i in range(8)]

    def norm_block(self, src, tok0, nt, m, scale_t, shift_lo, bufs, keys):
        P = self.P
        xt, sq, hT, rstd, tmp, pss = bufs['xt'], bufs['sq'], bufs['hT'], bufs['rstd'], bufs['tmp'], bufs['pss']
        kx, ksq, kh, kr, kt, kp = keys
        P.dma('sync', xt[:, :, :nt], src.rearrange("(kc p) t -> p kc t", p=128)[:, :, tok0:tok0 + nt], writes=[kx])
        P.op('scalar', lambda e: e.activation(out=sq[:, :, :nt], in_=xt[:, :, :nt], func=AF.Square), reads=[kx], writes=[ksq])
        for kc in range(KC):
            P.op('tensor', lambda e, kc=kc: e.matmul(pss[:, :nt], lhsT=self.ones_b[:], rhs=sq[:, kc, :nt],
                                                     start=(kc == 0), stop=(kc == KC - 1)), reads=[ksq, 'ones_b'], writes=[kp])
        P.op('scalar', lambda e: e.activation(out=rstd[:, :nt], in_=pss[:, :nt], func=AF.Sqrt, scale=1.0 / D, bias=self.eps_col[:]),
             reads=[kp, 'eps_col'], writes=[kr])
        P.op('vector', lambda e: e.reciprocal(out=rstd[:, :nt], in_=rstd[:, :nt]), reads=[kr], writes=[kr])
        for kc in range(KC):
            tk = kt + str(kc % 2)
            t = tmp[kc % 2]
            P.op('vector', lambda e, kc=kc, t=t: e.tensor_tensor(out=t[:, :nt], in0=xt[:, kc, :nt], in1=rstd[:, :nt], op=ALU.mult),
                 reads=[kx, kr], writes=[tk])
            P.op('scalar', lambda e, kc=kc, t=t: e.activation(out=hT[:, kc, :nt], in_=t[:, :nt], func=AF.Identity,
                                                              scale=scale_t[:, kc, m:m + 1], bias=self.modT[:, shift_lo + kc, m:m + 1]),
                 reads=[tk, 'modT', 's1', 's2'], writes=[kh + str(kc)])

    def phase_inproj(self, layer):
        P, nc, dr = self.P, self.nc, self.dr
        nin = EVEN_IN if layer == 0 else ODD_IN
        wsrc = dr["ev_in_w"] if layer == 0 else dr["od_in_w"]
        WB = dr["WB"]
        for r in range(0, D, 256):
            P.dma('gpsimd', WB[r:r + 256, :nin], wsrc[r:r + 256, :], writes=[('WB', r)])
        src = dr["xT"] if layer == 0 else dr["XS"]
        ntile = [(c0, min(128, nin - c0)) for c0 in range(0, nin, 128)]
        PWT = 4
        with ExitStack() as es:
            sb, ps = self.sb, self.ps
            bufs = dict(xt=sb(es, "xt", [128, KC, 512], F32), sq=sb(es, "sq", [128, KC, 512], BF16),
                        hT=sb(es, "hT", [128, KC, 512], BF16), rstd=sb(es, "rstd", [128, 512], F32),
                        tmp=[sb(es, f"ntmp{i}", [128, 512], F32) for i in range(2)],
                        pss=ps(es, "pss", [128, 512], F32))
            wb = [sb(es, f"wb{i}", [128, KC, PWT * 128], BF16) for i in range(2)]
            stg = [sb(es, f"stg{i}", [128, 512], F32) for i in range(3)]
            pacc = [ps(es, f"pacc{i}", [128, 512], F32) for i in range(4)]
            wv = WB.rearrange("(kc p) n -> p kc n", p=128)
            hkeys = ['hT' + str(kc) for kc in range(KC)]
            it = 0
            pi = 0
            for (tok0, nt, m) in self.blocks():
                self.norm_block(src, tok0, nt, m, self.s1, 0, bufs, ('xt', 'sq', 'hT', 'rstd', 'ntmp', 'PS:pss'))
                if 'dbg_h' in self.dbg:
                    hf = bufs['xt']
                    P.op('vector', lambda e: e.tensor_copy(out=hf[:, :, :nt], in_=bufs['hT'][:, :, :nt]), reads=hkeys + ['xt'], writes=['xt'])
                    P.dma('sync', dr['dbg_h'].rearrange("(kc p) t -> p kc t", p=128)[:, :, tok0:tok0 + nt], hf[:, :, :nt], reads=['xt'], writes=['dbg_h'])
                for p0 in range(0, len(ntile), PWT):
                    tiles = ntile[p0:p0 + PWT]
                    c0 = tiles[0][0]
                    cw = sum(t[1] for t in tiles)
                    w = wb[pi % 2]
                    wk = f'wb{pi % 2}'
                    pi += 1
                    P.dma('sync', w[:, :, :cw], wv[:, :, c0:c0 + cw], reads=[('WB', r) for r in range(0, D, 256)], writes=[wk])
                    for (tc0, tw) in tiles:
                        pa = pacc[it % 4]
                        pk = f'PS:pacc{it % 4}'
                        st = stg[it % 3]
                        sk = f'stg{it % 3}'
                        for kc in range(KC):
                            P.op('tensor', lambda e, kc=kc, pa=pa, w=w, tc0=tc0, tw=tw, c0=c0: e.matmul(
                                pa[:tw, :nt], lhsT=w[:, kc, tc0 - c0:tc0 - c0 + tw], rhs=bufs['hT'][:, kc, :nt],
                                start=(kc == 0), stop=(kc == KC - 1)), reads=[wk, hkeys[kc]], writes=[pk])
                        eng = 'scalar' if it % 2 == 0 else 'vector'
                        if eng == 'scalar':
                            P.op('scalar', lambda e, pa=pa, st=st, tw=tw: e.copy(out=st[:tw, :nt], in_=pa[:tw, :nt]), reads=[pk], writes=[sk])
                        else:
                            P.op('vector', lambda e, pa=pa, st=st, tw=tw: e.tensor_copy(out=st[:tw, :nt], in_=pa[:tw, :nt]), reads=[pk], writes=[sk])
                        P.dma('gpsimd', dr['PT'][tc0:tc0 + tw, tok0:tok0 + nt], st[:tw, :nt], reads=[sk], writes=[('PT', tc0, tok0)])
                        it += 1
            P.barrier()


def _ssd_methods():
    pass


def build_program(stage=99, dbg=()):
    b = B(stage, dbg)
    for name in dbg:
        pass
    return b


def make_inputs_small(inputs, b):
    f = np.float32
    x = np.asarray(inputs['x'], f)
    ctx = np.asarray(inputs['ctx'], f)
    c = np.asarray(inputs['c'], f)
    c_ctx = np.asarray(inputs['c_ctx'], f)
    xT = np.ascontiguousarray(np.concatenate([ctx[b], x[b]], axis=0).T)
    cc = np.stack([c[b], c_ctx], axis=0)
    ccT = np.ascontiguousarray(cc.reshape(2, KC, 128).transpose(2, 1, 0))
    return {"xT": xT, "ccT": ccT}


def make_inputs(inputs, b):
    f = np.float32
    x = np.asarray(inputs['x'], f)
    ctx = np.asarray(inputs['ctx'], f)
    c = np.asarray(inputs['c'], f)
    c_ctx = np.asarray(inputs['c_ctx'], f)
    xT = np.ascontiguousarray(np.concatenate([ctx[b], x[b]], axis=0).T)
    cc = np.stack([c[b], c_ctx], axis=0)
    ccT = np.ascontiguousarray(cc.reshape(2, KC, 128).transpose(2, 1, 0))
    mod_b = np.asarray(inputs['mod_b'], f)
    m = {
        "xT": xT, "ccT": ccT,
        "mod_w": np.ascontiguousarray(np.asarray(inputs['mod_w'], f)),
        "mod_bT": np.ascontiguousarray(mod_b.reshape(2, 96, 128).transpose(0, 2, 1)),
        "nmwT": np.ascontiguousarray(np.asarray(inputs['norm_mix_w'], f).reshape(2, KC, 128).transpose(0, 2, 1)),
        "nfwT": np.ascontiguousarray(np.asarray(inputs['norm_ffn_w'], f).reshape(2, KC, 128).transpose(0, 2, 1)),
        "ev_in_w": np.ascontiguousarray(np.asarray(inputs['ev_in_w'], f)[0]),
        "od_in_w": np.ascontiguousarray(np.asarray(inputs['od_in_w'], f)[0]),
    }
    m["ev_out_w"] = np.ascontiguousarray(np.asarray(inputs['ev_out_w'], f)[0])
    m["od_out_w"] = np.ascontiguousarray(np.asarray(inputs['od_out_w'], f)[0])
    m["ffn_up_w"] = np.ascontiguousarray(np.asarray(inputs['ffn_up_w'], f))
    m["ffn_down_w"] = np.ascontiguousarray(np.asarray(inputs['ffn_down_w'], f))
    fcw = np.asarray(inputs['ffn_conv_w'], f)
    fcw = np.concatenate([fcw.reshape(2, 9, FFN), np.zeros((2, 9, 43 * 128 - FFN), f)], axis=2)
    m["ffn_cw"] = np.ascontiguousarray(fcw.reshape(2, 9, 43, 128).transpose(0, 3, 2, 1))
    m["fnwT"] = np.ascontiguousarray(np.asarray(inputs['final_norm_w'], f).reshape(KC, 128).T)
    m["gdn_dtb"] = np.ascontiguousarray(np.asarray(inputs['gdn_dt_bias'], f)[0].reshape(16, 1))
    m["gdn_alog"] = np.ascontiguousarray(np.asarray(inputs['gdn_a_log'], f)[0].reshape(16, 1))
    gcw = np.asarray(inputs['gdn_conv_w'], f)[0]
    m["gdn_cw"] = np.ascontiguousarray(gcw.reshape(3, 24, 128).transpose(2, 1, 0))
    m["gdn_nw"] = np.ascontiguousarray(np.asarray(inputs['gdn_norm_w'], f)[0].reshape(1, 128))
    m["ml_ib"] = np.ascontiguousarray(np.asarray(inputs['mlstm_igate_b'], f)[0].reshape(8, 1))
    m["ml_fb"] = np.ascontiguousarray(np.asarray(inputs['mlstm_fgate_b'], f)[0].reshape(8, 1))
    m["ml_nw"] = np.ascontiguousarray(np.asarray(inputs['mlstm_norm_w'], f)[0].reshape(1, 1024))
    def pair(x):
        sh = x.shape[3:]
        x = x.reshape((2, 16, 2, 64) + sh)
        x = np.moveaxis(x, (2, 3), (0, 1))
        return np.ascontiguousarray(x.reshape((128, 32) + sh))
    m["s5_lre"] = pair(np.asarray(inputs['s5_lam_re'], f)[0])
    m["s5_lim"] = pair(np.asarray(inputs['s5_lam_im'], f)[0])
    m["s5_dlt"] = pair(np.repeat(np.asarray(inputs['s5_log_step'], f)[0][:, :, None], 64, axis=2))
    m["s5_bre"] = pair(np.asarray(inputs['s5_b_re'], f)[0])
    m["s5_bim"] = pair(np.asarray(inputs['s5_b_im'], f)[0])
    m["s5_cre"] = pair(np.asarray(inputs['s5_c_re'], f)[0].transpose(0, 1, 3, 2))
    m["s5_cim"] = pair(np.asarray(inputs['s5_c_im'], f)[0].transpose(0, 1, 3, 2))
    m["s5_dsk"] = np.ascontiguousarray(np.asarray(inputs['s5_d'], f)[0].reshape(4, 128).T)
    m["s5_glu_w"] = np.ascontiguousarray(np.asarray(inputs['s5_glu_w'], f)[0])
    cw = np.asarray(inputs['ssd_conv_w'], f)[0]
    m["ssd_cw"] = np.ascontiguousarray(cw.reshape(3, 20, 128).transpose(2, 1, 0))
    m["ssd_cb"] = np.ascontiguousarray(np.asarray(inputs['ssd_conv_b'], f)[0].reshape(20, 128).T)
    m["ssd_dtb"] = np.ascontiguousarray(np.asarray(inputs['ssd_dt_bias'], f)[0].reshape(48, 1))
    m["ssd_alog"] = np.ascontiguousarray(np.asarray(inputs['ssd_a_log'], f)[0].reshape(48, 1))
    m["ssd_d"] = np.ascontiguousarray(np.asarray(inputs['ssd_d'], f)[0].reshape(1, 24))
    m["ssd_nw"] = np.ascontiguousarray(np.asarray(inputs['ssd_norm_w'], f)[0].reshape(12, 128).T)
    return m


def kernel(**inputs):
    b = B()
    nc = b.build()
    n = 8
    shared = make_inputs(inputs, 0)
    in_maps = []
    for i in range(n):
        mi = dict(shared)
        if i % 4 != 0:
            pi = make_inputs_small(inputs, i % 4)
            mi.update(pi)
        in_maps.append(mi)
    res = run_bass_kernel_spmd(nc, in_maps, core_ids=list(range(n)))
    out = np.stack([np.ascontiguousarray(res.results[i]["outT"].T) for i in range(4)], axis=0)
    return out.astype(np.float32)


def chunk_order(d):
    if d == 0:
        return list(range(34))
    return [1, 0] + list(range(33, 1, -1))


def phase_ssd(self):
    import os
    P, nc, dr = self.P, self.nc, self.dr
    sb, ps = self.sb, self.ps
    PT = dr['PT']
    with ExitStack() as es0:
        dt_tok = sb(es0, "dt_tok", [128, 34, 48], F32)
        la_tok = sb(es0, "la_tok", [128, 34, 48], F32)
        with ExitStack() as es:
            dtr = sb(es, "dtr", [128, T], F32)
            laT = sb(es, "laT", [128, T], F32)
            P.op('gpsimd', lambda e: e.memset(dtr[:], 0.0), writes=['dtr'])
            P.op('gpsimd', lambda e: e.memset(laT[:], 0.0), writes=['laT'])
            dtb = sb(es, "dtb", [48, 1], F32)
            alog = sb(es, "alog", [48, 1], F32)
            nega = sb(es, "nega", [48, 1], F32)
            ptr = ps(es, "ptr_g", [128, 512], F32)
            if os.environ.get('SKIP12') or os.environ.get('DTRMEM'):
                P.op('gpsimd', lambda e: e.memset(dtr[:48, :], 0.5), writes=['dtr'])
            else:
                P.dma('sync', dtr[:48, :], PT[4608:4656, :], writes=['dtr'])
            P.dma('sync', dtb[:], dr['ssd_dtb'], writes=['dtb'])
            P.dma('sync', alog[:], dr['ssd_alog'], writes=['alog'])
            P.op('scalar', lambda e: e.activation(out=nega[:], in_=alog[:], func=AF.Exp), reads=['alog'], writes=['nega'])
            P.op('vector', lambda e: e.tensor_scalar(out=nega[:], in0=nega[:], scalar1=-1.0, scalar2=None, op0=ALU.mult), reads=['nega'], writes=['nega'])
            P.op('scalar', lambda e: e.activation(out=dtr[:48, :], in_=dtr[:48, :], func=AF.Exp, bias=dtb[:]), reads=['dtr', 'dtb'], writes=['dtr'])
            P.op('scalar', lambda e: e.activation(out=dtr[:48, :], in_=dtr[:48, :], func=AF.Ln, bias=self.one_col[:48, :]), reads=['dtr', 'one_col'], writes=['dtr'])
            P.op('vector', lambda e: e.tensor_scalar(out=laT[:48, :], in0=dtr[:48, :], scalar1=nega[:], scalar2=None, op0=ALU.mult), reads=['dtr', 'nega'], writes=['laT'])
            import os
            CUT = int(os.environ.get('CUT', '99'))
            if CUT <= 2:
                P.op('gpsimd', lambda e: e.memset(dt_tok[:], 0.05), writes=['dt_tok'])
                P.op('gpsimd', lambda e: e.memset(la_tok[:], -0.01), writes=['la_tok'])
            VV = os.environ.get('VV', 'ABCD')
            for c in range((34 if CUT > 3 else 1) if CUT > 2 else 0):
                if 'A' in VV:
                    P.op('tensor', lambda e, c=c: e.matmul(ptr[:, 0:48], lhsT=dtr[:, c * 128:(c + 1) * 128], rhs=self.ident_f[:, :48], start=True, stop=True),
                         reads=['dtr', 'ident_f'], writes=['PS:ptr_g'])
                if 'B' in VV:
                    P.op('tensor', lambda e, c=c: e.matmul(ptr[:, 64:112], lhsT=laT[:, c * 128:(c + 1) * 128], rhs=self.ident_f[:, :48], start=True, stop=True),
                         reads=['laT', 'ident_f'], writes=['PS:ptr_g'])
                if 'C' in VV:
                    P.op('vector', lambda e, c=c: e.tensor_copy(out=dt_tok[:, c, :], in_=ptr[:, 0:48]), reads=['PS:ptr_g'], writes=['dt_tok'])
                if 'D' in VV:
                    P.op('scalar', lambda e, c=c: e.copy(out=la_tok[:, c, :], in_=ptr[:, 64:112]), reads=['PS:ptr_g'], writes=['la_tok'])
            P.barrier()
        if self.sub <= 1:
            return
        with ExitStack() as es:
            xin = [sb(es, f"xin{i}", [128, 20, 514], F32) for i in range(2)]
            cs = sb(es, "cs", [128, 20, 512], BF16)
            acc = [sb(es, f"cacc{i}", [128, 512], F32) for i in range(2)]
            cw = sb(es, "cw", [128, 20, 3], F32)
            cb = sb(es, "cb", [128, 20], F32)
            tokst = [sb(es, f"tokst{i}", [128, 2048], BF16) for i in range(2)]
            ptA = ps(es, "ptA", [128, 8, 128], BF16)
            ptB = ps(es, "ptB", [128, 8, 128], BF16)
            P.dma('sync', cw[:], dr['ssd_cw'], writes=['cw'])
            P.dma('sync', cb[:], dr['ssd_cb'], writes=['cb'])
            src = PT[2048:4608, :].rearrange("(f p) t -> p f t", p=128)
            ci = 0
            for bi, (tok0, nt, m) in enumerate(self.blocks()):
                x = xin[bi % 2]
                xk = f'xin{bi % 2}'
                seq0, seq1 = (0, TC) if m == 1 else (TC, T)
                lo = max(tok0 - 1, seq0)
                hi = min(tok0 + nt + 1, seq1)
                P.dma('sync', x[:, :, lo - (tok0 - 1):hi - (tok0 - 1)], src[:, :, lo:hi], writes=[xk])
                if lo > tok0 - 1:
                    P.op('gpsimd', lambda e, x=x: e.memset(x[:, :, 0:1], 0.0), writes=[xk])
                if hi < tok0 + nt + 1:
                    P.op('gpsimd', lambda e, x=x, nt=nt: e.memset(x[:, :, nt + 1:nt + 2], 0.0), writes=[xk])
                for f in range(20):
                    a = acc[f % 2]
                    ak = f'cacc{f % 2}'
                    P.op('vector', lambda e, x=x, a=a, f=f, nt=nt: e.tensor_scalar(out=a[:, :nt], in0=x[:, f, 0:nt], scalar1=cw[:, f, 0:1], scalar2=None, op0=ALU.mult),
                         reads=[xk, 'cw'], writes=[ak])
                    P.op('vector', lambda e, x=x, a=a, f=f, nt=nt: e.scalar_tensor_tensor(out=a[:, :nt], in0=x[:, f, 1:nt + 1], scalar=cw[:, f, 1:2], in1=a[:, :nt], op0=ALU.mult, op1=ALU.add),
                         reads=[xk, 'cw', ak], writes=[ak])
                    P.op('vector', lambda e, x=x, a=a, f=f, nt=nt: e.scalar_tensor_tensor(out=a[:, :nt], in0=x[:, f, 2:nt + 2], scalar=cw[:, f, 2:3], in1=a[:, :nt], op0=ALU.mult, op1=ALU.add),
                         reads=[xk, 'cw', ak], writes=[ak])
                    P.op('scalar', lambda e, a=a, f=f, nt=nt: e.activation(out=cs[:, f, :nt], in_=a[:, :nt], func=AF.Silu, bias=cb[:, f:f + 1]),
                         reads=[ak, 'cb'], writes=[('cs', f)])
                P.dma('gpsimd', dr['BCT'][0:1024, :].rearrange("(f p) t -> p f t", p=128)[:, :, tok0:tok0 + nt], cs[:, 12:20, :nt],
                      reads=[('cs', f) for f in range(12, 20)], writes=[('BCT', tok0)])
                for ch in range(nt // 128):
                    tk = tokst[ci % 2]
                    tkk = f'tokst{ci % 2}'
                    ci += 1
                    for f in range(16):
                        pt_ = ptA if f < 8 else ptB
                        P.op('tensor', lambda e, f=f, ch=ch, pt_=pt_: e.transpose(out=pt_[:, f % 8, :], in_=cs[:, f, ch * 128:(ch + 1) * 128], identity=self.ident_b[:]),
                             reads=[('cs', f), 'ident_b'], writes=['PS:ptA' if f < 8 else 'PS:ptB'])
                    P.op('vector', lambda e, tk=tk: e.tensor_copy(out=tk[:, 0:1024], in_=ptA[:].rearrange("p a b -> p (a b)")), reads=['PS:ptA'], writes=[tkk])
                    P.op('scalar', lambda e, tk=tk: e.copy(out=tk[:, 1024:2048], in_=ptB[:].rearrange("p a b -> p (a b)")), reads=['PS:ptB'], writes=[tkk])
                    P.dma('gpsimd', dr['TOK'][tok0 + ch * 128:tok0 + (ch + 1) * 128, :], tk[:], reads=[tkk], writes=[('TOK', tok0 + ch * 128)])
            P.barrier()
        if 'dbg_gates' in self.dbg:
            P.dma('sync', dr['dbg_gates'][0], dt_tok[:].rearrange("p a b -> p (a b)"), reads=[], writes=['dbg_gates'])
            P.dma('sync', dr['dbg_gates'][1], la_tok[:].rearrange("p a b -> p (a b)"), reads=[], writes=['dbg_gates'])
            P.barrier()
        if self.sub <= 2:
            return
        with ExitStack() as es:
            S_f = sb(es, "S_f", [128, 24, 64], F32)
            S_b = sb(es, "S_b", [128, 24, 64], BF16)
            tok = [sb(es, f"tok{i}", [128, 2048], BF16) for i in range(2)]
            bct = [sb(es, f"bct{i}", [128, 8, 128], BF16) for i in range(2)]
            Lm = [sb(es, f"Lm{i}", [128, 128], F32) for i in range(6)]
            Dm = [sb(es, f"Dm{i}", [128, 128], F32) for i in range(6)]
            WT = [sb(es, f"WT{i}", [128, 128], BF16) for i in range(6)]
            v1 = [sb(es, f"v1_{i}", [128, 64], BF16) for i in range(6)]
            v2 = [sb(es, f"v2_{i}", [128, 64], BF16) for i in range(6)]
            yA = sb(es, "yA", [128, 384], F32)
            ysb = [sb(es, f"ysb{i}", [128, 1536], F32) for i in range(2)]
            yfl = sb(es, "yfl", [128, 1536], F32)
            ytk = sb(es, "ytk", [128, 1536], BF16)
            ecum = sb(es, "ecum", [128, 24], F32)
            cdec = sb(es, "cdec", [128, 24], F32)
            dB = sb(es, "dB", [128, 24], F32)
            nw = sb(es, "ssd_nw_sb", [128, 12], F32)
            ytb = sb(es, "ytb", [128, 12, 512], BF16)
            zT = sb(es, "zT", [128, 12, 512], F32)
            yg = sb(es, "yg", [128, 12, 512], F32)
            sqg = sb(es, "sqg", [128, 12, 512], BF16)
            rstd = sb(es, "rstd_g", [128, 512], F32)
            ot = sb(es, "ot", [128, 12, 512], BF16)
            psegA = ps(es, "psegA", [128, 4, 128], F32)
            psegB = ps(es, "psegB", [128, 4, 128], F32)
            pyA = ps(es, "pyA", [128, 512], F32)
            pyB = ps(es, "pyB", [128, 512], F32)
            pS = ps(es, "pS", [128, 512], F32)
            ptT = ps(es, "ptT", [128, 8, 128], BF16)
            ptU = ps(es, "ptU", [128, 8, 128], BF16)
            pss = ps(es, "pss_g", [128, 512], F32)
            P.dma('sync', dB[:], dr['ssd_d'].partition_broadcast(128), writes=['dB'])
            P.dma('sync', nw[:], dr['ssd_nw'], writes=['ssd_nw'])
            seg = lambda r: (psegA[:, r, :] if r < 4 else psegB[:, r - 4, :])
            segk = lambda r: ('PS:psegA' if r < 4 else 'PS:psegB')
            li = 0
            for d in range(2):
                MASK = self.MGT if d == 0 else self.MLT
                TRI = self.TLE if d == 0 else self.TGE
                mk, tk_ = ('MGT', 'TLE') if d == 0 else ('MLT', 'TGE')
                ecol = 127 if d == 0 else 0
                P.op('gpsimd', lambda e: e.memset(S_f[:], 0.0), writes=[('S_f', h) for h in range(24)])
                P.op('gpsimd', lambda e: e.memset(S_b[:], 0.0), writes=[('S_b', h) for h in range(24)])
                for c in chunk_order(d):
                    t0 = c * 128
                    tb = tok[li % 2]
                    tbk = f'tok{li % 2}'
                    bc = bct[li % 2]
                    bck = f'bct{li % 2}'
                    ys = ysb[li % 2]
                    ysk = f'ysb{li % 2}'
                    li += 1
                    P.dma('sync', tb[:], dr['TOK'][t0:t0 + 128, :], writes=[tbk])
                    P.dma('sync', bc[:], dr['BCT'][0:1024, :].rearrange("(f p) t -> p f t", p=128)[:, :, t0:t0 + 128], writes=[bck])
                    lac = la_tok[:, c, d * 24:(d + 1) * 24]
                    P.op('tensor', lambda e, lac=lac: e.matmul(pss[:, 128:152], lhsT=TRI[:], rhs=lac, start=True, stop=True),
                         reads=[tk_, 'la_tok'], writes=['PS:pss_g'])
                    P.op('tensor', lambda e, lac=lac: e.matmul(pss[:, 160:184], lhsT=self.ones_f[:], rhs=lac, start=True, stop=True),
                         reads=['ones_f', 'la_tok'], writes=['PS:pss_g'])
                    P.op('scalar', lambda e: e.activation(out=ecum[:], in_=pss[:, 128:152], func=AF.Exp), reads=['PS:pss_g'], writes=['ecum'])
                    P.op('scalar', lambda e: e.activation(out=cdec[:], in_=pss[:, 160:184], func=AF.Exp), reads=['PS:pss_g'], writes=['cdec'])
                    for g in range(4):
                        hs = [g * 6 + r for r in range(6)]
                        P.op('tensor', lambda e, g=g, bc=bc: e.matmul(pss[:, 0:128], lhsT=bc[:, g, :], rhs=bc[:, 4 + g, :], start=True, stop=True),
                             reads=[bck], writes=['PS:pss_g'])
                        for r, h in enumerate(hs):
                            P.op('vector', lambda e, r=r, h=h, c=c: e.tensor_scalar(out=Lm[r][:], in0=MASK[:], scalar1=la_tok[:, c, d * 24 + h:d * 24 + h + 1], scalar2=None, op0=ALU.mult),
                                 reads=[mk, 'la_tok'], writes=[f'Lm{r}'])
                        for r, h in enumerate(hs):
                            P.op('tensor', lambda e, r=r: e.matmul(seg(r), lhsT=Lm[r][:], rhs=TRI[:], start=True, stop=False), reads=[f'Lm{r}', tk_], writes=[segk(r)])
                            P.op('tensor', lambda e, r=r: e.matmul(seg(r), lhsT=self.negI[:], rhs=MASK[:], start=False, stop=True), reads=['negI', mk], writes=[segk(r)])
                        for r, h in enumerate(hs):
                            P.op('scalar', lambda e, r=r: e.activation(out=Dm[r][:], in_=seg(r), func=AF.Exp), reads=[segk(r)], writes=[f'Dm{r}'])
                        for r, h in enumerate(hs):
                            P.op('vector', lambda e, r=r: e.tensor_tensor(out=WT[r][:], in0=Dm[r][:], in1=pss[:, 0:128], op=ALU.mult),
                                 reads=[f'Dm{r}', 'PS:pss_g'], writes=[f'WT{r}'])
                            P.op('gpsimd', lambda e, r=r, h=h, c=c, tb=tb: e.tensor_scalar(out=v1[r][:], in0=tb[:, h * 64:(h + 1) * 64], scalar1=dt_tok[:, c, d * 24 + h:d * 24 + h + 1], scalar2=None, op0=ALU.mult),
                                 reads=[tbk, 'dt_tok'], writes=[f'v1_{r}'])
                            P.op('gpsimd', lambda e, r=r: e.tensor_scalar(out=v2[r][:], in0=v1[r][:], scalar1=Dm[r][:, ecol:ecol + 1], scalar2=None, op0=ALU.mult),
                                 reads=[f'v1_{r}', f'Dm{r}'], writes=[f'v2_{r}'])
                        for r, h in enumerate(hs):
                            P.op('tensor', lambda e, r=r: e.matmul(pyA[:, r * 64:(r + 1) * 64], lhsT=WT[r][:], rhs=v1[r][:], start=True, stop=True),
                                 reads=[f'WT{r}', f'v1_{r}'], writes=['PS:pyA'])
                            P.op('tensor', lambda e, r=r, h=h, g=g, bc=bc: e.matmul(pyB[:, r * 64:(r + 1) * 64], lhsT=bc[:, 4 + g, :], rhs=S_b[:, h, :], start=True, stop=True),
                                 reads=[bck, ('S_b', h)], writes=['PS:pyB'])
                            P.op('tensor', lambda e, r=r, g=g, tb=tb: e.matmul(pS[:, r * 64:(r + 1) * 64], lhsT=tb[:, 1536 + g * 128:1536 + (g + 1) * 128], rhs=v2[r][:], start=True, stop=True),
                                 reads=[tbk, f'v2_{r}'], writes=['PS:pS'])
                        P.op('scalar', lambda e: e.copy(out=yA[:], in_=pyA[:, 0:384]), reads=['PS:pyA'], writes=['yA'])
                        for r, h in enumerate(hs):
                            P.op('vector', lambda e, r=r, h=h, ys=ys: e.scalar_tensor_tensor(out=ys[:, h * 64:(h + 1) * 64], in0=pyB[:, r * 64:(r + 1) * 64], scalar=ecum[:, h:h + 1], in1=yA[:, r * 64:(r + 1) * 64], op0=ALU.mult, op1=ALU.add),
                                 reads=['PS:pyB', 'ecum', 'yA'], writes=[ysk])
                            P.op('vector', lambda e, r=r, h=h: e.scalar_tensor_tensor(out=S_f[:, h, :], in0=S_f[:, h, :], scalar=cdec[:, h:h + 1], in1=pS[:, r * 64:(r + 1) * 64], op0=ALU.mult, op1=ALU.add),
                                 reads=[('S_f', h), 'cdec', 'PS:pS'], writes=[('S_f', h)])
                            P.op('scalar', lambda e, h=h: e.copy(out=S_b[:, h, :], in_=S_f[:, h, :]), reads=[('S_f', h)], writes=[('S_b', h)])
                    if d == 0:
                        P.dma('gpsimd', dr['YF'][t0:t0 + 128, 0:1536], ys[:], reads=[ysk], writes=[('YF', c)])
                    else:
                        P.dma('sync', yfl[:], dr['YF'][t0:t0 + 128, 0:1536], reads=[('YF', c)], writes=['yfl'])
                        P.op('vector', lambda e, ys=ys: e.tensor_tensor(out=yfl[:], in0=yfl[:], in1=ys[:], op=ALU.add), reads=['yfl', ysk], writes=['yfl'])
                        for h in range(24):
                            P.op('vector', lambda e, h=h, tb=tb: e.scalar_tensor_tensor(out=ytk[:, h * 64:(h + 1) * 64], in0=tb[:, h * 64:(h + 1) * 64], scalar=dB[:, h:h + 1], in1=yfl[:, h * 64:(h + 1) * 64], op0=ALU.mult, op1=ALU.add),
                                 reads=[tbk, 'dB', 'yfl'], writes=['ytk'])
                        if c < 2:
                            tok0, nt, ch = 0, TC, c
                        else:
                            tok0, nt, ch = TC + ((c - 2) // 4) * 512, 512, (c - 2) % 4
                        for f in range(12):
                            pt_ = ptT if f < 8 else ptU
                            P.op('tensor', lambda e, f=f, pt_=pt_: e.transpose(out=pt_[:, f % 8, :], in_=ytk[:, f * 128:(f + 1) * 128], identity=self.ident_b[:]),
                                 reads=['ytk', 'ident_b'], writes=['PS:ptT' if f < 8 else 'PS:ptU'])
                        P.op('vector', lambda e, ch=ch: e.tensor_copy(out=ytb[:, 0:8, ch * 128:(ch + 1) * 128], in_=ptT[:]), reads=['PS:ptT'], writes=['ytb'])
                        P.op('scalar', lambda e, ch=ch: e.copy(out=ytb[:, 8:12, ch * 128:(ch + 1) * 128], in_=ptU[:, 0:4, :]), reads=['PS:ptU'], writes=['ytb'])
                        if ch == 0:
                            P.dma('sync', zT[:, :, :nt], PT[512:2048, :].rearrange("(f p) t -> p f t", p=128)[:, :, tok0:tok0 + nt], writes=['zT'])
                            P.op('scalar', lambda e, nt=nt: e.activation(out=zT[:, :, :nt], in_=zT[:, :, :nt], func=AF.Silu), reads=['zT'], writes=['zT'])
                            P.op('vector', lambda e, nt=nt: e.tensor_tensor(out=yg[:, :, :nt], in0=ytb[:, :, :nt], in1=zT[:, :, :nt], op=ALU.mult), reads=['ytb', 'zT'], writes=['yg'])
                            P.op('scalar', lambda e, nt=nt: e.activation(out=sqg[:, :, :nt], in_=yg[:, :, :nt], func=AF.Square), reads=['yg'], writes=['sqg'])
                            for gq in range(4):
                                for i3 in range(3):
                                    P.op('tensor', lambda e, gq=gq, i3=i3, nt=nt: e.matmul(pss[:, :nt], lhsT=self.ones_b[:], rhs=sqg[:, gq * 3 + i3, :nt], start=(i3 == 0), stop=(i3 == 2)),
                                         reads=['sqg', 'ones_b'], writes=['PS:pss_g'])
                                P.op('scalar', lambda e, nt=nt: e.activation(out=rstd[:, :nt], in_=pss[:, :nt], func=AF.Sqrt, scale=1.0 / 384, bias=self.eps_col[:]),
                                     reads=['PS:pss_g', 'eps_col'], writes=['rstd_g'])
                                P.op('vector', lambda e, nt=nt: e.reciprocal(out=rstd[:, :nt], in_=rstd[:, :nt]), reads=['rstd_g'], writes=['rstd_g'])
                                for i3 in range(3):
                                    f = gq * 3 + i3
                                    P.op('vector', lambda e, f=f, nt=nt: e.scalar_tensor_tensor(out=ot[:, f, :nt], in0=yg[:, f, :nt], scalar=nw[:, f:f + 1], in1=rstd[:, :nt], op0=ALU.mult, op1=ALU.mult),
                                         reads=['yg', 'ssd_nw', 'rstd_g'], writes=['ot'])
                            P.dma('gpsimd', dr['YT'][512:2048, :].rearrange("(f p) t -> p f t", p=128)[:, :, tok0:tok0 + nt], ot[:, :, :nt], reads=['ot'], writes=[('YT', 'ssd', tok0)])
            P.barrier()


B.phase_ssd = phase_ssd


def norm_cols(self, xt, sq, hT, rstd, tmp, pss, c0, n, m, scale_t, shift_lo, tag):
    P = self.P
    kx, ksq, kr, kp = tag + 'xt', tag + 'sq', tag + 'rstd', 'PS:' + tag + 'pss'
    P.op('scalar', lambda e: e.activation(out=sq[:, :, c0:c0 + n], in_=xt[:, :, c0:c0 + n], func=AF.Square), reads=[kx], writes=[ksq])
    for kc in range(KC):
        P.op('tensor', lambda e, kc=kc: e.matmul(pss[:, :n], lhsT=self.ones_b[:], rhs=sq[:, kc, c0:c0 + n],
                                                 start=(kc == 0), stop=(kc == KC - 1)), reads=[ksq, 'ones_b'], writes=[kp])
    P.op('scalar', lambda e: e.activation(out=rstd[:, :n], in_=pss[:, :n], func=AF.Sqrt, scale=1.0 / D, bias=self.eps_col[:]),
         reads=[kp, 'eps_col'], writes=[kr])
    P.op('vector', lambda e: e.reciprocal(out=rstd[:, :n], in_=rstd[:, :n]), reads=[kr], writes=[kr])
    for kc in range(KC):
        tk = tag + 'ntmp' + str(kc % 2)
        t = tmp[kc % 2]
        P.op('vector', lambda e, kc=kc, t=t: e.tensor_tensor(out=t[:, :n], in0=xt[:, kc, c0:c0 + n], in1=rstd[:, :n], op=ALU.mult),
             reads=[kx, kr], writes=[tk])
        P.op('scalar', lambda e, kc=kc, t=t: e.activation(out=hT[:, kc, c0:c0 + n], in_=t[:, :n], func=AF.Identity,
                                                          scale=scale_t[:, kc, m:m + 1], bias=self.modT[:, shift_lo + kc, m:m + 1]),
             reads=[tk, 'modT', 's1', 's2'], writes=[tag + 'hT'])


def phase_outproj(self, layer, src, dst):
    P, nc, dr = self.P, self.nc, self.dr
    sb, ps = self.sb, self.ps
    wsrc = dr['ev_out_w'] if layer == 0 else dr['od_out_w']
    WO = dr['WO']
    for r in range(0, D, 512):
        P.dma('gpsimd', WO[r:r + 512, :], wsrc[r:r + 512, :], writes=[('WO', r)])
    blocks = self.blocks() if layer == 0 else self.blocks()[1:]
    with ExitStack() as es:
        W = sb(es, "wo_sb", [128, KC, D], BF16)
        xt = [sb(es, f"ox{i}", [128, KC, 512], F32) for i in range(2)]
        yt = [sb(es, f"oy{i}", [128, KC, 512], BF16) for i in range(2)]
        pacc = [ps(es, f"opacc{i}", [128, 512], F32) for i in range(4)]
        P.dma('sync', W[:], WO.rearrange("(kc p) n -> p kc n", p=128), reads=[('WO', r) for r in range(0, D, 512)], writes=['wo_sb'])
        it = 0
        for bi, (tok0, nt, m) in enumerate(blocks):
            x, xk = xt[bi % 2], f'ox{bi % 2}'
            y, yk = yt[bi % 2], f'oy{bi % 2}'
            P.dma('sync', x[:, :, :nt], src.rearrange("(kc p) t -> p kc t", p=128)[:, :, tok0:tok0 + nt], writes=[xk])
            P.dma('sync', y[:, :, :nt], dr['YT'].rearrange("(kc p) t -> p kc t", p=128)[:, :, tok0:tok0 + nt], writes=[yk])
            for d in range(KC):
                pa, pk = pacc[it % 4], f'PS:opacc{it % 4}'
                it += 1
                for kc in range(KC):
                    P.op('tensor', lambda e, kc=kc, d=d, pa=pa, y=y, nt=nt: e.matmul(pa[:, :nt], lhsT=W[:, kc, d * 128:(d + 1) * 128], rhs=y[:, kc, :nt],
                                                                                   start=(kc == 0), stop=(kc == KC - 1)), reads=['wo_sb', yk], writes=[pk])
                P.op('vector', lambda e, d=d, pa=pa, x=x, nt=nt, m=m: e.scalar_tensor_tensor(out=x[:, d, :nt], in0=pa[:, :nt], scalar=self.modT[:, 32 + d, m:m + 1], in1=x[:, d, :nt], op0=ALU.mult, op1=ALU.add),
                     reads=[pk, 'modT', xk], writes=[xk])
            P.dma('gpsimd', dst.rearrange("(kc p) t -> p kc t", p=128)[:, :, tok0:tok0 + nt], x[:, :, :nt], reads=[xk], writes=[('X1', tok0)])
        P.barrier()


def phase_ffn(self, layer, src, dst):
    P, nc, dr = self.P, self.nc, self.dr
    sb, ps = self.sb, self.ps
    WU, WD = dr['WU'], dr['WD']
    for r in range(0, D, 128):
        P.dma('gpsimd', WU[r:r + 128, :], dr['ffn_up_w'][layer][r:r + 128, :], writes=[('WU', r)])
    for r in range(0, FFN, 512):
        r1 = min(r + 512, FFN)
        P.dma('gpsimd', WD[r:r1, :], dr['ffn_down_w'][layer][r:r1, :], writes=[('WD', r)])
    wu_keys = [('WU', r) for r in range(0, D, 128)]
    wd_keys = [('WD', r) for r in range(0, FFN, 512)]
    NF = 43
    blocks = self.blocks() if layer == 0 else self.blocks()[1:]
    with ExitStack() as es0:
        gT = sb(es0, "gT", [128, NF, 512], BF16)
        xt = sb(es0, "fx", [128, KC, 640], F32)
        cwt = sb(es0, "fcw", [128, NF, 9], F32)
        P.dma('sync', cwt[:], dr['ffn_cw'][layer], writes=['fcw'])
        for bi, (tok0, nt, m) in enumerate(blocks):
            if m == 1:
                lo, hi, off = 0, TC, 0
            else:
                lo = max(tok0 - 64, TC)
                hi = min(tok0 + nt + 64, T)
                off = lo - (tok0 - 64)
            nw = hi - lo
            with ExitStack() as es:
                sq = sb(es, "fsq", [128, KC, 640], BF16)
                hT = sb(es, "fh", [128, KC, 640], BF16)
                rstd = sb(es, "frstd", [128, 512], F32)
                tmp = [sb(es, f"ftmp{i}", [128, 512], F32) for i in range(2)]
                wa = [sb(es, f"fwa{i}", [128, KC, 256], BF16) for i in range(2)]
                wv = [sb(es, f"fwv{i}", [128, KC, 256], BF16) for i in range(2)]
                asb = [sb(es, f"fasb{i}", [128, 10, 66], F32) for i in range(2)]
                acc = [sb(es, f"facc{i}", [128, 8, 64], F32) for i in range(2)]
                sg = [sb(es, f"fsg{i}", [128, 512], F32) for i in range(2)]
                pss = ps(es, "fpss", [128, 512], F32)
                pa0 = [ps(es, f"fpa0_{i}", [128, 512], F32) for i in range(2)]
                pa1 = [ps(es, f"fpa1_{i}", [128, 512], F32) for i in range(2)]
                pv = [ps(es, f"fpv{i}", [128, 512], F32) for i in range(2)]
                P.dma('sync', xt[:, :, off:off + nw], src.rearrange("(kc p) t -> p kc t", p=128)[:, :, lo:hi], writes=['fxt'])
                c = off
                while c < off + nw:
                    n = min(320, off + nw - c)
                    norm_cols(self, xt, sq, hT, rstd, tmp, pss, c, n, m, self.s2, 48, 'f')
                    c += n
                for i in range(2):
                    P.op('gpsimd', lambda e, i=i: e.memset(asb[i][:], 0.0), writes=[f'fasb{i}'])
                wuv = WU.rearrange("(kc p) n -> p kc n", p=128)
                for fp in range(0, NF, 2):
                    nf = min(2, NF - fp)
                    wi = (fp // 2) % 2
                    P.dma('sync', wa[wi][:, :, :nf * 128], wuv[:, :, fp * 128:(fp + nf) * 128], reads=wu_keys, writes=[f'fwa{wi}'])
                    P.dma('sync', wv[wi][:, :, :nf * 128], wuv[:, :, FFN + fp * 128:FFN + (fp + nf) * 128], reads=wu_keys, writes=[f'fwv{wi}'])
                    for ff in range(nf):
                        f = fp + ff
                        b2 = f % 2
                        A, Ak = asb[b2], f'fasb{b2}'
                        if m == 1:
                            P0, P0k = pa0[b2], f'PS:fpa0_{b2}'
                            for kc in range(KC):
                                P.op('tensor', lambda e, kc=kc, ff=ff, wi=wi, P0=P0: e.matmul(P0[:, :256], lhsT=wa[wi][:, kc, ff * 128:(ff + 1) * 128], rhs=hT[:, kc, 0:256], start=(kc == 0), stop=(kc == KC - 1)),
                                     reads=[f'fwa{wi}', 'fhT'], writes=[P0k])
                            Av = A[:].rearrange("p a b -> p (a b)")
                            P.op('scalar', lambda e, Av=Av, P0=P0: e.copy(out=Av[:, 1:257], in_=P0[:, :256]), reads=[P0k], writes=[Ak])
                            ac, ack = acc[b2][:].rearrange("p a b -> p (a b)"), f'facc{b2}'
                            P.op('vector', lambda e, Av=Av, ac=ac, f=f: e.tensor_scalar(out=ac[:, :256], in0=Av[:, 0:256], scalar1=cwt[:, f, 3:4], scalar2=None, op0=ALU.mult), reads=[Ak, 'fcw'], writes=[ack])
                            for dx in (1, 2):
                                P.op('vector', lambda e, Av=Av, ac=ac, f=f, dx=dx: e.scalar_tensor_tensor(out=ac[:, :256], in0=Av[:, dx:dx + 256], scalar=cwt[:, f, 3 + dx:4 + dx], in1=ac[:, :256], op0=ALU.mult, op1=ALU.add),
                                     reads=[Ak, 'fcw', ack], writes=[ack])
                            vlo = 0
                        else:
                            P0, P0k = pa0[b2], f'PS:fpa0_{b2}'
                            P1, P1k = pa1[b2], f'PS:fpa1_{b2}'
                            h0 = min(320, nw)
                            h1 = nw - h0
                            for kc in range(KC):
                                P.op('tensor', lambda e, kc=kc, ff=ff, wi=wi, P0=P0, h0=h0: e.matmul(P0[:, :h0], lhsT=wa[wi][:, kc, ff * 128:(ff + 1) * 128], rhs=hT[:, kc, off:off + h0], start=(kc == 0), stop=(kc == KC - 1)),
                                     reads=[f'fwa{wi}', 'fhT'], writes=[P0k])
                            for kc in range(KC):
                                P.op('tensor', lambda e, kc=kc, ff=ff, wi=wi, P1=P1, h0=h0, h1=h1: e.matmul(P1[:, :h1], lhsT=wa[wi][:, kc, ff * 128:(ff + 1) * 128], rhs=hT[:, kc, off + h0:off + h0 + h1], start=(kc == 0), stop=(kc == KC - 1)),
                                     reads=[f'fwa{wi}', 'fhT'], writes=[P1k])
                            r_off = off // 64
                            P.op('scalar', lambda e, A=A, P0=P0, h0=h0, r_off=r_off: e.copy(out=A[:, r_off:r_off + h0 // 64, 1:65], in_=P0[:, :h0].rearrange("p (a b) -> p a b", b=64)), reads=[P0k], writes=[Ak])
                            P.op('scalar', lambda e, A=A, P1=P1, h0=h0, h1=h1, r_off=r_off: e.copy(out=A[:, r_off + h0 // 64:r_off + (h0 + h1) // 64, 1:65], in_=P1[:, :h1].rearrange("p (a b) -> p a b", b=64)), reads=[P1k], writes=[Ak])
                            ac3, ack = acc[b2], f'facc{b2}'
                            first = True
                            for dy in range(3):
                                for dx in range(3):
                                    tap = dy * 3 + dx
                                    if first:
                                        P.op('vector', lambda e, A=A, ac3=ac3, f=f, dy=dy, dx=dx, tap=tap: e.tensor_scalar(out=ac3[:], in0=A[:, dy:dy + 8, dx:dx + 64], scalar1=cwt[:, f, tap:tap + 1], scalar2=None, op0=ALU.mult),
                                             reads=[Ak, 'fcw'], writes=[ack])
                                        first = False
                                    else:
                                        P.op('vector', lambda e, A=A, ac3=ac3, f=f, dy=dy, dx=dx, tap=tap: e.scalar_tensor_tensor(out=ac3[:], in0=A[:, dy:dy + 8, dx:dx + 64], scalar=cwt[:, f, tap:tap + 1], in1=ac3[:], op0=ALU.mult, op1=ALU.add),
                                             reads=[Ak, 'fcw', ack], writes=[ack])
                            ac = ac3[:].rearrange("p a b -> p (a b)")
                            vlo = 64
                        PV, PVk = pv[b2], f'PS:fpv{b2}'
                        for kc in range(KC):
                            P.op('tensor', lambda e, kc=kc, ff=ff, wi=wi, PV=PV, vlo=vlo, nt=nt: e.matmul(PV[:, :nt], lhsT=wv[wi][:, kc, ff * 128:(ff + 1) * 128], rhs=hT[:, kc, vlo:vlo + nt], start=(kc == 0), stop=(kc == KC - 1)),
                                 reads=[f'fwv{wi}', 'fhT'], writes=[PVk])
                        S, Sk = sg[b2], f'fsg{b2}'
                        P.op('scalar', lambda e, S=S, ac=ac, nt=nt: e.activation(out=S[:, :nt], in_=ac[:, :nt], func=AF.Silu), reads=[ack], writes=[Sk])
                        P.op('vector', lambda e, S=S, PV=PV, f=f, nt=nt: e.tensor_tensor(out=gT[:, f, :nt], in0=S[:, :nt], in1=PV[:, :nt], op=ALU.mult), reads=[Sk, PVk], writes=[('gT', f)])
                P.barrier()
            with ExitStack() as es:
                wd = [sb(es, f"fwd{i}", [128, 1024], BF16) for i in range(3)]
                pacc = [ps(es, f"fdacc{i}", [128, 512], F32) for i in range(8)]
                vlo = 0 if m == 1 else 64
                for half in range(2):
                    for f in range(NF):
                        w, wk = wd[f % 3], f'fwd{f % 3}'
                        P.dma('sync', w[:], WD[f * 128:(f + 1) * 128, half * 1024:(half + 1) * 1024], reads=wd_keys, writes=[wk])
                        for d in range(8):
                            P.op('tensor', lambda e, d=d, f=f, w=w, nt=nt: e.matmul(pacc[d][:, :nt], lhsT=w[:, d * 128:(d + 1) * 128], rhs=gT[:, f, :nt], start=(f == 0), stop=(f == NF - 1)),
                                 reads=[wk, ('gT', f)], writes=[f'PS:fdacc{d}'])
                    for d in range(8):
                        dd = half * 8 + d
                        P.op('vector', lambda e, d=d, dd=dd, nt=nt, m=m, vlo=vlo: e.scalar_tensor_tensor(out=xt[:, dd, vlo:vlo + nt], in0=pacc[d][:, :nt], scalar=self.modT[:, 80 + dd, m:m + 1], in1=xt[:, dd, vlo:vlo + nt], op0=ALU.mult, op1=ALU.add),
                             reads=[f'PS:fdacc{d}', 'modT', 'fxt'], writes=['fxt'])
                P.dma('gpsimd', dst.rearrange("(kc p) t -> p kc t", p=128)[:, :, tok0:tok0 + nt], xt[:, :, vlo:vlo + nt], reads=['fxt'], writes=[('X2', tok0)])
                P.barrier()


B.phase_outproj = phase_outproj
B.phase_ffn = phase_ffn


NCH = T // 8
HALF = NCH // 2
TWO_PI = 6.283185307179586
PI = 3.141592653589793


def reduce_angle(self, t, n, ki, kf, msk, tag):
    P = self.P
    v = 'vector'
    P.op(v, lambda e: e.tensor_scalar(out=ki[:, :n], in0=t, scalar1=1.0 / TWO_PI, scalar2=None, op0=ALU.mult), reads=[tag], writes=[tag + 'ki'])
    P.op(v, lambda e: e.tensor_copy(out=kf[:, :n], in_=ki[:, :n]), reads=[tag + 'ki'], writes=[tag + 'kf'])
    P.op(v, lambda e: e.scalar_tensor_tensor(out=t, in0=kf[:, :n], scalar=-TWO_PI, in1=t, op0=ALU.mult, op1=ALU.add), reads=[tag + 'kf', tag], writes=[tag])
    P.op(v, lambda e: e.tensor_single_scalar(out=msk[:, :n], in_=t, scalar=PI, op=ALU.is_gt), reads=[tag], writes=[tag + 'm'])
    P.op(v, lambda e: e.scalar_tensor_tensor(out=t, in0=msk[:, :n], scalar=-TWO_PI, in1=t, op0=ALU.mult, op1=ALU.add), reads=[tag + 'm', tag], writes=[tag])
    P.op(v, lambda e: e.tensor_single_scalar(out=msk[:, :n], in_=t, scalar=-PI, op=ALU.is_lt), reads=[tag], writes=[tag + 'm'])
    P.op(v, lambda e: e.scalar_tensor_tensor(out=t, in0=msk[:, :n], scalar=TWO_PI, in1=t, op0=ALU.mult, op1=ALU.add), reads=[tag + 'm', tag], writes=[tag])
    P.op(v, lambda e: e.tensor_scalar(out=t, in0=t, scalar1=PI, scalar2=-PI, op0=ALU.min, op1=ALU.max), reads=[tag], writes=[tag])


def cmul_cols(self, o_re, o_im, a_re, a_im, s_re, s_im, tmp, rk, wk):
    P = self.P
    v = 'vector'
    P.op(v, lambda e: e.tensor_scalar(out=tmp, in0=a_im, scalar1=s_im, scalar2=None, op0=ALU.mult), reads=rk, writes=[wk + 't'])
    P.op(v, lambda e: e.scalar_tensor_tensor(out=o_re, in0=a_re, scalar=s_re, in1=tmp, op0=ALU.mult, op1=ALU.subtract), reads=rk + [wk + 't'], writes=[wk + 're'])
    P.op(v, lambda e: e.tensor_scalar(out=tmp, in0=a_re, scalar1=s_im, scalar2=None, op0=ALU.mult), reads=rk + [wk + 're'], writes=[wk + 't'])
    P.op(v, lambda e: e.scalar_tensor_tensor(out=o_im, in0=a_im, scalar=s_re, in1=tmp, op0=ALU.mult, op1=ALU.add), reads=rk + [wk + 't'], writes=[wk + 'im'])


def phase_s5(self):
    P, nc, dr = self.P, self.nc, self.dr
    sb, ps = self.sb, self.ps
    PT = dr['PT']
    v, a, g_ = 'vector', 'scalar', 'gpsimd'
    with ExitStack() as es0:
        Sel = sb(es0, "Sel", [128, 8, 8, 128], BF16)
        bmask = sb(es0, "bmask", [128, 8, 16], F32)
        krow = sb(es0, "krow", [128, 16], F32)
        crow = sb(es0, "crow", [128, NCH + 1], F32)
        lre = sb(es0, "lre", [128, 32], F32)
        lim = sb(es0, "lim", [128, 32], F32)
        dlt = sb(es0, "dlt", [128, 32], F32)
        reD = sb(es0, "reD", [128, 32], F32)
        imD = sb(es0, "imD", [128, 32], F32)
        bre = sb(es0, "bre", [128, 32, 16], F32)
        bim = sb(es0, "bim", [128, 32, 16], F32)
        cre = sb(es0, "cre", [128, 32, 16], F32)
        cim = sb(es0, "cim", [128, 32, 16], F32)
        dsk = sb(es0, "dsk", [128, 4], F32)
        gy = sb(es0, "gy", [128, 4, T], BF16)
        uT = sb(es0, "uT", [128, 2, T], BF16)
        uF = sb(es0, "uF", [128, T], F32)
        ki = sb(es0, "ki", [128, NCH + 1], I32)
        kf = sb(es0, "kf", [128, NCH + 1], F32)
        msk = sb(es0, "msk", [128, NCH + 1], F32)
        P.op(g_, lambda e: e.memset(Sel[:], 0.0), writes=['Sel'])
        for a_ in range(8):
            for b_ in range(8):
                P.op(g_ if (a_ + b_) % 2 else v, lambda e, a_=a_, b_=b_: e.tensor_copy(out=Sel[:, a_, b_, b_ * 16:(b_ + 1) * 16], in_=self.ident_f[:, a_ * 16:(a_ + 1) * 16]),
                     reads=['ident_f', 'Sel'], writes=['Sel'])
        P.op(g_, lambda e: e.memset(bmask[:], 1.0), writes=['bmask'])
        P.op(g_, lambda e: e.affine_select(out=bmask[:], in_=bmask[:], pattern=[[16, 8], [0, 16]], base=15, channel_multiplier=-1, compare_op=ALU.is_ge, fill=0.0),
             reads=['bmask'], writes=['bmask'])
        P.op(g_, lambda e: e.iota(krow[:], pattern=[[1, 16]], base=-7, channel_multiplier=0, allow_small_or_imprecise_dtypes=True), writes=['krow'])
        P.op(g_, lambda e: e.iota(crow[:], pattern=[[1, NCH + 1]], base=0, channel_multiplier=0, allow_small_or_imprecise_dtypes=True), writes=['crow'])
        for nm, t_ in (('s5_lre', lre), ('s5_lim', lim), ('s5_dlt', dlt)):
            P.dma('sync', t_[:], dr[nm], writes=[nm])
        for nm, t_ in (('s5_bre', bre), ('s5_bim', bim), ('s5_cre', cre), ('s5_cim', cim)):
            P.dma('sync', t_[:], dr[nm], writes=[nm])
        P.dma('sync', dsk[:], dr['s5_dsk'], writes=['dsk'])
        P.op(a, lambda e: e.activation(out=dlt[:], in_=dlt[:], func=AF.Exp), reads=['s5_dlt'], writes=['s5_dlt'])
        P.op(v, lambda e: e.tensor_tensor(out=reD[:], in0=lre[:], in1=dlt[:], op=ALU.mult), reads=['s5_lre', 's5_dlt'], writes=['reD'])
        P.op(v, lambda e: e.tensor_tensor(out=imD[:], in0=lim[:], in1=dlt[:], op=ALU.mult), reads=['s5_lim', 's5_dlt'], writes=['imD'])
        self.cfre = sb(es0, "cfre", [128, 32], F32)
        self.cfim = sb(es0, "cfim", [128, 32], F32)
        with ExitStack() as es:
            mg = sb(es, "c_mg", [128, 32], F32)
            an = sb(es, "c_an", [128, 32], F32)
            an2 = sb(es, "c_an2", [128, 32], F32)
            nre = sb(es, "c_nre", [128, 32], F32)
            nim = sb(es, "c_nim", [128, 32], F32)
            den = sb(es, "c_den", [128, 32], F32)
            t1 = sb(es, "c_t1", [128, 32], F32)
            cfre, cfim = self.cfre, self.cfim
            P.op(a, lambda e: e.activation(out=mg[:], in_=reD[:], func=AF.Exp), reads=['reD'], writes=['c_mg'])
            P.op(v, lambda e: e.tensor_copy(out=an[:], in_=imD[:]), reads=['imD'], writes=['c_an'])
            reduce_angle(self, an[:], 32, ki, kf, msk, 'c_an')
            P.op(v, lambda e: e.tensor_scalar(out=an2[:], in0=imD[:], scalar1=PI / 2, scalar2=None, op0=ALU.add), reads=['imD'], writes=['c_an2'])
            reduce_angle(self, an2[:], 32, ki, kf, msk, 'c_an2')
            P.op(a, lambda e: e.activation(out=an[:], in_=an[:], func=AF.Sin), reads=['c_an'], writes=['c_an'])
            P.op(a, lambda e: e.activation(out=an2[:], in_=an2[:], func=AF.Sin), reads=['c_an2'], writes=['c_an2'])
            P.op(v, lambda e: e.tensor_tensor(out=nre[:], in0=mg[:], in1=an2[:], op=ALU.mult), reads=['c_mg', 'c_an2'], writes=['c_nre'])
            P.op(v, lambda e: e.tensor_scalar(out=nre[:], in0=nre[:], scalar1=-1.0, scalar2=None, op0=ALU.add), reads=['c_nre'], writes=['c_nre'])
            P.op(v, lambda e: e.tensor_tensor(out=nim[:], in0=mg[:], in1=an[:], op=ALU.mult), reads=['c_mg', 'c_an'], writes=['c_nim'])
            P.op(v, lambda e: e.tensor_tensor(out=den[:], in0=lre[:], in1=lre[:], op=ALU.mult), reads=['s5_lre'], writes=['c_den'])
            P.op(v, lambda e: e.tensor_tensor(out=t1[:], in0=lim[:], in1=lim[:], op=ALU.mult), reads=['s5_lim'], writes=['c_t1'])
            P.op(v, lambda e: e.tensor_tensor(out=den[:], in0=den[:], in1=t1[:], op=ALU.add), reads=['c_den', 'c_t1'], writes=['c_den'])
            P.op(v, lambda e: e.reciprocal(out=den[:], in_=den[:]), reads=['c_den'], writes=['c_den'])
            P.op(v, lambda e: e.tensor_tensor(out=cfre[:], in0=nre[:], in1=lre[:], op=ALU.mult), reads=['c_nre', 's5_lre'], writes=['cfre'])
            P.op(v, lambda e: e.tensor_tensor(out=t1[:], in0=nim[:], in1=lim[:], op=ALU.mult), reads=['c_nim', 's5_lim', 'c_den'], writes=['c_t1'])
            P.op(v, lambda e: e.tensor_tensor(out=cfre[:], in0=cfre[:], in1=t1[:], op=ALU.add), reads=['cfre', 'c_t1'], writes=['cfre'])
            P.op(v, lambda e: e.tensor_tensor(out=cfre[:], in0=cfre[:], in1=den[:], op=ALU.mult), reads=['cfre', 'c_den'], writes=['cfre'])
            P.op(v, lambda e: e.tensor_tensor(out=cfim[:], in0=nim[:], in1=lre[:], op=ALU.mult), reads=['c_nim', 's5_lre'], writes=['cfim'])
            P.op(v, lambda e: e.tensor_tensor(out=t1[:], in0=nre[:], in1=lim[:], op=ALU.mult), reads=['c_nre', 's5_lim', 'cfre'], writes=['c_t1'])
            P.op(v, lambda e: e.tensor_tensor(out=cfim[:], in0=cfim[:], in1=t1[:], op=ALU.subtract), reads=['cfim', 'c_t1'], writes=['cfim'])
            P.op(v, lambda e: e.tensor_tensor(out=cfim[:], in0=cfim[:], in1=den[:], op=ALU.mult), reads=['cfim', 'c_den'], writes=['cfim'])
            P.barrier()
        cfre, cfim = self.cfre, self.cfim
        with ExitStack() as es:
            mgk = sb(es, "mgk", [128, 16], F32)
            ank = sb(es, "ank", [128, 16], F32)
            ank2 = sb(es, "ank2", [128, 16], F32)
            LPre = sb(es, "LPre", [128, 16], F32)
            LPim = sb(es, "LPim", [128, 16], F32)
            bbre = sb(es, "bbre", [128, 16], F32)
            bbim = sb(es, "bbim", [128, 16], F32)
            ctmp = sb(es, "ctmp", [128, 128], F32)
            Bmre = sb(es, "Bmre", [128, 8, 16], F32)
            Bmim = sb(es, "Bmim", [128, 8, 16], F32)
            Cmre = sb(es, "Cmre", [128, 8, 16], F32)
            Cmim = sb(es, "Cmim", [128, 8, 16], F32)
            WiTre = sb(es, "WiTre", [128, 128], F32)
            WiTim = sb(es, "WiTim", [128, 128], F32)
            Wore = sb(es, "Wore", [128, 128], BF16)
            Woim = sb(es, "Woim", [128, 128], BF16)
            WoFre = sb(es, "WoFre", [128, 128], F32)
            WoFim = sb(es, "WoFim", [128, 128], F32)
            Tin = sb(es, "Tin", [128, 2, 128], BF16)
            Win = sb(es, "Win", [128, 2, 2, 64], BF16)
            th = sb(es, "th", [128, 1], F32)
            rr = sb(es, "rr", [128, 1], F32)
            rtab = sb(es, "rtab", [128, NCH], F32)
            ctab = sb(es, "ctab", [128, NCH + 1], F32)
            stab = sb(es, "stab", [128, NCH + 1], F32)
            Usb = sb(es, "Usb", [128, 2, 2, NCH], BF16)
            Sre = sb(es, "Sre", [128, NCH], F32)
            Sim = sb(es, "Sim", [128, NCH], F32)
            s2re = sb(es, "s2re", [128, NCH], F32)
            s2im = sb(es, "s2im", [128, NCH], F32)
            Zre = sb(es, "Zre", [128, NCH + 1], F32)
            Zim = sb(es, "Zim", [128, NCH + 1], F32)
            Xre = sb(es, "Xre", [128, NCH], BF16)
            Xim = sb(es, "Xim", [128, NCH], BF16)
            xt1 = sb(es, "xt1", [128, NCH], F32)
            Ysb = sb(es, "Ysb", [128, 2, 8, NCH], BF16)
            yT = sb(es, "yT5", [128, 2, T], F32)
            gl_w = sb(es, "gluw", [128, 4, 512], BF16)
            sgm = sb(es, "sgm", [128, 512], F32)
            og = sb(es, "og5", [128, 512], BF16)
            pU = [ps(es, f"pU{i}", [128, 512], F32) for i in range(2)]
            pSr = ps(es, "pSr", [128, 512], F32)
            pSi = ps(es, "pSi", [128, 512], F32)
            pY = [ps(es, f"pY{i}", [128, 512], F32) for i in range(2)]
            pB = ps(es, "pB5", [128, 512], F32)
            P.dma('gpsimd', gl_w[:], dr['s5_glu_w'].rearrange("(kt p) n -> p kt n", p=128), writes=['gluw'])
            P.op(g_, lambda e: e.memset(Zre[:, 0:1], 0.0), writes=['Zre'])
            P.op(g_, lambda e: e.memset(Zim[:, 0:1], 0.0), writes=['Zim'])
            hs = [(0, HALF), (HALF, NCH)]
            for ft in range(4):
                P.dma('sync', uF[:], PT[ft * 128:(ft + 1) * 128, :], writes=['uF'])
                P.op(a, lambda e: e.copy(out=uT[:, 0, :], in_=uF[:]), reads=['uF'], writes=['uT'])
                P.op(v, lambda e: e.tensor_copy(out=uT[:, 1, 0:TC], in_=uF[:, TC - 1::-1]), reads=['uF'], writes=['uT'])
                P.op(v, lambda e: e.tensor_copy(out=uT[:, 1, TC:T], in_=uF[:, T - 1:TC - 1:-1]), reads=['uF'], writes=['uT'])
                for gp4 in range(4):
                    gp = ft * 4 + gp4
                    for d in range(2):
                        for g2 in range(2):
                            gl = gp4 * 2 + g2
                            for hi_, (c0, c1) in enumerate(hs):
                                for j in range(8):
                                    P.op('tensor', lambda e, d=d, gl=gl, j=j, c0=c0, c1=c1, hi_=hi_: e.matmul(
                                        pU[hi_][:, :c1 - c0], lhsT=Sel[:, gl, j, :], rhs=uT[:, d, c0 * 8 + j:c1 * 8:8], start=(j == 0), stop=(j == 7)),
                                        reads=['Sel', 'uT'], writes=[f'PS:pU{hi_}'])
                                eng = a if hi_ == 0 else v
                                if eng == a:
                                    P.op(a, lambda e, d=d, g2=g2, c0=c0, c1=c1, hi_=hi_: e.copy(out=Usb[:, d, g2, c0:c1], in_=pU[hi_][:, :c1 - c0]), reads=[f'PS:pU{hi_}'], writes=[('Usb', d, g2)])
                                else:
                                    P.op(v, lambda e, d=d, g2=g2, c0=c0, c1=c1, hi_=hi_: e.tensor_copy(out=Usb[:, d, g2, c0:c1], in_=pU[hi_][:, :c1 - c0]), reads=[f'PS:pU{hi_}'], writes=[('Usb', d, g2)])
                    for d in range(2):
                        col = d * 16 + gp
                        P.op(v, lambda e, col=col: e.tensor_scalar(out=mgk[:], in0=krow[:], scalar1=reD[:, col:col + 1], scalar2=None, op0=ALU.mult), reads=['krow', 'reD'], writes=['mgk'])
                        P.op(a, lambda e: e.activation(out=mgk[:], in_=mgk[:], func=AF.Exp), reads=['mgk'], writes=['mgk'])
                        P.op(v, lambda e, col=col: e.tensor_scalar(out=ank[:], in0=krow[:], scalar1=imD[:, col:col + 1], scalar2=None, op0=ALU.mult), reads=['krow', 'imD'], writes=['ank'])
                        P.op(v, lambda e: e.tensor_scalar(out=ank2[:], in0=ank[:], scalar1=PI / 2, scalar2=None, op0=ALU.add), reads=['ank'], writes=['ank2'])
                        reduce_angle(self, ank[:], 16, ki, kf, msk, 'ank')
                        reduce_angle(self, ank2[:], 16, ki, kf, msk, 'ank2')
                        P.op(a, lambda e: e.activation(out=ank[:], in_=ank[:], func=AF.Sin), reads=['ank'], writes=['ank'])
                        P.op(a, lambda e: e.activation(out=ank2[:], in_=ank2[:], func=AF.Sin), reads=['ank2'], writes=['ank2'])
                        P.op(v, lambda e: e.tensor_tensor(out=LPre[:], in0=mgk[:], in1=ank2[:], op=ALU.mult), reads=['mgk', 'ank2'], writes=['LPre'])
                        P.op(v, lambda e: e.tensor_tensor(out=LPim[:], in0=mgk[:], in1=ank[:], op=ALU.mult), reads=['mgk', 'ank'], writes=['LPim'])
                        LP = ['LPre', 'LPim']
                        cmul_cols(self, bbre[:], bbim[:], bre[:, col, :], bim[:, col, :], cfre[:, col:col + 1], cfim[:, col:col + 1], ctmp[:, 0:16], ['s5_bre', 's5_bim', 'cfre', 'cfim'], 'bb')
                        for j in range(8):
                            kk = 7 - j
                            cmul_cols(self, Bmre[:, j, :], Bmim[:, j, :], bbre[:], bbim[:], LPre[:, kk:kk + 1], LPim[:, kk:kk + 1], ctmp[:, 0:16], ['bbre', 'bbim'] + LP, 'Bm')
                            kk = 7 + j
                            cmul_cols(self, Cmre[:, j, :], Cmim[:, j, :], cre[:, col, :], cim[:, col, :], LPre[:, kk:kk + 1], LPim[:, kk:kk + 1], ctmp[:, 0:16], ['s5_cre', 's5_cim'] + LP, 'Cm')
                        P.op(v, lambda e: e.tensor_scalar(out=Cmim[:], in0=Cmim[:], scalar1=-1.0, scalar2=None, op0=ALU.mult), reads=['Cmim'], writes=['Cmim'])
                        B2re, B2im = Bmre[:].rearrange("p a b -> p (a b)"), Bmim[:].rearrange("p a b -> p (a b)")
                        C2re, C2im = Cmre[:].rearrange("p a b -> p (a b)"), Cmim[:].rearrange("p a b -> p (a b)")
                        cmul_cols(self, WiTre[:], WiTim[:], B2re, B2im, LPre[:, 14:15], LPim[:, 14:15], ctmp[:], ['Bmre', 'Bmim'] + LP, 'WiT')
                        P.op(v, lambda e: e.tensor_scalar(out=ctmp[:], in0=C2im, scalar1=LPim[:, 8:9], scalar2=None, op0=ALU.mult), reads=['Cmim', 'LPim'], writes=['Wot'])
                        P.op(v, lambda e: e.scalar_tensor_tensor(out=WoFre[:], in0=C2re, scalar=LPre[:, 8:9], in1=ctmp[:], op0=ALU.mult, op1=ALU.add), reads=['Cmre', 'LPre', 'Wot'], writes=['WoFre'])
                        P.op(v, lambda e: e.tensor_scalar(out=ctmp[:], in0=C2re, scalar1=LPim[:, 8:9], scalar2=None, op0=ALU.mult), reads=['Cmre', 'LPim', 'WoFre'], writes=['Wot'])
                        P.op(v, lambda e: e.scalar_tensor_tensor(out=WoFim[:], in0=C2im, scalar=LPre[:, 8:9], in1=ctmp[:], op0=ALU.mult, op1=ALU.subtract), reads=['Cmim', 'LPre', 'Wot'], writes=['WoFim'])
                        P.op(a, lambda e: e.copy(out=Wore[:], in_=WoFre[:]), reads=['WoFre'], writes=['Wore'])
                        P.op(a, lambda e: e.copy(out=Woim[:], in_=WoFim[:]), reads=['WoFim'], writes=['Woim'])
                        for g2 in range(2):
                            r0, r1 = g2 * 64, (g2 + 1) * 64
                            P.op('tensor', lambda e, r0=r0, r1=r1: e.matmul(pB[:, 0:128], lhsT=B2re[r0:r1, :], rhs=C2re[r0:r1, :], start=True, stop=False), reads=['Bmre', 'Cmre'], writes=['PS:pB5'])
                            P.op('tensor', lambda e, r0=r0, r1=r1: e.matmul(pB[:, 0:128], lhsT=B2im[r0:r1, :], rhs=C2im[r0:r1, :], start=False, stop=True), reads=['Bmim', 'Cmim'], writes=['PS:pB5'])
                            P.op(v, lambda e, g2=g2: e.tensor_tensor(out=Tin[:, g2, :], in0=pB[:, 0:128], in1=bmask[:].rearrange("p a b -> p (a b)"), op=ALU.mult), reads=['PS:pB5', 'bmask'], writes=[('Tin', g2)])
                            P.op('tensor', lambda e, r0=r0, r1=r1: e.matmul(pB[:, 128:192], lhsT=WiTre[r0:r1, :], rhs=self.ident_f[r0:r1, r0:r1], start=True, stop=True), reads=['WiTre', 'ident_f'], writes=['PS:pB5'])
                            P.op('tensor', lambda e, r0=r0, r1=r1: e.matmul(pB[:, 192:256], lhsT=WiTim[r0:r1, :], rhs=self.ident_f[r0:r1, r0:r1], start=True, stop=True), reads=['WiTim', 'ident_f'], writes=['PS:pB5'])
                            P.op(a, lambda e, g2=g2: e.copy(out=Win[:, g2, :, :].rearrange("p a b -> p (a b)"), in_=pB[:, 128:256]), reads=['PS:pB5'], writes=[('Win', g2)])
                        P.op(v, lambda e, col=col: e.tensor_scalar(out=th[:], in0=imD[:, col:col + 1], scalar1=8.0, scalar2=None, op0=ALU.mult), reads=['imD'], writes=['th'])
                        reduce_angle(self, th[:], 1, ki, kf, msk, 'th')
                        P.op(a, lambda e, col=col: e.activation(out=rr[:], in_=reD[:, col:col + 1], func=AF.Exp, scale=8.0), reads=['reD'], writes=['rr'])
                        P.op(v, lambda e: e.tensor_scalar(out=rtab[:], in0=crow[:, 0:NCH], scalar1=0.0, scalar2=rr[:], op0=ALU.mult, op1=ALU.add), reads=['crow', 'rr'], writes=['rtab'])
                        P.op(v, lambda e: e.tensor_scalar(out=stab[:], in0=crow[:], scalar1=th[:], scalar2=None, op0=ALU.mult), reads=['crow', 'th'], writes=['stab'])
                        P.op(v, lambda e: e.tensor_scalar(out=ctab[:], in0=stab[:], scalar1=PI / 2, scalar2=None, op0=ALU.add), reads=['stab'], writes=['ctab'])
                        reduce_angle(self, stab[:], NCH + 1, ki, kf, msk, 'stab')
                        reduce_angle(self, ctab[:], NCH + 1, ki, kf, msk, 'ctab')
                        P.op(a, lambda e: e.activation(out=stab[:], in_=stab[:], func=AF.Sin), reads=['stab'], writes=['stab'])
                        P.op(a, lambda e: e.activation(out=ctab[:], in_=ctab[:], func=AF.Sin), reads=['ctab'], writes=['ctab'])
                        for g2 in range(2):
                            r0, r1 = g2 * 64, (g2 + 1) * 64
                            for hi_, (c0, c1) in enumerate(hs):
                                w_ = 272 * hi_
                                P.op('tensor', lambda e, d=d, g2=g2, r0=r0, r1=r1, c0=c0, c1=c1: e.matmul(pSr[r0:r1, 0:c1 - c0], lhsT=Win[:, g2, 0, :], rhs=Usb[:, d, g2, c0:c1], start=True, stop=True),
                                     reads=[('Win', g2), ('Usb', d, g2)], writes=['PS:pSr'])
                                P.op('tensor', lambda e, d=d, g2=g2, r0=r0, r1=r1, c0=c0, c1=c1: e.matmul(pSi[r0:r1, 0:c1 - c0], lhsT=Win[:, g2, 1, :], rhs=Usb[:, d, g2, c0:c1], start=True, stop=True),
                                     reads=[('Win', g2), ('Usb', d, g2)], writes=['PS:pSi'])
                                P.op(a, lambda e, r0=r0, r1=r1, c0=c0, c1=c1: e.copy(out=Sre[r0:r1, c0:c1], in_=pSr[r0:r1, 0:c1 - c0]), reads=['PS:pSr'], writes=['Sre'])
                                P.op(v, lambda e, r0=r0, r1=r1, c0=c0, c1=c1: e.tensor_copy(out=Sim[r0:r1, c0:c1], in_=pSi[r0:r1, 0:c1 - c0]), reads=['PS:pSi'], writes=['Sim'])
                        P.op(v, lambda e: e.tensor_tensor(out=xt1[:], in0=Sim[:], in1=stab[:, 1:NCH + 1], op=ALU.mult), reads=['Sim', 'stab'], writes=['xt1'])
                        P.op(v, lambda e: e.tensor_tensor(out=s2re[:], in0=Sre[:], in1=ctab[:, 1:NCH + 1], op=ALU.mult), reads=['Sre', 'ctab'], writes=['s2re'])
                        P.op(v, lambda e: e.tensor_tensor(out=s2re[:], in0=s2re[:], in1=xt1[:], op=ALU.add), reads=['s2re', 'xt1'], writes=['s2re'])
                        P.op(v, lambda e: e.tensor_tensor(out=xt1[:], in0=Sre[:], in1=stab[:, 1:NCH + 1], op=ALU.mult), reads=['Sre', 'stab', 's2re'], writes=['xt1'])
                        P.op(v, lambda e: e.tensor_tensor(out=s2im[:], in0=Sim[:], in1=ctab[:, 1:NCH + 1], op=ALU.mult), reads=['Sim', 'ctab'], writes=['s2im'])
                        P.op(v, lambda e: e.tensor_tensor(out=s2im[:], in0=s2im[:], in1=xt1[:], op=ALU.subtract), reads=['s2im', 'xt1'], writes=['s2im'])
                        P.op(v, lambda e: e.tensor_tensor_scan(out=Zre[:, 1:NCH + 1], data0=rtab[:], data1=s2re[:], initial=0.0, op0=ALU.mult, op1=ALU.add), reads=['rtab', 's2re'], writes=['Zre'])
                        P.op(v, lambda e: e.tensor_tensor_scan(out=Zim[:, 1:NCH + 1], data0=rtab[:], data1=s2im[:], initial=0.0, op0=ALU.mult, op1=ALU.add), reads=['rtab', 's2im'], writes=['Zim'])
                        P.op(v, lambda e: e.tensor_tensor(out=xt1[:], in0=Zim[:, 0:NCH], in1=stab[:, 0:NCH], op=ALU.mult), reads=['Zim', 'stab', 's2im'], writes=['xt1'])
                        P.op(v, lambda e: e.tensor_tensor(out=s2re[:], in0=Zre[:, 0:NCH], in1=ctab[:, 0:NCH], op=ALU.mult), reads=['Zre', 'ctab'], writes=['s2re'])
                        P.op(v, lambda e: e.tensor_tensor(out=Xre[:], in0=s2re[:], in1=xt1[:], op=ALU.subtract), reads=['s2re', 'xt1'], writes=['Xre'])
                        P.op(v, lambda e: e.tensor_tensor(out=xt1[:], in0=Zre[:, 0:NCH], in1=stab[:, 0:NCH], op=ALU.mult), reads=['Zre', 'stab', 'Xre'], writes=['xt1'])
                        P.op(v, lambda e: e.tensor_tensor(out=s2im[:], in0=Zim[:, 0:NCH], in1=ctab[:, 0:NCH], op=ALU.mult), reads=['Zim', 'ctab'], writes=['s2im'])
                        P.op(v, lambda e: e.tensor_tensor(out=Xim[:], in0=s2im[:], in1=xt1[:], op=ALU.add), reads=['s2im', 'xt1'], writes=['Xim'])
                        for g2 in range(2):
                            r0, r1 = g2 * 64, (g2 + 1) * 64
                            gl = gp4 * 2 + g2
                            for hi_, (c0, c1) in enumerate(hs):
                                py = pY[hi_]
                                pk = f'PS:pY{hi_}'
                                P.op('tensor', lambda e, d=d, g2=g2, c0=c0, c1=c1, py=py: e.matmul(py[:, 0:c1 - c0], lhsT=Tin[:, g2, :], rhs=Usb[:, d, g2, c0:c1], start=True, stop=False),
                                     reads=[('Tin', g2), ('Usb', d, g2)], writes=[pk])
                                P.op('tensor', lambda e, r0=r0, r1=r1, c0=c0, c1=c1, py=py: e.matmul(py[:, 0:c1 - c0], lhsT=Wore[r0:r1, :], rhs=Xre[r0:r1, c0:c1], start=False, stop=False),
                                     reads=['Wore', 'Xre'], writes=[pk])
                                P.op('tensor', lambda e, r0=r0, r1=r1, c0=c0, c1=c1, py=py: e.matmul(py[:, 0:c1 - c0], lhsT=Woim[r0:r1, :], rhs=Xim[r0:r1, c0:c1], start=False, stop=True),
                                     reads=['Woim', 'Xim'], writes=[pk])
                                if hi_ == 0:
                                    P.op(a, lambda e, d=d, gl=gl, c0=c0, c1=c1, py=py: e.copy(out=Ysb[:, d, gl, c0:c1], in_=py[:, 0:c1 - c0]), reads=[pk], writes=[('Ysb', d, gl)])
                                else:
                                    P.op(v, lambda e, d=d, gl=gl, c0=c0, c1=c1, py=py: e.tensor_copy(out=Ysb[:, d, gl, c0:c1], in_=py[:, 0:c1 - c0]), reads=[pk], writes=[('Ysb', d, gl)])
                for d in range(2):
                    for i in range(8):
                        for hi_, (c0, c1) in enumerate(hs):
                            py = pY[hi_]
                            pk = f'PS:pY{hi_}'
                            for gl in range(8):
                                P.op('tensor', lambda e, d=d, gl=gl, i=i, c0=c0, c1=c1, py=py: e.matmul(py[:, 0:c1 - c0], lhsT=Sel[:, i, gl, :], rhs=Ysb[:, d, gl, c0:c1], start=(gl == 0), stop=(gl == 7)),
                                     reads=['Sel', ('Ysb', d, gl)], writes=[pk])
                            if hi_ == 0:
                                P.op(a, lambda e, d=d, i=i, c0=c0, c1=c1, py=py: e.copy(out=yT[:, d, c0 * 8 + i:c1 * 8:8], in_=py[:, 0:c1 - c0]), reads=[pk], writes=['yT5'])
                            else:
                                P.op(v, lambda e, d=d, i=i, c0=c0, c1=c1, py=py: e.tensor_copy(out=yT[:, d, c0 * 8 + i:c1 * 8:8], in_=py[:, 0:c1 - c0]), reads=[pk], writes=['yT5'])
                P.op(v, lambda e: e.tensor_tensor(out=yT[:, 0, 0:TC], in0=yT[:, 0, 0:TC], in1=yT[:, 1, TC - 1::-1], op=ALU.add), reads=['yT5'], writes=['yT5'])
                P.op(v, lambda e: e.tensor_tensor(out=yT[:, 0, TC:T], in0=yT[:, 0, TC:T], in1=yT[:, 1, T - 1:TC - 1:-1], op=ALU.add), reads=['yT5'], writes=['yT5'])
                P.op(v, lambda e, ft=ft: e.scalar_tensor_tensor(out=yT[:, 0, :], in0=uF[:], scalar=dsk[:, ft:ft + 1], in1=yT[:, 0, :], op0=ALU.mult, op1=ALU.add), reads=['uF', 'dsk', 'yT5'], writes=['yT5'])
                P.op(a, lambda e: e.activation(out=yT[:, 1, :], in_=yT[:, 0, :], func=AF.Square), reads=['yT5'], writes=['yT5b'])
                P.op(v, lambda e: e.tensor_scalar(out=yT[:, 1, :], in0=yT[:, 1, :], scalar1=0.044715, scalar2=1.0, op0=ALU.mult, op1=ALU.add), reads=['yT5b'], writes=['yT5b'])
                P.op(v, lambda e: e.tensor_tensor(out=yT[:, 1, :], in0=yT[:, 1, :], in1=yT[:, 0, :], op=ALU.mult), reads=['yT5b', 'yT5'], writes=['yT5b'])
                P.op(a, lambda e: e.activation(out=yT[:, 1, :], in_=yT[:, 1, :], func=AF.Sigmoid, scale=1.5957691216057308), reads=['yT5b'], writes=['yT5b'])
                P.op(v, lambda e, ft=ft: e.tensor_tensor(out=gy[:, ft, :], in0=yT[:, 1, :], in1=yT[:, 0, :], op=ALU.mult), reads=['yT5b', 'yT5'], writes=[('gy', ft)])
            it = 0
            for fo in range(4):
                for (tok0, nt, m) in self.blocks():
                    py = pY[it % 2]
                    pk = f'PS:pY{it % 2}'
                    it += 1
                    for fi in range(4):
                        P.op('tensor', lambda e, fo=fo, fi=fi, tok0=tok0, nt=nt, py=py: e.matmul(py[:, :nt], lhsT=gl_w[:, fi, fo * 128:(fo + 1) * 128], rhs=gy[:, fi, tok0:tok0 + nt], start=(fi == 0), stop=(fi == 3)),
                             reads=['gluw'] + [('gy', f) for f in range(4)], writes=[pk])
                    P.op(a, lambda e, nt=nt, py=py: e.activation(out=sgm[:, :nt], in_=py[:, :nt], func=AF.Sigmoid), reads=[pk], writes=['sgm'])
                    P.op(v, lambda e, fo=fo, tok0=tok0, nt=nt: e.tensor_tensor(out=og[:, :nt], in0=sgm[:, :nt], in1=gy[:, fo, tok0:tok0 + nt], op=ALU.mult), reads=['sgm', ('gy', fo)], writes=['og5'])
                    P.dma('gpsimd', dr['YT'][fo * 128:(fo + 1) * 128, tok0:tok0 + nt], og[:, :nt], reads=['og5'], writes=[('YT5', fo, tok0)])
            P.barrier()


B.phase_s5 = phase_s5


R_QM, R_KM, R_VM, R_OM, R_IG, R_FG = 4128, 4640, 5152, 6176, 7200, 7208


def phase_mlstm(self):
    P, nc, dr = self.P, self.nc, self.dr
    sb, ps = self.sb, self.ps
    PT = dr['PT']
    v, a, g_ = 'vector', 'scalar', 'gpsimd'
    with ExitStack() as es0:
        gt_tok = sb(es0, "m_gt", [128, 34, 40], F32)
        with ExitStack() as es:
            G = sb(es, "m_G", [128, T], F32)
            bcol = sb(es, "m_bcol", [128, 1], F32)
            ptr = ps(es, "m_ptr", [128, 512], F32)
            P.op(g_, lambda e: e.memset(G[:], 0.0), writes=['m_G'])
            P.op(g_, lambda e: e.memset(bcol[:], 0.0), writes=['m_bcol'])
            P.dma('sync', G[0:8, :], PT[R_IG:R_IG + 8, :], writes=['m_G'])
            P.dma('sync', G[32:40, :], PT[R_FG:R_FG + 8, :], writes=['m_G'])
            P.dma('sync', bcol[0:8, :], dr['ml_ib'], writes=['m_bcol'])
            P.dma('sync', bcol[32:40, :], dr['ml_fb'], writes=['m_bcol'])
            P.op(a, lambda e: e.activation(out=G[0:32, :], in_=G[0:32, :], func=AF.Exp, bias=bcol[0:32, :]), reads=['m_G', 'm_bcol'], writes=['m_G'])
            P.op(v, lambda e: e.tensor_scalar(out=bcol[32:64, :], in0=bcol[32:64, :], scalar1=-1.0, scalar2=None, op0=ALU.mult), reads=['m_bcol', 'm_G'], writes=['m_bcol'])
            P.op(a, lambda e: e.activation(out=G[32:64, :], in_=G[32:64, :], func=AF.Exp, scale=-1.0, bias=bcol[32:64, :]), reads=['m_G', 'm_bcol'], writes=['m_G'])
            P.op(a, lambda e: e.activation(out=G[32:64, :], in_=G[32:64, :], func=AF.Ln, bias=self.one_col[32:64, :]), reads=['m_G', 'one_col'], writes=['m_G'])
            P.op(v, lambda e: e.tensor_scalar(out=G[32:64, :], in0=G[32:64, :], scalar1=-1.0, scalar2=None, op0=ALU.mult), reads=['m_G'], writes=['m_G'])
            for c in range(34):
                P.op('tensor', lambda e, c=c: e.matmul(ptr[:, 0:40], lhsT=G[:, c * 128:(c + 1) * 128], rhs=self.ident_f[:, 0:40], start=True, stop=True), reads=['m_G', 'ident_f'], writes=['PS:m_ptr'])
                P.op(v, lambda e, c=c: e.tensor_copy(out=gt_tok[:, c, :], in_=ptr[:, 0:40]), reads=['PS:m_ptr'], writes=['m_gt'])
            P.barrier()
        with ExitStack() as es:
            xin = [sb(es, f"m_xin{i}", [128, 16, 512], F32) for i in range(2)]
            cs = sb(es, "m_cs", [128, 16, 512], BF16)
            tokst = [sb(es, f"m_tokst{i}", [128, 1536], BF16) for i in range(2)]
            ptA = ps(es, "m_ptA", [128, 8, 128], BF16)
            ptB = ps(es, "m_ptB", [128, 8, 128], BF16)
            ci = 0
            for bi, (tok0, nt, m) in enumerate(self.blocks()):
                x = xin[bi % 2]
                xk = f'm_xin{bi % 2}'
                P.dma('sync', x[:, 0:8, :nt], PT[R_QM:R_QM + 1024, :].rearrange("(f p) t -> p f t", p=128)[:, :, tok0:tok0 + nt], writes=[xk])
                P.dma('sync', x[:, 8:16, :nt], PT[R_VM:R_VM + 1024, :].rearrange("(f p) t -> p f t", p=128)[:, :, tok0:tok0 + nt], writes=[xk])
                P.op(a, lambda e, x=x, nt=nt: e.copy(out=cs[:, 0:4, :nt], in_=x[:, 0:4, :nt]), reads=[xk], writes=['m_cs'])
                P.op(a, lambda e, x=x, nt=nt: e.activation(out=cs[:, 4:8, :nt], in_=x[:, 4:8, :nt], func=AF.Copy, scale=128 ** -0.5), reads=[xk], writes=['m_cs'])
                P.op(v, lambda e, x=x, nt=nt: e.tensor_copy(out=cs[:, 8:16, :nt], in_=x[:, 8:16, :nt]), reads=[xk], writes=['m_cs'])
                P.dma('gpsimd', dr['BCT'][0:1024, :].rearrange("(f p) t -> p f t", p=128)[:, :, tok0:tok0 + nt], cs[:, 0:8, :nt], reads=['m_cs'], writes=[('BCT', tok0)])
                for ch in range(nt // 128):
                    tk = tokst[ci % 2]
                    tkk = f'm_tokst{ci % 2}'
                    ci += 1
                    for f in range(8):
                        P.op('tensor', lambda e, f=f, ch=ch: e.transpose(out=ptA[:, f, :], in_=cs[:, 8 + f, ch * 128:(ch + 1) * 128], identity=self.ident_b[:]), reads=['m_cs', 'ident_b'], writes=['PS:m_ptA'])
                    for f in range(4):
                        P.op('tensor', lambda e, f=f, ch=ch: e.transpose(out=ptB[:, f, :], in_=cs[:, 4 + f, ch * 128:(ch + 1) * 128], identity=self.ident_b[:]), reads=['m_cs', 'ident_b'], writes=['PS:m_ptB'])
                    P.op(v, lambda e, tk=tk: e.tensor_copy(out=tk[:, 0:1024], in_=ptA[:].rearrange("p a b -> p (a b)")), reads=['PS:m_ptA'], writes=[tkk])
                    P.op(a, lambda e, tk=tk: e.copy(out=tk[:, 1024:1536], in_=ptB[:, 0:4, :].rearrange("p a b -> p (a b)")), reads=['PS:m_ptB'], writes=[tkk])
                    P.dma('gpsimd', dr['TOK'][tok0 + ch * 128:tok0 + (ch + 1) * 128, 0:1536], tk[:], reads=[tkk], writes=[('TOK', tok0 + ch * 128)])
            P.barrier()
        with ExitStack() as es:
            S_f = sb(es, "m_Sf", [128, 4, 257], F32)
            S_b = sb(es, "m_Sb", [128, 4, 257], BF16)
            tok = [sb(es, f"m_tok{i}", [128, 1536], BF16) for i in range(2)]
            qk = [sb(es, f"m_qk{i}", [128, 8, 128], BF16) for i in range(2)]
            Lm = [sb(es, f"m_Lm{i}", [128, 128], F32) for i in range(2)]
            Dm = [sb(es, f"m_Dm{i}", [128, 128], F32) for i in range(2)]
            WT = [sb(es, f"m_WT{i}", [128, 128], BF16) for i in range(2)]
            v1 = [sb(es, f"m_v1_{i}", [128, 257], BF16) for i in range(2)]
            v2 = [sb(es, f"m_v2_{i}", [128, 257], BF16) for i in range(2)]
            yA = [sb(es, f"m_yA{i}", [128, 257], F32) for i in range(2)]
            num = [sb(es, f"m_num{i}", [128, 257], F32) for i in range(2)]
            den = [sb(es, f"m_den{i}", [128, 1], F32) for i in range(2)]
            hsb = [sb(es, f"m_hsb{i}", [128, 1024], F32) for i in range(2)]
            hfl = sb(es, "m_hfl", [128, 1024], F32)
            sqj = sb(es, "m_sqj", [128, 256], F32)
            ss = sb(es, "m_ss", [128, 4], F32)
            hn = sb(es, "m_hn", [128, 1024], BF16)
            ecum = sb(es, "m_ecum", [128, 4], F32)
            cdec = sb(es, "m_cdec", [128, 4], F32)
            nwB = sb(es, "m_nwB", [128, 1024], F32)
            ytb = sb(es, "m_ytb", [128, 8, 512], BF16)
            oT = sb(es, "m_oT", [128, 8, 512], F32)
            ot = sb(es, "m_ot", [128, 8, 512], BF16)
            pX = [ps(es, f"m_pX{i}", [128, 512], F32) for i in range(2)]
            pyA = [ps(es, f"m_pyA{i}", [128, 512], F32) for i in range(2)]
            pyB = [ps(es, f"m_pyB{i}", [128, 512], F32) for i in range(2)]
            pS = ps(es, "m_pS", [128, 512], F32)
            ptT = ps(es, "m_ptT", [128, 8, 128], BF16)
            P.dma('sync', nwB[:], dr['ml_nw'].partition_broadcast(128), writes=['m_nwB'])
            li = 0
            for d in range(2):
                MASK = self.MGT if d == 0 else self.MLT
                TRI = self.TLE if d == 0 else self.TGE
                mk, tk_ = ('MGT', 'TLE') if d == 0 else ('MLT', 'TGE')
                ecol = 127 if d == 0 else 0
                P.op(g_, lambda e: e.memset(S_f[:], 0.0), writes=[('m_Sf', h) for h in range(4)])
                P.op(g_, lambda e: e.memset(S_b[:], 0.0), writes=[('m_Sb', h) for h in range(4)])
                for c in chunk_order(d):
                    t0 = c * 128
                    tb, tbk = tok[li % 2], f'm_tok{li % 2}'
                    qb, qbk = qk[li % 2], f'm_qk{li % 2}'
                    hs_, hsk = hsb[li % 2], f'm_hsb{li % 2}'
                    li += 1
                    P.dma('sync', tb[:], dr['TOK'][t0:t0 + 128, 0:1536], writes=[tbk])
                    P.dma('sync', qb[:], dr['BCT'][0:1024, :].rearrange("(f p) t -> p f t", p=128)[:, :, t0:t0 + 128], writes=[qbk])
                    lac = gt_tok[:, c, 32 + d * 4:32 + (d + 1) * 4]
                    P.op('tensor', lambda e, lac=lac: e.matmul(pS[:, 300:304], lhsT=TRI[:], rhs=lac, start=True, stop=True), reads=[tk_, 'm_gt'], writes=['PS:m_pS'])
                    P.op('tensor', lambda e, lac=lac: e.matmul(pS[:, 320:324], lhsT=self.ones_f[:], rhs=lac, start=True, stop=True), reads=['ones_f', 'm_gt'], writes=['PS:m_pS'])
                    P.op(a, lambda e: e.activation(out=ecum[:], in_=pS[:, 300:304], func=AF.Exp), reads=['PS:m_pS'], writes=['m_ecum'])
                    P.op(a, lambda e: e.activation(out=cdec[:], in_=pS[:, 320:324], func=AF.Exp), reads=['PS:m_pS'], writes=['m_cdec'])
                    for h in range(4):
                        r = h % 2
                        X, Xk = pX[r], f'PS:m_pX{r}'
                        A_, Ak = pyA[r], f'PS:m_pyA{r}'
                        B_, Bk = pyB[r], f'PS:m_pyB{r}'
                        gcol = gt_tok[:, c, 32 + d * 4 + h:32 + d * 4 + h + 1]
                        icol = gt_tok[:, c, d * 4 + h:d * 4 + h + 1]
                        P.op(v, lambda e, r=r, gcol=gcol: e.tensor_scalar(out=Lm[r][:], in0=MASK[:], scalar1=gcol, scalar2=None, op0=ALU.mult), reads=[mk, 'm_gt'], writes=[f'm_Lm{r}'])
                        P.op('tensor', lambda e, r=r, X=X: e.matmul(X[:, 0:128], lhsT=Lm[r][:], rhs=TRI[:], start=True, stop=False), reads=[f'm_Lm{r}', tk_], writes=[Xk])
                        P.op('tensor', lambda e, r=r, X=X: e.matmul(X[:, 0:128], lhsT=self.negI[:], rhs=MASK[:], start=False, stop=True), reads=['negI', mk], writes=[Xk])
                        P.op('tensor', lambda e, h=h, X=X, qb=qb: e.matmul(X[:, 128:256], lhsT=qb[:, 4 + h, :], rhs=qb[:, h, :], start=True, stop=True), reads=[qbk], writes=[Xk])
                        P.op(a, lambda e, r=r, X=X: e.activation(out=Dm[r][:], in_=X[:, 0:128], func=AF.Exp), reads=[Xk], writes=[f'm_Dm{r}'])
                        P.op(v, lambda e, r=r, X=X: e.tensor_tensor(out=WT[r][:], in0=Dm[r][:], in1=X[:, 128:256], op=ALU.mult), reads=[f'm_Dm{r}', Xk], writes=[f'm_WT{r}'])
                        P.op(g_, lambda e, r=r, h=h, tb=tb, icol=icol: e.tensor_scalar(out=v1[r][:, 0:256], in0=tb[:, h * 256:(h + 1) * 256], scalar1=icol, scalar2=None, op0=ALU.mult), reads=[tbk, 'm_gt'], writes=[f'm_v1_{r}'])
                        P.op(g_, lambda e, r=r, icol=icol: e.tensor_copy(out=v1[r][:, 256:257], in_=icol), reads=['m_gt'], writes=[f'm_v1_{r}'])
                        P.op(g_, lambda e, r=r: e.tensor_scalar(out=v2[r][:], in0=v1[r][:], scalar1=Dm[r][:, ecol:ecol + 1], scalar2=None, op0=ALU.mult), reads=[f'm_v1_{r}', f'm_Dm{r}'], writes=[f'm_v2_{r}'])
                        P.op('tensor', lambda e, r=r, A_=A_: e.matmul(A_[:, 0:257], lhsT=WT[r][:], rhs=v1[r][:], start=True, stop=True), reads=[f'm_WT{r}', f'm_v1_{r}'], writes=[Ak])
                        P.op('tensor', lambda e, h=h, B_=B_, qb=qb: e.matmul(B_[:, 0:257], lhsT=qb[:, h, :], rhs=S_b[:, h, :], start=True, stop=True), reads=[qbk, ('m_Sb', h)], writes=[Bk])
                        P.op('tensor', lambda e, r=r, h=h, tb=tb: e.matmul(pS[:, 0:257], lhsT=tb[:, 1024 + h * 128:1024 + (h + 1) * 128], rhs=v2[r][:], start=True, stop=True), reads=[tbk, f'm_v2_{r}'], writes=['PS:m_pS'])
                        P.op(a, lambda e, r=r, A_=A_: e.copy(out=yA[r][:], in_=A_[:, 0:257]), reads=[Ak], writes=[f'm_yA{r}'])
                        P.op(v, lambda e, r=r, h=h, B_=B_: e.scalar_tensor_tensor(out=num[r][:], in0=B_[:, 0:257], scalar=ecum[:, h:h + 1], in1=yA[r][:], op0=ALU.mult, op1=ALU.add), reads=[Bk, 'm_ecum', f'm_yA{r}'], writes=[f'm_num{r}'])
                        P.op(v, lambda e, h=h: e.scalar_tensor_tensor(out=S_f[:, h, :], in0=S_f[:, h, :], scalar=cdec[:, h:h + 1], in1=pS[:, 0:257], op0=ALU.mult, op1=ALU.add), reads=[('m_Sf', h), 'm_cdec', 'PS:m_pS'], writes=[('m_Sf', h)])
                        P.op(a, lambda e, h=h: e.copy(out=S_b[:, h, :], in_=S_f[:, h, :]), reads=[('m_Sf', h)], writes=[('m_Sb', h)])
                        P.op(a, lambda e, r=r: e.activation(out=den[r][:], in_=num[r][:, 256:257], func=AF.Abs), reads=[f'm_num{r}'], writes=[f'm_den{r}'])
                        P.op(v, lambda e, r=r: e.tensor_scalar(out=den[r][:], in0=den[r][:], scalar1=1.0, scalar2=None, op0=ALU.max), reads=[f'm_den{r}'], writes=[f'm_den{r}'])
                        P.op(v, lambda e, r=r: e.reciprocal(out=den[r][:], in_=den[r][:]), reads=[f'm_den{r}'], writes=[f'm_den{r}'])
                        P.op(v, lambda e, r=r, h=h, hs_=hs_: e.tensor_scalar(out=hs_[:, h * 256:(h + 1) * 256], in0=num[r][:, 0:256], scalar1=den[r][:], scalar2=None, op0=ALU.mult), reads=[f'm_num{r}', f'm_den{r}'], writes=[hsk])
                    if d == 0:
                        P.dma('gpsimd', dr['YF'][t0:t0 + 128, 0:1024], hs_[:], reads=[hsk], writes=[('YF', c)])
                    else:
                        P.dma('sync', hfl[:], dr['YF'][t0:t0 + 128, 0:1024], reads=[('YF', c)], writes=['m_hfl'])
                        P.op(v, lambda e, hs_=hs_: e.tensor_tensor(out=hfl[:], in0=hfl[:], in1=hs_[:], op=ALU.add), reads=['m_hfl', hsk], writes=['m_hfl'])
                        for h in range(4):
                            P.op(a, lambda e, h=h: e.activation(out=sqj[:], in_=hfl[:, h * 256:(h + 1) * 256], func=AF.Square, accum_out=ss[:, h:h + 1]), reads=['m_hfl'], writes=['m_sqj', 'm_ss'])
                        P.op(a, lambda e: e.activation(out=ss[:], in_=ss[:], func=AF.Sqrt, scale=1.0 / 256, bias=self.eps_col[:]), reads=['m_ss', 'eps_col'], writes=['m_ss'])
                        P.op(v, lambda e: e.reciprocal(out=ss[:], in_=ss[:]), reads=['m_ss'], writes=['m_ss'])
                        for h in range(4):
                            P.op(v, lambda e, h=h: e.scalar_tensor_tensor(out=hn[:, h * 256:(h + 1) * 256], in0=hfl[:, h * 256:(h + 1) * 256], scalar=ss[:, h:h + 1], in1=nwB[:, h * 256:(h + 1) * 256], op0=ALU.mult, op1=ALU.mult),
                                 reads=['m_hfl', 'm_ss', 'm_nwB'], writes=['m_hn'])
                        if c < 2:
                            tok0, nt, ch = 0, TC, c
                        else:
                            tok0, nt, ch = TC + ((c - 2) // 4) * 512, 512, (c - 2) % 4
                        for f in range(8):
                            P.op('tensor', lambda e, f=f: e.transpose(out=ptT[:, f, :], in_=hn[:, f * 128:(f + 1) * 128], identity=self.ident_b[:]), reads=['m_hn', 'ident_b'], writes=['PS:m_ptT'])
                        P.op(v, lambda e, ch=ch: e.tensor_copy(out=ytb[:, :, ch * 128:(ch + 1) * 128], in_=ptT[:]), reads=['PS:m_ptT'], writes=['m_ytb'])
                        if ch == 0:
                            P.dma('sync', oT[:, :, :nt], PT[R_OM:R_OM + 1024, :].rearrange("(f p) t -> p f t", p=128)[:, :, tok0:tok0 + nt], writes=['m_oT'])
                            P.op(a, lambda e, nt=nt: e.activation(out=oT[:, :, :nt], in_=oT[:, :, :nt], func=AF.Sigmoid), reads=['m_oT'], writes=['m_oT'])
                            P.op(v, lambda e, nt=nt: e.tensor_tensor(out=ot[:, :, :nt], in0=ytb[:, :, :nt], in1=oT[:, :, :nt], op=ALU.mult), reads=['m_ytb', 'm_oT'], writes=['m_ot'])
                            P.dma('gpsimd', dr['YT'][1024:2048, :].rearrange("(f p) t -> p f t", p=128)[:, :, tok0:tok0 + nt], ot[:, :, :nt], reads=['m_ot'], writes=[('YT', 'ml', tok0)])
            P.barrier()


B.phase_mlstm = phase_mlstm


R_QKV, R_ZG, R_BETA, R_A = 0, 3072, 4096, 4112


def phase_gdn(self):
    P, nc, dr = self.P, self.nc, self.dr
    sb, ps = self.sb, self.ps
    PT = dr['PT']
    v, a, g_ = 'vector', 'scalar', 'gpsimd'
    with ExitStack() as es0:
        gt_tok = sb(es0, "g_gt", [128, 34, 48], F32)
        with ExitStack() as es:
            G = sb(es, "g_G", [128, T], F32)
            bcol = sb(es, "g_bcol", [128, 1], F32)
            acol = sb(es, "g_acol", [128, 1], F32)
            ptr = ps(es, "g_ptr", [128, 512], F32)
            P.op(g_, lambda e: e.memset(G[:], 0.0), writes=['g_G'])
            P.op(g_, lambda e: e.memset(bcol[:], 0.0), writes=['g_bcol'])
            P.op(g_, lambda e: e.memset(acol[:], 0.0), writes=['g_acol'])
            P.dma('sync', G[0:16, :], PT[R_BETA:R_BETA + 16, :], writes=['g_G'])
            P.dma('sync', G[32:48, :], PT[R_A:R_A + 16, :], writes=['g_G'])
            P.dma('sync', bcol[32:48, :], dr['gdn_dtb'], writes=['g_bcol'])
            P.dma('sync', acol[32:48, :], dr['gdn_alog'], writes=['g_acol'])
            P.op(a, lambda e: e.activation(out=G[0:32, :], in_=G[0:32, :], func=AF.Sigmoid), reads=['g_G'], writes=['g_G'])
            P.op(a, lambda e: e.activation(out=acol[32:64, :], in_=acol[32:64, :], func=AF.Exp), reads=['g_acol'], writes=['g_acol'])
            P.op(v, lambda e: e.tensor_scalar(out=acol[32:64, :], in0=acol[32:64, :], scalar1=-1.0, scalar2=None, op0=ALU.mult), reads=['g_acol'], writes=['g_acol'])
            P.op(a, lambda e: e.activation(out=G[32:64, :], in_=G[32:64, :], func=AF.Exp, bias=bcol[32:64, :]), reads=['g_G', 'g_bcol'], writes=['g_G'])
            P.op(a, lambda e: e.activation(out=G[32:64, :], in_=G[32:64, :], func=AF.Ln, bias=self.one_col[32:64, :]), reads=['g_G', 'one_col'], writes=['g_G'])
            P.op(v, lambda e: e.tensor_scalar(out=G[32:64, :], in0=G[32:64, :], scalar1=acol[32:64, :], scalar2=None, op0=ALU.mult), reads=['g_G', 'g_acol'], writes=['g_G'])
            for c in range(34):
                P.op('tensor', lambda e, c=c: e.matmul(ptr[:, 0:48], lhsT=G[:, c * 128:(c + 1) * 128], rhs=self.ident_f[:, 0:48], start=True, stop=True), reads=['g_G', 'ident_f'], writes=['PS:g_ptr'])
                P.op(v, lambda e, c=c: e.tensor_copy(out=gt_tok[:, c, :], in_=ptr[:, 0:48]), reads=['PS:g_ptr'], writes=['g_gt'])
            P.barrier()
        with ExitStack() as es:
            xin = [sb(es, f"g_xin{i}", [128, 24, 514], F32) for i in range(2)]
            cs = sb(es, "g_cs", [128, 24, 512], BF16)
            cf = sb(es, "g_cf", [128, 512], F32)
            sqb = sb(es, "g_sqb", [128, 512], BF16)
            rn = sb(es, "g_rn", [128, 512], F32)
            acc = [sb(es, f"g_cacc{i}", [128, 512], F32) for i in range(2)]
            cw = sb(es, "g_cw", [128, 24, 3], F32)
            tokst = [sb(es, f"g_tokst{i}", [128, 2048], BF16) for i in range(2)]
            ptA = ps(es, "g_ptA", [128, 8, 128], BF16)
            ptB = ps(es, "g_ptB", [128, 8, 128], BF16)
            pss = ps(es, "g_pss", [128, 512], F32)
            P.dma('sync', cw[:], dr['gdn_cw'], writes=['g_cw'])
            src = PT[0:3072, :].rearrange("(f p) t -> p f t", p=128)
            ci = 0
            for bi, (tok0, nt, m) in enumerate(self.blocks()):
                x = xin[bi % 2]
                xk = f'g_xin{bi % 2}'
                seq0, seq1 = (0, TC) if m == 1 else (TC, T)
                lo = max(tok0 - 1, seq0)
                hi = min(tok0 + nt + 1, seq1)
                P.dma('sync', x[:, 0:12, lo - (tok0 - 1):hi - (tok0 - 1)], src[:, 0:12, lo:hi], writes=[xk])
                P.dma('sync', x[:, 12:24, lo - (tok0 - 1):hi - (tok0 - 1)], src[:, 12:24, lo:hi], writes=[xk])
                if lo > tok0 - 1:
                    P.op(g_, lambda e, x=x: e.memset(x[:, :, 0:1], 0.0), writes=[xk])
                if hi < tok0 + nt + 1:
                    P.op(g_, lambda e, x=x, nt=nt: e.memset(x[:, :, nt + 1:nt + 2], 0.0), writes=[xk])
                for f in range(24):
                    ac = acc[f % 2]
                    ak = f'g_cacc{f % 2}'
                    P.op(v, lambda e, x=x, ac=ac, f=f, nt=nt: e.tensor_scalar(out=ac[:, :nt], in0=x[:, f, 0:nt], scalar1=cw[:, f, 0:1], scalar2=None, op0=ALU.mult), reads=[xk, 'g_cw'], writes=[ak])
                    P.op(v, lambda e, x=x, ac=ac, f=f, nt=nt: e.scalar_tensor_tensor(out=ac[:, :nt], in0=x[:, f, 1:nt + 1], scalar=cw[:, f, 1:2], in1=ac[:, :nt], op0=ALU.mult, op1=ALU.add), reads=[xk, 'g_cw', ak], writes=[ak])
                    P.op(v, lambda e, x=x, ac=ac, f=f, nt=nt: e.scalar_tensor_tensor(out=ac[:, :nt], in0=x[:, f, 2:nt + 2], scalar=cw[:, f, 2:3], in1=ac[:, :nt], op0=ALU.mult, op1=ALU.add), reads=[xk, 'g_cw', ak], writes=[ak])
                    if f >= 16:
                        P.op(a, lambda e, ac=ac, f=f, nt=nt: e.activation(out=cs[:, f, :nt], in_=ac[:, :nt], func=AF.Silu), reads=[ak], writes=[('g_cs', f)])
                    else:
                        P.op(a, lambda e, ac=ac, nt=nt: e.activation(out=cf[:, :nt], in_=ac[:, :nt], func=AF.Silu), reads=[ak], writes=['g_cf'])
                        P.op(a, lambda e, nt=nt: e.activation(out=sqb[:, :nt], in_=cf[:, :nt], func=AF.Square), reads=['g_cf'], writes=['g_sqb'])
                        P.op('tensor', lambda e, nt=nt: e.matmul(pss[:, :nt], lhsT=self.ones_b[:], rhs=sqb[:, :nt], start=True, stop=True), reads=['g_sqb', 'ones_b'], writes=['PS:g_pss'])
                        P.op(a, lambda e, nt=nt: e.activation(out=rn[:, :nt], in_=pss[:, :nt], func=AF.Sqrt, bias=self.eps_col[:]), reads=['PS:g_pss', 'eps_col'], writes=['g_rn'])
                        P.op(v, lambda e, nt=nt: e.reciprocal(out=rn[:, :nt], in_=rn[:, :nt]), reads=['g_rn'], writes=['g_rn'])
                        sc = 128 ** -0.5 if f < 8 else 1.0
                        P.op(v, lambda e, f=f, nt=nt, sc=sc: e.scalar_tensor_tensor(out=cs[:, f, :nt], in0=cf[:, :nt], scalar=sc, in1=rn[:, :nt], op0=ALU.mult, op1=ALU.mult), reads=['g_cf', 'g_rn'], writes=[('g_cs', f)])
                P.dma('gpsimd', dr['BCT'].rearrange("(f p) t -> p f t", p=128)[:, :, tok0:tok0 + nt], cs[:, 0:16, :nt],
                      reads=[('g_cs', f) for f in range(16)], writes=[('BCT', tok0)])
                for ch in range(nt // 128):
                    tk = tokst[ci % 2]
                    tkk = f'g_tokst{ci % 2}'
                    ci += 1
                    for f in range(8):
                        P.op('tensor', lambda e, f=f, ch=ch: e.transpose(out=ptA[:, f, :], in_=cs[:, 16 + f, ch * 128:(ch + 1) * 128], identity=self.ident_b[:]), reads=[('g_cs', 16 + f), 'ident_b'], writes=['PS:g_ptA'])
                    for f in range(8):
                        P.op('tensor', lambda e, f=f, ch=ch: e.transpose(out=ptB[:, f, :], in_=cs[:, 8 + f, ch * 128:(ch + 1) * 128], identity=self.ident_b[:]), reads=[('g_cs', 8 + f), 'ident_b'], writes=['PS:g_ptB'])
                    P.op(v, lambda e, tk=tk: e.tensor_copy(out=tk[:, 0:1024], in_=ptA[:].rearrange("p a b -> p (a b)")), reads=['PS:g_ptA'], writes=[tkk])
                    P.op(a, lambda e, tk=tk: e.copy(out=tk[:, 1024:2048], in_=ptB[:].rearrange("p a b -> p (a b)")), reads=['PS:g_ptB'], writes=[tkk])
                    P.dma('gpsimd', dr['TOK'][tok0 + ch * 128:tok0 + (ch + 1) * 128, :], tk[:], reads=[tkk], writes=[('TOK', tok0 + ch * 128)])
            P.barrier()
        with ExitStack() as es:
            S_f = sb(es, "g_Sf", [128, 8, 128], F32)
            S_b = sb(es, "g_Sb", [128, 8, 128], BF16)
            tok = [sb(es, f"g_tok{i}", [128, 2048], BF16) for i in range(2)]
            qk = [sb(es, f"g_qk{i}", [128, 16, 128], BF16) for i in range(2)]
            Lm = sb(es, "g_Lm", [128, 128], F32)
            Dm = sb(es, "g_Dm", [128, 128], F32)
            WT = sb(es, "g_WT", [128, 128], BF16)
            tX = sb(es, "g_tX", [128, 128], F32)
            Xp = [sb(es, f"g_Xp{i}", [128, 128], F32) for i in range(2)]
            Np = [sb(es, f"g_Np{i}", [128, 128], F32) for i in range(2)]
            rr = [sb(es, f"g_rr{i}", [128, 256], F32) for i in range(2)]
            wT = sb(es, "g_wT", [128, 128], F32)
            vn = sb(es, "g_vn", [128, 128], F32)
            v1 = sb(es, "g_v1", [128, 128], BF16)
            v2 = sb(es, "g_v2", [128, 128], BF16)
            yA = sb(es, "g_yA", [128, 128], F32)
            osb = [sb(es, f"g_osb{i}", [128, 1024], F32) for i in range(2)]
            ofl = sb(es, "g_ofl", [128, 1024], F32)
            sqj = sb(es, "g_sqj", [128, 128], F32)
            ss = sb(es, "g_ss", [128, 8], F32)
            hn = sb(es, "g_hn", [128, 1024], BF16)
            ecum = sb(es, "g_ecum", [128, 8], F32)
            cdec = sb(es, "g_cdec", [128, 8], F32)
            nwB = sb(es, "g_nwB", [128, 128], F32)
            ytb = sb(es, "g_ytb", [128, 8, 512], BF16)
            zT = sb(es, "g_zT", [128, 8, 512], F32)
            ot = sb(es, "g_ot", [128, 8, 512], BF16)
            pX = ps(es, "g_pX", [128, 512], F32)
            pN = ps(es, "g_pN", [128, 512], F32)
            pR = ps(es, "g_pR", [128, 512], F32)
            pA = ps(es, "g_pA", [128, 512], F32)
            pB = ps(es, "g_pB", [128, 512], F32)
            pS = ps(es, "g_pS", [128, 512], F32)
            ptT = ps(es, "g_ptT", [128, 8, 128], BF16)
            P.dma('sync', nwB[:], dr['gdn_nw'].partition_broadcast(128), writes=['g_nwB'])
            li = 0
            for d in range(2):
                MASK = self.MGT if d == 0 else self.MLT
                TRI = self.TLE if d == 0 else self.TGE
                STR = self.MLT if d == 0 else self.MGT
                mk, tk_, sk_ = ('MGT', 'TLE', 'MLT') if d == 0 else ('MLT', 'TGE', 'MGT')
                ecol = 127 if d == 0 else 0
                P.op(g_, lambda e: e.memset(S_f[:], 0.0), writes=[('g_Sf', h) for h in range(8)])
                P.op(g_, lambda e: e.memset(S_b[:], 0.0), writes=[('g_Sb', h) for h in range(8)])
                for c in chunk_order(d):
                    t0 = c * 128
                    tb, tbk = tok[li % 2], f'g_tok{li % 2}'
                    qb, qbk = qk[li % 2], f'g_qk{li % 2}'
                    os_, osk = osb[li % 2], f'g_osb{li % 2}'
                    li += 1
                    P.dma('sync', tb[:], dr['TOK'][t0:t0 + 128, :], writes=[tbk])
                    P.dma('sync', qb[:], dr['BCT'].rearrange("(f p) t -> p f t", p=128)[:, :, t0:t0 + 128], writes=[qbk])
                    lac = gt_tok[:, c, 32 + d * 8:32 + (d + 1) * 8]
                    P.op('tensor', lambda e, lac=lac: e.matmul(pS[:, 300:308], lhsT=TRI[:], rhs=lac, start=True, stop=True), reads=[tk_, 'g_gt'], writes=['PS:g_pS'])
                    P.op('tensor', lambda e, lac=lac: e.matmul(pS[:, 320:328], lhsT=self.ones_f[:], rhs=lac, start=True, stop=True), reads=['ones_f', 'g_gt'], writes=['PS:g_pS'])
                    P.op(a, lambda e: e.activation(out=ecum[:], in_=pS[:, 300:308], func=AF.Exp), reads=['PS:g_pS'], writes=['g_ecum'])
                    P.op(a, lambda e: e.activation(out=cdec[:], in_=pS[:, 320:328], func=AF.Exp), reads=['PS:g_pS'], writes=['g_cdec'])
                    for h in range(8):
                        gcol = gt_tok[:, c, 32 + d * 8 + h:32 + d * 8 + h + 1]
                        bcol_ = gt_tok[:, c, d * 8 + h:d * 8 + h + 1]
                        kT_, qT_ = qb[:, 8 + h, :], qb[:, h, :]
                        vtok, ktok = tb[:, h * 128:(h + 1) * 128], tb[:, 1024 + h * 128:1024 + (h + 1) * 128]
                        P.op(v, lambda e, gcol=gcol: e.tensor_scalar(out=Lm[:], in0=MASK[:], scalar1=gcol, scalar2=None, op0=ALU.mult), reads=[mk, 'g_gt'], writes=['g_Lm'])
                        P.op('tensor', lambda e: e.matmul(pX[:, 0:128], lhsT=Lm[:], rhs=TRI[:], start=True, stop=False), reads=['g_Lm', tk_], writes=['PS:g_pX'])
                        P.op('tensor', lambda e: e.matmul(pX[:, 0:128], lhsT=self.negI[:], rhs=MASK[:], start=False, stop=True), reads=['negI', mk], writes=['PS:g_pX'])
                        P.op('tensor', lambda e, kT_=kT_: e.matmul(pX[:, 128:256], lhsT=kT_, rhs=kT_, start=True, stop=True), reads=[qbk], writes=['PS:g_pX'])
                        P.op('tensor', lambda e, kT_=kT_, qT_=qT_: e.matmul(pX[:, 256:384], lhsT=kT_, rhs=qT_, start=True, stop=True), reads=[qbk], writes=['PS:g_pX'])
                        P.op(a, lambda e: e.activation(out=Dm[:], in_=pX[:, 0:128], func=AF.Exp), reads=['PS:g_pX'], writes=['g_Dm'])
                        P.op(v, lambda e: e.tensor_tensor(out=tX[:], in0=Dm[:], in1=pX[:, 128:256], op=ALU.mult), reads=['g_Dm', 'PS:g_pX'], writes=['g_tX'])
                        P.op(v, lambda e: e.tensor_tensor(out=WT[:], in0=Dm[:], in1=pX[:, 256:384], op=ALU.mult), reads=['g_Dm', 'PS:g_pX'], writes=['g_WT'])
                        P.op(v, lambda e, bcol_=bcol_: e.scalar_tensor_tensor(out=Xp[0][:], in0=tX[:], scalar=bcol_, in1=STR[:], op0=ALU.mult, op1=ALU.mult), reads=['g_tX', 'g_gt', sk_], writes=['g_Xp0'])
                        P.op('tensor', lambda e: e.matmul(pN[:, 0:128], lhsT=Xp[0][:], rhs=self.ident_f[:], start=True, stop=True), reads=['g_Xp0', 'ident_f'], writes=['PS:g_pN'])
                        P.op(a, lambda e: e.copy(out=Np[0][:], in_=pN[:, 0:128]), reads=['PS:g_pN'], writes=['g_Np0'])
                        P.op(g_, lambda e, vtok=vtok: e.tensor_copy(out=rr[0][:, 0:128], in_=vtok), reads=[tbk], writes=['g_rr0'])
                        P.op(g_, lambda e, ktok=ktok, h=h: e.tensor_scalar(out=rr[0][:, 128:256], in0=ktok, scalar1=ecum[:, h:h + 1], scalar2=None, op0=ALU.mult), reads=[tbk, 'g_ecum'], writes=['g_rr0'])
                        P.op('tensor', lambda e: e.matmul(pR[:, 0:256], lhsT=Xp[0][:], rhs=rr[0][:], start=True, stop=True), reads=['g_Xp0', 'g_rr0'], writes=['PS:g_pR'])
                        P.op(v, lambda e: e.tensor_tensor(out=rr[1][:], in0=rr[0][:], in1=pR[:, 0:256], op=ALU.subtract), reads=['g_rr0', 'PS:g_pR'], writes=['g_rr1'])
                        cur = 1
                        xi = 0
                        for lev in range(6):
                            nx = 1 - xi
                            P.op('tensor', lambda e, xi=xi: e.matmul(pN[:, 0:128], lhsT=Np[xi][:], rhs=Xp[xi][:], start=True, stop=True), reads=[f'g_Np{xi}', f'g_Xp{xi}'], writes=['PS:g_pN'])
                            if lev < 5:
                                P.op('tensor', lambda e, xi=xi: e.matmul(pN[:, 128:256], lhsT=Xp[xi][:], rhs=Np[xi][:], start=True, stop=True), reads=[f'g_Np{xi}', f'g_Xp{xi}'], writes=['PS:g_pN'])
                            P.op(a, lambda e, nx=nx: e.copy(out=Xp[nx][:], in_=pN[:, 0:128]), reads=['PS:g_pN'], writes=[f'g_Xp{nx}'])
                            if lev < 5:
                                P.op(v, lambda e, nx=nx: e.tensor_copy(out=Np[nx][:], in_=pN[:, 128:256]), reads=['PS:g_pN'], writes=[f'g_Np{nx}'])
                            P.op('tensor', lambda e, nx=nx, cur=cur: e.matmul(pR[:, 0:256], lhsT=Xp[nx][:], rhs=rr[cur][:], start=True, stop=True), reads=[f'g_Xp{nx}', f'g_rr{cur}'], writes=['PS:g_pR'])
                            P.op(v, lambda e, cur=cur: e.tensor_tensor(out=rr[1 - cur][:], in0=rr[cur][:], in1=pR[:, 0:256], op=ALU.add), reads=[f'g_rr{cur}', 'PS:g_pR'], writes=[f'g_rr{1 - cur}'])
                            cur = 1 - cur
                            xi = nx
                        R_ = rr[cur]
                        Rk = f'g_rr{cur}'
                        P.op('tensor', lambda e, R_=R_: e.matmul(pN[:, 256:384], lhsT=R_[:, 128:256], rhs=self.ident_f[:], start=True, stop=True), reads=[Rk, 'ident_f'], writes=['PS:g_pN'])
                        P.op(a, lambda e: e.copy(out=wT[:], in_=pN[:, 256:384]), reads=['PS:g_pN'], writes=['g_wT'])
                        P.op('tensor', lambda e, h=h: e.matmul(pA[:, 128:256], lhsT=wT[:], rhs=S_f[:, h, :], start=True, stop=True), reads=['g_wT', ('g_Sf', h)], writes=['PS:g_pA'])
                        P.op(v, lambda e, R_=R_: e.scalar_tensor_tensor(out=vn[:], in0=pA[:, 128:256], scalar=-1.0, in1=R_[:, 0:128], op0=ALU.mult, op1=ALU.add), reads=['PS:g_pA', Rk], writes=['g_vn'])
                        P.op(v, lambda e, bcol_=bcol_: e.tensor_scalar(out=v1[:], in0=vn[:], scalar1=bcol_, scalar2=None, op0=ALU.mult), reads=['g_vn', 'g_gt'], writes=['g_v1'])
                        P.op(g_, lambda e: e.tensor_scalar(out=v2[:], in0=v1[:], scalar1=Dm[:, ecol:ecol + 1], scalar2=None, op0=ALU.mult), reads=['g_v1', 'g_Dm'], writes=['g_v2'])
                        P.op('tensor', lambda e: e.matmul(pA[:, 0:128], lhsT=WT[:], rhs=v1[:], start=True, stop=True), reads=['g_WT', 'g_v1'], writes=['PS:g_pA'])
                        P.op('tensor', lambda e, h=h, qT_=qT_: e.matmul(pB[:, 0:128], lhsT=qT_, rhs=S_b[:, h, :], start=True, stop=True), reads=[qbk, ('g_Sb', h)], writes=['PS:g_pB'])
                        P.op('tensor', lambda e, ktok=ktok: e.matmul(pS[:, 0:128], lhsT=ktok, rhs=v2[:], start=True, stop=True), reads=[tbk, 'g_v2'], writes=['PS:g_pS'])
                        P.op(a, lambda e: e.copy(out=yA[:], in_=pA[:, 0:128]), reads=['PS:g_pA'], writes=['g_yA'])
                        P.op(v, lambda e, h=h, os_=os_: e.scalar_tensor_tensor(out=os_[:, h * 128:(h + 1) * 128], in0=pB[:, 0:128], scalar=ecum[:, h:h + 1], in1=yA[:], op0=ALU.mult, op1=ALU.add), reads=['PS:g_pB', 'g_ecum', 'g_yA'], writes=[osk])
                        P.op(v, lambda e, h=h: e.scalar_tensor_tensor(out=S_f[:, h, :], in0=S_f[:, h, :], scalar=cdec[:, h:h + 1], in1=pS[:, 0:128], op0=ALU.mult, op1=ALU.add), reads=[('g_Sf', h), 'g_cdec', 'PS:g_pS'], writes=[('g_Sf', h)])
                        P.op(a, lambda e, h=h: e.copy(out=S_b[:, h, :], in_=S_f[:, h, :]), reads=[('g_Sf', h)], writes=[('g_Sb', h)])
                    if d == 0:
                        P.dma('gpsimd', dr['YF'][t0:t0 + 128, 1024:2048], os_[:], reads=[osk], writes=[('YFg', c)])
                    else:
                        P.dma('sync', ofl[:], dr['YF'][t0:t0 + 128, 1024:2048], reads=[('YFg', c)], writes=['g_ofl'])
                        P.op(v, lambda e, os_=os_: e.tensor_tensor(out=ofl[:], in0=ofl[:], in1=os_[:], op=ALU.add), reads=['g_ofl', osk], writes=['g_ofl'])
                        for h in range(8):
                            P.op(a, lambda e, h=h: e.activation(out=sqj[:], in_=ofl[:, h * 128:(h + 1) * 128], func=AF.Square, accum_out=ss[:, h:h + 1]), reads=['g_ofl'], writes=['g_sqj', 'g_ss'])
                        P.op(a, lambda e: e.activation(out=ss[:], in_=ss[:], func=AF.Sqrt, scale=1.0 / 128, bias=self.eps_col[:]), reads=['g_ss', 'eps_col'], writes=['g_ss'])
                        P.op(v, lambda e: e.reciprocal(out=ss[:], in_=ss[:]), reads=['g_ss'], writes=['g_ss'])
                        for h in range(8):
                            P.op(v, lambda e, h=h: e.scalar_tensor_tensor(out=hn[:, h * 128:(h + 1) * 128], in0=ofl[:, h * 128:(h + 1) * 128], scalar=ss[:, h:h + 1], in1=nwB[:], op0=ALU.mult, op1=ALU.mult),
                                 reads=['g_ofl', 'g_ss', 'g_nwB'], writes=['g_hn'])
                        if c < 2:
                            tok0, nt, ch = 0, TC, c
                        else:
                            tok0, nt, ch = TC + ((c - 2) // 4) * 512, 512, (c - 2) % 4
                        for f in range(8):
                            P.op('tensor', lambda e, f=f: e.transpose(out=ptT[:, f, :], in_=hn[:, f * 128:(f + 1) * 128], identity=self.ident_b[:]), reads=['g_hn', 'ident_b'], writes=['PS:g_ptT'])
                        P.op(v, lambda e, ch=ch: e.tensor_copy(out=ytb[:, :, ch * 128:(ch + 1) * 128], in_=ptT[:]), reads=['PS:g_ptT'], writes=['g_ytb'])
                        if ch == 0:
                            P.dma('sync', zT[:, :, :nt], PT[R_ZG:R_ZG + 1024, :].rearrange("(f p) t -> p f t", p=128)[:, :, tok0:tok0 + nt], writes=['g_zT'])
                            P.op(a, lambda e, nt=nt: e.activation(out=zT[:, :, :nt], in_=zT[:, :, :nt], func=AF.Silu), reads=['g_zT'], writes=['g_zT'])
                            P.op(v, lambda e, nt=nt: e.tensor_tensor(out=ot[:, :, :nt], in0=ytb[:, :, :nt], in1=zT[:, :, :nt], op=ALU.mult), reads=['g_ytb', 'g_zT'], writes=['g_ot'])
                            P.dma('gpsimd', dr['YT'][0:1024, :].rearrange("(f p) t -> p f t", p=128)[:, :, tok0:tok0 + nt], ot[:, :, :nt], reads=['g_ot'], writes=[('YT', 'gdn', tok0)])
            P.barrier()


B.phase_gdn = phase_gdn


def phase_final(self, src):
    P, nc, dr = self.P, self.nc, self.dr
    sb, ps = self.sb, self.ps
    with ExitStack() as es:
        xt = [sb(es, f"fin_x{i}", [128, KC, 512], F32) for i in range(2)]
        sq = sb(es, "fin_sq", [128, KC, 512], BF16)
        rstd = sb(es, "fin_rstd", [128, 512], F32)
        fw = sb(es, "fin_w", [128, KC], F32)
        pss = ps(es, "fin_pss", [128, 512], F32)
        P.dma('sync', fw[:], dr['fnwT'], writes=['fin_w'])
        for bi, (tok0, nt, m) in enumerate(self.blocks()[1:]):
            x, xk = xt[bi % 2], f'fin_x{bi % 2}'
            P.dma('sync', x[:, :, :nt], src.rearrange("(kc p) t -> p kc t", p=128)[:, :, tok0:tok0 + nt], writes=[xk])
            P.op('scalar', lambda e, x=x, nt=nt: e.activation(out=sq[:, :, :nt], in_=x[:, :, :nt], func=AF.Square), reads=[xk], writes=['fin_sq'])
            for kc in range(KC):
                P.op('tensor', lambda e, kc=kc, nt=nt: e.matmul(pss[:, :nt], lhsT=self.ones_b[:], rhs=sq[:, kc, :nt], start=(kc == 0), stop=(kc == KC - 1)), reads=['fin_sq', 'ones_b'], writes=['PS:fin_pss'])
            P.op('scalar', lambda e, nt=nt: e.activation(out=rstd[:, :nt], in_=pss[:, :nt], func=AF.Sqrt, scale=1.0 / D, bias=self.eps_col[:]), reads=['PS:fin_pss', 'eps_col'], writes=['fin_rstd'])
            P.op('vector', lambda e, nt=nt: e.reciprocal(out=rstd[:, :nt], in_=rstd[:, :nt]), reads=['fin_rstd'], writes=['fin_rstd'])
            for kc in range(KC):
                P.op('vector', lambda e, kc=kc, x=x, nt=nt: e.scalar_tensor_tensor(out=x[:, kc, :nt], in0=x[:, kc, :nt], scalar=fw[:, kc:kc + 1], in1=rstd[:, :nt], op0=ALU.mult, op1=ALU.mult),
                     reads=[xk, 'fin_w', 'fin_rstd'], writes=[xk])
            P.dma('gpsimd', dr['outT'].rearrange("(kc p) t -> p kc t", p=128)[:, :, tok0 - TC:tok0 - TC + nt], x[:, :, :nt], reads=[xk], writes=[('outT', tok0)])
        P.barrier()


B.phase_final = phase_final
```

```python
import numpy as np
from contextlib import ExitStack
import concourse.bass as bass
import concourse.mybir as mybir
from concourse.bass_utils import run_bass_kernel_spmd

F32 = mybir.dt.float32
BF16 = mybir.dt.bfloat16
I32 = mybir.dt.int32
AF = mybir.ActivationFunctionType
ALU = mybir.AluOpType
AX = mybir.AxisListType

D = 2048
KC = 16
TC = 256
TL = 4096
T = TC + TL
FFN = 5504
EPS = 1e-6
EVEN_IN = 4656
ODD_IN = 7216
NEG = -30000.0

SEM_LIMIT = 4000
DMA_SLOT_LIMIT = 1200
DMA_POOL = 6


class Prog:
    ENG = ('sync', 'scalar', 'vector', 'gpsimd', 'tensor')

    def __init__(self, nc, es):
        self.nc = nc
        self.es = es
        self.e = dict(sync=nc.sync, scalar=nc.scalar, vector=nc.vector,
                      gpsimd=nc.gpsimd, tensor=nc.tensor)
        self.semh = []
        self.cur = {}
        self.cnt = {}
        self.known = {e: {} for e in self.ENG}
        self.lastw = {}
        self.readers = {}
        self.pool = {}
        self.pidx = {}
        self.nins = {e: 0 for e in self.ENG}
        for e in self.ENG:
            self._fresh(e)

    def _newsem(self, name):
        h = self.es.enter_context(self.nc.semaphore(name))
        self.semh.append(h)
        return len(self.semh) - 1

    def _fresh(self, e):
        self.cur[e] = self._newsem(f"s_{e}_{len(self.semh)}")
        self.cnt[e] = 0

    def _deps(self, reads, writes):
        d = {}

        def add(tok):
            if tok is None:
                return
            sk, val, pe = tok
            if sk not in d or d[sk][0] < val:
                d[sk] = (val, pe)
        for k in reads:
            add(self.lastw.get(k))
        for k in writes:
            add(self.lastw.get(k))
            for t in self.readers.get(k, ()):
                add(t)
        return d

    def _update(self, reads, writes, tok):
        for k in reads:
            self.readers.setdefault(k, []).append(tok)
        for k in writes:
            self.lastw[k] = tok
            self.readers[k] = []

    def _wait(self, eng, sk, val):
        if self.known[eng].get(sk, 0) >= val:
            return
        self.e[eng].wait_ge(self.semh[sk], val)
        self.known[eng][sk] = val
        self.nins[eng] += 1

    def _waits(self, eng, deps):
        for sk, (val, pe) in deps.items():
            if pe == 'tensor' and eng == 'tensor':
                continue
            self._wait(eng, sk, val)

    @staticmethod
    def _excl(reads, writes):
        ex = [k for k in reads if isinstance(k, str) and k.startswith('PS:')]
        if ex:
            reads = [k for k in reads if k not in ex]
            writes = list(writes) + ex
        return reads, writes

    def op(self, eng, fn, reads=(), writes=()):
        reads, writes = self._excl(reads, writes)
        deps = self._deps(reads, writes)
        self._waits(eng, deps)
        ins = fn(self.e[eng])
        if self.cnt[eng] >= SEM_LIMIT:
            self._fresh(eng)
        self.cnt[eng] += 1
        ins.then_inc(self.semh[self.cur[eng]], 1)
        tok = (self.cur[eng], self.cnt[eng], eng)
        self._update(reads, writes, tok)
        self.nins[eng] += 1
        return tok

    def dma(self, q, out, in_, reads=(), writes=(), **kw):
        deps = self._deps(reads, writes)
        self._waits(q, deps)
        E = self.e[q]
        if q not in self.pool:
            self.pool[q] = [[self._newsem(f"d_{q}_{i}_{len(self.semh)}"), 0] for i in range(DMA_POOL)]
            self.pidx[q] = 0
        i = self.pidx[q] % DMA_POOL
        self.pidx[q] += 1
        slot = self.pool[q][i]
        if slot[1] >= DMA_SLOT_LIMIT:
            self._wait(q, slot[0], 16 * slot[1])
            slot[0] = self._newsem(f"d_{q}_{i}_{len(self.semh)}")
            slot[1] = 0
        sk, n = slot
        if n > 0:
            self._wait(q, sk, 16 * n)
        ins = E.dma_start(out=out, in_=in_, **kw)
        ins.then_inc(self.semh[sk], 16)
        slot[1] = n + 1
        tok = (sk, 16 * (n + 1), 'dma')
        self._update(reads, writes, tok)
        self.nins[q] += 1
        return tok

    def barrier(self):
        toks = []
        for q, slots in self.pool.items():
            for sk, n in slots:
                if n > 0:
                    toks.append((sk, 16 * n))
        for e in self.ENG:
            if self.cnt[e] > 0:
                toks.append((self.cur[e], self.cnt[e]))
        for e in self.ENG:
            for sk, val in toks:
                self._wait(e, sk, val)
        self.lastw.clear()
        self.readers.clear()

    def finish(self):
        self.barrier()


class B:
    def __init__(self, stage=99, dbg=(), sub=99):
        self.sub = sub
        self.stage = stage
        self.dbg = set(dbg)
        self.nc = bass.Bass("TRN2", target_bir_lowering=False)
        self.dr = {}

    def din(self, name, shape, dt=F32):
        self.dr[name] = self.nc.dram_tensor(name, list(shape), dt, kind="ExternalInput").ap()
        return self.dr[name]

    def dout(self, name, shape, dt=F32):
        self.dr[name] = self.nc.dram_tensor(name, list(shape), dt, kind="ExternalOutput").ap()
        return self.dr[name]

    def dscr(self, name, shape, dt=F32):
        kind = "ExternalOutput" if name in self.dbg else "Internal"
        self.dr[name] = self.nc.dram_tensor(name, list(shape), dt, kind=kind).ap()
        return self.dr[name]

    def _uniq(self, name):
        self._names = getattr(self, '_names', {})
        n = self._names.get(name, 0)
        self._names[name] = n + 1
        return name if n == 0 else f"{name}_u{n}"

    def sb(self, es, name, shape, dt=F32):
        return es.enter_context(self.nc.sbuf_tensor(self._uniq(name), list(shape), dt))

    def ps(self, es, name, shape, dt=F32):
        return es.enter_context(self.nc.psum_tensor(self._uniq(name), list(shape), dt))

    def build(self):
        nc = self.nc
        din = self.din
        din("xT", [D, T])
        din("ccT", [128, KC, 2])
        din("mod_w", [2, D, 6 * D])
        din("mod_bT", [2, 128, 96])
        din("nmwT", [2, 128, KC])
        din("nfwT", [2, 128, KC])
        din("ev_in_w", [D, EVEN_IN])
        din("od_in_w", [D, ODD_IN])
        self.dscr("XS", [D, T])
        self.dscr("PT", [ODD_IN, T])
        self.dscr("WB", [D, ODD_IN], BF16)
        din("ssd_cw", [128, 20, 3])
        din("ssd_cb", [128, 20])
        din("ssd_dtb", [48, 1])
        din("ssd_alog", [48, 1])
        din("ssd_d", [1, 24])
        din("ssd_nw", [128, 12])
        self.dscr("TOK", [T, 2048], BF16)
        self.dscr("BCT", [2048, T], BF16)
        self.dscr("YF", [T, 2048])
        self.dscr("YT", [D, T], BF16)
        for nm in ("s5_lre", "s5_lim", "s5_dlt"):
            din(nm, [128, 32])
        for nm in ("s5_bre", "s5_bim", "s5_cre", "s5_cim"):
            din(nm, [128, 32, 16])
        din("s5_dsk", [128, 4])
        din("s5_glu_w", [512, 512])
        din("fnwT", [128, KC])
        din("gdn_dtb", [16, 1])
        din("gdn_alog", [16, 1])
        din("gdn_cw", [128, 24, 3])
        din("gdn_nw", [1, 128])
        din("ml_ib", [8, 1])
        din("ml_fb", [8, 1])
        din("ml_nw", [1, 1024])
        din("ev_out_w", [D, D])
        din("od_out_w", [D, D])
        din("ffn_up_w", [2, D, 2 * FFN])
        din("ffn_down_w", [2, FFN, D])
        din("ffn_cw", [2, 128, 43, 9])
        self.dscr("WO", [D, D], BF16)
        self.dscr("WU", [D, 2 * FFN], BF16)
        self.dscr("WD", [FFN, D], BF16)
        self.dscr("XS2", [D, T])
        self.dout("outT", [D, TL])
        if 'YTin' in self.dbg:
            din('YTin', [D, T])
        if 'XSin' in self.dbg:
            din('XSin', [D, T])
        dshapes = {'dbg_mod': [128, 96, 2], 'dbg_h': [D, T], 'dbg_gates': [2, 128, 34 * 48]}
        for k in self.dbg:
            if k in dshapes:
                self.dout(k, dshapes[k])
        with ExitStack() as es:
            self.P = Prog(nc, es)
            self.consts(es)
            for layer in range(2):
                if 'XSin' in self.dbg:
                    if layer == 0:
                        for r in range(0, D, 512):
                            self.P.dma('gpsimd', self.dr['XS'][r:r + 512, :], self.dr['XSin'][r:r + 512, :], writes=['xsin'])
                        self.P.barrier()
                        continue
                self.layer(layer)
                if self.stage <= layer * 10 + 9:
                    break
            self.P.finish()
        return nc

    def consts(self, es):
        P = self.P
        sb = self.sb
        self.ident_b = sb(es, "ident_b", [128, 128], BF16)
        self.ident_f = sb(es, "ident_f", [128, 128], F32)
        self.ones_b = sb(es, "ones_b", [128, 128], BF16)
        self.ones_f = sb(es, "ones_f", [128, 128], F32)
        self.negI = sb(es, "negI", [128, 128], F32)
        self.MGT = sb(es, "MGT", [128, 128], F32)
        self.MLT = sb(es, "MLT", [128, 128], F32)
        self.TLE = sb(es, "TLE", [128, 128], F32)
        self.TGE = sb(es, "TGE", [128, 128], F32)
        g = 'gpsimd'
        P.op(g, lambda e: e.memset(self.ones_f[:], 1.0), writes=['ones_f'])
        P.op(g, lambda e: e.memset(self.ones_b[:], 1.0), writes=['ones_b'])

        def sel(out, cm, step, cmp, key, fill=0.0, src=None):
            src = self.ones_f if src is None else src
            P.op(g, lambda e: e.affine_select(out=out[:], in_=src[:], pattern=[[step, 128]], base=0,
                                              channel_multiplier=cm, compare_op=cmp, fill=fill),
                 reads=['ones_f'], writes=[key])
        sel(self.ident_f, 1, -1, ALU.is_equal, 'ident_f')
        sel(self.MGT, 1, -1, ALU.is_gt, 'MGT')
        sel(self.MLT, -1, 1, ALU.is_gt, 'MLT')
        sel(self.TLE, -1, 1, ALU.is_ge, 'TLE')
        sel(self.TGE, 1, -1, ALU.is_ge, 'TGE')
        P.op(g, lambda e: e.tensor_copy(out=self.ident_b[:], in_=self.ident_f[:]), reads=['ident_f'], writes=['ident_b'])
        P.op(g, lambda e: e.tensor_scalar(out=self.negI[:], in0=self.ident_f[:], scalar1=NEG, scalar2=None, op0=ALU.mult),
             reads=['ident_f'], writes=['negI'])
        self.modT = sb(es, "modT", [128, 96, 2], F32)
        self.s1 = sb(es, "s1", [128, KC, 2], F32)
        self.s2 = sb(es, "s2", [128, KC, 2], F32)
        self.eps_col = sb(es, "eps_col", [128, 1], F32)
        P.op(g, lambda e: e.memset(self.eps_col[:], EPS), writes=['eps_col'])
        self.one_col = sb(es, "one_col", [128, 1], F32)
        P.op(g, lambda e: e.memset(self.one_col[:], 1.0), writes=['one_col'])
        P.barrier()

    def layer(self, layer):
        import os
        if os.environ.get('SKIP12'):
            self.phase_ssd()
            return
        self.phase_mod(layer)
        if self.stage <= layer * 10 + 1:
            return
        self.phase_inproj(layer)
        if self.stage <= layer * 10 + 2:
            return
        if layer == 0 and 'YTin' not in self.dbg:
            import os
            if not os.environ.get('NOS5'):
                self.phase_s5()
            if self.stage <= layer * 10 + 2 or os.environ.get('NOSSD'):
                return
            self.phase_ssd()
            if self.stage <= layer * 10 + 3:
                return
        if layer == 1 and 'YTin' not in self.dbg:
            import os
            if not os.environ.get('NOGDN'):
                self.phase_gdn()
            if not os.environ.get('NOML'):
                self.phase_mlstm()
            if self.stage <= layer * 10 + 3:
                return
        if 'YTin' in self.dbg:
            self.P.dma('gpsimd', self.dr['YT'], self.dr['YTin'], writes=['ytin'])
            self.P.barrier()
        src = self.dr['xT'] if layer == 0 else self.dr['XS']
        self.phase_outproj(layer, src, self.dr['XS2'])
        if self.stage <= layer * 10 + 4:
            return
        self.phase_ffn(layer, self.dr['XS2'], self.dr['XS'])
        if layer == 1:
            self.phase_final(self.dr['XS'])

    def phase_mod(self, layer):
        P, nc = self.P, self.nc
        dr = self.dr
        with ExitStack() as es:
            sb, ps = self.sb, self.ps
            PW = 768
            wt = [sb(es, f"modw{i}", [128, KC, PW], F32) for i in range(2)]
            cc = sb(es, "cc", [128, KC, 2], F32)
            scc = sb(es, "scc", [128, KC, 2], F32)
            mb = sb(es, "mb", [128, 96], F32)
            nmw = sb(es, "nmw", [128, KC], F32)
            nfw = sb(es, "nfw", [128, KC], F32)
            pm = ps(es, "pm", [128, 96, 2], F32)
            P.dma('sync', cc[:], dr["ccT"], writes=['cc'])
            P.dma('sync', mb[:], dr["mod_bT"][layer], writes=['mb'])
            P.dma('sync', nmw[:], dr["nmwT"][layer], writes=['nmw'])
            P.dma('sync', nfw[:], dr["nfwT"][layer], writes=['nfw'])
            P.op('scalar', lambda e: e.activation(out=scc[:], in_=cc[:], func=AF.Silu), reads=['cc'], writes=['scc'])
            wsrc = dr["mod_w"][layer].rearrange("(kc p) n -> p kc n", p=128)
            for pn in range(16):
                w = wt[pn % 2]
                wk = f'modw{pn % 2}'
                P.dma('sync' if pn % 2 == 0 else 'gpsimd', w[:], wsrc[:, :, pn * PW:(pn + 1) * PW], writes=[wk])
                for jj in range(6):
                    j = pn * 6 + jj
                    for kc in range(KC):
                        P.op('tensor', lambda e, w=w, jj=jj, kc=kc, j=j: e.matmul(
                            pm[:, j, :], lhsT=w[:, kc, jj * 128:(jj + 1) * 128], rhs=scc[:, kc, :],
                            start=(kc == 0), stop=(kc == KC - 1)), reads=[wk, 'scc'], writes=['PS:pm'])
            for m in range(2):
                P.op('vector', lambda e, m=m: e.tensor_tensor(out=self.modT[:, :, m], in0=pm[:, :, m], in1=mb[:], op=ALU.add),
                     reads=['PS:pm', 'mb'], writes=['modT'])
            for m in range(2):
                P.op('vector', lambda e, m=m: e.scalar_tensor_tensor(
                    out=self.s1[:, :, m], in0=self.modT[:, 16:32, m], scalar=1.0, in1=nmw[:], op0=ALU.add, op1=ALU.mult),
                    reads=['modT', 'nmw'], writes=['s1'])
                P.op('vector', lambda e, m=m: e.scalar_tensor_tensor(
                    out=self.s2[:, :, m], in0=self.modT[:, 64:80, m], scalar=1.0, in1=nfw[:], op0=ALU.add, op1=ALU.mult),
                    reads=['modT', 'nfw'], writes=['s2'])
            if 'dbg_mod' in self.dbg:
                P.dma('sync', dr['dbg_mod'], self.modT[:], reads=['modT'], writes=['dbg_mod'])
            P.barrier()

    def blocks(self):
        return [(0, TC, 1)] + [(TC + i * 512, 512, 0) for i in range(8)]

    def norm_block(self, src, tok0, nt, m, scale_t, shift_lo, bufs, keys):
        P = self.P
        xt, sq, hT, rstd, tmp, pss = bufs['xt'], bufs['sq'], bufs['hT'], bufs['rstd'], bufs['tmp'], bufs['pss']
        kx, ksq, kh, kr, kt, kp = keys
        P.dma('sync', xt[:, :, :nt], src.rearrange("(kc p) t -> p kc t", p=128)[:, :, tok0:tok0 + nt], writes=[kx])
        P.op('scalar', lambda e: e.activation(out=sq[:, :, :nt], in_=xt[:, :, :nt], func=AF.Square), reads=[kx], writes=[ksq])
        for kc in range(KC):
            P.op('tensor', lambda e, kc=kc: e.matmul(pss[:, :nt], lhsT=self.ones_b[:], rhs=sq[:, kc, :nt],
                                                     start=(kc == 0), stop=(kc == KC - 1)), reads=[ksq, 'ones_b'], writes=[kp])
        P.op('scalar', lambda e: e.activation(out=rstd[:, :nt], in_=pss[:, :nt], func=AF.Sqrt, scale=1.0 / D, bias=self.eps_col[:]),
             reads=[kp, 'eps_col'], writes=[kr])
        P.op('vector', lambda e: e.reciprocal(out=rstd[:, :nt], in_=rstd[:, :nt]), reads=[kr], writes=[kr])
        for kc in range(KC):
            tk = kt + str(kc % 2)
            t = tmp[kc % 2]
            P.op('vector', lambda e, kc=kc, t=t: e.tensor_tensor(out=t[:, :nt], in0=xt[:, kc, :nt], in1=rstd[:, :nt], op=ALU.mult),
                 reads=[kx, kr], writes=[tk])
            P.op('scalar', lambda e, kc=kc, t=t: e.activation(out=hT[:, kc, :nt], in_=t[:, :nt], func=AF.Identity,
                                                              scale=scale_t[:, kc, m:m + 1], bias=self.modT[:, shift_lo + kc, m:m + 1]),
                 reads=[tk, 'modT', 's1', 's2'], writes=[kh + str(kc)])

    def phase_inproj(self, layer):
        P, nc, dr = self.P, self.nc, self.dr
        nin = EVEN_IN if layer == 0 else ODD_IN
        wsrc = dr["ev_in_w"] if layer == 0 else dr["od_in_w"]
        WB = dr["WB"]
        for r in range(0, D, 256):
            P.dma('gpsimd', WB[r:r + 256, :nin], wsrc[r:r + 256, :], writes=[('WB', r)])
        src = dr["xT"] if layer == 0 else dr["XS"]
        ntile = [(c0, min(128, nin - c0)) for c0 in range(0, nin, 128)]
        PWT = 4
        with ExitStack() as es:
            sb, ps = self.sb, self.ps
            bufs = dict(xt=sb(es, "xt", [128, KC, 512], F32), sq=sb(es, "sq", [128, KC, 512], BF16),
                        hT=sb(es, "hT", [128, KC, 512], BF16), rstd=sb(es, "rstd", [128, 512], F32),
                        tmp=[sb(es, f"ntmp{i}", [128, 512], F32) for i in range(2)],
                        pss=ps(es, "pss", [128, 512], F32))
            wb = [sb(es, f"wb{i}", [128, KC, PWT * 128], BF16) for i in range(2)]
            stg = [sb(es, f"stg{i}", [128, 512], F32) for i in range(3)]
            pacc = [ps(es, f"pacc{i}", [128, 512], F32) for i in range(4)]
            wv = WB.rearrange("(kc p) n -> p kc n", p=128)
            hkeys = ['hT' + str(kc) for kc in range(KC)]
            it = 0
            pi = 0
            for (tok0, nt, m) in self.blocks():
                self.norm_block(src, tok0, nt, m, self.s1, 0, bufs, ('xt', 'sq', 'hT', 'rstd', 'ntmp', 'PS:pss'))
                if 'dbg_h' in self.dbg:
                    hf = bufs['xt']
                    P.op('vector', lambda e: e.tensor_copy(out=hf[:, :, :nt], in_=bufs['hT'][:, :, :nt]), reads=hkeys + ['xt'], writes=['xt'])
                    P.dma('sync', dr['dbg_h'].rearrange("(kc p) t -> p kc t", p=128)[:, :, tok0:tok0 + nt], hf[:, :, :nt], reads=['xt'], writes=['dbg_h'])
                for p0 in range(0, len(ntile), PWT):
                    tiles = ntile[p0:p0 + PWT]
                    c0 = tiles[0][0]
                    cw = sum(t[1] for t in tiles)
                    w = wb[pi % 2]
                    wk = f'wb{pi % 2}'
                    pi += 1
                    P.dma('sync', w[:, :, :cw], wv[:, :, c0:c0 + cw], reads=[('WB', r) for r in range(0, D, 256)], writes=[wk])
                    for (tc0, tw) in tiles:
                        pa = pacc[it % 4]
                        pk = f'PS:pacc{it % 4}'
                        st = stg[it % 3]
                        sk = f'stg{it % 3}'
                        for kc in range(KC):
                            P.op('tensor', lambda e, kc=kc, pa=pa, w=w, tc0=tc0, tw=tw, c0=c0: e.matmul(
                                pa[:tw, :nt], lhsT=w[:, kc, tc0 - c0:tc0 - c0 + tw], rhs=bufs['hT'][:, kc, :nt],
                                start=(kc == 0), stop=(kc == KC - 1)), reads=[wk, hkeys[kc]], writes=[pk])
                        eng = 'scalar' if it % 2 == 0 else 'vector'
                        if eng == 'scalar':
                            P.op('scalar', lambda e, pa=pa, st=st, tw=tw: e.copy(out=st[:tw, :nt], in_=pa[:tw, :nt]), reads=[pk], writes=[sk])
                        else:
                            P.op('vector', lambda e, pa=pa, st=st, tw=tw: e.tensor_copy(out=st[:tw, :nt], in_=pa[:tw, :nt]), reads=[pk], writes=[sk])
                        P.dma('gpsimd', dr['PT'][tc0:tc0 + tw, tok0:tok0 + nt], st[:tw, :nt], reads=[sk], writes=[('PT', tc0, tok0)])
                        it += 1
            P.barrier()


def _ssd_methods():
    pass


def build_program(stage=99, dbg=()):
    b = B(stage, dbg)
    for name in dbg:
        pass
    return b


def make_inputs_small(inputs, b):
    f = np.float32
    x = np.asarray(inputs['x'], f)
    ctx = np.asarray(inputs['ctx'], f)
    c = np.asarray(inputs['c'], f)
    c_ctx = np.asarray(inputs['c_ctx'], f)
    xT = np.ascontiguousarray(np.concatenate([ctx[b], x[b]], axis=0).T)
    cc = np.stack([c[b], c_ctx], axis=0)
    ccT = np.ascontiguousarray(cc.reshape(2, KC, 128).transpose(2, 1, 0))
    return {"xT": xT, "ccT": ccT}


def make_inputs(inputs, b):
    f = np.float32
    x = np.asarray(inputs['x'], f)
    ctx = np.asarray(inputs['ctx'], f)
    c = np.asarray(inputs['c'], f)
    c_ctx = np.asarray(inputs['c_ctx'], f)
    xT = np.ascontiguousarray(np.concatenate([ctx[b], x[b]], axis=0).T)
    cc = np.stack([c[b], c_ctx], axis=0)
    ccT = np.ascontiguousarray(cc.reshape(2, KC, 128).transpose(2, 1, 0))
    mod_b = np.asarray(inputs['mod_b'], f)
    m = {
        "xT": xT, "ccT": ccT,
        "mod_w": np.ascontiguousarray(np.asarray(inputs['mod_w'], f)),
        "mod_bT": np.ascontiguousarray(mod_b.reshape(2, 96, 128).transpose(0, 2, 1)),
        "nmwT": np.ascontiguousarray(np.asarray(inputs['norm_mix_w'], f).reshape(2, KC, 128).transpose(0, 2, 1)),
        "nfwT": np.ascontiguousarray(np.asarray(inputs['norm_ffn_w'], f).reshape(2, KC, 128).transpose(0, 2, 1)),
        "ev_in_w": np.ascontiguousarray(np.asarray(inputs['ev_in_w'], f)[0]),
        "od_in_w": np.ascontiguousarray(np.asarray(inputs['od_in_w'], f)[0]),
    }
    m["ev_out_w"] = np.ascontiguousarray(np.asarray(inputs['ev_out_w'], f)[0])
    m["od_out_w"] = np.ascontiguousarray(np.asarray(inputs['od_out_w'], f)[0])
    m["ffn_up_w"] = np.ascontiguousarray(np.asarray(inputs['ffn_up_w'], f))
    m["ffn_down_w"] = np.ascontiguousarray(np.asarray(inputs['ffn_down_w'], f))
    fcw = np.asarray(inputs['ffn_conv_w'], f)
    fcw = np.concatenate([fcw.reshape(2, 9, FFN), np.zeros((2, 9, 43 * 128 - FFN), f)], axis=2)
    m["ffn_cw"] = np.ascontiguousarray(fcw.reshape(2, 9, 43, 128).transpose(0, 3, 2, 1))
    m["fnwT"] = np.ascontiguousarray(np.asarray(inputs['final_norm_w'], f).reshape(KC, 128).T)
    m["gdn_dtb"] = np.ascontiguousarray(np.asarray(inputs['gdn_dt_bias'], f)[0].reshape(16, 1))
    m["gdn_alog"] = np.ascontiguousarray(np.asarray(inputs['gdn_a_log'], f)[0].reshape(16, 1))
    gcw = np.asarray(inputs['gdn_conv_w'], f)[0]
    m["gdn_cw"] = np.ascontiguousarray(gcw.reshape(3, 24, 128).transpose(2, 1, 0))
    m["gdn_nw"] = np.ascontiguousarray(np.asarray(inputs['gdn_norm_w'], f)[0].reshape(1, 128))
    m["ml_ib"] = np.ascontiguousarray(np.asarray(inputs['mlstm_igate_b'], f)[0].reshape(8, 1))
    m["ml_fb"] = np.ascontiguousarray(np.asarray(inputs['mlstm_fgate_b'], f)[0].reshape(8, 1))
    m["ml_nw"] = np.ascontiguousarray(np.asarray(inputs['mlstm_norm_w'], f)[0].reshape(1, 1024))
    def pair(x):
        sh = x.shape[3:]
        x = x.reshape((2, 16, 2, 64) + sh)
        x = np.moveaxis(x, (2, 3), (0, 1))
        return np.ascontiguousarray(x.reshape((128, 32) + sh))
    m["s5_lre"] = pair(np.asarray(inputs['s5_lam_re'], f)[0])
    m["s5_lim"] = pair(np.asarray(inputs['s5_lam_im'], f)[0])
    m["s5_dlt"] = pair(np.repeat(np.asarray(inputs['s5_log_step'], f)[0][:, :, None], 64, axis=2))
    m["s5_bre"] = pair(np.asarray(inputs['s5_b_re'], f)[0])
    m["s5_bim"] = pair(np.asarray(inputs['s5_b_im'], f)[0])
    m["s5_cre"] = pair(np.asarray(inputs['s5_c_re'], f)[0].transpose(0, 1, 3, 2))
    m["s5_cim"] = pair(np.asarray(inputs['s5_c_im'], f)[0].transpose(0, 1, 3, 2))
    m["s5_dsk"] = np.ascontiguousarray(np.asarray(inputs['s5_d'], f)[0].reshape(4, 128).T)
    m["s5_glu_w"] = np.ascontiguousarray(np.asarray(inputs['s5_glu_w'], f)[0])
    cw = np.asarray(inputs['ssd_conv_w'], f)[0]
    m["ssd_cw"] = np.ascontiguousarray(cw.reshape(3, 20, 128).transpose(2, 1, 0))
    m["ssd_cb"] = np.ascontiguousarray(np.asarray(inputs['ssd_conv_b'], f)[0].reshape(20, 128).T)
    m["ssd_dtb"] = np.ascontiguousarray(np.asarray(inputs['ssd_dt_bias'], f)[0].reshape(48, 1))
    m["ssd_alog"] = np.ascontiguousarray(np.asarray(inputs['ssd_a_log'], f)[0].reshape(48, 1))
    m["ssd_d"] = np.ascontiguousarray(np.asarray(inputs['ssd_d'], f)[0].reshape(1, 24))
    m["ssd_nw"] = np.ascontiguousarray(np.asarray(inputs['ssd_norm_w'], f)[0].reshape(12, 128).T)
    return m


def kernel(**inputs):
    b = B()
    nc = b.build()
    n = 8
    shared = make_inputs(inputs, 0)
    in_maps = []
    for i in range(n):
        mi = dict(shared)
        if i % 4 != 0:
            pi = make_inputs_small(inputs, i % 4)
            mi.update(pi)
        in_maps.append(mi)
    res = run_bass_kernel_spmd(nc, in_maps, core_ids=list(range(n)))
    out = np.stack([np.ascontiguousarray(res.results[i]["outT"].T) for i in range(4)], axis=0)
    return out.astype(np.float32)


def chunk_order(d):
    if d == 0:
        return list(range(34))
    return [1, 0] + list(range(33, 1, -1))


def phase_ssd(self):
    import os
    P, nc, dr = self.P, self.nc, self.dr
    sb, ps = self.sb, self.ps
    PT = dr['PT']
    with ExitStack() as es0:
        dt_tok = sb(es0, "dt_tok", [128, 34, 48], F32)
        la_tok = sb(es0, "la_tok", [128, 34, 48], F32)
        with ExitStack() as es:
            dtr = sb(es, "dtr", [128, T], F32)
            laT = sb(es, "laT", [128, T], F32)
            P.op('gpsimd', lambda e: e.memset(dtr[:], 0.0), writes=['dtr'])
            P.op('gpsimd', lambda e: e.memset(laT[:], 0.0), writes=['laT'])
            dtb = sb(es, "dtb", [48, 1], F32)
            alog = sb(es, "alog", [48, 1], F32)
            nega = sb(es, "nega", [48, 1], F32)
            ptr = ps(es, "ptr_g", [128, 512], F32)
            if os.environ.get('SKIP12') or os.environ.get('DTRMEM'):
                P.op('gpsimd', lambda e: e.memset(dtr[:48, :], 0.5), writes=['dtr'])
            else:
                P.dma('sync', dtr[:48, :], PT[4608:4656, :], writes=['dtr'])
            P.dma('sync', dtb[:], dr['ssd_dtb'], writes=['dtb'])
            P.dma('sync', alog[:], dr['ssd_alog'], writes=['alog'])
            P.op('scalar', lambda e: e.activation(out=nega[:], in_=alog[:], func=AF.Exp), reads=['alog'], writes=['nega'])
            P.op('vector', lambda e: e.tensor_scalar(out=nega[:], in0=nega[:], scalar1=-1.0, scalar2=None, op0=ALU.mult), reads=['nega'], writes=['nega'])
            P.op('scalar', lambda e: e.activation(out=dtr[:48, :], in_=dtr[:48, :], func=AF.Exp, bias=dtb[:]), reads=['dtr', 'dtb'], writes=['dtr'])
            P.op('scalar', lambda e: e.activation(out=dtr[:48, :], in_=dtr[:48, :], func=AF.Ln, bias=self.one_col[:48, :]), reads=['dtr', 'one_col'], writes=['dtr'])
            P.op('vector', lambda e: e.tensor_scalar(out=laT[:48, :], in0=dtr[:48, :], scalar1=nega[:], scalar2=None, op0=ALU.mult), reads=['dtr', 'nega'], writes=['laT'])
            import os
            CUT = int(os.environ.get('CUT', '99'))
            if CUT <= 2:
                P.op('gpsimd', lambda e: e.memset(dt_tok[:], 0.05), writes=['dt_tok'])
                P.op('gpsimd', lambda e: e.memset(la_tok[:], -0.01), writes=['la_tok'])
            VV = os.environ.get('VV', 'ABCD')
            for c in range((34 if CUT > 3 else 1) if CUT > 2 else 0):
                if 'A' in VV:
                    P.op('tensor', lambda e, c=c: e.matmul(ptr[:, 0:48], lhsT=dtr[:, c * 128:(c + 1) * 128], rhs=self.ident_f[:, :48], start=True, stop=True),
                         reads=['dtr', 'ident_f'], writes=['PS:ptr_g'])
                if 'B' in VV:
                    P.op('tensor', lambda e, c=c: e.matmul(ptr[:, 64:112], lhsT=laT[:, c * 128:(c + 1) * 128], rhs=self.ident_f[:, :48], start=True, stop=True),
                         reads=['laT', 'ident_f'], writes=['PS:ptr_g'])
                if 'C' in VV:
                    P.op('vector', lambda e, c=c: e.tensor_copy(out=dt_tok[:, c, :], in_=ptr[:, 0:48]), reads=['PS:ptr_g'], writes=['dt_tok'])
                if 'D' in VV:
                    P.op('scalar', lambda e, c=c: e.copy(out=la_tok[:, c, :], in_=ptr[:, 64:112]), reads=['PS:ptr_g'], writes=['la_tok'])
            P.barrier()
        if self.sub <= 1:
            return
        with ExitStack() as es:
            xin = [sb(es, f"xin{i}", [128, 20, 514], F32) for i in range(2)]
            cs = sb(es, "cs", [128, 20, 512], BF16)
            acc = [sb(es, f"cacc{i}", [128, 512], F32) for i in range(2)]
            cw = sb(es, "cw", [128, 20, 3], F32)
            cb = sb(es, "cb", [128, 20], F32)
            tokst = [sb(es, f"tokst{i}", [128, 2048], BF16) for i in range(2)]
            ptA = ps(es, "ptA", [128, 8, 128], BF16)
            ptB = ps(es, "ptB", [128, 8, 128], BF16)
            P.dma('sync', cw[:], dr['ssd_cw'], writes=['cw'])
            P.dma('sync', cb[:], dr['ssd_cb'], writes=['cb'])
            src = PT[2048:4608, :].rearrange("(f p) t -> p f t", p=128)
            ci = 0
            for bi, (tok0, nt, m) in enumerate(self.blocks()):
                x = xin[bi % 2]
                xk = f'xin{bi % 2}'
                seq0, seq1 = (0, TC) if m == 1 else (TC, T)
                lo = max(tok0 - 1, seq0)
                hi = min(tok0 + nt + 1, seq1)
                P.dma('sync', x[:, :, lo - (tok0 - 1):hi - (tok0 - 1)], src[:, :, lo:hi], writes=[xk])
                if lo > tok0 - 1:
                    P.op('gpsimd', lambda e, x=x: e.memset(x[:, :, 0:1], 0.0), writes=[xk])
                if hi < tok0 + nt + 1:
                    P.op('gpsimd', lambda e, x=x, nt=nt: e.memset(x[:, :, nt + 1:nt + 2], 0.0), writes=[xk])
                for f in range(20):
                    a = acc[f % 2]
                    ak = f'cacc{f % 2}'
                    P.op('vector', lambda e, x=x, a=a, f=f, nt=nt: e.tensor_scalar(out=a[:, :nt], in0=x[:, f, 0:nt], scalar1=cw[:, f, 0:1], scalar2=None, op0=ALU.mult),
                         reads=[xk, 'cw'], writes=[ak])
                    P.op('vector', lambda e, x=x, a=a, f=f, nt=nt: e.scalar_tensor_tensor(out=a[:, :nt], in0=x[:, f, 1:nt + 1], scalar=cw[:, f, 1:2], in1=a[:, :nt], op0=ALU.mult, op1=ALU.add),
                         reads=[xk, 'cw', ak], writes=[ak])
                    P.op('vector', lambda e, x=x, a=a, f=f, nt=nt: e.scalar_tensor_tensor(out=a[:, :nt], in0=x[:, f, 2:nt + 2], scalar=cw[:, f, 2:3], in1=a[:, :nt], op0=ALU.mult, op1=ALU.add),
                         reads=[xk, 'cw', ak], writes=[ak])
                    P.op('scalar', lambda e, a=a, f=f, nt=nt: e.activation(out=cs[:, f, :nt], in_=a[:, :nt], func=AF.Silu, bias=cb[:, f:f + 1]),
                         reads=[ak, 'cb'], writes=[('cs', f)])
                P.dma('gpsimd', dr['BCT'][0:1024, :].rearrange("(f p) t -> p f t", p=128)[:, :, tok0:tok0 + nt], cs[:, 12:20, :nt],
                      reads=[('cs', f) for f in range(12, 20)], writes=[('BCT', tok0)])
                for ch in range(nt // 128):
                    tk = tokst[ci % 2]
                    tkk = f'tokst{ci % 2}'
                    ci += 1
                    for f in range(16):
                        pt_ = ptA if f < 8 else ptB
                        P.op('tensor', lambda e, f=f, ch=ch, pt_=pt_: e.transpose(out=pt_[:, f % 8, :], in_=cs[:, f, ch * 128:(ch + 1) * 128], identity=self.ident_b[:]),
                             reads=[('cs', f), 'ident_b'], writes=['PS:ptA' if f < 8 else 'PS:ptB'])
                    P.op('vector', lambda e, tk=tk: e.tensor_copy(out=tk[:, 0:1024], in_=ptA[:].rearrange("p a b -> p (a b)")), reads=['PS:ptA'], writes=[tkk])
                    P.op('scalar', lambda e, tk=tk: e.copy(out=tk[:, 1024:2048], in_=ptB[:].rearrange("p a b -> p (a b)")), reads=['PS:ptB'], writes=[tkk])
                    P.dma('gpsimd', dr['TOK'][tok0 + ch * 128:tok0 + (ch + 1) * 128, :], tk[:], reads=[tkk], writes=[('TOK', tok0 + ch * 128)])
            P.barrier()
        if 'dbg_gates' in self.dbg:
            P.dma('sync', dr['dbg_gates'][0], dt_tok[:].rearrange("p a b -> p (a b)"), reads=[], writes=['dbg_gates'])
            P.dma('sync', dr['dbg_gates'][1], la_tok[:].rearrange("p a b -> p (a b)"), reads=[], writes=['dbg_gates'])
            P.barrier()
        if self.sub <= 2:
            return
        with ExitStack() as es:
            S_f = sb(es, "S_f", [128, 24, 64], F32)
            S_b = sb(es, "S_b", [128, 24, 64], BF16)
            tok = [sb(es, f"tok{i}", [128, 2048], BF16) for i in range(2)]
            bct = [sb(es, f"bct{i}", [128, 8, 128], BF16) for i in range(2)]
            Lm = [sb(es, f"Lm{i}", [128, 128], F32) for i in range(6)]
            Dm = [sb(es, f"Dm{i}", [128, 128], F32) for i in range(6)]
            WT = [sb(es, f"WT{i}", [128, 128], BF16) for i in range(6)]
            v1 = [sb(es, f"v1_{i}", [128, 64], BF16) for i in range(6)]
            v2 = [sb(es, f"v2_{i}", [128, 64], BF16) for i in range(6)]
            yA = sb(es, "yA", [128, 384], F32)
            ysb = [sb(es, f"ysb{i}", [128, 1536], F32) for i in range(2)]
            yfl = sb(es, "yfl", [128, 1536], F32)
            ytk = sb(es, "ytk", [128, 1536], BF16)
            ecum = sb(es, "ecum", [128, 24], F32)
            cdec = sb(es, "cdec", [128, 24], F32)
            dB = sb(es, "dB", [128, 24], F32)
            nw = sb(es, "ssd_nw_sb", [128, 12], F32)
            ytb = sb(es, "ytb", [128, 12, 512], BF16)
            zT = sb(es, "zT", [128, 12, 512], F32)
            yg = sb(es, "yg", [128, 12, 512], F32)
            sqg = sb(es, "sqg", [128, 12, 512], BF16)
            rstd = sb(es, "rstd_g", [128, 512], F32)
            ot = sb(es, "ot", [128, 12, 512], BF16)
            psegA = ps(es, "psegA", [128, 4, 128], F32)
            psegB = ps(es, "psegB", [128, 4, 128], F32)
            pyA = ps(es, "pyA", [128, 512], F32)
            pyB = ps(es, "pyB", [128, 512], F32)
            pS = ps(es, "pS", [128, 512], F32)
            ptT = ps(es, "ptT", [128, 8, 128], BF16)
            ptU = ps(es, "ptU", [128, 8, 128], BF16)
            pss = ps(es, "pss_g", [128, 512], F32)
            P.dma('sync', dB[:], dr['ssd_d'].partition_broadcast(128), writes=['dB'])
            P.dma('sync', nw[:], dr['ssd_nw'], writes=['ssd_nw'])
            seg = lambda r: (psegA[:, r, :] if r < 4 else psegB[:, r - 4, :])
            segk = lambda r: ('PS:psegA' if r < 4 else 'PS:psegB')
            li = 0
            for d in range(2):
                MASK = self.MGT if d == 0 else self.MLT
                TRI = self.TLE if d == 0 else self.TGE
                mk, tk_ = ('MGT', 'TLE') if d == 0 else ('MLT', 'TGE')
                ecol = 127 if d == 0 else 0
                P.op('gpsimd', lambda e: e.memset(S_f[:], 0.0), writes=[('S_f', h) for h in range(24)])
                P.op('gpsimd', lambda e: e.memset(S_b[:], 0.0), writes=[('S_b', h) for h in range(24)])
                for c in chunk_order(d):
                    t0 = c * 128
                    tb = tok[li % 2]
                    tbk = f'tok{li % 2}'
                    bc = bct[li % 2]
                    bck = f'bct{li % 2}'
                    ys = ysb[li % 2]
                    ysk = f'ysb{li % 2}'
                    li += 1
                    P.dma('sync', tb[:], dr['TOK'][t0:t0 + 128, :], writes=[tbk])
                    P.dma('sync', bc[:], dr['BCT'][0:1024, :].rearrange("(f p) t -> p f t", p=128)[:, :, t0:t0 + 128], writes=[bck])
                    lac = la_tok[:, c, d * 24:(d + 1) * 24]
                    P.op('tensor', lambda e, lac=lac: e.matmul(pss[:, 128:152], lhsT=TRI[:], rhs=lac, start=True, stop=True),
                         reads=[tk_, 'la_tok'], writes=['PS:pss_g'])
                    P.op('tensor', lambda e, lac=lac: e.matmul(pss[:, 160:184], lhsT=self.ones_f[:], rhs=lac, start=True, stop=True),
                         reads=['ones_f', 'la_tok'], writes=['PS:pss_g'])
                    P.op('scalar', lambda e: e.activation(out=ecum[:], in_=pss[:, 128:152], func=AF.Exp), reads=['PS:pss_g'], writes=['ecum'])
                    P.op('scalar', lambda e: e.activation(out=cdec[:], in_=pss[:, 160:184], func=AF.Exp), reads=['PS:pss_g'], writes=['cdec'])
                    for g in range(4):
                        hs = [g * 6 + r for r in range(6)]
                        P.op('tensor', lambda e, g=g, bc=bc: e.matmul(pss[:, 0:128], lhsT=bc[:, g, :], rhs=bc[:, 4 + g, :], start=True, stop=True),
                             reads=[bck], writes=['PS:pss_g'])
                        for r, h in enumerate(hs):
                            P.op('vector', lambda e, r=r, h=h, c=c: e.tensor_scalar(out=Lm[r][:], in0=MASK[:], scalar1=la_tok[:, c, d * 24 + h:d * 24 + h + 1], scalar2=None, op0=ALU.mult),
                                 reads=[mk, 'la_tok'], writes=[f'Lm{r}'])
                        for r, h in enumerate(hs):
                            P.op('tensor', lambda e, r=r: e.matmul(seg(r), lhsT=Lm[r][:], rhs=TRI[:], start=True, stop=False), reads=[f'Lm{r}', tk_], writes=[segk(r)])
                            P.op('tensor', lambda e, r=r: e.matmul(seg(r), lhsT=self.negI[:], rhs=MASK[:], start=False, stop=True), reads=['negI', mk], writes=[segk(r)])
                        for r, h in enumerate(hs):
                            P.op('scalar', lambda e, r=r: e.activation(out=Dm[r][:], in_=seg(r), func=AF.Exp), reads=[segk(r)], writes=[f'Dm{r}'])
                        for r, h in enumerate(hs):
                            P.op('vector', lambda e, r=r: e.tensor_tensor(out=WT[r][:], in0=Dm[r][:], in1=pss[:, 0:128], op=ALU.mult),
                                 reads=[f'Dm{r}', 'PS:pss_g'], writes=[f'WT{r}'])
                            P.op('gpsimd', lambda e, r=r, h=h, c=c, tb=tb: e.tensor_scalar(out=v1[r][:], in0=tb[:, h * 64:(h + 1) * 64], scalar1=dt_tok[:, c, d * 24 + h:d * 24 + h + 1], scalar2=None, op0=ALU.mult),
                                 reads=[tbk, 'dt_tok'], writes=[f'v1_{r}'])
                            P.op('gpsimd', lambda e, r=r: e.tensor_scalar(out=v2[r][:], in0=v1[r][:], scalar1=Dm[r][:, ecol:ecol + 1], scalar2=None, op0=ALU.mult),
                                 reads=[f'v1_{r}', f'Dm{r}'], writes=[f'v2_{r}'])
                        for r, h in enumerate(hs):
                            P.op('tensor', lambda e, r=r: e.matmul(pyA[:, r * 64:(r + 1) * 64], lhsT=WT[r][:], rhs=v1[r][:], start=True, stop=True),
                                 reads=[f'WT{r}', f'v1_{r}'], writes=['PS:pyA'])
                            P.op('tensor', lambda e, r=r, h=h, g=g, bc=bc: e.matmul(pyB[:, r * 64:(r + 1) * 64], lhsT=bc[:, 4 + g, :], rhs=S_b[:, h, :], start=True, stop=True),
                                 reads=[bck, ('S_b', h)], writes=['PS:pyB'])
                            P.op('tensor', lambda e, r=r, g=g, tb=tb: e.matmul(pS[:, r * 64:(r + 1) * 64], lhsT=tb[:, 1536 + g * 128:1536 + (g + 1) * 128], rhs=v2[r][:], start=True, stop=True),
                                 reads=[tbk, f'v2_{r}'], writes=['PS:pS'])
                        P.op('scalar', lambda e: e.copy(out=yA[:], in_=pyA[:, 0:384]), reads=['PS:pyA'], writes=['yA'])
                        for r, h in enumerate(hs):
                            P.op('vector', lambda e, r=r, h=h, ys=ys: e.scalar_tensor_tensor(out=ys[:, h * 64:(h + 1) * 64], in0=pyB[:, r * 64:(r + 1) * 64], scalar=ecum[:, h:h + 1], in1=yA[:, r * 64:(r + 1) * 64], op0=ALU.mult, op1=ALU.add),
                                 reads=['PS:pyB', 'ecum', 'yA'], writes=[ysk])
                            P.op('vector', lambda e, r=r, h=h: e.scalar_tensor_tensor(out=S_f[:, h, :], in0=S_f[:, h, :], scalar=cdec[:, h:h + 1], in1=pS[:, r * 64:(r + 1) * 64], op0=ALU.mult, op1=ALU.add),
                                 reads=[('S_f', h), 'cdec', 'PS:pS'], writes=[('S_f', h)])
                            P.op('scalar', lambda e, h=h: e.copy(out=S_b[:, h, :], in_=S_f[:, h, :]), reads=[('S_f', h)], writes=[('S_b', h)])
                    if d == 0:
                        P.dma('gpsimd', dr['YF'][t0:t0 + 128, 0:1536], ys[:], reads=[ysk], writes=[('YF', c)])
                    else:
                        P.dma('sync', yfl[:], dr['YF'][t0:t0 + 128, 0:1536], reads=[('YF', c)], writes=['yfl'])
                        P.op('vector', lambda e, ys=ys: e.tensor_tensor(out=yfl[:], in0=yfl[:], in1=ys[:], op=ALU.add), reads=['yfl', ysk], writes=['yfl'])
                        for h in range(24):
                            P.op('vector', lambda e, h=h, tb=tb: e.scalar_tensor_tensor(out=ytk[:, h * 64:(h + 1) * 64], in0=tb[:, h * 64:(h + 1) * 64], scalar=dB[:, h:h + 1], in1=yfl[:, h * 64:(h + 1) * 64], op0=ALU.mult, op1=ALU.add),
                                 reads=[tbk, 'dB', 'yfl'], writes=['ytk'])
                        if c < 2:
                            tok0, nt, ch = 0, TC, c
                        else:
                            tok0, nt, ch = TC + ((c - 2) // 4) * 512, 512, (c - 2) % 4
                        for f in range(12):
                            pt_ = ptT if f < 8 else ptU
                            P.op('tensor', lambda e, f=f, pt_=pt_: e.transpose(out=pt_[:, f % 8, :], in_=ytk[:, f * 128:(f + 1) * 128], identity=self.ident_b[:]),
                                 reads=['ytk', 'ident_b'], writes=['PS:ptT' if f < 8 else 'PS:ptU'])
                        P.op('vector', lambda e, ch=ch: e.tensor_copy(out=ytb[:, 0:8, ch * 128:(ch + 1) * 128], in_=ptT[:]), reads=['PS:ptT'], writes=['ytb'])
                        P.op('scalar', lambda e, ch=ch: e.copy(out=ytb[:, 8:12, ch * 128:(ch + 1) * 128], in_=ptU[:, 0:4, :]), reads=['PS:ptU'], writes=['ytb'])
                        if ch == 0:
                            P.dma('sync', zT[:, :, :nt], PT[512:2048, :].rearrange("(f p) t -> p f t", p=128)[:, :, tok0:tok0 + nt], writes=['zT'])
                            P.op('scalar', lambda e, nt=nt: e.activation(out=zT[:, :, :nt], in_=zT[:, :, :nt], func=AF.Silu), reads=['zT'], writes=['zT'])
                            P.op('vector', lambda e, nt=nt: e.tensor_tensor(out=yg[:, :, :nt], in0=ytb[:, :, :nt], in1=zT[:, :, :nt], op=ALU.mult), reads=['ytb', 'zT'], writes=['yg'])
                            P.op('scalar', lambda e, nt=nt: e.activation(out=sqg[:, :, :nt], in_=yg[:, :, :nt], func=AF.Square), reads=['yg'], writes=['sqg'])
                            for gq in range(4):
                                for i3 in range(3):
                                    P.op('tensor', lambda e, gq=gq, i3=i3, nt=nt: e.matmul(pss[:, :nt], lhsT=self.ones_b[:], rhs=sqg[:, gq * 3 + i3, :nt], start=(i3 == 0), stop=(i3 == 2)),
                                         reads=['sqg', 'ones_b'], writes=['PS:pss_g'])
                                P.op('scalar', lambda e, nt=nt: e.activation(out=rstd[:, :nt], in_=pss[:, :nt], func=AF.Sqrt, scale=1.0 / 384, bias=self.eps_col[:]),
                                     reads=['PS:pss_g', 'eps_col'], writes=['rstd_g'])
                                P.op('vector', lambda e, nt=nt: e.reciprocal(out=rstd[:, :nt], in_=rstd[:, :nt]), reads=['rstd_g'], writes=['rstd_g'])
                                for i3 in range(3):
                                    f = gq * 3 + i3
                                    P.op('vector', lambda e, f=f, nt=nt: e.scalar_tensor_tensor(out=ot[:, f, :nt], in0=yg[:, f, :nt], scalar=nw[:, f:f + 1], in1=rstd[:, :nt], op0=ALU.mult, op1=ALU.mult),
                                         reads=['yg', 'ssd_nw', 'rstd_g'], writes=['ot'])
                            P.dma('gpsimd', dr['YT'][512:2048, :].rearrange("(f p) t -> p f t", p=128)[:, :, tok0:tok0 + nt], ot[:, :, :nt], reads=['ot'], writes=[('YT', 'ssd', tok0)])
            P.barrier()


B.phase_ssd = phase_ssd


def norm_cols(self, xt, sq, hT, rstd, tmp, pss, c0, n, m, scale_t, shift_lo, tag):
    P = self.P
    kx, ksq, kr, kp = tag + 'xt', tag + 'sq', tag + 'rstd', 'PS:' + tag + 'pss'
    P.op('scalar', lambda e: e.activation(out=sq[:, :, c0:c0 + n], in_=xt[:, :, c0:c0 + n], func=AF.Square), reads=[kx], writes=[ksq])
    for kc in range(KC):
        P.op('tensor', lambda e, kc=kc: e.matmul(pss[:, :n], lhsT=self.ones_b[:], rhs=sq[:, kc, c0:c0 + n],
                                                 start=(kc == 0), stop=(kc == KC - 1)), reads=[ksq, 'ones_b'], writes=[kp])
    P.op('scalar', lambda e: e.activation(out=rstd[:, :n], in_=pss[:, :n], func=AF.Sqrt, scale=1.0 / D, bias=self.eps_col[:]),
         reads=[kp, 'eps_col'], writes=[kr])
    P.op('vector', lambda e: e.reciprocal(out=rstd[:, :n], in_=rstd[:, :n]), reads=[kr], writes=[kr])
    for kc in range(KC):
        tk = tag + 'ntmp' + str(kc % 2)
        t = tmp[kc % 2]
        P.op('vector', lambda e, kc=kc, t=t: e.tensor_tensor(out=t[:, :n], in0=xt[:, kc, c0:c0 + n], in1=rstd[:, :n], op=ALU.mult),
             reads=[kx, kr], writes=[tk])
        P.op('scalar', lambda e, kc=kc, t=t: e.activation(out=hT[:, kc, c0:c0 + n], in_=t[:, :n], func=AF.Identity,
                                                          scale=scale_t[:, kc, m:m + 1], bias=self.modT[:, shift_lo + kc, m:m + 1]),
             reads=[tk, 'modT', 's1', 's2'], writes=[tag + 'hT'])


def phase_outproj(self, layer, src, dst):
    P, nc, dr = self.P, self.nc, self.dr
    sb, ps = self.sb, self.ps
    wsrc = dr['ev_out_w'] if layer == 0 else dr['od_out_w']
    WO = dr['WO']
    for r in range(0, D, 512):
        P.dma('gpsimd', WO[r:r + 512, :], wsrc[r:r + 512, :], writes=[('WO', r)])
    blocks = self.blocks() if layer == 0 else self.blocks()[1:]
    with ExitStack() as es:
        W = sb(es, "wo_sb", [128, KC, D], BF16)
        xt = [sb(es, f"ox{i}", [128, KC, 512], F32) for i in range(2)]
        yt = [sb(es, f"oy{i}", [128, KC, 512], BF16) for i in range(2)]
        pacc = [ps(es, f"opacc{i}", [128, 512], F32) for i in range(4)]
        P.dma('sync', W[:], WO.rearrange("(kc p) n -> p kc n", p=128), reads=[('WO', r) for r in range(0, D, 512)], writes=['wo_sb'])
        it = 0
        for bi, (tok0, nt, m) in enumerate(blocks):
            x, xk = xt[bi % 2], f'ox{bi % 2}'
            y, yk = yt[bi % 2], f'oy{bi % 2}'
            P.dma('sync', x[:, :, :nt], src.rearrange("(kc p) t -> p kc t", p=128)[:, :, tok0:tok0 + nt], writes=[xk])
            P.dma('sync', y[:, :, :nt], dr['YT'].rearrange("(kc p) t -> p kc t", p=128)[:, :, tok0:tok0 + nt], writes=[yk])
            for d in range(KC):
                pa, pk = pacc[it % 4], f'PS:opacc{it % 4}'
                it += 1
                for kc in range(KC):
                    P.op('tensor', lambda e, kc=kc, d=d, pa=pa, y=y, nt=nt: e.matmul(pa[:, :nt], lhsT=W[:, kc, d * 128:(d + 1) * 128], rhs=y[:, kc, :nt],
                                                                                   start=(kc == 0), stop=(kc == KC - 1)), reads=['wo_sb', yk], writes=[pk])
                P.op('vector', lambda e, d=d, pa=pa, x=x, nt=nt, m=m: e.scalar_tensor_tensor(out=x[:, d, :nt], in0=pa[:, :nt], scalar=self.modT[:, 32 + d, m:m + 1], in1=x[:, d, :nt], op0=ALU.mult, op1=ALU.add),
                     reads=[pk, 'modT', xk], writes=[xk])
            P.dma('gpsimd', dst.rearrange("(kc p) t -> p kc t", p=128)[:, :, tok0:tok0 + nt], x[:, :, :nt], reads=[xk], writes=[('X1', tok0)])
        P.barrier()


def phase_ffn(self, layer, src, dst):
    P, nc, dr = self.P, self.nc, self.dr
    sb, ps = self.sb, self.ps
    WU, WD = dr['WU'], dr['WD']
    for r in range(0, D, 128):
        P.dma('gpsimd', WU[r:r + 128, :], dr['ffn_up_w'][layer][r:r + 128, :], writes=[('WU', r)])
    for r in range(0, FFN, 512):
        r1 = min(r + 512, FFN)
        P.dma('gpsimd', WD[r:r1, :], dr['ffn_down_w'][layer][r:r1, :], writes=[('WD', r)])
    wu_keys = [('WU', r) for r in range(0, D, 128)]
    wd_keys = [('WD', r) for r in range(0, FFN, 512)]
    NF = 43
    blocks = self.blocks() if layer == 0 else self.blocks()[1:]
    with ExitStack() as es0:
        gT = sb(es0, "gT", [128, NF, 512], BF16)
        xt = sb(es0, "fx", [128, KC, 640], F32)
        cwt = sb(es0, "fcw", [128, NF, 9], F32)
        P.dma('sync', cwt[:], dr['ffn_cw'][layer], writes=['fcw'])
        for bi, (tok0, nt, m) in enumerate(blocks):
            if m == 1:
                lo, hi, off = 0, TC, 0
            else:
                lo = max(tok0 - 64, TC)
                hi = min(tok0 + nt + 64, T)
                off = lo - (tok0 - 64)
            nw = hi - lo
            with ExitStack() as es:
                sq = sb(es, "fsq", [128, KC, 640], BF16)
                hT = sb(es, "fh", [128, KC, 640], BF16)
                rstd = sb(es, "frstd", [128, 512], F32)
                tmp = [sb(es, f"ftmp{i}", [128, 512], F32) for i in range(2)]
                wa = [sb(es, f"fwa{i}", [128, KC, 256], BF16) for i in range(2)]
                wv = [sb(es, f"fwv{i}", [128, KC, 256], BF16) for i in range(2)]
                asb = [sb(es, f"fasb{i}", [128, 10, 66], F32) for i in range(2)]
                acc = [sb(es, f"facc{i}", [128, 8, 64], F32) for i in range(2)]
                sg = [sb(es, f"fsg{i}", [128, 512], F32) for i in range(2)]
                pss = ps(es, "fpss", [128, 512], F32)
                pa0 = [ps(es, f"fpa0_{i}", [128, 512], F32) for i in range(2)]
                pa1 = [ps(es, f"fpa1_{i}", [128, 512], F32) for i in range(2)]
                pv = [ps(es, f"fpv{i}", [128, 512], F32) for i in range(2)]
                P.dma('sync', xt[:, :, off:off + nw], src.rearrange("(kc p) t -> p kc t", p=128)[:, :, lo:hi], writes=['fxt'])
                c = off
                while c < off + nw:
                    n = min(320, off + nw - c)
                    norm_cols(self, xt, sq, hT, rstd, tmp, pss, c, n, m, self.s2, 48, 'f')
                    c += n
                for i in range(2):
                    P.op('gpsimd', lambda e, i=i: e.memset(asb[i][:], 0.0), writes=[f'fasb{i}'])
                wuv = WU.rearrange("(kc p) n -> p kc n", p=128)
                for fp in range(0, NF, 2):
                    nf = min(2, NF - fp)
                    wi = (fp // 2) % 2
                    P.dma('sync', wa[wi][:, :, :nf * 128], wuv[:, :, fp * 128:(fp + nf) * 128], reads=wu_keys, writes=[f'fwa{wi}'])
                    P.dma('sync', wv[wi][:, :, :nf * 128], wuv[:, :, FFN + fp * 128:FFN + (fp + nf) * 128], reads=wu_keys, writes=[f'fwv{wi}'])
                    for ff in range(nf):
                        f = fp + ff
                        b2 = f % 2
                        A, Ak = asb[b2], f'fasb{b2}'
                        if m == 1:
                            P0, P0k = pa0[b2], f'PS:fpa0_{b2}'
                            for kc in range(KC):
                                P.op('tensor', lambda e, kc=kc, ff=ff, wi=wi, P0=P0: e.matmul(P0[:, :256], lhsT=wa[wi][:, kc, ff * 128:(ff + 1) * 128], rhs=hT[:, kc, 0:256], start=(kc == 0), stop=(kc == KC - 1)),
                                     reads=[f'fwa{wi}', 'fhT'], writes=[P0k])
                            Av = A[:].rearrange("p a b -> p (a b)")
                            P.op('scalar', lambda e, Av=Av, P0=P0: e.copy(out=Av[:, 1:257], in_=P0[:, :256]), reads=[P0k], writes=[Ak])
                            ac, ack = acc[b2][:].rearrange("p a b -> p (a b)"), f'facc{b2}'
                            P.op('vector', lambda e, Av=Av, ac=ac, f=f: e.tensor_scalar(out=ac[:, :256], in0=Av[:, 0:256], scalar1=cwt[:, f, 3:4], scalar2=None, op0=ALU.mult), reads=[Ak, 'fcw'], writes=[ack])
                            for dx in (1, 2):
                                P.op('vector', lambda e, Av=Av, ac=ac, f=f, dx=dx: e.scalar_tensor_tensor(out=ac[:, :256], in0=Av[:, dx:dx + 256], scalar=cwt[:, f, 3 + dx:4 + dx], in1=ac[:, :256], op0=ALU.mult, op1=ALU.add),
                                     reads=[Ak, 'fcw', ack], writes=[ack])
                            vlo = 0
                        else:
                            P0, P0k = pa0[b2], f'PS:fpa0_{b2}'
                            P1, P1k = pa1[b2], f'PS:fpa1_{b2}'
                            h0 = min(320, nw)
                            h1 = nw - h0
                            for kc in range(KC):
                                P.op('tensor', lambda e, kc=kc, ff=ff, wi=wi, P0=P0, h0=h0: e.matmul(P0[:, :h0], lhsT=wa[wi][:, kc, ff * 128:(ff + 1) * 128], rhs=hT[:, kc, off:off + h0], start=(kc == 0), stop=(kc == KC - 1)),
                                     reads=[f'fwa{wi}', 'fhT'], writes=[P0k])
                            for kc in range(KC):
                                P.op('tensor', lambda e, kc=kc, ff=ff, wi=wi, P1=P1, h0=h0, h1=h1: e.matmul(P1[:, :h1], lhsT=wa[wi][:, kc, ff * 128:(ff + 1) * 128], rhs=hT[:, kc, off + h0:off + h0 + h1], start=(kc == 0), stop=(kc == KC - 1)),
                                     reads=[f'fwa{wi}', 'fhT'], writes=[P1k])
                            r_off = off // 64
                            P.op('scalar', lambda e, A=A, P0=P0, h0=h0, r_off=r_off: e.copy(out=A[:, r_off:r_off + h0 // 64, 1:65], in_=P0[:, :h0].rearrange("p (a b) -> p a b", b=64)), reads=[P0k], writes=[Ak])
                            P.op('scalar', lambda e, A=A, P1=P1, h0=h0, h1=h1, r_off=r_off: e.copy(out=A[:, r_off + h0 // 64:r_off + (h0 + h1) // 64, 1:65], in_=P1[:, :h1].rearrange("p (a b) -> p a b", b=64)), reads=[P1k], writes=[Ak])
                            ac3, ack = acc[b2], f'facc{b2}'
                            first = True
                            for dy in range(3):
                                for dx in range(3):
                                    tap = dy * 3 + dx
                                    if first:
                                        P.op('vector', lambda e, A=A, ac3=ac3, f=f, dy=dy, dx=dx, tap=tap: e.tensor_scalar(out=ac3[:], in0=A[:, dy:dy + 8, dx:dx + 64], scalar1=cwt[:, f, tap:tap + 1], scalar2=None, op0=ALU.mult),
                                             reads=[Ak, 'fcw'], writes=[ack])
                                        first = False
                                    else:
                                        P.op('vector', lambda e, A=A, ac3=ac3, f=f, dy=dy, dx=dx, tap=tap: e.scalar_tensor_tensor(out=ac3[:], in0=A[:, dy:dy + 8, dx:dx + 64], scalar=cwt[:, f, tap:tap + 1], in1=ac3[:], op0=ALU.mult, op1=ALU.add),
                                             reads=[Ak, 'fcw', ack], writes=[ack])
                            ac = ac3[:].rearrange("p a b -> p (a b)")
                            vlo = 64
                        PV, PVk = pv[b2], f'PS:fpv{b2}'
                        for kc in range(KC):
                            P.op('tensor', lambda e, kc=kc, ff=ff, wi=wi, PV=PV, vlo=vlo, nt=nt: e.matmul(PV[:, :nt], lhsT=wv[wi][:, kc, ff * 128:(ff + 1) * 128], rhs=hT[:, kc, vlo:vlo + nt], start=(kc == 0), stop=(kc == KC - 1)),
                                 reads=[f'fwv{wi}', 'fhT'], writes=[PVk])
                        S, Sk = sg[b2], f'fsg{b2}'
                        P.op('scalar', lambda e, S=S, ac=ac, nt=nt: e.activation(out=S[:, :nt], in_=ac[:, :nt], func=AF.Silu), reads=[ack], writes=[Sk])
                        P.op('vector', lambda e, S=S, PV=PV, f=f, nt=nt: e.tensor_tensor(out=gT[:, f, :nt], in0=S[:, :nt], in1=PV[:, :nt], op=ALU.mult), reads=[Sk, PVk], writes=[('gT', f)])
                P.barrier()
            with ExitStack() as es:
                wd = [sb(es, f"fwd{i}", [128, 1024], BF16) for i in range(3)]
                pacc = [ps(es, f"fdacc{i}", [128, 512], F32) for i in range(8)]
                vlo = 0 if m == 1 else 64
                for half in range(2):
                    for f in range(NF):
                        w, wk = wd[f % 3], f'fwd{f % 3}'
                        P.dma('sync', w[:], WD[f * 128:(f + 1) * 128, half * 1024:(half + 1) * 1024], reads=wd_keys, writes=[wk])
                        for d in range(8):
                            P.op('tensor', lambda e, d=d, f=f, w=w, nt=nt: e.matmul(pacc[d][:, :nt], lhsT=w[:, d * 128:(d + 1) * 128], rhs=gT[:, f, :nt], start=(f == 0), stop=(f == NF - 1)),
                                 reads=[wk, ('gT', f)], writes=[f'PS:fdacc{d}'])
                    for d in range(8):
                        dd = half * 8 + d
                        P.op('vector', lambda e, d=d, dd=dd, nt=nt, m=m, vlo=vlo: e.scalar_tensor_tensor(out=xt[:, dd, vlo:vlo + nt], in0=pacc[d][:, :nt], scalar=self.modT[:, 80 + dd, m:m + 1], in1=xt[:, dd, vlo:vlo + nt], op0=ALU.mult, op1=ALU.add),
                             reads=[f'PS:fdacc{d}', 'modT', 'fxt'], writes=['fxt'])
                P.dma('gpsimd', dst.rearrange("(kc p) t -> p kc t", p=128)[:, :, tok0:tok0 + nt], xt[:, :, vlo:vlo + nt], reads=['fxt'], writes=[('X2', tok0)])
                P.barrier()


B.phase_outproj = phase_outproj
B.phase_ffn = phase_ffn


NCH = T // 8
HALF = NCH // 2
TWO_PI = 6.283185307179586
PI = 3.141592653589793


def reduce_angle(self, t, n, ki, kf, msk, tag):
    P = self.P
    v = 'vector'
    P.op(v, lambda e: e.tensor_scalar(out=ki[:, :n], in0=t, scalar1=1.0 / TWO_PI, scalar2=None, op0=ALU.mult), reads=[tag], writes=[tag + 'ki'])
    P.op(v, lambda e: e.tensor_copy(out=kf[:, :n], in_=ki[:, :n]), reads=[tag + 'ki'], writes=[tag + 'kf'])
    P.op(v, lambda e: e.scalar_tensor_tensor(out=t, in0=kf[:, :n], scalar=-TWO_PI, in1=t, op0=ALU.mult, op1=ALU.add), reads=[tag + 'kf', tag], writes=[tag])
    P.op(v, lambda e: e.tensor_single_scalar(out=msk[:, :n], in_=t, scalar=PI, op=ALU.is_gt), reads=[tag], writes=[tag + 'm'])
    P.op(v, lambda e: e.scalar_tensor_tensor(out=t, in0=msk[:, :n], scalar=-TWO_PI, in1=t, op0=ALU.mult, op1=ALU.add), reads=[tag + 'm', tag], writes=[tag])
    P.op(v, lambda e: e.tensor_single_scalar(out=msk[:, :n], in_=t, scalar=-PI, op=ALU.is_lt), reads=[tag], writes=[tag + 'm'])
    P.op(v, lambda e: e.scalar_tensor_tensor(out=t, in0=msk[:, :n], scalar=TWO_PI, in1=t, op0=ALU.mult, op1=ALU.add), reads=[tag + 'm', tag], writes=[tag])
    P.op(v, lambda e: e.tensor_scalar(out=t, in0=t, scalar1=PI, scalar2=-PI, op0=ALU.min, op1=ALU.max), reads=[tag], writes=[tag])


def cmul_cols(self, o_re, o_im, a_re, a_im, s_re, s_im, tmp, rk, wk):
    P = self.P
    v = 'vector'
    P.op(v, lambda e: e.tensor_scalar(out=tmp, in0=a_im, scalar1=s_im, scalar2=None, op0=ALU.mult), reads=rk, writes=[wk + 't'])
    P.op(v, lambda e: e.scalar_tensor_tensor(out=o_re, in0=a_re, scalar=s_re, in1=tmp, op0=ALU.mult, op1=ALU.subtract), reads=rk + [wk + 't'], writes=[wk + 're'])
    P.op(v, lambda e: e.tensor_scalar(out=tmp, in0=a_re, scalar1=s_im, scalar2=None, op0=ALU.mult), reads=rk + [wk + 're'], writes=[wk + 't'])
    P.op(v, lambda e: e.scalar_tensor_tensor(out=o_im, in0=a_im, scalar=s_re, in1=tmp, op0=ALU.mult, op1=ALU.add), reads=rk + [wk + 't'], writes=[wk + 'im'])


def phase_s5(self):
    P, nc, dr = self.P, self.nc, self.dr
    sb, ps = self.sb, self.ps
    PT = dr['PT']
    v, a, g_ = 'vector', 'scalar', 'gpsimd'
    with ExitStack() as es0:
        Sel = sb(es0, "Sel", [128, 8, 8, 128], BF16)
        bmask = sb(es0, "bmask", [128, 8, 16], F32)
        krow = sb(es0, "krow", [128, 16], F32)
        crow = sb(es0, "crow", [128, NCH + 1], F32)
        lre = sb(es0, "lre", [128, 32], F32)
        lim = sb(es0, "lim", [128, 32], F32)
        dlt = sb(es0, "dlt", [128, 32], F32)
        reD = sb(es0, "reD", [128, 32], F32)
        imD = sb(es0, "imD", [128, 32], F32)
        bre = sb(es0, "bre", [128, 32, 16], F32)
        bim = sb(es0, "bim", [128, 32, 16], F32)
        cre = sb(es0, "cre", [128, 32, 16], F32)
        cim = sb(es0, "cim", [128, 32, 16], F32)
        dsk = sb(es0, "dsk", [128, 4], F32)
        gy = sb(es0, "gy", [128, 4, T], BF16)
        uT = sb(es0, "uT", [128, 2, T], BF16)
        uF = sb(es0, "uF", [128, T], F32)
        ki = sb(es0, "ki", [128, NCH + 1], I32)
        kf = sb(es0, "kf", [128, NCH + 1], F32)
        msk = sb(es0, "msk", [128, NCH + 1], F32)
        P.op(g_, lambda e: e.memset(Sel[:], 0.0), writes=['Sel'])
        for a_ in range(8):
            for b_ in range(8):
                P.op(g_ if (a_ + b_) % 2 else v, lambda e, a_=a_, b_=b_: e.tensor_copy(out=Sel[:, a_, b_, b_ * 16:(b_ + 1) * 16], in_=self.ident_f[:, a_ * 16:(a_ + 1) * 16]),
                     reads=['ident_f', 'Sel'], writes=['Sel'])
        P.op(g_, lambda e: e.memset(bmask[:], 1.0), writes=['bmask'])
        P.op(g_, lambda e: e.affine_select(out=bmask[:], in_=bmask[:], pattern=[[16, 8], [0, 16]], base=15, channel_multiplier=-1, compare_op=ALU.is_ge, fill=0.0),
             reads=['bmask'], writes=['bmask'])
        P.op(g_, lambda e: e.iota(krow[:], pattern=[[1, 16]], base=-7, channel_multiplier=0, allow_small_or_imprecise_dtypes=True), writes=['krow'])
        P.op(g_, lambda e: e.iota(crow[:], pattern=[[1, NCH + 1]], base=0, channel_multiplier=0, allow_small_or_imprecise_dtypes=True), writes=['crow'])
        for nm, t_ in (('s5_lre', lre), ('s5_lim', lim), ('s5_dlt', dlt)):
            P.dma('sync', t_[:], dr[nm], writes=[nm])
        for nm, t_ in (('s5_bre', bre), ('s5_bim', bim), ('s5_cre', cre), ('s5_cim', cim)):
            P.dma('sync', t_[:], dr[nm], writes=[nm])
        P.dma('sync', dsk[:], dr['s5_dsk'], writes=['dsk'])
        P.op(a, lambda e: e.activation(out=dlt[:], in_=dlt[:], func=AF.Exp), reads=['s5_dlt'], writes=['s5_dlt'])
        P.op(v, lambda e: e.tensor_tensor(out=reD[:], in0=lre[:], in1=dlt[:], op=ALU.mult), reads=['s5_lre', 's5_dlt'], writes=['reD'])
        P.op(v, lambda e: e.tensor_tensor(out=imD[:], in0=lim[:], in1=dlt[:], op=ALU.mult), reads=['s5_lim', 's5_dlt'], writes=['imD'])
        self.cfre = sb(es0, "cfre", [128, 32], F32)
        self.cfim = sb(es0, "cfim", [128, 32], F32)
        with ExitStack() as es:
            mg = sb(es, "c_mg", [128, 32], F32)
            an = sb(es, "c_an", [128, 32], F32)
            an2 = sb(es, "c_an2", [128, 32], F32)
            nre = sb(es, "c_nre", [128, 32], F32)
            nim = sb(es, "c_nim", [128, 32], F32)
            den = sb(es, "c_den", [128, 32], F32)
            t1 = sb(es, "c_t1", [128, 32], F32)
            cfre, cfim = self.cfre, self.cfim
            P.op(a, lambda e: e.activation(out=mg[:], in_=reD[:], func=AF.Exp), reads=['reD'], writes=['c_mg'])
            P.op(v, lambda e: e.tensor_copy(out=an[:], in_=imD[:]), reads=['imD'], writes=['c_an'])
            reduce_angle(self, an[:], 32, ki, kf, msk, 'c_an')
            P.op(v, lambda e: e.tensor_scalar(out=an2[:], in0=imD[:], scalar1=PI / 2, scalar2=None, op0=ALU.add), reads=['imD'], writes=['c_an2'])
            reduce_angle(self, an2[:], 32, ki, kf, msk, 'c_an2')
            P.op(a, lambda e: e.activation(out=an[:], in_=an[:], func=AF.Sin), reads=['c_an'], writes=['c_an'])
            P.op(a, lambda e: e.activation(out=an2[:], in_=an2[:], func=AF.Sin), reads=['c_an2'], writes=['c_an2'])
            P.op(v, lambda e: e.tensor_tensor(out=nre[:], in0=mg[:], in1=an2[:], op=ALU.mult), reads=['c_mg', 'c_an2'], writes=['c_nre'])
            P.op(v, lambda e: e.tensor_scalar(out=nre[:], in0=nre[:], scalar1=-1.0, scalar2=None, op0=ALU.add), reads=['c_nre'], writes=['c_nre'])
            P.op(v, lambda e: e.tensor_tensor(out=nim[:], in0=mg[:], in1=an[:], op=ALU.mult), reads=['c_mg', 'c_an'], writes=['c_nim'])
            P.op(v, lambda e: e.tensor_tensor(out=den[:], in0=lre[:], in1=lre[:], op=ALU.mult), reads=['s5_lre'], writes=['c_den'])
            P.op(v, lambda e: e.tensor_tensor(out=t1[:], in0=lim[:], in1=lim[:], op=ALU.mult), reads=['s5_lim'], writes=['c_t1'])
            P.op(v, lambda e: e.tensor_tensor(out=den[:], in0=den[:], in1=t1[:], op=ALU.add), reads=['c_den', 'c_t1'], writes=['c_den'])
            P.op(v, lambda e: e.reciprocal(out=den[:], in_=den[:]), reads=['c_den'], writes=['c_den'])
            P.op(v, lambda e: e.tensor_tensor(out=cfre[:], in0=nre[:], in1=lre[:], op=ALU.mult), reads=['c_nre', 's5_lre'], writes=['cfre'])
            P.op(v, lambda e: e.tensor_tensor(out=t1[:], in0=nim[:], in1=lim[:], op=ALU.mult), reads=['c_nim', 's5_lim', 'c_den'], writes=['c_t1'])
            P.op(v, lambda e: e.tensor_tensor(out=cfre[:], in0=cfre[:], in1=t1[:], op=ALU.add), reads=['cfre', 'c_t1'], writes=['cfre'])
            P.op(v, lambda e: e.tensor_tensor(out=cfre[:], in0=cfre[:], in1=den[:], op=ALU.mult), reads=['cfre', 'c_den'], writes=['cfre'])
            P.op(v, lambda e: e.tensor_tensor(out=cfim[:], in0=nim[:], in1=lre[:], op=ALU.mult), reads=['c_nim', 's5_lre'], writes=['cfim'])
            P.op(v, lambda e: e.tensor_tensor(out=t1[:], in0=nre[:], in1=lim[:], op=ALU.mult), reads=['c_nre', 's5_lim', 'cfre'], writes=['c_t1'])
            P.op(v, lambda e: e.tensor_tensor(out=cfim[:], in0=cfim[:], in1=t1[:], op=ALU.subtract), reads=['cfim', 'c_t1'], writes=['cfim'])
            P.op(v, lambda e: e.tensor_tensor(out=cfim[:], in0=cfim[:], in1=den[:], op=ALU.mult), reads=['cfim', 'c_den'], writes=['cfim'])
            P.barrier()
        cfre, cfim = self.cfre, self.cfim
        with ExitStack() as es:
            mgk = sb(es, "mgk", [128, 16], F32)
            ank = sb(es, "ank", [128, 16], F32)
            ank2 = sb(es, "ank2", [128, 16], F32)
            LPre = sb(es, "LPre", [128, 16], F32)
            LPim = sb(es, "LPim", [128, 16], F32)
            bbre = sb(es, "bbre", [128, 16], F32)
            bbim = sb(es, "bbim", [128, 16], F32)
            ctmp = sb(es, "ctmp", [128, 128], F32)
            Bmre = sb(es, "Bmre", [128, 8, 16], F32)
            Bmim = sb(es, "Bmim", [128, 8, 16], F32)
            Cmre = sb(es, "Cmre", [128, 8, 16], F32)
            Cmim = sb(es, "Cmim", [128, 8, 16], F32)
            WiTre = sb(es, "WiTre", [128, 128], F32)
            WiTim = sb(es, "WiTim", [128, 128], F32)
            Wore = sb(es, "Wore", [128, 128], BF16)
            Woim = sb(es, "Woim", [128, 128], BF16)
            WoFre = sb(es, "WoFre", [128, 128], F32)
            WoFim = sb(es, "WoFim", [128, 128], F32)
            Tin = sb(es, "Tin", [128, 2, 128], BF16)
            Win = sb(es, "Win", [128, 2, 2, 64], BF16)
            th = sb(es, "th", [128, 1], F32)
            rr = sb(es, "rr", [128, 1], F32)
            rtab = sb(es, "rtab", [128, NCH], F32)
            ctab = sb(es, "ctab", [128, NCH + 1], F32)
            stab = sb(es, "stab", [128, NCH + 1], F32)
            Usb = sb(es, "Usb", [128, 2, 2, NCH], BF16)
            Sre = sb(es, "Sre", [128, NCH], F32)
            Sim = sb(es, "Sim", [128, NCH], F32)
            s2re = sb(es, "s2re", [128, NCH], F32)
            s2im = sb(es, "s2im", [128, NCH], F32)
            Zre = sb(es, "Zre", [128, NCH + 1], F32)
            Zim = sb(es, "Zim", [128, NCH + 1], F32)
            Xre = sb(es, "Xre", [128, NCH], BF16)
            Xim = sb(es, "Xim", [128, NCH], BF16)
            xt1 = sb(es, "xt1", [128, NCH], F32)
            Ysb = sb(es, "Ysb", [128, 2, 8, NCH], BF16)
            yT = sb(es, "yT5", [128, 2, T], F32)
            gl_w = sb(es, "gluw", [128, 4, 512], BF16)
            sgm = sb(es, "sgm", [128, 512], F32)
            og = sb(es, "og5", [128, 512], BF16)
            pU = [ps(es, f"pU{i}", [128, 512], F32) for i in range(2)]
            pSr = ps(es, "pSr", [128, 512], F32)
            pSi = ps(es, "pSi", [128, 512], F32)
            pY = [ps(es, f"pY{i}", [128, 512], F32) for i in range(2)]
            pB = ps(es, "pB5", [128, 512], F32)
            P.dma('gpsimd', gl_w[:], dr['s5_glu_w'].rearrange("(kt p) n -> p kt n", p=128), writes=['gluw'])
            P.op(g_, lambda e: e.memset(Zre[:, 0:1], 0.0), writes=['Zre'])
            P.op(g_, lambda e: e.memset(Zim[:, 0:1], 0.0), writes=['Zim'])
            hs = [(0, HALF), (HALF, NCH)]
            for ft in range(4):
                P.dma('sync', uF[:], PT[ft * 128:(ft + 1) * 128, :], writes=['uF'])
                P.op(a, lambda e: e.copy(out=uT[:, 0, :], in_=uF[:]), reads=['uF'], writes=['uT'])
                P.op(v, lambda e: e.tensor_copy(out=uT[:, 1, 0:TC], in_=uF[:, TC - 1::-1]), reads=['uF'], writes=['uT'])
                P.op(v, lambda e: e.tensor_copy(out=uT[:, 1, TC:T], in_=uF[:, T - 1:TC - 1:-1]), reads=['uF'], writes=['uT'])
                for gp4 in range(4):
                    gp = ft * 4 + gp4
                    for d in range(2):
                        for g2 in range(2):
                            gl = gp4 * 2 + g2
                            for hi_, (c0, c1) in enumerate(hs):
                                for j in range(8):
                                    P.op('tensor', lambda e, d=d, gl=gl, j=j, c0=c0, c1=c1, hi_=hi_: e.matmul(
                                        pU[hi_][:, :c1 - c0], lhsT=Sel[:, gl, j, :], rhs=uT[:, d, c0 * 8 + j:c1 * 8:8], start=(j == 0), stop=(j == 7)),
                                        reads=['Sel', 'uT'], writes=[f'PS:pU{hi_}'])
                                eng = a if hi_ == 0 else v
                                if eng == a:
                                    P.op(a, lambda e, d=d, g2=g2, c0=c0, c1=c1, hi_=hi_: e.copy(out=Usb[:, d, g2, c0:c1], in_=pU[hi_][:, :c1 - c0]), reads=[f'PS:pU{hi_}'], writes=[('Usb', d, g2)])
                                else:
                                    P.op(v, lambda e, d=d, g2=g2, c0=c0, c1=c1, hi_=hi_: e.tensor_copy(out=Usb[:, d, g2, c0:c1], in_=pU[hi_][:, :c1 - c0]), reads=[f'PS:pU{hi_}'], writes=[('Usb', d, g2)])
                    for d in range(2):
                        col = d * 16 + gp
                        P.op(v, lambda e, col=col: e.tensor_scalar(out=mgk[:], in0=krow[:], scalar1=reD[:, col:col + 1], scalar2=None, op0=ALU.mult), reads=['krow', 'reD'], writes=['mgk'])
                        P.op(a, lambda e: e.activation(out=mgk[:], in_=mgk[:], func=AF.Exp), reads=['mgk'], writes=['mgk'])
                        P.op(v, lambda e, col=col: e.tensor_scalar(out=ank[:], in0=krow[:], scalar1=imD[:, col:col + 1], scalar2=None, op0=ALU.mult), reads=['krow', 'imD'], writes=['ank'])
                        P.op(v, lambda e: e.tensor_scalar(out=ank2[:], in0=ank[:], scalar1=PI / 2, scalar2=None, op0=ALU.add), reads=['ank'], writes=['ank2'])
                        reduce_angle(self, ank[:], 16, ki, kf, msk, 'ank')
                        reduce_angle(self, ank2[:], 16, ki, kf, msk, 'ank2')
                        P.op(a, lambda e: e.activation(out=ank[:], in_=ank[:], func=AF.Sin), reads=['ank'], writes=['ank'])
                        P.op(a, lambda e: e.activation(out=ank2[:], in_=ank2[:], func=AF.Sin), reads=['ank2'], writes=['ank2'])
                        P.op(v, lambda e: e.tensor_tensor(out=LPre[:], in0=mgk[:], in1=ank2[:], op=ALU.mult), reads=['mgk', 'ank2'], writes=['LPre'])
                        P.op(v, lambda e: e.tensor_tensor(out=LPim[:], in0=mgk[:], in1=ank[:], op=ALU.mult), reads=['mgk', 'ank'], writes=['LPim'])
                        LP = ['LPre', 'LPim']
                        cmul_cols(self, bbre[:], bbim[:], bre[:, col, :], bim[:, col, :], cfre[:, col:col + 1], cfim[:, col:col + 1], ctmp[:, 0:16], ['s5_bre', 's5_bim', 'cfre', 'cfim'], 'bb')
                        for j in range(8):
                            kk = 7 - j
                            cmul_cols(self, Bmre[:, j, :], Bmim[:, j, :], bbre[:], bbim[:], LPre[:, kk:kk + 1], LPim[:, kk:kk + 1], ctmp[:, 0:16], ['bbre', 'bbim'] + LP, 'Bm')
                            kk = 7 + j
                            cmul_cols(self, Cmre[:, j, :], Cmim[:, j, :], cre[:, col, :], cim[:, col, :], LPre[:, kk:kk + 1], LPim[:, kk:kk + 1], ctmp[:, 0:16], ['s5_cre', 's5_cim'] + LP, 'Cm')
                        P.op(v, lambda e: e.tensor_scalar(out=Cmim[:], in0=Cmim[:], scalar1=-1.0, scalar2=None, op0=ALU.mult), reads=['Cmim'], writes=['Cmim'])
                        B2re, B2im = Bmre[:].rearrange("p a b -> p (a b)"), Bmim[:].rearrange("p a b -> p (a b)")
                        C2re, C2im = Cmre[:].rearrange("p a b -> p (a b)"), Cmim[:].rearrange("p a b -> p (a b)")
                        cmul_cols(self, WiTre[:], WiTim[:], B2re, B2im, LPre[:, 14:15], LPim[:, 14:15], ctmp[:], ['Bmre', 'Bmim'] + LP, 'WiT')
                        P.op(v, lambda e: e.tensor_scalar(out=ctmp[:], in0=C2im, scalar1=LPim[:, 8:9], scalar2=None, op0=ALU.mult), reads=['Cmim', 'LPim'], writes=['Wot'])
                        P.op(v, lambda e: e.scalar_tensor_tensor(out=WoFre[:], in0=C2re, scalar=LPre[:, 8:9], in1=ctmp[:], op0=ALU.mult, op1=ALU.add), reads=['Cmre', 'LPre', 'Wot'], writes=['WoFre'])
                        P.op(v, lambda e: e.tensor_scalar(out=ctmp[:], in0=C2re, scalar1=LPim[:, 8:9], scalar2=None, op0=ALU.mult), reads=['Cmre', 'LPim', 'WoFre'], writes=['Wot'])
                        P.op(v, lambda e: e.scalar_tensor_tensor(out=WoFim[:], in0=C2im, scalar=LPre[:, 8:9], in1=ctmp[:], op0=ALU.mult, op1=ALU.subtract), reads=['Cmim', 'LPre', 'Wot'], writes=['WoFim'])
                        P.op(a, lambda e: e.copy(out=Wore[:], in_=WoFre[:]), reads=['WoFre'], writes=['Wore'])
                        P.op(a, lambda e: e.copy(out=Woim[:], in_=WoFim[:]), reads=['WoFim'], writes=['Woim'])
                        for g2 in range(2):
                            r0, r1 = g2 * 64, (g2 + 1) * 64
                            P.op('tensor', lambda e, r0=r0, r1=r1: e.matmul(pB[:, 0:128], lhsT=B2re[r0:r1, :], rhs=C2re[r0:r1, :], start=True, stop=False), reads=['Bmre', 'Cmre'], writes=['PS:pB5'])
                            P.op('tensor', lambda e, r0=r0, r1=r1: e.matmul(pB[:, 0:128], lhsT=B2im[r0:r1, :], rhs=C2im[r0:r1, :], start=False, stop=True), reads=['Bmim', 'Cmim'], writes=['PS:pB5'])
                            P.op(v, lambda e, g2=g2: e.tensor_tensor(out=Tin[:, g2, :], in0=pB[:, 0:128], in1=bmask[:].rearrange("p a b -> p (a b)"), op=ALU.mult), reads=['PS:pB5', 'bmask'], writes=[('Tin', g2)])
                            P.op('tensor', lambda e, r0=r0, r1=r1: e.matmul(pB[:, 128:192], lhsT=WiTre[r0:r1, :], rhs=self.ident_f[r0:r1, r0:r1], start=True, stop=True), reads=['WiTre', 'ident_f'], writes=['PS:pB5'])
                            P.op('tensor', lambda e, r0=r0, r1=r1: e.matmul(pB[:, 192:256], lhsT=WiTim[r0:r1, :], rhs=self.ident_f[r0:r1, r0:r1], start=True, stop=True), reads=['WiTim', 'ident_f'], writes=['PS:pB5'])
                            P.op(a, lambda e, g2=g2: e.copy(out=Win[:, g2, :, :].rearrange("p a b -> p (a b)"), in_=pB[:, 128:256]), reads=['PS:pB5'], writes=[('Win', g2)])
                        P.op(v, lambda e, col=col: e.tensor_scalar(out=th[:], in0=imD[:, col:col + 1], scalar1=8.0, scalar2=None, op0=ALU.mult), reads=['imD'], writes=['th'])
                        reduce_angle(self, th[:], 1, ki, kf, msk, 'th')
                        P.op(a, lambda e, col=col: e.activation(out=rr[:], in_=reD[:, col:col + 1], func=AF.Exp, scale=8.0), reads=['reD'], writes=['rr'])
                        P.op(v, lambda e: e.tensor_scalar(out=rtab[:], in0=crow[:, 0:NCH], scalar1=0.0, scalar2=rr[:], op0=ALU.mult, op1=ALU.add), reads=['crow', 'rr'], writes=['rtab'])
                        P.op(v, lambda e: e.tensor_scalar(out=stab[:], in0=crow[:], scalar1=th[:], scalar2=None, op0=ALU.mult), reads=['crow', 'th'], writes=['stab'])
                        P.op(v, lambda e: e.tensor_scalar(out=ctab[:], in0=stab[:], scalar1=PI / 2, scalar2=None, op0=ALU.add), reads=['stab'], writes=['ctab'])
                        reduce_angle(self, stab[:], NCH + 1, ki, kf, msk, 'stab')
                        reduce_angle(self, ctab[:], NCH + 1, ki, kf, msk, 'ctab')
                        P.op(a, lambda e: e.activation(out=stab[:], in_=stab[:], func=AF.Sin), reads=['stab'], writes=['stab'])
                        P.op(a, lambda e: e.activation(out=ctab[:], in_=ctab[:], func=AF.Sin), reads=['ctab'], writes=['ctab'])
                        for g2 in range(2):
                            r0, r1 = g2 * 64, (g2 + 1) * 64
                            for hi_, (c0, c1) in enumerate(hs):
                                w_ = 272 * hi_
                                P.op('tensor', lambda e, d=d, g2=g2, r0=r0, r1=r1, c0=c0, c1=c1: e.matmul(pSr[r0:r1, 0:c1 - c0], lhsT=Win[:, g2, 0, :], rhs=Usb[:, d, g2, c0:c1], start=True, stop=True),
                                     reads=[('Win', g2), ('Usb', d, g2)], writes=['PS:pSr'])
                                P.op('tensor', lambda e, d=d, g2=g2, r0=r0, r1=r1, c0=c0, c1=c1: e.matmul(pSi[r0:r1, 0:c1 - c0], lhsT=Win[:, g2, 1, :], rhs=Usb[:, d, g2, c0:c1], start=True, stop=True),
                                     reads=[('Win', g2), ('Usb', d, g2)], writes=['PS:pSi'])
                                P.op(a, lambda e, r0=r0, r1=r1, c0=c0, c1=c1: e.copy(out=Sre[r0:r1, c0:c1], in_=pSr[r0:r1, 0:c1 - c0]), reads=['PS:pSr'], writes=['Sre'])
                                P.op(v, lambda e, r0=r0, r1=r1, c0=c0, c1=c1: e.tensor_copy(out=Sim[r0:r1, c0:c1], in_=pSi[r0:r1, 0:c1 - c0]), reads=['PS:pSi'], writes=['Sim'])
                        P.op(v, lambda e: e.tensor_tensor(out=xt1[:], in0=Sim[:], in1=stab[:, 1:NCH + 1], op=ALU.mult), reads=['Sim', 'stab'], writes=['xt1'])
                        P.op(v, lambda e: e.tensor_tensor(out=s2re[:], in0=Sre[:], in1=ctab[:, 1:NCH + 1], op=ALU.mult), reads=['Sre', 'ctab'], writes=['s2re'])
                        P.op(v, lambda e: e.tensor_tensor(out=s2re[:], in0=s2re[:], in1=xt1[:], op=ALU.add), reads=['s2re', 'xt1'], writes=['s2re'])
                        P.op(v, lambda e: e.tensor_tensor(out=xt1[:], in0=Sre[:], in1=stab[:, 1:NCH + 1], op=ALU.mult), reads=['Sre', 'stab', 's2re'], writes=['xt1'])
                        P.op(v, lambda e: e.tensor_tensor(out=s2im[:], in0=Sim[:], in1=ctab[:, 1:NCH + 1], op=ALU.mult), reads=['Sim', 'ctab'], writes=['s2im'])
                        P.op(v, lambda e: e.tensor_tensor(out=s2im[:], in0=s2im[:], in1=xt1[:], op=ALU.subtract), reads=['s2im', 'xt1'], writes=['s2im'])
                        P.op(v, lambda e: e.tensor_tensor_scan(out=Zre[:, 1:NCH + 1], data0=rtab[:], data1=s2re[:], initial=0.0, op0=ALU.mult, op1=ALU.add), reads=['rtab', 's2re'], writes=['Zre'])
                        P.op(v, lambda e: e.tensor_tensor_scan(out=Zim[:, 1:NCH + 1], data0=rtab[:], data1=s2im[:], initial=0.0, op0=ALU.mult, op1=ALU.add), reads=['rtab', 's2im'], writes=['Zim'])
                        P.op(v, lambda e: e.tensor_tensor(out=xt1[:], in0=Zim[:, 0:NCH], in1=stab[:, 0:NCH], op=ALU.mult), reads=['Zim', 'stab', 's2im'], writes=['xt1'])
                        P.op(v, lambda e: e.tensor_tensor(out=s2re[:], in0=Zre[:, 0:NCH], in1=ctab[:, 0:NCH], op=ALU.mult), reads=['Zre', 'ctab'], writes=['s2re'])
                        P.op(v, lambda e: e.tensor_tensor(out=Xre[:], in0=s2re[:], in1=xt1[:], op=ALU.subtract), reads=['s2re', 'xt1'], writes=['Xre'])
                        P.op(v, lambda e: e.tensor_tensor(out=xt1[:], in0=Zre[:, 0:NCH], in1=stab[:, 0:NCH], op=ALU.mult), reads=['Zre', 'stab', 'Xre'], writes=['xt1'])
                        P.op(v, lambda e: e.tensor_tensor(out=s2im[:], in0=Zim[:, 0:NCH], in1=ctab[:, 0:NCH], op=ALU.mult), reads=['Zim', 'ctab'], writes=['s2im'])
                        P.op(v, lambda e: e.tensor_tensor(out=Xim[:], in0=s2im[:], in1=xt1[:], op=ALU.add), reads=['s2im', 'xt1'], writes=['Xim'])
                        for g2 in range(2):
                            r0, r1 = g2 * 64, (g2 + 1) * 64
                            gl = gp4 * 2 + g2
                            for hi_, (c0, c1) in enumerate(hs):
                                py = pY[hi_]
                                pk = f'PS:pY{hi_}'
                                P.op('tensor', lambda e, d=d, g2=g2, c0=c0, c1=c1, py=py: e.matmul(py[:, 0:c1 - c0], lhsT=Tin[:, g2, :], rhs=Usb[:, d, g2, c0:c1], start=True, stop=False),
                                     reads=[('Tin', g2), ('Usb', d, g2)], writes=[pk])
                                P.op('tensor', lambda e, r0=r0, r1=r1, c0=c0, c1=c1, py=py: e.matmul(py[:, 0:c1 - c0], lhsT=Wore[r0:r1, :], rhs=Xre[r0:r1, c0:c1], start=False, stop=False),
                                     reads=['Wore', 'Xre'], writes=[pk])
                                P.op('tensor', lambda e, r0=r0, r1=r1, c0=c0, c1=c1, py=py: e.matmul(py[:, 0:c1 - c0], lhsT=Woim[r0:r1, :], rhs=Xim[r0:r1, c0:c1], start=False, stop=True),
                                     reads=['Woim', 'Xim'], writes=[pk])
                                if hi_ == 0:
                                    P.op(a, lambda e, d=d, gl=gl, c0=c0, c1=c1, py=py: e.copy(out=Ysb[:, d, gl, c0:c1], in_=py[:, 0:c1 - c0]), reads=[pk], writes=[('Ysb', d, gl)])
                                else:
                                    P.op(v, lambda e, d=d, gl=gl, c0=c0, c1=c1, py=py: e.tensor_copy(out=Ysb[:, d, gl, c0:c1], in_=py[:, 0:c1 - c0]), reads=[pk], writes=[('Ysb', d, gl)])
                for d in range(2):
                    for i in range(8):
                        for hi_, (c0, c1) in enumerate(hs):
                            py = pY[hi_]
                            pk = f'PS:pY{hi_}'
                            for gl in range(8):
                                P.op('tensor', lambda e, d=d, gl=gl, i=i, c0=c0, c1=c1, py=py: e.matmul(py[:, 0:c1 - c0], lhsT=Sel[:, i, gl, :], rhs=Ysb[:, d, gl, c0:c1], start=(gl == 0), stop=(gl == 7)),
                                     reads=['Sel', ('Ysb', d, gl)], writes=[pk])
                            if hi_ == 0:
                                P.op(a, lambda e, d=d, i=i, c0=c0, c1=c1, py=py: e.copy(out=yT[:, d, c0 * 8 + i:c1 * 8:8], in_=py[:, 0:c1 - c0]), reads=[pk], writes=['yT5'])
                            else:
                                P.op(v, lambda e, d=d, i=i, c0=c0, c1=c1, py=py: e.tensor_copy(out=yT[:, d, c0 * 8 + i:c1 * 8:8], in_=py[:, 0:c1 - c0]), reads=[pk], writes=['yT5'])
                P.op(v, lambda e: e.tensor_tensor(out=yT[:, 0, 0:TC], in0=yT[:, 0, 0:TC], in1=yT[:, 1, TC - 1::-1], op=ALU.add), reads=['yT5'], writes=['yT5'])
                P.op(v, lambda e: e.tensor_tensor(out=yT[:, 0, TC:T], in0=yT[:, 0, TC:T], in1=yT[:, 1, T - 1:TC - 1:-1], op=ALU.add), reads=['yT5'], writes=['yT5'])
                P.op(v, lambda e, ft=ft: e.scalar_tensor_tensor(out=yT[:, 0, :], in0=uF[:], scalar=dsk[:, ft:ft + 1], in1=yT[:, 0, :], op0=ALU.mult, op1=ALU.add), reads=['uF', 'dsk', 'yT5'], writes=['yT5'])
                P.op(a, lambda e: e.activation(out=yT[:, 1, :], in_=yT[:, 0, :], func=AF.Square), reads=['yT5'], writes=['yT5b'])
                P.op(v, lambda e: e.tensor_scalar(out=yT[:, 1, :], in0=yT[:, 1, :], scalar1=0.044715, scalar2=1.0, op0=ALU.mult, op1=ALU.add), reads=['yT5b'], writes=['yT5b'])
                P.op(v, lambda e: e.tensor_tensor(out=yT[:, 1, :], in0=yT[:, 1, :], in1=yT[:, 0, :], op=ALU.mult), reads=['yT5b', 'yT5'], writes=['yT5b'])
                P.op(a, lambda e: e.activation(out=yT[:, 1, :], in_=yT[:, 1, :], func=AF.Sigmoid, scale=1.5957691216057308), reads=['yT5b'], writes=['yT5b'])
                P.op(v, lambda e, ft=ft: e.tensor_tensor(out=gy[:, ft, :], in0=yT[:, 1, :], in1=yT[:, 0, :], op=ALU.mult), reads=['yT5b', 'yT5'], writes=[('gy', ft)])
            it = 0
            for fo in range(4):
                for (tok0, nt, m) in self.blocks():
                    py = pY[it % 2]
                    pk = f'PS:pY{it % 2}'
                    it += 1
                    for fi in range(4):
                        P.op('tensor', lambda e, fo=fo, fi=fi, tok0=tok0, nt=nt, py=py: e.matmul(py[:, :nt], lhsT=gl_w[:, fi, fo * 128:(fo + 1) * 128], rhs=gy[:, fi, tok0:tok0 + nt], start=(fi == 0), stop=(fi == 3)),
                             reads=['gluw'] + [('gy', f) for f in range(4)], writes=[pk])
                    P.op(a, lambda e, nt=nt, py=py: e.activation(out=sgm[:, :nt], in_=py[:, :nt], func=AF.Sigmoid), reads=[pk], writes=['sgm'])
                    P.op(v, lambda e, fo=fo, tok0=tok0, nt=nt: e.tensor_tensor(out=og[:, :nt], in0=sgm[:, :nt], in1=gy[:, fo, tok0:tok0 + nt], op=ALU.mult), reads=['sgm', ('gy', fo)], writes=['og5'])
                    P.dma('gpsimd', dr['YT'][fo * 128:(fo + 1) * 128, tok0:tok0 + nt], og[:, :nt], reads=['og5'], writes=[('YT5', fo, tok0)])
            P.barrier()


B.phase_s5 = phase_s5


R_QM, R_KM, R_VM, R_OM, R_IG, R_FG = 4128, 4640, 5152, 6176, 7200, 7208


def phase_mlstm(self):
    P, nc, dr = self.P, self.nc, self.dr
    sb, ps = self.sb, self.ps
    PT = dr['PT']
    v, a, g_ = 'vector', 'scalar', 'gpsimd'
    with ExitStack() as es0:
        gt_tok = sb(es0, "m_gt", [128, 34, 40], F32)
        with ExitStack() as es:
            G = sb(es, "m_G", [128, T], F32)
            bcol = sb(es, "m_bcol", [128, 1], F32)
            ptr = ps(es, "m_ptr", [128, 512], F32)
            P.op(g_, lambda e: e.memset(G[:], 0.0), writes=['m_G'])
            P.op(g_, lambda e: e.memset(bcol[:], 0.0), writes=['m_bcol'])
            P.dma('sync', G[0:8, :], PT[R_IG:R_IG + 8, :], writes=['m_G'])
            P.dma('sync', G[32:40, :], PT[R_FG:R_FG + 8, :], writes=['m_G'])
            P.dma('sync', bcol[0:8, :], dr['ml_ib'], writes=['m_bcol'])
            P.dma('sync', bcol[32:40, :], dr['ml_fb'], writes=['m_bcol'])
            P.op(a, lambda e: e.activation(out=G[0:32, :], in_=G[0:32, :], func=AF.Exp, bias=bcol[0:32, :]), reads=['m_G', 'm_bcol'], writes=['m_G'])
            P.op(v, lambda e: e.tensor_scalar(out=bcol[32:64, :], in0=bcol[32:64, :], scalar1=-1.0, scalar2=None, op0=ALU.mult), reads=['m_bcol', 'm_G'], writes=['m_bcol'])
            P.op(a, lambda e: e.activation(out=G[32:64, :], in_=G[32:64, :], func=AF.Exp, scale=-1.0, bias=bcol[32:64, :]), reads=['m_G', 'm_bcol'], writes=['m_G'])
            P.op(a, lambda e: e.activation(out=G[32:64, :], in_=G[32:64, :], func=AF.Ln, bias=self.one_col[32:64, :]), reads=['m_G', 'one_col'], writes=['m_G'])
            P.op(v, lambda e: e.tensor_scalar(out=G[32:64, :], in0=G[32:64, :], scalar1=-1.0, scalar2=None, op0=ALU.mult), reads=['m_G'], writes=['m_G'])
            for c in range(34):
                P.op('tensor', lambda e, c=c: e.matmul(ptr[:, 0:40], lhsT=G[:, c * 128:(c + 1) * 128], rhs=self.ident_f[:, 0:40], start=True, stop=True), reads=['m_G', 'ident_f'], writes=['PS:m_ptr'])
                P.op(v, lambda e, c=c: e.tensor_copy(out=gt_tok[:, c, :], in_=ptr[:, 0:40]), reads=['PS:m_ptr'], writes=['m_gt'])
            P.barrier()
        with ExitStack() as es:
            xin = [sb(es, f"m_xin{i}", [128, 16, 512], F32) for i in range(2)]
            cs = sb(es, "m_cs", [128, 16, 512], BF16)
            tokst = [sb(es, f"m_tokst{i}", [128, 1536], BF16) for i in range(2)]
            ptA = ps(es, "m_ptA", [128, 8, 128], BF16)
            ptB = ps(es, "m_ptB", [128, 8, 128], BF16)
            ci = 0
            for bi, (tok0, nt, m) in enumerate(self.blocks()):
                x = xin[bi % 2]
                xk = f'm_xin{bi % 2}'
                P.dma('sync', x[:, 0:8, :nt], PT[R_QM:R_QM + 1024, :].rearrange("(f p) t -> p f t", p=128)[:, :, tok0:tok0 + nt], writes=[xk])
                P.dma('sync', x[:, 8:16, :nt], PT[R_VM:R_VM + 1024, :].rearrange("(f p) t -> p f t", p=128)[:, :, tok0:tok0 + nt], writes=[xk])
                P.op(a, lambda e, x=x, nt=nt: e.copy(out=cs[:, 0:4, :nt], in_=x[:, 0:4, :nt]), reads=[xk], writes=['m_cs'])
                P.op(a, lambda e, x=x, nt=nt: e.activation(out=cs[:, 4:8, :nt], in_=x[:, 4:8, :nt], func=AF.Copy, scale=128 ** -0.5), reads=[xk], writes=['m_cs'])
                P.op(v, lambda e, x=x, nt=nt: e.tensor_copy(out=cs[:, 8:16, :nt], in_=x[:, 8:16, :nt]), reads=[xk], writes=['m_cs'])
                P.dma('gpsimd', dr['BCT'][0:1024, :].rearrange("(f p) t -> p f t", p=128)[:, :, tok0:tok0 + nt], cs[:, 0:8, :nt], reads=['m_cs'], writes=[('BCT', tok0)])
                for ch in range(nt // 128):
                    tk = tokst[ci % 2]
                    tkk = f'm_tokst{ci % 2}'
                    ci += 1
                    for f in range(8):
                        P.op('tensor', lambda e, f=f, ch=ch: e.transpose(out=ptA[:, f, :], in_=cs[:, 8 + f, ch * 128:(ch + 1) * 128], identity=self.ident_b[:]), reads=['m_cs', 'ident_b'], writes=['PS:m_ptA'])
                    for f in range(4):
                        P.op('tensor', lambda e, f=f, ch=ch: e.transpose(out=ptB[:, f, :], in_=cs[:, 4 + f, ch * 128:(ch + 1) * 128], identity=self.ident_b[:]), reads=['m_cs', 'ident_b'], writes=['PS:m_ptB'])
                    P.op(v, lambda e, tk=tk: e.tensor_copy(out=tk[:, 0:1024], in_=ptA[:].rearrange("p a b -> p (a b)")), reads=['PS:m_ptA'], writes=[tkk])
                    P.op(a, lambda e, tk=tk: e.copy(out=tk[:, 1024:1536], in_=ptB[:, 0:4, :].rearrange("p a b -> p (a b)")), reads=['PS:m_ptB'], writes=[tkk])
                    P.dma('gpsimd', dr['TOK'][tok0 + ch * 128:tok0 + (ch + 1) * 128, 0:1536], tk[:], reads=[tkk], writes=[('TOK', tok0 + ch * 128)])
            P.barrier()
        with ExitStack() as es:
            S_f = sb(es, "m_Sf", [128, 4, 257], F32)
            S_b = sb(es, "m_Sb", [128, 4, 257], BF16)
            tok = [sb(es, f"m_tok{i}", [128, 1536], BF16) for i in range(2)]
            qk = [sb(es, f"m_qk{i}", [128, 8, 128], BF16) for i in range(2)]
            Lm = [sb(es, f"m_Lm{i}", [128, 128], F32) for i in range(2)]
            Dm = [sb(es, f"m_Dm{i}", [128, 128], F32) for i in range(2)]
            WT = [sb(es, f"m_WT{i}", [128, 128], BF16) for i in range(2)]
            v1 = [sb(es, f"m_v1_{i}", [128, 257], BF16) for i in range(2)]
            v2 = [sb(es, f"m_v2_{i}", [128, 257], BF16) for i in range(2)]
            yA = [sb(es, f"m_yA{i}", [128, 257], F32) for i in range(2)]
            num = [sb(es, f"m_num{i}", [128, 257], F32) for i in range(2)]
            den = [sb(es, f"m_den{i}", [128, 1], F32) for i in range(2)]
            hsb = [sb(es, f"m_hsb{i}", [128, 1024], F32) for i in range(2)]
            hfl = sb(es, "m_hfl", [128, 1024], F32)
            sqj = sb(es, "m_sqj", [128, 256], F32)
            ss = sb(es, "m_ss", [128, 4], F32)
            hn = sb(es, "m_hn", [128, 1024], BF16)
            ecum = sb(es, "m_ecum", [128, 4], F32)
            cdec = sb(es, "m_cdec", [128, 4], F32)
            nwB = sb(es, "m_nwB", [128, 1024], F32)
            ytb = sb(es, "m_ytb", [128, 8, 512], BF16)
            oT = sb(es, "m_oT", [128, 8, 512], F32)
            ot = sb(es, "m_ot", [128, 8, 512], BF16)
            pX = [ps(es, f"m_pX{i}", [128, 512], F32) for i in range(2)]
            pyA = [ps(es, f"m_pyA{i}", [128, 512], F32) for i in range(2)]
            pyB = [ps(es, f"m_pyB{i}", [128, 512], F32) for i in range(2)]
            pS = ps(es, "m_pS", [128, 512], F32)
            ptT = ps(es, "m_ptT", [128, 8, 128], BF16)
            P.dma('sync', nwB[:], dr['ml_nw'].partition_broadcast(128), writes=['m_nwB'])
            li = 0
            for d in range(2):
                MASK = self.MGT if d == 0 else self.MLT
                TRI = self.TLE if d == 0 else self.TGE
                mk, tk_ = ('MGT', 'TLE') if d == 0 else ('MLT', 'TGE')
                ecol = 127 if d == 0 else 0
                P.op(g_, lambda e: e.memset(S_f[:], 0.0), writes=[('m_Sf', h) for h in range(4)])
                P.op(g_, lambda e: e.memset(S_b[:], 0.0), writes=[('m_Sb', h) for h in range(4)])
                for c in chunk_order(d):
                    t0 = c * 128
                    tb, tbk = tok[li % 2], f'm_tok{li % 2}'
                    qb, qbk = qk[li % 2], f'm_qk{li % 2}'
                    hs_, hsk = hsb[li % 2], f'm_hsb{li % 2}'
                    li += 1
                    P.dma('sync', tb[:], dr['TOK'][t0:t0 + 128, 0:1536], writes=[tbk])
                    P.dma('sync', qb[:], dr['BCT'][0:1024, :].rearrange("(f p) t -> p f t", p=128)[:, :, t0:t0 + 128], writes=[qbk])
                    lac = gt_tok[:, c, 32 + d * 4:32 + (d + 1) * 4]
                    P.op('tensor', lambda e, lac=lac: e.matmul(pS[:, 300:304], lhsT=TRI[:], rhs=lac, start=True, stop=True), reads=[tk_, 'm_gt'], writes=['PS:m_pS'])
                    P.op('tensor', lambda e, lac=lac: e.matmul(pS[:, 320:324], lhsT=self.ones_f[:], rhs=lac, start=True, stop=True), reads=['ones_f', 'm_gt'], writes=['PS:m_pS'])
                    P.op(a, lambda e: e.activation(out=ecum[:], in_=pS[:, 300:304], func=AF.Exp), reads=['PS:m_pS'], writes=['m_ecum'])
                    P.op(a, lambda e: e.activation(out=cdec[:], in_=pS[:, 320:324], func=AF.Exp), reads=['PS:m_pS'], writes=['m_cdec'])
                    for h in range(4):
                        r = h % 2
                        X, Xk = pX[r], f'PS:m_pX{r}'
                        A_, Ak = pyA[r], f'PS:m_pyA{r}'
                        B_, Bk = pyB[r], f'PS:m_pyB{r}'
                        gcol = gt_tok[:, c, 32 + d * 4 + h:32 + d * 4 + h + 1]
                        icol = gt_tok[:, c, d * 4 + h:d * 4 + h + 1]
                        P.op(v, lambda e, r=r, gcol=gcol: e.tensor_scalar(out=Lm[r][:], in0=MASK[:], scalar1=gcol, scalar2=None, op0=ALU.mult), reads=[mk, 'm_gt'], writes=[f'm_Lm{r}'])
                        P.op('tensor', lambda e, r=r, X=X: e.matmul(X[:, 0:128], lhsT=Lm[r][:], rhs=TRI[:], start=True, stop=False), reads=[f'm_Lm{r}', tk_], writes=[Xk])
                        P.op('tensor', lambda e, r=r, X=X: e.matmul(X[:, 0:128], lhsT=self.negI[:], rhs=MASK[:], start=False, stop=True), reads=['negI', mk], writes=[Xk])
                        P.op('tensor', lambda e, h=h, X=X, qb=qb: e.matmul(X[:, 128:256], lhsT=qb[:, 4 + h, :], rhs=qb[:, h, :], start=True, stop=True), reads=[qbk], writes=[Xk])
                        P.op(a, lambda e, r=r, X=X: e.activation(out=Dm[r][:], in_=X[:, 0:128], func=AF.Exp), reads=[Xk], writes=[f'm_Dm{r}'])
                        P.op(v, lambda e, r=r, X=X: e.tensor_tensor(out=WT[r][:], in0=Dm[r][:], in1=X[:, 128:256], op=ALU.mult), reads=[f'm_Dm{r}', Xk], writes=[f'm_WT{r}'])
                        P.op(g_, lambda e, r=r, h=h, tb=tb, icol=icol: e.tensor_scalar(out=v1[r][:, 0:256], in0=tb[:, h * 256:(h + 1) * 256], scalar1=icol, scalar2=None, op0=ALU.mult), reads=[tbk, 'm_gt'], writes=[f'm_v1_{r}'])
                        P.op(g_, lambda e, r=r, icol=icol: e.tensor_copy(out=v1[r][:, 256:257], in_=icol), reads=['m_gt'], writes=[f'm_v1_{r}'])
                        P.op(g_, lambda e, r=r: e.tensor_scalar(out=v2[r][:], in0=v1[r][:], scalar1=Dm[r][:, ecol:ecol + 1], scalar2=None, op0=ALU.mult), reads=[f'm_v1_{r}', f'm_Dm{r}'], writes=[f'm_v2_{r}'])
                        P.op('tensor', lambda e, r=r, A_=A_: e.matmul(A_[:, 0:257], lhsT=WT[r][:], rhs=v1[r][:], start=True, stop=True), reads=[f'm_WT{r}', f'm_v1_{r}'], writes=[Ak])
                        P.op('tensor', lambda e, h=h, B_=B_, qb=qb: e.matmul(B_[:, 0:257], lhsT=qb[:, h, :], rhs=S_b[:, h, :], start=True, stop=True), reads=[qbk, ('m_Sb', h)], writes=[Bk])
                        P.op('tensor', lambda e, r=r, h=h, tb=tb: e.matmul(pS[:, 0:257], lhsT=tb[:, 1024 + h * 128:1024 + (h + 1) * 128], rhs=v2[r][:], start=True, stop=True), reads=[tbk, f'm_v2_{r}'], writes=['PS:m_pS'])
                        P.op(a, lambda e, r=r, A_=A_: e.copy(out=yA[r][:], in_=A_[:, 0:257]), reads=[Ak], writes=[f'm_yA{r}'])
                        P.op(v, lambda e, r=r, h=h, B_=B_: e.scalar_tensor_tensor(out=num[r][:], in0=B_[:, 0:257], scalar=ecum[:, h:h + 1], in1=yA[r][:], op0=ALU.mult, op1=ALU.add), reads=[Bk, 'm_ecum', f'm_yA{r}'], writes=[f'm_num{r}'])
                        P.op(v, lambda e, h=h: e.scalar_tensor_tensor(out=S_f[:, h, :], in0=S_f[:, h, :], scalar=cdec[:, h:h + 1], in1=pS[:, 0:257], op0=ALU.mult, op1=ALU.add), reads=[('m_Sf', h), 'm_cdec', 'PS:m_pS'], writes=[('m_Sf', h)])
                        P.op(a, lambda e, h=h: e.copy(out=S_b[:, h, :], in_=S_f[:, h, :]), reads=[('m_Sf', h)], writes=[('m_Sb', h)])
                        P.op(a, lambda e, r=r: e.activation(out=den[r][:], in_=num[r][:, 256:257], func=AF.Abs), reads=[f'm_num{r}'], writes=[f'm_den{r}'])
                        P.op(v, lambda e, r=r: e.tensor_scalar(out=den[r][:], in0=den[r][:], scalar1=1.0, scalar2=None, op0=ALU.max), reads=[f'm_den{r}'], writes=[f'm_den{r}'])
                        P.op(v, lambda e, r=r: e.reciprocal(out=den[r][:], in_=den[r][:]), reads=[f'm_den{r}'], writes=[f'm_den{r}'])
                        P.op(v, lambda e, r=r, h=h, hs_=hs_: e.tensor_scalar(out=hs_[:, h * 256:(h + 1) * 256], in0=num[r][:, 0:256], scalar1=den[r][:], scalar2=None, op0=ALU.mult), reads=[f'm_num{r}', f'm_den{r}'], writes=[hsk])
                    if d == 0:
                        P.dma('gpsimd', dr['YF'][t0:t0 + 128, 0:1024], hs_[:], reads=[hsk], writes=[('YF', c)])
                    else:
                        P.dma('sync', hfl[:], dr['YF'][t0:t0 + 128, 0:1024], reads=[('YF', c)], writes=['m_hfl'])
                        P.op(v, lambda e, hs_=hs_: e.tensor_tensor(out=hfl[:], in0=hfl[:], in1=hs_[:], op=ALU.add), reads=['m_hfl', hsk], writes=['m_hfl'])
                        for h in range(4):
                            P.op(a, lambda e, h=h: e.activation(out=sqj[:], in_=hfl[:, h * 256:(h + 1) * 256], func=AF.Square, accum_out=ss[:, h:h + 1]), reads=['m_hfl'], writes=['m_sqj', 'm_ss'])
                        P.op(a, lambda e: e.activation(out=ss[:], in_=ss[:], func=AF.Sqrt, scale=1.0 / 256, bias=self.eps_col[:]), reads=['m_ss', 'eps_col'], writes=['m_ss'])
                        P.op(v, lambda e: e.reciprocal(out=ss[:], in_=ss[:]), reads=['m_ss'], writes=['m_ss'])
                        for h in range(4):
                            P.op(v, lambda e, h=h: e.scalar_tensor_tensor(out=hn[:, h * 256:(h + 1) * 256], in0=hfl[:, h * 256:(h + 1) * 256], scalar=ss[:, h:h + 1], in1=nwB[:, h * 256:(h + 1) * 256], op0=ALU.mult, op1=ALU.mult),
                                 reads=['m_hfl', 'm_ss', 'm_nwB'], writes=['m_hn'])
                        if c < 2:
                            tok0, nt, ch = 0, TC, c
                        else:
                            tok0, nt, ch = TC + ((c - 2) // 4) * 512, 512, (c - 2) % 4
                        for f in range(8):
                            P.op('tensor', lambda e, f=f: e.transpose(out=ptT[:, f, :], in_=hn[:, f * 128:(f + 1) * 128], identity=self.ident_b[:]), reads=['m_hn', 'ident_b'], writes=['PS:m_ptT'])
                        P.op(v, lambda e, ch=ch: e.tensor_copy(out=ytb[:, :, ch * 128:(ch + 1) * 128], in_=ptT[:]), reads=['PS:m_ptT'], writes=['m_ytb'])
                        if ch == 0:
                            P.dma('sync', oT[:, :, :nt], PT[R_OM:R_OM + 1024, :].rearrange("(f p) t -> p f t", p=128)[:, :, tok0:tok0 + nt], writes=['m_oT'])
                            P.op(a, lambda e, nt=nt: e.activation(out=oT[:, :, :nt], in_=oT[:, :, :nt], func=AF.Sigmoid), reads=['m_oT'], writes=['m_oT'])
                            P.op(v, lambda e, nt=nt: e.tensor_tensor(out=ot[:, :, :nt], in0=ytb[:, :, :nt], in1=oT[:, :, :nt], op=ALU.mult), reads=['m_ytb', 'm_oT'], writes=['m_ot'])
                            P.dma('gpsimd', dr['YT'][1024:2048, :].rearrange("(f p) t -> p f t", p=128)[:, :, tok0:tok0 + nt], ot[:, :, :nt], reads=['m_ot'], writes=[('YT', 'ml', tok0)])
            P.barrier()


B.phase_mlstm = phase_mlstm


R_QKV, R_ZG, R_BETA, R_A = 0, 3072, 4096, 4112


def phase_gdn(self):
    P, nc, dr = self.P, self.nc, self.dr
    sb, ps = self.sb, self.ps
    PT = dr['PT']
    v, a, g_ = 'vector', 'scalar', 'gpsimd'
    with ExitStack() as es0:
        gt_tok = sb(es0, "g_gt", [128, 34, 48], F32)
        with ExitStack() as es:
            G = sb(es, "g_G", [128, T], F32)
            bcol = sb(es, "g_bcol", [128, 1], F32)
            acol = sb(es, "g_acol", [128, 1], F32)
            ptr = ps(es, "g_ptr", [128, 512], F32)
            P.op(g_, lambda e: e.memset(G[:], 0.0), writes=['g_G'])
            P.op(g_, lambda e: e.memset(bcol[:], 0.0), writes=['g_bcol'])
            P.op(g_, lambda e: e.memset(acol[:], 0.0), writes=['g_acol'])
            P.dma('sync', G[0:16, :], PT[R_BETA:R_BETA + 16, :], writes=['g_G'])
            P.dma('sync', G[32:48, :], PT[R_A:R_A + 16, :], writes=['g_G'])
            P.dma('sync', bcol[32:48, :], dr['gdn_dtb'], writes=['g_bcol'])
            P.dma('sync', acol[32:48, :], dr['gdn_alog'], writes=['g_acol'])
            P.op(a, lambda e: e.activation(out=G[0:32, :], in_=G[0:32, :], func=AF.Sigmoid), reads=['g_G'], writes=['g_G'])
            P.op(a, lambda e: e.activation(out=acol[32:64, :], in_=acol[32:64, :], func=AF.Exp), reads=['g_acol'], writes=['g_acol'])
            P.op(v, lambda e: e.tensor_scalar(out=acol[32:64, :], in0=acol[32:64, :], scalar1=-1.0, scalar2=None, op0=ALU.mult), reads=['g_acol'], writes=['g_acol'])
            P.op(a, lambda e: e.activation(out=G[32:64, :], in_=G[32:64, :], func=AF.Exp, bias=bcol[32:64, :]), reads=['g_G', 'g_bcol'], writes=['g_G'])
            P.op(a, lambda e: e.activation(out=G[32:64, :], in_=G[32:64, :], func=AF.Ln, bias=self.one_col[32:64, :]), reads=['g_G', 'one_col'], writes=['g_G'])
            P.op(v, lambda e: e.tensor_scalar(out=G[32:64, :], in0=G[32:64, :], scalar1=acol[32:64, :], scalar2=None, op0=ALU.mult), reads=['g_G', 'g_acol'], writes=['g_G'])
            for c in range(34):
                P.op('tensor', lambda e, c=c: e.matmul(ptr[:, 0:48], lhsT=G[:, c * 128:(c + 1) * 128], rhs=self.ident_f[:, 0:48], start=True, stop=True), reads=['g_G', 'ident_f'], writes=['PS:g_ptr'])
                P.op(v, lambda e, c=c: e.tensor_copy(out=gt_tok[:, c, :], in_=ptr[:, 0:48]), reads=['PS:g_ptr'], writes=['g_gt'])
            P.barrier()
        with ExitStack() as es:
            xin = [sb(es, f"g_xin{i}", [128, 24, 514], F32) for i in range(2)]
            cs = sb(es, "g_cs", [128, 24, 512], BF16)
            cf = sb(es, "g_cf", [128, 512], F32)
            sqb = sb(es, "g_sqb", [128, 512], BF16)
            rn = sb(es, "g_rn", [128, 512], F32)
            acc = [sb(es, f"g_cacc{i}", [128, 512], F32) for i in range(2)]
            cw = sb(es, "g_cw", [128, 24, 3], F32)
            tokst = [sb(es, f"g_tokst{i}", [128, 2048], BF16) for i in range(2)]
            ptA = ps(es, "g_ptA", [128, 8, 128], BF16)
            ptB = ps(es, "g_ptB", [128, 8, 128], BF16)
            pss = ps(es, "g_pss", [128, 512], F32)
            P.dma('sync', cw[:], dr['gdn_cw'], writes=['g_cw'])
            src = PT[0:3072, :].rearrange("(f p) t -> p f t", p=128)
            ci = 0
            for bi, (tok0, nt, m) in enumerate(self.blocks()):
                x = xin[bi % 2]
                xk = f'g_xin{bi % 2}'
                seq0, seq1 = (0, TC) if m == 1 else (TC, T)
                lo = max(tok0 - 1, seq0)
                hi = min(tok0 + nt + 1, seq1)
                P.dma('sync', x[:, 0:12, lo - (tok0 - 1):hi - (tok0 - 1)], src[:, 0:12, lo:hi], writes=[xk])
                P.dma('sync', x[:, 12:24, lo - (tok0 - 1):hi - (tok0 - 1)], src[:, 12:24, lo:hi], writes=[xk])
                if lo > tok0 - 1:
                    P.op(g_, lambda e, x=x: e.memset(x[:, :, 0:1], 0.0), writes=[xk])
                if hi < tok0 + nt + 1:
                    P.op(g_, lambda e, x=x, nt=nt: e.memset(x[:, :, nt + 1:nt + 2], 0.0), writes=[xk])
                for f in range(24):
                    ac = acc[f % 2]
                    ak = f'g_cacc{f % 2}'
                    P.op(v, lambda e, x=x, ac=ac, f=f, nt=nt: e.tensor_scalar(out=ac[:, :nt], in0=x[:, f, 0:nt], scalar1=cw[:, f, 0:1], scalar2=None, op0=ALU.mult), reads=[xk, 'g_cw'], writes=[ak])
                    P.op(v, lambda e, x=x, ac=ac, f=f, nt=nt: e.scalar_tensor_tensor(out=ac[:, :nt], in0=x[:, f, 1:nt + 1], scalar=cw[:, f, 1:2], in1=ac[:, :nt], op0=ALU.mult, op1=ALU.add), reads=[xk, 'g_cw', ak], writes=[ak])
                    P.op(v, lambda e, x=x, ac=ac, f=f, nt=nt: e.scalar_tensor_tensor(out=ac[:, :nt], in0=x[:, f, 2:nt + 2], scalar=cw[:, f, 2:3], in1=ac[:, :nt], op0=ALU.mult, op1=ALU.add), reads=[xk, 'g_cw', ak], writes=[ak])
                    if f >= 16:
                        P.op(a, lambda e, ac=ac, f=f, nt=nt: e.activation(out=cs[:, f, :nt], in_=ac[:, :nt], func=AF.Silu), reads=[ak], writes=[('g_cs', f)])
                    else:
                        P.op(a, lambda e, ac=ac, nt=nt: e.activation(out=cf[:, :nt], in_=ac[:, :nt], func=AF.Silu), reads=[ak], writes=['g_cf'])
                        P.op(a, lambda e, nt=nt: e.activation(out=sqb[:, :nt], in_=cf[:, :nt], func=AF.Square), reads=['g_cf'], writes=['g_sqb'])
                        P.op('tensor', lambda e, nt=nt: e.matmul(pss[:, :nt], lhsT=self.ones_b[:], rhs=sqb[:, :nt], start=True, stop=True), reads=['g_sqb', 'ones_b'], writes=['PS:g_pss'])
                        P.op(a, lambda e, nt=nt: e.activation(out=rn[:, :nt], in_=pss[:, :nt], func=AF.Sqrt, bias=self.eps_col[:]), reads=['PS:g_pss', 'eps_col'], writes=['g_rn'])
                        P.op(v, lambda e, nt=nt: e.reciprocal(out=rn[:, :nt], in_=rn[:, :nt]), reads=['g_rn'], writes=['g_rn'])
                        sc = 128 ** -0.5 if f < 8 else 1.0
                        P.op(v, lambda e, f=f, nt=nt, sc=sc: e.scalar_tensor_tensor(out=cs[:, f, :nt], in0=cf[:, :nt], scalar=sc, in1=rn[:, :nt], op0=ALU.mult, op1=ALU.mult), reads=['g_cf', 'g_rn'], writes=[('g_cs', f)])
                P.dma('gpsimd', dr['BCT'].rearrange("(f p) t -> p f t", p=128)[:, :, tok0:tok0 + nt], cs[:, 0:16, :nt],
                      reads=[('g_cs', f) for f in range(16)], writes=[('BCT', tok0)])
                for ch in range(nt // 128):
                    tk = tokst[ci % 2]
                    tkk = f'g_tokst{ci % 2}'
                    ci += 1
                    for f in range(8):
                        P.op('tensor', lambda e, f=f, ch=ch: e.transpose(out=ptA[:, f, :], in_=cs[:, 16 + f, ch * 128:(ch + 1) * 128], identity=self.ident_b[:]), reads=[('g_cs', 16 + f), 'ident_b'], writes=['PS:g_ptA'])
                    for f in range(8):
                        P.op('tensor', lambda e, f=f, ch=ch: e.transpose(out=ptB[:, f, :], in_=cs[:, 8 + f, ch * 128:(ch + 1) * 128], identity=self.ident_b[:]), reads=[('g_cs', 8 + f), 'ident_b'], writes=['PS:g_ptB'])
                    P.op(v, lambda e, tk=tk: e.tensor_copy(out=tk[:, 0:1024], in_=ptA[:].rearrange("p a b -> p (a b)")), reads=['PS:g_ptA'], writes=[tkk])
                    P.op(a, lambda e, tk=tk: e.copy(out=tk[:, 1024:2048], in_=ptB[:].rearrange("p a b -> p (a b)")), reads=['PS:g_ptB'], writes=[tkk])
                    P.dma('gpsimd', dr['TOK'][tok0 + ch * 128:tok0 + (ch + 1) * 128, :], tk[:], reads=[tkk], writes=[('TOK', tok0 + ch * 128)])
            P.barrier()
        with ExitStack() as es:
            S_f = sb(es, "g_Sf", [128, 8, 128], F32)
            S_b = sb(es, "g_Sb", [128, 8, 128], BF16)
            tok = [sb(es, f"g_tok{i}", [128, 2048], BF16) for i in range(2)]
            qk = [sb(es, f"g_qk{i}", [128, 16, 128], BF16) for i in range(2)]
            Lm = [sb(es, f"g_Lm{i}", [128, 128], F32) for i in range(2)]
            Dm = [sb(es, f"g_Dm{i}", [128, 128], F32) for i in range(2)]
            WT = [sb(es, f"g_WT{i}", [128, 128], BF16) for i in range(2)]
            tX = [sb(es, f"g_tX{i}", [128, 128], F32) for i in range(2)]
            Xp = [[sb(es, f"g_Xp{r}{i}", [128, 128], F32) for i in range(2)] for r in range(2)]
            Np = [[sb(es, f"g_Np{r}{i}", [128, 128], F32) for i in range(2)] for r in range(2)]
            rr = [[sb(es, f"g_rr{r}{i}", [128, 256], F32) for i in range(2)] for r in range(2)]
            wT = [sb(es, f"g_wT{i}", [128, 128], F32) for i in range(2)]
            vn = [sb(es, f"g_vn{i}", [128, 128], F32) for i in range(2)]
            v1 = [sb(es, f"g_v1{i}", [128, 128], BF16) for i in range(2)]
            v2 = [sb(es, f"g_v2{i}", [128, 128], BF16) for i in range(2)]
            yA = [sb(es, f"g_yA{i}", [128, 128], F32) for i in range(2)]
            osb = [sb(es, f"g_osb{i}", [128, 1024], F32) for i in range(2)]
            ofl = sb(es, "g_ofl", [128, 1024], F32)
            sqj = sb(es, "g_sqj", [128, 128], F32)
            ss = sb(es, "g_ss", [128, 8], F32)
            hn = sb(es, "g_hn", [128, 1024], BF16)
            ecum = sb(es, "g_ecum", [128, 8], F32)
            cdec = sb(es, "g_cdec", [128, 8], F32)
            nwB = sb(es, "g_nwB", [128, 128], F32)
            ytb = sb(es, "g_ytb", [128, 8, 512], BF16)
            zT = sb(es, "g_zT", [128, 8, 512], F32)
            ot = sb(es, "g_ot", [128, 8, 512], BF16)
            pX = [ps(es, f"g_pX{i}", [128, 512], F32) for i in range(2)]
            pN = [ps(es, f"g_pN{i}", [128, 512], F32) for i in range(2)]
            pA = [ps(es, f"g_pA{i}", [128, 512], F32) for i in range(2)]
            ptT = ps(es, "g_ptT", [128, 8, 128], BF16)
            P.dma('sync', nwB[:], dr['gdn_nw'].partition_broadcast(128), writes=['g_nwB'])
            li = 0
            for d in range(2):
                MASK = self.MGT if d == 0 else self.MLT
                TRI = self.TLE if d == 0 else self.TGE
                STR = self.MLT if d == 0 else self.MGT
                mk, tk_, sk_ = ('MGT', 'TLE', 'MLT') if d == 0 else ('MLT', 'TGE', 'MGT')
                ecol = 127 if d == 0 else 0
                P.op(g_, lambda e: e.memset(S_f[:], 0.0), writes=[('g_Sf', h) for h in range(8)])
                P.op(g_, lambda e: e.memset(S_b[:], 0.0), writes=[('g_Sb', h) for h in range(8)])
                for c in chunk_order(d):
                    t0 = c * 128
                    tb, tbk = tok[li % 2], f'g_tok{li % 2}'
                    qb, qbk = qk[li % 2], f'g_qk{li % 2}'
                    os_, osk = osb[li % 2], f'g_osb{li % 2}'
                    li += 1
                    P.dma('sync', tb[:], dr['TOK'][t0:t0 + 128, :], writes=[tbk])
                    P.dma('sync', qb[:], dr['BCT'].rearrange("(f p) t -> p f t", p=128)[:, :, t0:t0 + 128], writes=[qbk])
                    lac = gt_tok[:, c, 32 + d * 8:32 + (d + 1) * 8]
                    P.op('tensor', lambda e, lac=lac: e.matmul(pX[0][:, 400:408], lhsT=TRI[:], rhs=lac, start=True, stop=True), reads=[tk_, 'g_gt'], writes=['PS:g_pX0'])
                    P.op('tensor', lambda e, lac=lac: e.matmul(pX[0][:, 420:428], lhsT=self.ones_f[:], rhs=lac, start=True, stop=True), reads=['ones_f', 'g_gt'], writes=['PS:g_pX0'])
                    P.op(a, lambda e: e.activation(out=ecum[:], in_=pX[0][:, 400:408], func=AF.Exp), reads=['PS:g_pX0'], writes=['g_ecum'])
                    P.op(a, lambda e: e.activation(out=cdec[:], in_=pX[0][:, 420:428], func=AF.Exp), reads=['PS:g_pX0'], writes=['g_cdec'])
                    for hp in range(4):
                        hs = (2 * hp, 2 * hp + 1)
                        HV = {}
                        for r, h in enumerate(hs):
                            HV[r] = dict(
                                gcol=gt_tok[:, c, 32 + d * 8 + h:32 + d * 8 + h + 1],
                                bcol=gt_tok[:, c, d * 8 + h:d * 8 + h + 1],
                                kT=qb[:, 8 + h, :], qT=qb[:, h, :],
                                vtok=tb[:, h * 128:(h + 1) * 128], ktok=tb[:, 1024 + h * 128:1024 + (h + 1) * 128])
                        for r, h in enumerate(hs):
                            H = HV[r]
                            P.op(v, lambda e, r=r, H=H: e.tensor_scalar(out=Lm[r][:], in0=MASK[:], scalar1=H['gcol'], scalar2=None, op0=ALU.mult), reads=[mk, 'g_gt'], writes=[f'g_Lm{r}'])
                        for r, h in enumerate(hs):
                            H = HV[r]
                            P.op('tensor', lambda e, r=r: e.matmul(pX[r][:, 0:128], lhsT=Lm[r][:], rhs=TRI[:], start=True, stop=False), reads=[f'g_Lm{r}', tk_], writes=[f'PS:g_pX{r}'])
                            P.op('tensor', lambda e, r=r: e.matmul(pX[r][:, 0:128], lhsT=self.negI[:], rhs=MASK[:], start=False, stop=True), reads=['negI', mk], writes=[f'PS:g_pX{r}'])
                            P.op('tensor', lambda e, r=r, H=H: e.matmul(pX[r][:, 128:256], lhsT=H['kT'], rhs=H['kT'], start=True, stop=True), reads=[qbk], writes=[f'PS:g_pX{r}'])
                            P.op('tensor', lambda e, r=r, H=H: e.matmul(pX[r][:, 256:384], lhsT=H['kT'], rhs=H['qT'], start=True, stop=True), reads=[qbk], writes=[f'PS:g_pX{r}'])
                        for r, h in enumerate(hs):
                            H = HV[r]
                            P.op(a, lambda e, r=r: e.activation(out=Dm[r][:], in_=pX[r][:, 0:128], func=AF.Exp), reads=[f'PS:g_pX{r}'], writes=[f'g_Dm{r}'])
                            P.op(v, lambda e, r=r: e.tensor_tensor(out=tX[r][:], in0=Dm[r][:], in1=pX[r][:, 128:256], op=ALU.mult), reads=[f'g_Dm{r}', f'PS:g_pX{r}'], writes=[f'g_tX{r}'])
                            P.op(v, lambda e, r=r: e.tensor_tensor(out=WT[r][:], in0=Dm[r][:], in1=pX[r][:, 256:384], op=ALU.mult), reads=[f'g_Dm{r}', f'PS:g_pX{r}'], writes=[f'g_WT{r}'])
                            P.op(v, lambda e, r=r, H=H: e.scalar_tensor_tensor(out=Xp[r][0][:], in0=tX[r][:], scalar=H['bcol'], in1=STR[:], op0=ALU.mult, op1=ALU.mult), reads=[f'g_tX{r}', 'g_gt', sk_], writes=[f'g_Xp{r}0'])
                            P.op(g_, lambda e, r=r, H=H: e.tensor_copy(out=rr[r][0][:, 0:128], in_=H['vtok']), reads=[tbk], writes=[f'g_rr{r}0'])
                            P.op(g_, lambda e, r=r, H=H, h=h: e.tensor_scalar(out=rr[r][0][:, 128:256], in0=H['ktok'], scalar1=ecum[:, h:h + 1], scalar2=None, op0=ALU.mult), reads=[tbk, 'g_ecum'], writes=[f'g_rr{r}0'])
                        for r, h in enumerate(hs):
                            P.op('tensor', lambda e, r=r: e.matmul(pN[r][:, 0:128], lhsT=Xp[r][0][:], rhs=self.ident_f[:], start=True, stop=True), reads=[f'g_Xp{r}0', 'ident_f'], writes=[f'PS:g_pN{r}'])
                            P.op('tensor', lambda e, r=r: e.matmul(pX[r][:, 0:256], lhsT=Xp[r][0][:], rhs=rr[r][0][:], start=True, stop=True), reads=[f'g_Xp{r}0', f'g_rr{r}0'], writes=[f'PS:g_pX{r}'])
                        for r, h in enumerate(hs):
                            P.op(a, lambda e, r=r: e.copy(out=Np[r][0][:], in_=pN[r][:, 0:128]), reads=[f'PS:g_pN{r}'], writes=[f'g_Np{r}0'])
                            P.op(v, lambda e, r=r: e.tensor_tensor(out=rr[r][1][:], in0=rr[r][0][:], in1=pX[r][:, 0:256], op=ALU.subtract), reads=[f'g_rr{r}0', f'PS:g_pX{r}'], writes=[f'g_rr{r}1'])
                        cur = 1
                        xi = 0
                        for lev in range(6):
                            nx = 1 - xi
                            for r, h in enumerate(hs):
                                P.op('tensor', lambda e, r=r, xi=xi: e.matmul(pN[r][:, 0:128], lhsT=Np[r][xi][:], rhs=Xp[r][xi][:], start=True, stop=True), reads=[f'g_Np{r}{xi}', f'g_Xp{r}{xi}'], writes=[f'PS:g_pN{r}'])
                                if lev < 5:
                                    P.op('tensor', lambda e, r=r, xi=xi: e.matmul(pN[r][:, 128:256], lhsT=Xp[r][xi][:], rhs=Np[r][xi][:], start=True, stop=True), reads=[f'g_Np{r}{xi}', f'g_Xp{r}{xi}'], writes=[f'PS:g_pN{r}'])
                            for r, h in enumerate(hs):
                                P.op(a, lambda e, r=r, nx=nx: e.copy(out=Xp[r][nx][:], in_=pN[r][:, 0:128]), reads=[f'PS:g_pN{r}'], writes=[f'g_Xp{r}{nx}'])
                                if lev < 5:
                                    P.op(v, lambda e, r=r, nx=nx: e.tensor_copy(out=Np[r][nx][:], in_=pN[r][:, 128:256]), reads=[f'PS:g_pN{r}'], writes=[f'g_Np{r}{nx}'])
                            for r, h in enumerate(hs):
                                P.op('tensor', lambda e, r=r, nx=nx, cur=cur: e.matmul(pX[r][:, 0:256], lhsT=Xp[r][nx][:], rhs=rr[r][cur][:], start=True, stop=True), reads=[f'g_Xp{r}{nx}', f'g_rr{r}{cur}'], writes=[f'PS:g_pX{r}'])
                            for r, h in enumerate(hs):
                                P.op(v, lambda e, r=r, cur=cur: e.tensor_tensor(out=rr[r][1 - cur][:], in0=rr[r][cur][:], in1=pX[r][:, 0:256], op=ALU.add), reads=[f'g_rr{r}{cur}', f'PS:g_pX{r}'], writes=[f'g_rr{r}{1 - cur}'])
                            cur = 1 - cur
                            xi = nx
                        for r, h in enumerate(hs):
                            R_ = rr[r][cur]
                            Rk = f'g_rr{r}{cur}'
                            P.op('tensor', lambda e, r=r, R_=R_: e.matmul(pN[r][:, 256:384], lhsT=R_[:, 128:256], rhs=self.ident_f[:], start=True, stop=True), reads=[Rk, 'ident_f'], writes=[f'PS:g_pN{r}'])
                        for r, h in enumerate(hs):
                            P.op(a, lambda e, r=r: e.copy(out=wT[r][:], in_=pN[r][:, 256:384]), reads=[f'PS:g_pN{r}'], writes=[f'g_wT{r}'])
                        for r, h in enumerate(hs):
                            P.op('tensor', lambda e, r=r, h=h: e.matmul(pA[r][:, 128:256], lhsT=wT[r][:], rhs=S_f[:, h, :], start=True, stop=True), reads=[f'g_wT{r}', ('g_Sf', h)], writes=[f'PS:g_pA{r}'])
                        for r, h in enumerate(hs):
                            H = HV[r]
                            R_ = rr[r][cur]
                            Rk = f'g_rr{r}{cur}'
                            P.op(v, lambda e, r=r, R_=R_: e.scalar_tensor_tensor(out=vn[r][:], in0=pA[r][:, 128:256], scalar=-1.0, in1=R_[:, 0:128], op0=ALU.mult, op1=ALU.add), reads=[f'PS:g_pA{r}', Rk], writes=[f'g_vn{r}'])
                            P.op(v, lambda e, r=r, H=H: e.tensor_scalar(out=v1[r][:], in0=vn[r][:], scalar1=H['bcol'], scalar2=None, op0=ALU.mult), reads=[f'g_vn{r}', 'g_gt'], writes=[f'g_v1{r}'])
                            P.op(g_, lambda e, r=r: e.tensor_scalar(out=v2[r][:], in0=v1[r][:], scalar1=Dm[r][:, ecol:ecol + 1], scalar2=None, op0=ALU.mult), reads=[f'g_v1{r}', f'g_Dm{r}'], writes=[f'g_v2{r}'])
                        for r, h in enumerate(hs):
                            H = HV[r]
                            P.op('tensor', lambda e, r=r: e.matmul(pA[r][:, 0:128], lhsT=WT[r][:], rhs=v1[r][:], start=True, stop=True), reads=[f'g_WT{r}', f'g_v1{r}'], writes=[f'PS:g_pA{r}'])
                            P.op('tensor', lambda e, r=r, h=h, H=H: e.matmul(pA[r][:, 256:384], lhsT=H['qT'], rhs=S_b[:, h, :], start=True, stop=True), reads=[qbk, ('g_Sb', h)], writes=[f'PS:g_pA{r}'])
                            P.op('tensor', lambda e, r=r, H=H: e.matmul(pA[r][:, 384:512], lhsT=H['ktok'], rhs=v2[r][:], start=True, stop=True), reads=[tbk, f'g_v2{r}'], writes=[f'PS:g_pA{r}'])
                        for r, h in enumerate(hs):
                            P.op(a, lambda e, r=r: e.copy(out=yA[r][:], in_=pA[r][:, 0:128]), reads=[f'PS:g_pA{r}'], writes=[f'g_yA{r}'])
                            P.op(v, lambda e, r=r, h=h, os_=os_: e.scalar_tensor_tensor(out=os_[:, h * 128:(h + 1) * 128], in0=pA[r][:, 256:384], scalar=ecum[:, h:h + 1], in1=yA[r][:], op0=ALU.mult, op1=ALU.add), reads=[f'PS:g_pA{r}', 'g_ecum', f'g_yA{r}'], writes=[osk])
                            P.op(v, lambda e, r=r, h=h: e.scalar_tensor_tensor(out=S_f[:, h, :], in0=S_f[:, h, :], scalar=cdec[:, h:h + 1], in1=pA[r][:, 384:512], op0=ALU.mult, op1=ALU.add), reads=[('g_Sf', h), 'g_cdec', f'PS:g_pA{r}'], writes=[('g_Sf', h)])
                            P.op(a, lambda e, h=h: e.copy(out=S_b[:, h, :], in_=S_f[:, h, :]), reads=[('g_Sf', h)], writes=[('g_Sb', h)])
                    if d == 0:
                        P.dma('gpsimd', dr['YF'][t0:t0 + 128, 1024:2048], os_[:], reads=[osk], writes=[('YFg', c)])
                    else:
                        P.dma('sync', ofl[:], dr['YF'][t0:t0 + 128, 1024:2048], reads=[('YFg', c)], writes=['g_ofl'])
                        P.op(v, lambda e, os_=os_: e.tensor_tensor(out=ofl[:], in0=ofl[:], in1=os_[:], op=ALU.add), reads=['g_ofl', osk], writes=['g_ofl'])
                        for h in range(8):
                            P.op(a, lambda e, h=h: e.activation(out=sqj[:], in_=ofl[:, h * 128:(h + 1) * 128], func=AF.Square, accum_out=ss[:, h:h + 1]), reads=['g_ofl'], writes=['g_sqj', 'g_ss'])
                        P.op(a, lambda e: e.activation(out=ss[:], in_=ss[:], func=AF.Sqrt, scale=1.0 / 128, bias=self.eps_col[:]), reads=['g_ss', 'eps_col'], writes=['g_ss'])
                        P.op(v, lambda e: e.reciprocal(out=ss[:], in_=ss[:]), reads=['g_ss'], writes=['g_ss'])
                        for h in range(8):
                            P.op(v, lambda e, h=h: e.scalar_tensor_tensor(out=hn[:, h * 128:(h + 1) * 128], in0=ofl[:, h * 128:(h + 1) * 128], scalar=ss[:, h:h + 1], in1=nwB[:], op0=ALU.mult, op1=ALU.mult),
                                 reads=['g_ofl', 'g_ss', 'g_nwB'], writes=['g_hn'])
                        if c < 2:
                            tok0, nt, ch = 0, TC, c
                        else:
                            tok0, nt, ch = TC + ((c - 2) // 4) * 512, 512, (c - 2) % 4
                        for f in range(8):
                            P.op('tensor', lambda e, f=f: e.transpose(out=ptT[:, f, :], in_=hn[:, f * 128:(f + 1) * 128], identity=self.ident_b[:]), reads=['g_hn', 'ident_b'], writes=['PS:g_ptT'])
                        P.op(v, lambda e, ch=ch: e.tensor_copy(out=ytb[:, :, ch * 128:(ch + 1) * 128], in_=ptT[:]), reads=['PS:g_ptT'], writes=['g_ytb'])
                        if ch == 0:
                            P.dma('sync', zT[:, :, :nt], PT[R_ZG:R_ZG + 1024, :].rearrange("(f p) t -> p f t", p=128)[:, :, tok0:tok0 + nt], writes=['g_zT'])
                            P.op(a, lambda e, nt=nt: e.activation(out=zT[:, :, :nt], in_=zT[:, :, :nt], func=AF.Silu), reads=['g_zT'], writes=['g_zT'])
                            P.op(v, lambda e, nt=nt: e.tensor_tensor(out=ot[:, :, :nt], in0=ytb[:, :, :nt], in1=zT[:, :, :nt], op=ALU.mult), reads=['g_ytb', 'g_zT'], writes=['g_ot'])
                            P.dma('gpsimd', dr['YT'][0:1024, :].rearrange("(f p) t -> p f t", p=128)[:, :, tok0:tok0 + nt], ot[:, :, :nt], reads=['g_ot'], writes=[('YT', 'gdn', tok0)])
            P.barrier()


B.phase_gdn = phase_gdn


def phase_final(self, src):
    P, nc, dr = self.P, self.nc, self.dr
    sb, ps = self.sb, self.ps
    with ExitStack() as es:
        xt = [sb(es, f"fin_x{i}", [128, KC, 512], F32) for i in range(2)]
        sq = sb(es, "fin_sq", [128, KC, 512], BF16)
        rstd = sb(es, "fin_rstd", [128, 512], F32)
        fw = sb(es, "fin_w", [128, KC], F32)
        pss = ps(es, "fin_pss", [128, 512], F32)
        P.dma('sync', fw[:], dr['fnwT'], writes=['fin_w'])
        for bi, (tok0, nt, m) in enumerate(self.blocks()[1:]):
            x, xk = xt[bi % 2], f'fin_x{bi % 2}'
            P.dma('sync', x[:, :, :nt], src.rearrange("(kc p) t -> p kc t", p=128)[:, :, tok0:tok0 + nt], writes=[xk])
            P.op('scalar', lambda e, x=x, nt=nt: e.activation(out=sq[:, :, :nt], in_=x[:, :, :nt], func=AF.Square), reads=[xk], writes=['fin_sq'])
            for kc in range(KC):
                P.op('tensor', lambda e, kc=kc, nt=nt: e.matmul(pss[:, :nt], lhsT=self.ones_b[:], rhs=sq[:, kc, :nt], start=(kc == 0), stop=(kc == KC - 1)), reads=['fin_sq', 'ones_b'], writes=['PS:fin_pss'])
            P.op('scalar', lambda e, nt=nt: e.activation(out=rstd[:, :nt], in_=pss[:, :nt], func=AF.Sqrt, scale=1.0 / D, bias=self.eps_col[:]), reads=['PS:fin_pss', 'eps_col'], writes=['fin_rstd'])
            P.op('vector', lambda e, nt=nt: e.reciprocal(out=rstd[:, :nt], in_=rstd[:, :nt]), reads=['fin_rstd'], writes=['fin_rstd'])
            for kc in range(KC):
                P.op('vector', lambda e, kc=kc, x=x, nt=nt: e.scalar_tensor_tensor(out=x[:, kc, :nt], in0=x[:, kc, :nt], scalar=fw[:, kc:kc + 1], in1=rstd[:, :nt], op0=ALU.mult, op1=ALU.mult),
                     reads=[xk, 'fin_w', 'fin_rstd'], writes=[xk])
            P.dma('gpsimd', dr['outT'].rearrange("(kc p) t -> p kc t", p=128)[:, :, tok0 - TC:tok0 - TC + nt], x[:, :, :nt], reads=[xk], writes=[('outT', tok0)])
        P.barrier()


B.phase_final = phase_final
```

```python
import numpy as np
from contextlib import ExitStack
import concourse.bass as bass
import concourse.mybir as mybir
from concourse.bass_utils import run_bass_kernel_spmd

F32 = mybir.dt.float32
BF16 = mybir.dt.bfloat16
I32 = mybir.dt.int32
AF = mybir.ActivationFunctionType
ALU = mybir.AluOpType
AX = mybir.AxisListType

D = 2048
KC = 16
TC = 256
TL = 4096
T = TC + TL
FFN = 5504
EPS = 1e-6
EVEN_IN = 4656
ODD_IN = 7216
NEG = -30000.0

SEM_LIMIT = 4000
DMA_SLOT_LIMIT = 1200
DMA_POOL = 6


class Prog:
    ENG = ('sync', 'scalar', 'vector', 'gpsimd', 'tensor')

    def __init__(self, nc, es):
        self.nc = nc
        self.es = es
        self.e = dict(sync=nc.sync, scalar=nc.scalar, vector=nc.vector,
                      gpsimd=nc.gpsimd, tensor=nc.tensor)
        self.semh = []
        self.cur = {}
        self.cnt = {}
        self.known = {e: {} for e in self.ENG}
        self.lastw = {}
        self.readers = {}
        self.pool = {}
        self.pidx = {}
        self.nins = {e: 0 for e in self.ENG}
        for e in self.ENG:
            self._fresh(e)

    def _newsem(self, name):
        h = self.es.enter_context(self.nc.semaphore(name))
        self.semh.append(h)
        return len(self.semh) - 1

    def _fresh(self, e):
        self.cur[e] = self._newsem(f"s_{e}_{len(self.semh)}")
        self.cnt[e] = 0

    def _deps(self, reads, writes):
        d = {}

        def add(tok):
            if tok is None:
                return
            sk, val, pe = tok
            if sk not in d or d[sk][0] < val:
                d[sk] = (val, pe)
        for k in reads:
            add(self.lastw.get(k))
        for k in writes:
            add(self.lastw.get(k))
            for t in self.readers.get(k, ()):
                add(t)
        return d

    def _update(self, reads, writes, tok):
        for k in reads:
            self.readers.setdefault(k, []).append(tok)
        for k in writes:
            self.lastw[k] = tok
            self.readers[k] = []

    def _wait(self, eng, sk, val):
        if self.known[eng].get(sk, 0) >= val:
            return
        self.e[eng].wait_ge(self.semh[sk], val)
        self.known[eng][sk] = val
        self.nins[eng] += 1

    def _waits(self, eng, deps):
        for sk, (val, pe) in deps.items():
            if pe == 'tensor' and eng == 'tensor':
                continue
            self._wait(eng, sk, val)

    @staticmethod
    def _excl(reads, writes):
        ex = [k for k in reads if isinstance(k, str) and k.startswith('PS:')]
        if ex:
            reads = [k for k in reads if k not in ex]
            writes = list(writes) + ex
        return reads, writes

    def op(self, eng, fn, reads=(), writes=()):
        reads, writes = self._excl(reads, writes)
        deps = self._deps(reads, writes)
        self._waits(eng, deps)
        ins = fn(self.e[eng])
        if self.cnt[eng] >= SEM_LIMIT:
            self._fresh(eng)
        self.cnt[eng] += 1
        ins.then_inc(self.semh[self.cur[eng]], 1)
        tok = (self.cur[eng], self.cnt[eng], eng)
        self._update(reads, writes, tok)
        self.nins[eng] += 1
        return tok

    def dma(self, q, out, in_, reads=(), writes=(), **kw):
        deps = self._deps(reads, writes)
        self._waits(q, deps)
        E = self.e[q]
        if q not in self.pool:
            self.pool[q] = [[self._newsem(f"d_{q}_{i}_{len(self.semh)}"), 0] for i in range(DMA_POOL)]
            self.pidx[q] = 0
        i = self.pidx[q] % DMA_POOL
        self.pidx[q] += 1
        slot = self.pool[q][i]
        if slot[1] >= DMA_SLOT_LIMIT:
            self._wait(q, slot[0], 16 * slot[1])
            slot[0] = self._newsem(f"d_{q}_{i}_{len(self.semh)}")
            slot[1] = 0
        sk, n = slot
        if n > 0:
            self._wait(q, sk, 16 * n)
        ins = E.dma_start(out=out, in_=in_, **kw)
        ins.then_inc(self.semh[sk], 16)
        slot[1] = n + 1
        tok = (sk, 16 * (n + 1), 'dma')
        self._update(reads, writes, tok)
        self.nins[q] += 1
        return tok

    def barrier(self):
        toks = []
        for q, slots in self.pool.items():
            for sk, n in slots:
                if n > 0:
                    toks.append((sk, 16 * n))
        for e in self.ENG:
            if self.cnt[e] > 0:
                toks.append((self.cur[e], self.cnt[e]))
        for e in self.ENG:
            for sk, val in toks:
                self._wait(e, sk, val)
        self.lastw.clear()
        self.readers.clear()

    def finish(self):
        self.barrier()


class B:
    def __init__(self, stage=99, dbg=(), sub=99):
        self.sub = sub
        self.stage = stage
        self.dbg = set(dbg)
        self.nc = bass.Bass("TRN2", target_bir_lowering=False)
        self.dr = {}

    def din(self, name, shape, dt=F32):
        self.dr[name] = self.nc.dram_tensor(name, list(shape), dt, kind="ExternalInput").ap()
        return self.dr[name]

    def dout(self, name, shape, dt=F32):
        self.dr[name] = self.nc.dram_tensor(name, list(shape), dt, kind="ExternalOutput").ap()
        return self.dr[name]

    def dscr(self, name, shape, dt=F32):
        kind = "ExternalOutput" if name in self.dbg else "Internal"
        self.dr[name] = self.nc.dram_tensor(name, list(shape), dt, kind=kind).ap()
        return self.dr[name]

    def _uniq(self, name):
        self._names = getattr(self, '_names', {})
        n = self._names.get(name, 0)
        self._names[name] = n + 1
        return name if n == 0 else f"{name}_u{n}"

    def sb(self, es, name, shape, dt=F32):
        return es.enter_context(self.nc.sbuf_tensor(self._uniq(name), list(shape), dt))

    def ps(self, es, name, shape, dt=F32):
        return es.enter_context(self.nc.psum_tensor(self._uniq(name), list(shape), dt))

    def build(self):
        nc = self.nc
        din = self.din
        din("xT", [D, T])
        din("ccT", [128, KC, 2])
        din("mod_w", [2, D, 6 * D])
        din("mod_bT", [2, 128, 96])
        din("nmwT", [2, 128, KC])
        din("nfwT", [2, 128, KC])
        din("ev_in_w", [D, EVEN_IN])
        din("od_in_w", [D, ODD_IN])
        self.dscr("XS", [D, T])
        self.dscr("PT", [ODD_IN, T])
        self.dscr("WB", [D, ODD_IN], BF16)
        din("ssd_cw", [128, 20, 3])
        din("ssd_cb", [128, 20])
        din("ssd_dtb", [48, 1])
        din("ssd_alog", [48, 1])
        din("ssd_d", [1, 24])
        din("ssd_nw", [128, 12])
        self.dscr("TOK", [T, 2048], BF16)
        self.dscr("BCT", [2048, T], BF16)
        self.dscr("YF", [T, 2048])
        self.dscr("YT", [D, T], BF16)
        for nm in ("s5_lre", "s5_lim", "s5_dlt"):
            din(nm, [128, 32])
        for nm in ("s5_bre", "s5_bim", "s5_cre", "s5_cim"):
            din(nm, [128, 32, 16])
        din("s5_dsk", [128, 4])
        din("s5_glu_w", [512, 512])
        din("fnwT", [128, KC])
        din("gdn_dtb", [16, 1])
        din("gdn_alog", [16, 1])
        din("gdn_cw", [128, 24, 3])
        din("gdn_nw", [1, 128])
        din("ml_ib", [8, 1])
        din("ml_fb", [8, 1])
        din("ml_nw", [1, 1024])
        din("ev_out_w", [D, D])
        din("od_out_w", [D, D])
        din("ffn_up_w", [2, D, 2 * FFN])
        din("ffn_down_w", [2, FFN, D])
        din("ffn_cw", [2, 128, 43, 9])
        self.dscr("WO", [D, D], BF16)
        self.dscr("WU", [D, 2 * FFN], BF16)
        self.dscr("WD", [FFN, D], BF16)
        self.dscr("XS2", [D, T])
        self.dout("outT", [D, TL])
        if 'YTin' in self.dbg:
            din('YTin', [D, T])
        if 'XSin' in self.dbg:
            din('XSin', [D, T])
        dshapes = {'dbg_mod': [128, 96, 2], 'dbg_h': [D, T], 'dbg_gates': [2, 128, 34 * 48]}
        for k in self.dbg:
            if k in dshapes:
                self.dout(k, dshapes[k])
        with ExitStack() as es:
            self.P = Prog(nc, es)
            self.consts(es)
            for layer in range(2):
                if 'XSin' in self.dbg:
                    if layer == 0:
                        for r in range(0, D, 512):
                            self.P.dma('gpsimd', self.dr['XS'][r:r + 512, :], self.dr['XSin'][r:r + 512, :], writes=['xsin'])
                        self.P.barrier()
                        continue
                self.layer(layer)
                if self.stage <= layer * 10 + 9:
                    break
            self.P.finish()
        return nc

    def consts(self, es):
        P = self.P
        sb = self.sb
        self.ident_b = sb(es, "ident_b", [128, 128], BF16)
        self.ident_f = sb(es, "ident_f", [128, 128], F32)
        self.ones_b = sb(es, "ones_b", [128, 128], BF16)
        self.ones_f = sb(es, "ones_f", [128, 128], F32)
        self.negI = sb(es, "negI", [128, 128], F32)
        self.MGT = sb(es, "MGT", [128, 128], F32)
        self.MLT = sb(es, "MLT", [128, 128], F32)
        self.TLE = sb(es, "TLE", [128, 128], F32)
        self.TGE = sb(es, "TGE", [128, 128], F32)
        g = 'gpsimd'
        P.op(g, lambda e: e.memset(self.ones_f[:], 1.0), writes=['ones_f'])
        P.op(g, lambda e: e.memset(self.ones_b[:], 1.0), writes=['ones_b'])

        def sel(out, cm, step, cmp, key, fill=0.0, src=None):
            src = self.ones_f if src is None else src
            P.op(g, lambda e: e.affine_select(out=out[:], in_=src[:], pattern=[[step, 128]], base=0,
                                              channel_multiplier=cm, compare_op=cmp, fill=fill),
                 reads=['ones_f'], writes=[key])
        sel(self.ident_f, 1, -1, ALU.is_equal, 'ident_f')
        sel(self.MGT, 1, -1, ALU.is_gt, 'MGT')
        sel(self.MLT, -1, 1, ALU.is_gt, 'MLT')
        sel(self.TLE, -1, 1, ALU.is_ge, 'TLE')
        sel(self.TGE, 1, -1, ALU.is_ge, 'TGE')
        P.op(g, lambda e: e.tensor_copy(out=self.ident_b[:], in_=self.ident_f[:]), reads=['ident_f'], writes=['ident_b'])
        P.op(g, lambda e: e.tensor_scalar(out=self.negI[:], in0=self.ident_f[:], scalar1=NEG, scalar2=None, op0=ALU.mult),
             reads=['ident_f'], writes=['negI'])
        self.modT = sb(es, "modT", [128, 96, 2], F32)
        self.s1 = sb(es, "s1", [128, KC, 2], F32)
        self.s2 = sb(es, "s2", [128, KC, 2], F32)
        self.eps_col = sb(es, "eps_col", [128, 1], F32)
        P.op(g, lambda e: e.memset(self.eps_col[:], EPS), writes=['eps_col'])
        self.one_col = sb(es, "one_col", [128, 1], F32)
        P.op(g, lambda e: e.memset(self.one_col[:], 1.0), writes=['one_col'])
        P.barrier()

    def layer(self, layer):
        import os
        if os.environ.get('SKIP12'):
            self.phase_ssd()
            return
        self.phase_mod(layer)
        if self.stage <= layer * 10 + 1:
            return
        self.phase_inproj(layer)
        if self.stage <= layer * 10 + 2:
            return
        if layer == 0 and 'YTin' not in self.dbg:
            import os
            if not os.environ.get('NOS5'):
                self.phase_s5()
            if self.stage <= layer * 10 + 2 or os.environ.get('NOSSD'):
                return
            self.phase_ssd()
            if self.stage <= layer * 10 + 3:
                return
        if layer == 1 and 'YTin' not in self.dbg:
            import os
            if not os.environ.get('NOGDN'):
                self.phase_gdn()
            if not os.environ.get('NOML'):
                self.phase_mlstm()
            if self.stage <= layer * 10 + 3:
                return
        if 'YTin' in self.dbg:
            self.P.dma('gpsimd', self.dr['YT'], self.dr['YTin'], writes=['ytin'])
            self.P.barrier()
        src = self.dr['xT'] if layer == 0 else self.dr['XS']
        self.phase_outproj(layer, src, self.dr['XS2'])
        if self.stage <= layer * 10 + 4:
            return
        self.phase_ffn(layer, self.dr['XS2'], self.dr['XS'])
        if layer == 1:
            self.phase_final(self.dr['XS'])

    def phase_mod(self, layer):
        P, nc = self.P, self.nc
        dr = self.dr
        with ExitStack() as es:
            sb, ps = self.sb, self.ps
            PW = 768
            wt = [sb(es, f"modw{i}", [128, KC, PW], F32) for i in range(2)]
            cc = sb(es, "cc", [128, KC, 2], F32)
            scc = sb(es, "scc", [128, KC, 2], F32)
            mb = sb(es, "mb", [128, 96], F32)
            nmw = sb(es, "nmw", [128, KC], F32)
            nfw = sb(es, "nfw", [128, KC], F32)
            pm = ps(es, "pm", [128, 96, 2], F32)
            P.dma('sync', cc[:], dr["ccT"], writes=['cc'])
            P.dma('sync', mb[:], dr["mod_bT"][layer], writes=['mb'])
            P.dma('sync', nmw[:], dr["nmwT"][layer], writes=['nmw'])
            P.dma('sync', nfw[:], dr["nfwT"][layer], writes=['nfw'])
            P.op('scalar', lambda e: e.activation(out=scc[:], in_=cc[:], func=AF.Silu), reads=['cc'], writes=['scc'])
            wsrc = dr["mod_w"][layer].rearrange("(kc p) n -> p kc n", p=128)
            for pn in range(16):
                w = wt[pn % 2]
                wk = f'modw{pn % 2}'
                P.dma('sync' if pn % 2 == 0 else 'gpsimd', w[:], wsrc[:, :, pn * PW:(pn + 1) * PW], writes=[wk])
                for jj in range(6):
                    j = pn * 6 + jj
                    for kc in range(KC):
                        P.op('tensor', lambda e, w=w, jj=jj, kc=kc, j=j: e.matmul(
                            pm[:, j, :], lhsT=w[:, kc, jj * 128:(jj + 1) * 128], rhs=scc[:, kc, :],
                            start=(kc == 0), stop=(kc == KC - 1)), reads=[wk, 'scc'], writes=['PS:pm'])
            for m in range(2):
                P.op('vector', lambda e, m=m: e.tensor_tensor(out=self.modT[:, :, m], in0=pm[:, :, m], in1=mb[:], op=ALU.add),
                     reads=['PS:pm', 'mb'], writes=['modT'])
            for m in range(2):
                P.op('vector', lambda e, m=m: e.scalar_tensor_tensor(
                    out=self.s1[:, :, m], in0=self.modT[:, 16:32, m], scalar=1.0, in1=nmw[:], op0=ALU.add, op1=ALU.mult),
                    reads=['modT', 'nmw'], writes=['s1'])
                P.op('vector', lambda e, m=m: e.scalar_tensor_tensor(
                    out=self.s2[:, :, m], in0=self.modT[:, 64:80, m], scalar=1.0, in1=nfw[:], op0=ALU.add, op1=ALU.mult),
                    reads=['modT', 'nfw'], writes=['s2'])
            if 'dbg_mod' in self.dbg:
                P.dma('sync', dr['dbg_mod'], self.modT[:], reads=['modT'], writes=['dbg_mod'])
            P.barrier()

    def blocks(self):
        return [(0, TC, 1)] + [(TC + i * 512, 512, 0) for i in range(8)]

    def norm_block(self, src, tok0, nt, m, scale_t, shift_lo, bufs, keys):
        P = self.P
        xt, sq, hT, rstd, tmp, pss = bufs['xt'], bufs['sq'], bufs['hT'], bufs['rstd'], bufs['tmp'], bufs['pss']
        kx, ksq, kh, kr, kt, kp = keys
        P.dma('sync', xt[:, :, :nt], src.rearrange("(kc p) t -> p kc t", p=128)[:, :, tok0:tok0 + nt], writes=[kx])
        P.op('scalar', lambda e: e.activation(out=sq[:, :, :nt], in_=xt[:, :, :nt], func=AF.Square), reads=[kx], writes=[ksq])
        for kc in range(KC):
            P.op('tensor', lambda e, kc=kc: e.matmul(pss[:, :nt], lhsT=self.ones_b[:], rhs=sq[:, kc, :nt],
                                                     start=(kc == 0), stop=(kc == KC - 1)), reads=[ksq, 'ones_b'], writes=[kp])
        P.op('scalar', lambda e: e.activation(out=rstd[:, :nt], in_=pss[:, :nt], func=AF.Sqrt, scale=1.0 / D, bias=self.eps_col[:]),
             reads=[kp, 'eps_col'], writes=[kr])
        P.op('vector', lambda e: e.reciprocal(out=rstd[:, :nt], in_=rstd[:, :nt]), reads=[kr], writes=[kr])
        for kc in range(KC):
            tk = kt + str(kc % 2)
            t = tmp[kc % 2]
            P.op('vector', lambda e, kc=kc, t=t: e.tensor_tensor(out=t[:, :nt], in0=xt[:, kc, :nt], in1=rstd[:, :nt], op=ALU.mult),
                 reads=[kx, kr], writes=[tk])
            P.op('scalar', lambda e, kc=kc, t=t: e.activation(out=hT[:, kc, :nt], in_=t[:, :nt], func=AF.Identity,
                                                              scale=scale_t[:, kc, m:m + 1], bias=self.modT[:, shift_lo + kc, m:m + 1]),
                 reads=[tk, 'modT', 's1', 's2'], writes=[kh + str(kc)])

    def phase_inproj(self, layer):
        P, nc, dr = self.P, self.nc, self.dr
        nin = EVEN_IN if layer == 0 else ODD_IN
        wsrc = dr["ev_in_w"] if layer == 0 else dr["od_in_w"]
        WB = dr["WB"]
        for r in range(0, D, 256):
            P.dma('gpsimd', WB[r:r + 256, :nin], wsrc[r:r + 256, :], writes=[('WB', r)])
        src = dr["xT"] if layer == 0 else dr["XS"]
        ntile = [(c0, min(128, nin - c0)) for c0 in range(0, nin, 128)]
        PWT = 4
        with ExitStack() as es:
            sb, ps = self.sb, self.ps
            bufs = dict(xt=sb(es, "xt", [128, KC, 512], F32), sq=sb(es, "sq", [128, KC, 512], BF16),
                        hT=sb(es, "hT", [128, KC, 512], BF16), rstd=sb(es, "rstd", [128, 512], F32),
                        tmp=[sb(es, f"ntmp{i}", [128, 512], F32) for i in range(2)],
                        pss=ps(es, "pss", [128, 512], F32))
            wb = [sb(es, f"wb{i}", [128, KC, PWT * 128], BF16) for i in range(2)]
            stg = [sb(es, f"stg{i}", [128, 512], F32) for i in range(3)]
            pacc = [ps(es, f"pacc{i}", [128, 512], F32) for i in range(4)]
            wv = WB.rearrange("(kc p) n -> p kc n", p=128)
            hkeys = ['hT' + str(kc) for kc in range(KC)]
            it = 0
            pi = 0
            for (tok0, nt, m) in self.blocks():
                self.norm_block(src, tok0, nt, m, self.s1, 0, bufs, ('xt', 'sq', 'hT', 'rstd', 'ntmp', 'PS:pss'))
                if 'dbg_h' in self.dbg:
                    hf = bufs['xt']
                    P.op('vector', lambda e: e.tensor_copy(out=hf[:, :, :nt], in_=bufs['hT'][:, :, :nt]), reads=hkeys + ['xt'], writes=['xt'])
                    P.dma('sync', dr['dbg_h'].rearrange("(kc p) t -> p kc t", p=128)[:, :, tok0:tok0 + nt], hf[:, :, :nt], reads=['xt'], writes=['dbg_h'])
                for p0 in range(0, len(ntile), PWT):
                    tiles = ntile[p0:p0 + PWT]
                    c0 = tiles[0][0]
                    cw = sum(t[1] for t in tiles)
                    w = wb[pi % 2]
                    wk = f'wb{pi % 2}'
                    pi += 1
                    P.dma('sync', w[:, :, :cw], wv[:, :, c0:c0 + cw], reads=[('WB', r) for r in range(0, D, 256)], writes=[wk])
                    for (tc0, tw) in tiles:
                        pa = pacc[it % 4]
                        pk = f'PS:pacc{it % 4}'
                        st = stg[it % 3]
                        sk = f'stg{it % 3}'
                        for kc in range(KC):
                            P.op('tensor', lambda e, kc=kc, pa=pa, w=w, tc0=tc0, tw=tw, c0=c0: e.matmul(
                                pa[:tw, :nt], lhsT=w[:, kc, tc0 - c0:tc0 - c0 + tw], rhs=bufs['hT'][:, kc, :nt],
                                start=(kc == 0), stop=(kc == KC - 1)), reads=[wk, hkeys[kc]], writes=[pk])
                        eng = 'scalar' if it % 2 == 0 else 'vector'
                        if eng == 'scalar':
                            P.op('scalar', lambda e, pa=pa, st=st, tw=tw: e.copy(out=st[:tw, :nt], in_=pa[:tw, :nt]), reads=[pk], writes=[sk])
                        else:
                            P.op('vector', lambda e, pa=pa, st=st, tw=tw: e.tensor_copy(out=st[:tw, :nt], in_=pa[:tw, :nt]), reads=[pk], writes=[sk])
                        P.dma('gpsimd', dr['PT'][tc0:tc0 + tw, tok0:tok0 + nt], st[:tw, :nt], reads=[sk], writes=[('PT', tc0, tok0)])
                        it += 1
            P.barrier()


def _ssd_methods():
    pass


def build_program(stage=99, dbg=()):
    b = B(stage, dbg)
    for name in dbg:
        pass
    return b


def make_inputs_small(inputs, b):
    f = np.float32
    x = np.asarray(inputs['x'], f)
    ctx = np.asarray(inputs['ctx'], f)
    c = np.asarray(inputs['c'], f)
    c_ctx = np.asarray(inputs['c_ctx'], f)
    xT = np.ascontiguousarray(np.concatenate([ctx[b], x[b]], axis=0).T)
    cc = np.stack([c[b], c_ctx], axis=0)
    ccT = np.ascontiguousarray(cc.reshape(2, KC, 128).transpose(2, 1, 0))
    return {"xT": xT, "ccT": ccT}


def make_inputs(inputs, b):
    f = np.float32
    x = np.asarray(inputs['x'], f)
    ctx = np.asarray(inputs['ctx'], f)
    c = np.asarray(inputs['c'], f)
    c_ctx = np.asarray(inputs['c_ctx'], f)
    xT = np.ascontiguousarray(np.concatenate([ctx[b], x[b]], axis=0).T)
    cc = np.stack([c[b], c_ctx], axis=0)
    ccT = np.ascontiguousarray(cc.reshape(2, KC, 128).transpose(2, 1, 0))
    mod_b = np.asarray(inputs['mod_b'], f)
    m = {
        "xT": xT, "ccT": ccT,
        "mod_w": np.ascontiguousarray(np.asarray(inputs['mod_w'], f)),
        "mod_bT": np.ascontiguousarray(mod_b.reshape(2, 96, 128).transpose(0, 2, 1)),
        "nmwT": np.ascontiguousarray(np.asarray(inputs['norm_mix_w'], f).reshape(2, KC, 128).transpose(0, 2, 1)),
        "nfwT": np.ascontiguousarray(np.asarray(inputs['norm_ffn_w'], f).reshape(2, KC, 128).transpose(0, 2, 1)),
        "ev_in_w": np.ascontiguousarray(np.asarray(inputs['ev_in_w'], f)[0]),
        "od_in_w": np.ascontiguousarray(np.asarray(inputs['od_in_w'], f)[0]),
    }
    m["ev_out_w"] = np.ascontiguousarray(np.asarray(inputs['ev_out_w'], f)[0])
    m["od_out_w"] = np.ascontiguousarray(np.asarray(inputs['od_out_w'], f)[0])
    m["ffn_up_w"] = np.ascontiguousarray(np.asarray(inputs['ffn_up_w'], f))
    m["ffn_down_w"] = np.ascontiguousarray(np.asarray(inputs['ffn_down_w'], f))
    fcw = np.asarray(inputs['ffn_conv_w'], f)
    fcw = np.concatenate([fcw.reshape(2, 9, FFN), np.zeros((2, 9, 43 * 128 - FFN), f)], axis=2)
    m["ffn_cw"] = np.ascontiguousarray(fcw.reshape(2, 9, 43, 128).transpose(0, 3, 2, 1))
    m["fnwT"] = np.ascontiguousarray(np.asarray(inputs['final_norm_w'], f).reshape(KC, 128).T)
    m["gdn_dtb"] = np.ascontiguousarray(np.asarray(inputs['gdn_dt_bias'], f)[0].reshape(16, 1))
    m["gdn_alog"] = np.ascontiguousarray(np.asarray(inputs['gdn_a_log'], f)[0].reshape(16, 1))
    gcw = np.asarray(inputs['gdn_conv_w'], f)[0]
    m["gdn_cw"] = np.ascontiguousarray(gcw.reshape(3, 24, 128).transpose(2, 1, 0))
    m["gdn_nw"] = np.ascontiguousarray(np.asarray(inputs['gdn_norm_w'], f)[0].reshape(1, 128))
    m["ml_ib"] = np.ascontiguousarray(np.asarray(inputs['mlstm_igate_b'], f)[0].reshape(8, 1))
    m["ml_fb"] = np.ascontiguousarray(np.asarray(inputs['mlstm_fgate_b'], f)[0].reshape(8, 1))
    m["ml_nw"] = np.ascontiguousarray(np.asarray(inputs['mlstm_norm_w'], f)[0].reshape(1, 1024))
    def pair(x):
        sh = x.shape[3:]
        x = x.reshape((2, 16, 2, 64) + sh)
        x = np.moveaxis(x, (2, 3), (0, 1))
        return np.ascontiguousarray(x.reshape((128, 32) + sh))
    m["s5_lre"] = pair(np.asarray(inputs['s5_lam_re'], f)[0])
    m["s5_lim"] = pair(np.asarray(inputs['s5_lam_im'], f)[0])
    m["s5_dlt"] = pair(np.repeat(np.asarray(inputs['s5_log_step'], f)[0][:, :, None], 64, axis=2))
    m["s5_bre"] = pair(np.asarray(inputs['s5_b_re'], f)[0])
    m["s5_bim"] = pair(np.asarray(inputs['s5_b_im'], f)[0])
    m["s5_cre"] = pair(np.asarray(inputs['s5_c_re'], f)[0].transpose(0, 1, 3, 2))
    m["s5_cim"] = pair(np.asarray(inputs['s5_c_im'], f)[0].transpose(0, 1, 3, 2))
    m["s5_dsk"] = np.ascontiguousarray(np.asarray(inputs['s5_d'], f)[0].reshape(4, 128).T)
    m["s5_glu_w"] = np.ascontiguousarray(np.asarray(inputs['s5_glu_w'], f)[0])
    cw = np.asarray(inputs['ssd_conv_w'], f)[0]
    m["ssd_cw"] = np.ascontiguousarray(cw.reshape(3, 20, 128).transpose(2, 1, 0))
    m["ssd_cb"] = np.ascontiguousarray(np.asarray(inputs['ssd_conv_b'], f)[0].reshape(20, 128).T)
    m["ssd_dtb"] = np.ascontiguousarray(np.asarray(inputs['ssd_dt_bias'], f)[0].reshape(48, 1))
    m["ssd_alog"] = np.ascontiguousarray(np.asarray(inputs['ssd_a_log'], f)[0].reshape(48, 1))
    m["ssd_d"] = np.ascontiguousarray(np.asarray(inputs['ssd_d'], f)[0].reshape(1, 24))
    m["ssd_nw"] = np.ascontiguousarray(np.asarray(inputs['ssd_norm_w'], f)[0].reshape(12, 128).T)
    return m


def kernel(**inputs):
    b = B()
    nc = b.build()
    n = 8
    shared = make_inputs(inputs, 0)
    in_maps = []
    for i in range(n):
        mi = dict(shared)
        if i % 4 != 0:
            pi = make_inputs_small(inputs, i % 4)
            mi.update(pi)
        in_maps.append(mi)
    res = run_bass_kernel_spmd(nc, in_maps, core_ids=list(range(n)))
    out = np.stack([np.ascontiguousarray(res.results[i]["outT"].T) for i in range(4)], axis=0)
    return out.astype(np.float32)


def chunk_order(d):
    if d == 0:
        return list(range(34))
    return [1, 0] + list(range(33, 1, -1))


def phase_ssd(self):
    import os
    P, nc, dr = self.P, self.nc, self.dr
    sb, ps = self.sb, self.ps
    PT = dr['PT']
    with ExitStack() as es0:
        dt_tok = sb(es0, "dt_tok", [128, 34, 48], F32)
        la_tok = sb(es0, "la_tok", [128, 34, 48], F32)
        with ExitStack() as es:
            dtr = sb(es, "dtr", [128, T], F32)
            laT = sb(es, "laT", [128, T], F32)
            P.op('gpsimd', lambda e: e.memset(dtr[:], 0.0), writes=['dtr'])
            P.op('gpsimd', lambda e: e.memset(laT[:], 0.0), writes=['laT'])
            dtb = sb(es, "dtb", [48, 1], F32)
            alog = sb(es, "alog", [48, 1], F32)
            nega = sb(es, "nega", [48, 1], F32)
            ptr = ps(es, "ptr_g", [128, 512], F32)
            if os.environ.get('SKIP12') or os.environ.get('DTRMEM'):
                P.op('gpsimd', lambda e: e.memset(dtr[:48, :], 0.5), writes=['dtr'])
            else:
                P.dma('sync', dtr[:48, :], PT[4608:4656, :], writes=['dtr'])
            P.dma('sync', dtb[:], dr['ssd_dtb'], writes=['dtb'])
            P.dma('sync', alog[:], dr['ssd_alog'], writes=['alog'])
            P.op('scalar', lambda e: e.activation(out=nega[:], in_=alog[:], func=AF.Exp), reads=['alog'], writes=['nega'])
            P.op('vector', lambda e: e.tensor_scalar(out=nega[:], in0=nega[:], scalar1=-1.0, scalar2=None, op0=ALU.mult), reads=['nega'], writes=['nega'])
            P.op('scalar', lambda e: e.activation(out=dtr[:48, :], in_=dtr[:48, :], func=AF.Exp, bias=dtb[:]), reads=['dtr', 'dtb'], writes=['dtr'])
            P.op('scalar', lambda e: e.activation(out=dtr[:48, :], in_=dtr[:48, :], func=AF.Ln, bias=self.one_col[:48, :]), reads=['dtr', 'one_col'], writes=['dtr'])
            P.op('vector', lambda e: e.tensor_scalar(out=laT[:48, :], in0=dtr[:48, :], scalar1=nega[:], scalar2=None, op0=ALU.mult), reads=['dtr', 'nega'], writes=['laT'])
            import os
            CUT = int(os.environ.get('CUT', '99'))
            if CUT <= 2:
                P.op('gpsimd', lambda e: e.memset(dt_tok[:], 0.05), writes=['dt_tok'])
                P.op('gpsimd', lambda e: e.memset(la_tok[:], -0.01), writes=['la_tok'])
            VV = os.environ.get('VV', 'ABCD')
            for c in range((34 if CUT > 3 else 1) if CUT > 2 else 0):
                if 'A' in VV:
                    P.op('tensor', lambda e, c=c: e.matmul(ptr[:, 0:48], lhsT=dtr[:, c * 128:(c + 1) * 128], rhs=self.ident_f[:, :48], start=True, stop=True),
                         reads=['dtr', 'ident_f'], writes=['PS:ptr_g'])
                if 'B' in VV:
                    P.op('tensor', lambda e, c=c: e.matmul(ptr[:, 64:112], lhsT=laT[:, c * 128:(c + 1) * 128], rhs=self.ident_f[:, :48], start=True, stop=True),
                         reads=['laT', 'ident_f'], writes=['PS:ptr_g'])
                if 'C' in VV:
                    P.op('vector', lambda e, c=c: e.tensor_copy(out=dt_tok[:, c, :], in_=ptr[:, 0:48]), reads=['PS:ptr_g'], writes=['dt_tok'])
                if 'D' in VV:
                    P.op('scalar', lambda e, c=c: e.copy(out=la_tok[:, c, :], in_=ptr[:, 64:112]), reads=['PS:ptr_g'], writes=['la_tok'])
            P.barrier()
        if self.sub <= 1:
            return
        with ExitStack() as es:
            xin = [sb(es, f"xin{i}", [128, 20, 514], F32) for i in range(2)]
            cs = sb(es, "cs", [128, 20, 512], BF16)
            acc = [sb(es, f"cacc{i}", [128, 512], F32) for i in range(2)]
            cw = sb(es, "cw", [128, 20, 3], F32)
            cb = sb(es, "cb", [128, 20], F32)
            tokst = [sb(es, f"tokst{i}", [128, 2048], BF16) for i in range(2)]
            ptA = ps(es, "ptA", [128, 8, 128], BF16)
            ptB = ps(es, "ptB", [128, 8, 128], BF16)
            P.dma('sync', cw[:], dr['ssd_cw'], writes=['cw'])
            P.dma('sync', cb[:], dr['ssd_cb'], writes=['cb'])
            src = PT[2048:4608, :].rearrange("(f p) t -> p f t", p=128)
            ci = 0
            for bi, (tok0, nt, m) in enumerate(self.blocks()):
                x = xin[bi % 2]
                xk = f'xin{bi % 2}'
                seq0, seq1 = (0, TC) if m == 1 else (TC, T)
                lo = max(tok0 - 1, seq0)
                hi = min(tok0 + nt + 1, seq1)
                P.dma('sync', x[:, :, lo - (tok0 - 1):hi - (tok0 - 1)], src[:, :, lo:hi], writes=[xk])
                if lo > tok0 - 1:
                    P.op('gpsimd', lambda e, x=x: e.memset(x[:, :, 0:1], 0.0), writes=[xk])
                if hi < tok0 + nt + 1:
                    P.op('gpsimd', lambda e, x=x, nt=nt: e.memset(x[:, :, nt + 1:nt + 2], 0.0), writes=[xk])
                for f in range(20):
                    a = acc[f % 2]
                    ak = f'cacc{f % 2}'
                    P.op('vector', lambda e, x=x, a=a, f=f, nt=nt: e.tensor_scalar(out=a[:, :nt], in0=x[:, f, 0:nt], scalar1=cw[:, f, 0:1], scalar2=None, op0=ALU.mult),
                         reads=[xk, 'cw'], writes=[ak])
                    P.op('vector', lambda e, x=x, a=a, f=f, nt=nt: e.scalar_tensor_tensor(out=a[:, :nt], in0=x[:, f, 1:nt + 1], scalar=cw[:, f, 1:2], in1=a[:, :nt], op0=ALU.mult, op1=ALU.add),
                         reads=[xk, 'cw', ak], writes=[ak])
                    P.op('vector', lambda e, x=x, a=a, f=f, nt=nt: e.scalar_tensor_tensor(out=a[:, :nt], in0=x[:, f, 2:nt + 2], scalar=cw[:, f, 2:3], in1=a[:, :nt], op0=ALU.mult, op1=ALU.add),
                         reads=[xk, 'cw', ak], writes=[ak])
                    P.op('scalar', lambda e, a=a, f=f, nt=nt: e.activation(out=cs[:, f, :nt], in_=a[:, :nt], func=AF.Silu, bias=cb[:, f:f + 1]),
                         reads=[ak, 'cb'], writes=[('cs', f)])
                P.dma('gpsimd', dr['BCT'][0:1024, :].rearrange("(f p) t -> p f t", p=128)[:, :, tok0:tok0 + nt], cs[:, 12:20, :nt],
                      reads=[('cs', f) for f in range(12, 20)], writes=[('BCT', tok0)])
                for ch in range(nt // 128):
                    tk = tokst[ci % 2]
                    tkk = f'tokst{ci % 2}'
                    ci += 1
                    for f in range(16):
                        pt_ = ptA if f < 8 else ptB
                        P.op('tensor', lambda e, f=f, ch=ch, pt_=pt_: e.transpose(out=pt_[:, f % 8, :], in_=cs[:, f, ch * 128:(ch + 1) * 128], identity=self.ident_b[:]),
                             reads=[('cs', f), 'ident_b'], writes=['PS:ptA' if f < 8 else 'PS:ptB'])
                    P.op('vector', lambda e, tk=tk: e.tensor_copy(out=tk[:, 0:1024], in_=ptA[:].rearrange("p a b -> p (a b)")), reads=['PS:ptA'], writes=[tkk])
                    P.op('scalar', lambda e, tk=tk: e.copy(out=tk[:, 1024:2048], in_=ptB[:].rearrange("p a b -> p (a b)")), reads=['PS:ptB'], writes=[tkk])
                    P.dma('gpsimd', dr['TOK'][tok0 + ch * 128:tok0 + (ch + 1) * 128, :], tk[:], reads=[tkk], writes=[('TOK', tok0 + ch * 128)])
            P.barrier()
        if 'dbg_gates' in self.dbg:
            P.dma('sync', dr['dbg_gates'][0], dt_tok[:].rearrange("p a b -> p (a b)"), reads=[], writes=['dbg_gates'])
            P.dma('sync', dr['dbg_gates'][1], la_tok[:].rearrange("p a b -> p (a b)"), reads=[], writes=['dbg_gates'])
            P.barrier()
        if self.sub <= 2:
            return
        with ExitStack() as es:
            S_f = sb(es, "S_f", [128, 24, 64], F32)
            S_b = sb(es, "S_b", [128, 24, 64], BF16)
            tok = [sb(es, f"tok{i}", [128, 2048], BF16) for i in range(2)]
            bct = [sb(es, f"bct{i}", [128, 8, 128], BF16) for i in range(2)]
            Lm = [sb(es, f"Lm{i}", [128, 128], F32) for i in range(8)]
            DmAll = sb(es, "DmAll", [128, 2, 24, 128], F32)
            WT = [sb(es, f"WT{i}", [128, 128], BF16) for i in range(6)]
            v1 = [sb(es, f"v1_{i}", [128, 64], BF16) for i in range(6)]
            v2 = [sb(es, f"v2_{i}", [128, 64], BF16) for i in range(6)]
            yA = sb(es, "yA", [128, 384], F32)
            ysb = [sb(es, f"ysb{i}", [128, 1536], F32) for i in range(2)]
            yfl = sb(es, "yfl", [128, 1536], F32)
            ytk = sb(es, "ytk", [128, 1536], BF16)
            ecum = sb(es, "ecum", [128, 24], F32)
            cdec = sb(es, "cdec", [128, 24], F32)
            dB = sb(es, "dB", [128, 24], F32)
            nw = sb(es, "ssd_nw_sb", [128, 12], F32)
            ytb = sb(es, "ytb", [128, 12, 512], BF16)
            zT = sb(es, "zT", [128, 12, 512], F32)
            yg = sb(es, "yg", [128, 12, 512], F32)
            sqg = sb(es, "sqg", [128, 12, 512], BF16)
            rstd = sb(es, "rstd_g", [128, 512], F32)
            ot = sb(es, "ot", [128, 12, 512], BF16)
            psegA = ps(es, "psegA", [128, 4, 128], F32)
            psegB = ps(es, "psegB", [128, 4, 128], F32)
            pyA = ps(es, "pyA", [128, 512], F32)
            pyB = ps(es, "pyB", [128, 512], F32)
            pS = ps(es, "pS", [128, 512], F32)
            ptT = ps(es, "ptT", [128, 8, 128], BF16)
            ptU = ps(es, "ptU", [128, 8, 128], BF16)
            pss = ps(es, "pss_g", [128, 512], F32)
            P.dma('sync', dB[:], dr['ssd_d'].partition_broadcast(128), writes=['dB'])
            P.dma('sync', nw[:], dr['ssd_nw'], writes=['ssd_nw'])
            seg = lambda r: (psegA[:, r, :] if r < 4 else psegB[:, r - 4, :])
            segk = lambda r: ('PS:psegA' if r < 4 else 'PS:psegB')
            li = 0
            for d in range(2):
                MASK = self.MGT if d == 0 else self.MLT
                TRI = self.TLE if d == 0 else self.TGE
                mk, tk_ = ('MGT', 'TLE') if d == 0 else ('MLT', 'TGE')
                ecol = 127 if d == 0 else 0
                P.op('gpsimd', lambda e: e.memset(S_f[:], 0.0), writes=[('S_f', h) for h in range(24)])
                P.op('gpsimd', lambda e: e.memset(S_b[:], 0.0), writes=[('S_b', h) for h in range(24)])
                order = chunk_order(d)

                def stageA(c, par):
                    t0 = c * 128
                    P.dma('sync', tok[par][:], dr['TOK'][t0:t0 + 128, :], writes=[f'tok{par}'])
                    P.dma('sync', bct[par][:], dr['BCT'][0:1024, :].rearrange("(f p) t -> p f t", p=128)[:, :, t0:t0 + 128], writes=[f'bct{par}'])
                    for b in range(6):
                        bank, bk = (psegA, 'PS:psegA') if b % 2 == 0 else (psegB, 'PS:psegB')
                        for r4 in range(4):
                            h = b * 4 + r4
                            L = Lm[(b % 2) * 4 + r4]
                            Lk = f'Lm{(b % 2) * 4 + r4}'
                            P.op('vector', lambda e, L=L, h=h, c=c: e.tensor_scalar(out=L[:], in0=MASK[:], scalar1=la_tok[:, c, d * 24 + h:d * 24 + h + 1], scalar2=None, op0=ALU.mult),
                                 reads=[mk, 'la_tok'], writes=[Lk])
                        for r4 in range(4):
                            L = Lm[(b % 2) * 4 + r4]
                            Lk = f'Lm{(b % 2) * 4 + r4}'
                            P.op('tensor', lambda e, L=L, bank=bank, r4=r4: e.matmul(bank[:, r4, :], lhsT=L[:], rhs=TRI[:], start=True, stop=False), reads=[Lk, tk_], writes=[bk])
                            P.op('tensor', lambda e, bank=bank, r4=r4: e.matmul(bank[:, r4, :], lhsT=self.negI[:], rhs=MASK[:], start=False, stop=True), reads=['negI', mk], writes=[bk])
                        for r4 in range(4):
                            h = b * 4 + r4
                            P.op('scalar', lambda e, bank=bank, r4=r4, h=h, par=par: e.activation(out=DmAll[:, par, h, :], in_=bank[:, r4, :], func=AF.Exp), reads=[bk], writes=[('DmAll', par, h)])

                stageA(order[0], li % 2)
                for oi, c in enumerate(order):
                    t0 = c * 128
                    par = li % 2
                    tb = tok[par]
                    tbk = f'tok{par}'
                    bc = bct[par]
                    bck = f'bct{par}'
                    ys = ysb[par]
                    ysk = f'ysb{par}'
                    li += 1
                    if oi + 1 < len(order):
                        stageA(order[oi + 1], li % 2)
                    lac = la_tok[:, c, d * 24:(d + 1) * 24]
                    P.op('tensor', lambda e, lac=lac: e.matmul(pss[:, 128:152], lhsT=TRI[:], rhs=lac, start=True, stop=True),
                         reads=[tk_, 'la_tok'], writes=['PS:pss_g'])
                    P.op('tensor', lambda e, lac=lac: e.matmul(pss[:, 160:184], lhsT=self.ones_f[:], rhs=lac, start=True, stop=True),
                         reads=['ones_f', 'la_tok'], writes=['PS:pss_g'])
                    P.op('scalar', lambda e: e.activation(out=ecum[:], in_=pss[:, 128:152], func=AF.Exp), reads=['PS:pss_g'], writes=['ecum'])
                    P.op('scalar', lambda e: e.activation(out=cdec[:], in_=pss[:, 160:184], func=AF.Exp), reads=['PS:pss_g'], writes=['cdec'])
                    for g in range(4):
                        hs = [g * 6 + r for r in range(6)]
                        P.op('tensor', lambda e, g=g, bc=bc: e.matmul(pss[:, 0:128], lhsT=bc[:, g, :], rhs=bc[:, 4 + g, :], start=True, stop=True),
                             reads=[bck], writes=['PS:pss_g'])
                        for r, h in enumerate(hs):
                            P.op('vector', lambda e, r=r, h=h, par=par: e.tensor_tensor(out=WT[r][:], in0=DmAll[:, par, h, :], in1=pss[:, 0:128], op=ALU.mult),
                                 reads=[('DmAll', par, h), 'PS:pss_g'], writes=[f'WT{r}'])
                            P.op('gpsimd', lambda e, r=r, h=h, c=c, tb=tb: e.tensor_scalar(out=v1[r][:], in0=tb[:, h * 64:(h + 1) * 64], scalar1=dt_tok[:, c, d * 24 + h:d * 24 + h + 1], scalar2=None, op0=ALU.mult),
                                 reads=[tbk, 'dt_tok'], writes=[f'v1_{r}'])
                            P.op('gpsimd', lambda e, r=r, h=h, par=par: e.tensor_scalar(out=v2[r][:], in0=v1[r][:], scalar1=DmAll[:, par, h, ecol:ecol + 1], scalar2=None, op0=ALU.mult),
                                 reads=[f'v1_{r}', ('DmAll', par, h)], writes=[f'v2_{r}'])
                        for r, h in enumerate(hs):
                            P.op('tensor', lambda e, r=r: e.matmul(pyA[:, r * 64:(r + 1) * 64], lhsT=WT[r][:], rhs=v1[r][:], start=True, stop=True),
                                 reads=[f'WT{r}', f'v1_{r}'], writes=['PS:pyA'])
                            P.op('tensor', lambda e, r=r, h=h, g=g, bc=bc: e.matmul(pyB[:, r * 64:(r + 1) * 64], lhsT=bc[:, 4 + g, :], rhs=S_b[:, h, :], start=True, stop=True),
                                 reads=[bck, ('S_b', h)], writes=['PS:pyB'])
                            P.op('tensor', lambda e, r=r, g=g, tb=tb: e.matmul(pS[:, r * 64:(r + 1) * 64], lhsT=tb[:, 1536 + g * 128:1536 + (g + 1) * 128], rhs=v2[r][:], start=True, stop=True),
                                 reads=[tbk, f'v2_{r}'], writes=['PS:pS'])
                        P.op('scalar', lambda e: e.copy(out=yA[:], in_=pyA[:, 0:384]), reads=['PS:pyA'], writes=['yA'])
                        for r, h in enumerate(hs):
                            P.op('vector', lambda e, r=r, h=h, ys=ys: e.scalar_tensor_tensor(out=ys[:, h * 64:(h + 1) * 64], in0=pyB[:, r * 64:(r + 1) * 64], scalar=ecum[:, h:h + 1], in1=yA[:, r * 64:(r + 1) * 64], op0=ALU.mult, op1=ALU.add),
                                 reads=['PS:pyB', 'ecum', 'yA'], writes=[ysk])
                            P.op('vector', lambda e, r=r, h=h: e.scalar_tensor_tensor(out=S_f[:, h, :], in0=S_f[:, h, :], scalar=cdec[:, h:h + 1], in1=pS[:, r * 64:(r + 1) * 64], op0=ALU.mult, op1=ALU.add),
                                 reads=[('S_f', h), 'cdec', 'PS:pS'], writes=[('S_f', h)])
                            P.op('scalar', lambda e, h=h: e.copy(out=S_b[:, h, :], in_=S_f[:, h, :]), reads=[('S_f', h)], writes=[('S_b', h)])
                    if d == 0:
                        P.dma('gpsimd', dr['YF'][t0:t0 + 128, 0:1536], ys[:], reads=[ysk], writes=[('YF', c)])
                    else:
                        P.dma('sync', yfl[:], dr['YF'][t0:t0 + 128, 0:1536], reads=[('YF', c)], writes=['yfl'])
                        P.op('vector', lambda e, ys=ys: e.tensor_tensor(out=yfl[:], in0=yfl[:], in1=ys[:], op=ALU.add), reads=['yfl', ysk], writes=['yfl'])
                        for h in range(24):
                            P.op('vector', lambda e, h=h, tb=tb: e.scalar_tensor_tensor(out=ytk[:, h * 64:(h + 1) * 64], in0=tb[:, h * 64:(h + 1) * 64], scalar=dB[:, h:h + 1], in1=yfl[:, h * 64:(h + 1) * 64], op0=ALU.mult, op1=ALU.add),
                                 reads=[tbk, 'dB', 'yfl'], writes=['ytk'])
                        if c < 2:
                            tok0, nt, ch = 0, TC, c
                        else:
                            tok0, nt, ch = TC + ((c - 2) // 4) * 512, 512, (c - 2) % 4
                        for f in range(12):
                            pt_ = ptT if f < 8 else ptU
                            P.op('tensor', lambda e, f=f, pt_=pt_: e.transpose(out=pt_[:, f % 8, :], in_=ytk[:, f * 128:(f + 1) * 128], identity=self.ident_b[:]),
                                 reads=['ytk', 'ident_b'], writes=['PS:ptT' if f < 8 else 'PS:ptU'])
                        P.op('vector', lambda e, ch=ch: e.tensor_copy(out=ytb[:, 0:8, ch * 128:(ch + 1) * 128], in_=ptT[:]), reads=['PS:ptT'], writes=['ytb'])
                        P.op('scalar', lambda e, ch=ch: e.copy(out=ytb[:, 8:12, ch * 128:(ch + 1) * 128], in_=ptU[:, 0:4, :]), reads=['PS:ptU'], writes=['ytb'])
                        if ch == 0:
                            P.dma('sync', zT[:, :, :nt], PT[512:2048, :].rearrange("(f p) t -> p f t", p=128)[:, :, tok0:tok0 + nt], writes=['zT'])
                            P.op('scalar', lambda e, nt=nt: e.activation(out=zT[:, :, :nt], in_=zT[:, :, :nt], func=AF.Silu), reads=['zT'], writes=['zT'])
                            P.op('vector', lambda e, nt=nt: e.tensor_tensor(out=yg[:, :, :nt], in0=ytb[:, :, :nt], in1=zT[:, :, :nt], op=ALU.mult), reads=['ytb', 'zT'], writes=['yg'])
                            P.op('scalar', lambda e, nt=nt: e.activation(out=sqg[:, :, :nt], in_=yg[:, :, :nt], func=AF.Square), reads=['yg'], writes=['sqg'])
                            for gq in range(4):
                                for i3 in range(3):
                                    P.op('tensor', lambda e, gq=gq, i3=i3, nt=nt: e.matmul(pss[:, :nt], lhsT=self.ones_b[:], rhs=sqg[:, gq * 3 + i3, :nt], start=(i3 == 0), stop=(i3 == 2)),
                                         reads=['sqg', 'ones_b'], writes=['PS:pss_g'])
                                P.op('scalar', lambda e, nt=nt: e.activation(out=rstd[:, :nt], in_=pss[:, :nt], func=AF.Sqrt, scale=1.0 / 384, bias=self.eps_col[:]),
                                     reads=['PS:pss_g', 'eps_col'], writes=['rstd_g'])
                                P.op('vector', lambda e, nt=nt: e.reciprocal(out=rstd[:, :nt], in_=rstd[:, :nt]), reads=['rstd_g'], writes=['rstd_g'])
                                for i3 in range(3):
                                    f = gq * 3 + i3
                                    P.op('vector', lambda e, f=f, nt=nt: e.scalar_tensor_tensor(out=ot[:, f, :nt], in0=yg[:, f, :nt], scalar=nw[:, f:f + 1], in1=rstd[:, :nt], op0=ALU.mult, op1=ALU.mult),
                                         reads=['yg', 'ssd_nw', 'rstd_g'], writes=['ot'])
                            P.dma('gpsimd', dr['YT'][512:2048, :].rearrange("(f p) t -> p f t", p=128)[:, :, tok0:tok0 + nt], ot[:, :, :nt], reads=['ot'], writes=[('YT', 'ssd', tok0)])
            P.barrier()


B.phase_ssd = phase_ssd


def norm_cols(self, xt, sq, hT, rstd, tmp, pss, c0, n, m, scale_t, shift_lo, tag):
    P = self.P
    kx, ksq, kr, kp = tag + 'xt', tag + 'sq', tag + 'rstd', 'PS:' + tag + 'pss'
    P.op('scalar', lambda e: e.activation(out=sq[:, :, c0:c0 + n], in_=xt[:, :, c0:c0 + n], func=AF.Square), reads=[kx], writes=[ksq])
    for kc in range(KC):
        P.op('tensor', lambda e, kc=kc: e.matmul(pss[:, :n], lhsT=self.ones_b[:], rhs=sq[:, kc, c0:c0 + n],
                                                 start=(kc == 0), stop=(kc == KC - 1)), reads=[ksq, 'ones_b'], writes=[kp])
    P.op('scalar', lambda e: e.activation(out=rstd[:, :n], in_=pss[:, :n], func=AF.Sqrt, scale=1.0 / D, bias=self.eps_col[:]),
         reads=[kp, 'eps_col'], writes=[kr])
    P.op('vector', lambda e: e.reciprocal(out=rstd[:, :n], in_=rstd[:, :n]), reads=[kr], writes=[kr])
    for kc in range(KC):
        tk = tag + 'ntmp' + str(kc % 2)
        t = tmp[kc % 2]
        P.op('vector', lambda e, kc=kc, t=t: e.tensor_tensor(out=t[:, :n], in0=xt[:, kc, c0:c0 + n], in1=rstd[:, :n], op=ALU.mult),
             reads=[kx, kr], writes=[tk])
        P.op('scalar', lambda e, kc=kc, t=t: e.activation(out=hT[:, kc, c0:c0 + n], in_=t[:, :n], func=AF.Identity,
                                                          scale=scale_t[:, kc, m:m + 1], bias=self.modT[:, shift_lo + kc, m:m + 1]),
             reads=[tk, 'modT', 's1', 's2'], writes=[tag + 'hT'])


def phase_outproj(self, layer, src, dst):
    P, nc, dr = self.P, self.nc, self.dr
    sb, ps = self.sb, self.ps
    wsrc = dr['ev_out_w'] if layer == 0 else dr['od_out_w']
    WO = dr['WO']
    for r in range(0, D, 512):
        P.dma('gpsimd', WO[r:r + 512, :], wsrc[r:r + 512, :], writes=[('WO', r)])
    blocks = self.blocks() if layer == 0 else self.blocks()[1:]
    with ExitStack() as es:
        W = sb(es, "wo_sb", [128, KC, D], BF16)
        xt = [sb(es, f"ox{i}", [128, KC, 512], F32) for i in range(2)]
        yt = [sb(es, f"oy{i}", [128, KC, 512], BF16) for i in range(2)]
        pacc = [ps(es, f"opacc{i}", [128, 512], F32) for i in range(4)]
        P.dma('sync', W[:], WO.rearrange("(kc p) n -> p kc n", p=128), reads=[('WO', r) for r in range(0, D, 512)], writes=['wo_sb'])
        it = 0
        for bi, (tok0, nt, m) in enumerate(blocks):
            x, xk = xt[bi % 2], f'ox{bi % 2}'
            y, yk = yt[bi % 2], f'oy{bi % 2}'
            P.dma('sync', x[:, :, :nt], src.rearrange("(kc p) t -> p kc t", p=128)[:, :, tok0:tok0 + nt], writes=[xk])
            P.dma('sync', y[:, :, :nt], dr['YT'].rearrange("(kc p) t -> p kc t", p=128)[:, :, tok0:tok0 + nt], writes=[yk])
            for d in range(KC):
                pa, pk = pacc[it % 4], f'PS:opacc{it % 4}'
                it += 1
                for kc in range(KC):
                    P.op('tensor', lambda e, kc=kc, d=d, pa=pa, y=y, nt=nt: e.matmul(pa[:, :nt], lhsT=W[:, kc, d * 128:(d + 1) * 128], rhs=y[:, kc, :nt],
                                                                                   start=(kc == 0), stop=(kc == KC - 1)), reads=['wo_sb', yk], writes=[pk])
                P.op('vector', lambda e, d=d, pa=pa, x=x, nt=nt, m=m: e.scalar_tensor_tensor(out=x[:, d, :nt], in0=pa[:, :nt], scalar=self.modT[:, 32 + d, m:m + 1], in1=x[:, d, :nt], op0=ALU.mult, op1=ALU.add),
                     reads=[pk, 'modT', xk], writes=[xk])
            P.dma('gpsimd', dst.rearrange("(kc p) t -> p kc t", p=128)[:, :, tok0:tok0 + nt], x[:, :, :nt], reads=[xk], writes=[('X1', tok0)])
        P.barrier()


def phase_ffn(self, layer, src, dst):
    P, nc, dr = self.P, self.nc, self.dr
    sb, ps = self.sb, self.ps
    WU, WD = dr['WU'], dr['WD']
    for r in range(0, D, 128):
        P.dma('gpsimd', WU[r:r + 128, :], dr['ffn_up_w'][layer][r:r + 128, :], writes=[('WU', r)])
    for r in range(0, FFN, 512):
        r1 = min(r + 512, FFN)
        P.dma('gpsimd', WD[r:r1, :], dr['ffn_down_w'][layer][r:r1, :], writes=[('WD', r)])
    wu_keys = [('WU', r) for r in range(0, D, 128)]
    wd_keys = [('WD', r) for r in range(0, FFN, 512)]
    NF = 43
    blocks = self.blocks() if layer == 0 else self.blocks()[1:]
    with ExitStack() as es0:
        gT = sb(es0, "gT", [128, NF, 512], BF16)
        xt = sb(es0, "fx", [128, KC, 640], F32)
        cwt = sb(es0, "fcw", [128, NF, 9], F32)
        P.dma('sync', cwt[:], dr['ffn_cw'][layer], writes=['fcw'])
        for bi, (tok0, nt, m) in enumerate(blocks):
            if m == 1:
                lo, hi, off = 0, TC, 0
            else:
                lo = max(tok0 - 64, TC)
                hi = min(tok0 + nt + 64, T)
                off = lo - (tok0 - 64)
            nw = hi - lo
            with ExitStack() as es:
                sq = sb(es, "fsq", [128, KC, 640], BF16)
                hT = sb(es, "fh", [128, KC, 640], BF16)
                rstd = sb(es, "frstd", [128, 512], F32)
                tmp = [sb(es, f"ftmp{i}", [128, 512], F32) for i in range(2)]
                wa = [sb(es, f"fwa{i}", [128, KC, 256], BF16) for i in range(2)]
                wv = [sb(es, f"fwv{i}", [128, KC, 256], BF16) for i in range(2)]
                asb = [sb(es, f"fasb{i}", [128, 10, 66], F32) for i in range(2)]
                acc = [sb(es, f"facc{i}", [128, 8, 64], F32) for i in range(2)]
                sg = [sb(es, f"fsg{i}", [128, 512], F32) for i in range(2)]
                pss = ps(es, "fpss", [128, 512], F32)
                pa0 = [ps(es, f"fpa0_{i}", [128, 512], F32) for i in range(2)]
                pa1 = [ps(es, f"fpa1_{i}", [128, 512], F32) for i in range(2)]
                pv = [ps(es, f"fpv{i}", [128, 512], F32) for i in range(2)]
                P.dma('sync', xt[:, :, off:off + nw], src.rearrange("(kc p) t -> p kc t", p=128)[:, :, lo:hi], writes=['fxt'])
                c = off
                while c < off + nw:
                    n = min(320, off + nw - c)
                    norm_cols(self, xt, sq, hT, rstd, tmp, pss, c, n, m, self.s2, 48, 'f')
                    c += n
                for i in range(2):
                    P.op('gpsimd', lambda e, i=i: e.memset(asb[i][:], 0.0), writes=[f'fasb{i}'])
                wuv = WU.rearrange("(kc p) n -> p kc n", p=128)
                for fp in range(0, NF, 2):
                    nf = min(2, NF - fp)
                    wi = (fp // 2) % 2
                    P.dma('sync', wa[wi][:, :, :nf * 128], wuv[:, :, fp * 128:(fp + nf) * 128], reads=wu_keys, writes=[f'fwa{wi}'])
                    P.dma('sync', wv[wi][:, :, :nf * 128], wuv[:, :, FFN + fp * 128:FFN + (fp + nf) * 128], reads=wu_keys, writes=[f'fwv{wi}'])
                    for ff in range(nf):
                        f = fp + ff
                        b2 = f % 2
                        A, Ak = asb[b2], f'fasb{b2}'
                        if m == 1:
                            P0, P0k = pa0[b2], f'PS:fpa0_{b2}'
                            for kc in range(KC):
                                P.op('tensor', lambda e, kc=kc, ff=ff, wi=wi, P0=P0: e.matmul(P0[:, :256], lhsT=wa[wi][:, kc, ff * 128:(ff + 1) * 128], rhs=hT[:, kc, 0:256], start=(kc == 0), stop=(kc == KC - 1)),
                                     reads=[f'fwa{wi}', 'fhT'], writes=[P0k])
                            Av = A[:].rearrange("p a b -> p (a b)")
                            P.op('scalar', lambda e, Av=Av, P0=P0: e.copy(out=Av[:, 1:257], in_=P0[:, :256]), reads=[P0k], writes=[Ak])
                            ac, ack = acc[b2][:].rearrange("p a b -> p (a b)"), f'facc{b2}'
                            P.op('vector', lambda e, Av=Av, ac=ac, f=f: e.tensor_scalar(out=ac[:, :256], in0=Av[:, 0:256], scalar1=cwt[:, f, 3:4], scalar2=None, op0=ALU.mult), reads=[Ak, 'fcw'], writes=[ack])
                            for dx in (1, 2):
                                P.op('vector', lambda e, Av=Av, ac=ac, f=f, dx=dx: e.scalar_tensor_tensor(out=ac[:, :256], in0=Av[:, dx:dx + 256], scalar=cwt[:, f, 3 + dx:4 + dx], in1=ac[:, :256], op0=ALU.mult, op1=ALU.add),
                                     reads=[Ak, 'fcw', ack], writes=[ack])
                            vlo = 0
                        else:
                            P0, P0k = pa0[b2], f'PS:fpa0_{b2}'
                            P1, P1k = pa1[b2], f'PS:fpa1_{b2}'
                            h0 = min(320, nw)
                            h1 = nw - h0
                            for kc in range(KC):
                                P.op('tensor', lambda e, kc=kc, ff=ff, wi=wi, P0=P0, h0=h0: e.matmul(P0[:, :h0], lhsT=wa[wi][:, kc, ff * 128:(ff + 1) * 128], rhs=hT[:, kc, off:off + h0], start=(kc == 0), stop=(kc == KC - 1)),
                                     reads=[f'fwa{wi}', 'fhT'], writes=[P0k])
                            for kc in range(KC):
                                P.op('tensor', lambda e, kc=kc, ff=ff, wi=wi, P1=P1, h0=h0, h1=h1: e.matmul(P1[:, :h1], lhsT=wa[wi][:, kc, ff * 128:(ff + 1) * 128], rhs=hT[:, kc, off + h0:off + h0 + h1], start=(kc == 0), stop=(kc == KC - 1)),
                                     reads=[f'fwa{wi}', 'fhT'], writes=[P1k])
                            r_off = off // 64
                            P.op('scalar', lambda e, A=A, P0=P0, h0=h0, r_off=r_off: e.copy(out=A[:, r_off:r_off + h0 // 64, 1:65], in_=P0[:, :h0].rearrange("p (a b) -> p a b", b=64)), reads=[P0k], writes=[Ak])
                            P.op('scalar', lambda e, A=A, P1=P1, h0=h0, h1=h1, r_off=r_off: e.copy(out=A[:, r_off + h0 // 64:r_off + (h0 + h1) // 64, 1:65], in_=P1[:, :h1].rearrange("p (a b) -> p a b", b=64)), reads=[P1k], writes=[Ak])
                            ac3, ack = acc[b2], f'facc{b2}'
                            first = True
                            for dy in range(3):
                                for dx in range(3):
                                    tap = dy * 3 + dx
                                    if first:
                                        P.op('vector', lambda e, A=A, ac3=ac3, f=f, dy=dy, dx=dx, tap=tap: e.tensor_scalar(out=ac3[:], in0=A[:, dy:dy + 8, dx:dx + 64], scalar1=cwt[:, f, tap:tap + 1], scalar2=None, op0=ALU.mult),
                                             reads=[Ak, 'fcw'], writes=[ack])
                                        first = False
                                    else:
                                        P.op('vector', lambda e, A=A, ac3=ac3, f=f, dy=dy, dx=dx, tap=tap: e.scalar_tensor_tensor(out=ac3[:], in0=A[:, dy:dy + 8, dx:dx + 64], scalar=cwt[:, f, tap:tap + 1], in1=ac3[:], op0=ALU.mult, op1=ALU.add),
                                             reads=[Ak, 'fcw', ack], writes=[ack])
                            ac = ac3[:].rearrange("p a b -> p (a b)")
                            vlo = 64
                        PV, PVk = pv[b2], f'PS:fpv{b2}'
                        for kc in range(KC):
                            P.op('tensor', lambda e, kc=kc, ff=ff, wi=wi, PV=PV, vlo=vlo, nt=nt: e.matmul(PV[:, :nt], lhsT=wv[wi][:, kc, ff * 128:(ff + 1) * 128], rhs=hT[:, kc, vlo:vlo + nt], start=(kc == 0), stop=(kc == KC - 1)),
                                 reads=[f'fwv{wi}', 'fhT'], writes=[PVk])
                        S, Sk = sg[b2], f'fsg{b2}'
                        P.op('scalar', lambda e, S=S, ac=ac, nt=nt: e.activation(out=S[:, :nt], in_=ac[:, :nt], func=AF.Silu), reads=[ack], writes=[Sk])
                        P.op('vector', lambda e, S=S, PV=PV, f=f, nt=nt: e.tensor_tensor(out=gT[:, f, :nt], in0=S[:, :nt], in1=PV[:, :nt], op=ALU.mult), reads=[Sk, PVk], writes=[('gT', f)])
                P.barrier()
            with ExitStack() as es:
                wd = [sb(es, f"fwd{i}", [128, 1024], BF16) for i in range(3)]
                pacc = [ps(es, f"fdacc{i}", [128, 512], F32) for i in range(8)]
                vlo = 0 if m == 1 else 64
                for half in range(2):
                    for f in range(NF):
                        w, wk = wd[f % 3], f'fwd{f % 3}'
                        P.dma('sync', w[:], WD[f * 128:(f + 1) * 128, half * 1024:(half + 1) * 1024], reads=wd_keys, writes=[wk])
                        for d in range(8):
                            P.op('tensor', lambda e, d=d, f=f, w=w, nt=nt: e.matmul(pacc[d][:, :nt], lhsT=w[:, d * 128:(d + 1) * 128], rhs=gT[:, f, :nt], start=(f == 0), stop=(f == NF - 1)),
                                 reads=[wk, ('gT', f)], writes=[f'PS:fdacc{d}'])
                    for d in range(8):
                        dd = half * 8 + d
                        P.op('vector', lambda e, d=d, dd=dd, nt=nt, m=m, vlo=vlo: e.scalar_tensor_tensor(out=xt[:, dd, vlo:vlo + nt], in0=pacc[d][:, :nt], scalar=self.modT[:, 80 + dd, m:m + 1], in1=xt[:, dd, vlo:vlo + nt], op0=ALU.mult, op1=ALU.add),
                             reads=[f'PS:fdacc{d}', 'modT', 'fxt'], writes=['fxt'])
                P.dma('gpsimd', dst.rearrange("(kc p) t -> p kc t", p=128)[:, :, tok0:tok0 + nt], xt[:, :, vlo:vlo + nt], reads=['fxt'], writes=[('X2', tok0)])
                P.barrier()


B.phase_outproj = phase_outproj
B.phase_ffn = phase_ffn


NCH = T // 8
HALF = NCH // 2
TWO_PI = 6.283185307179586
PI = 3.141592653589793


def reduce_angle(self, t, n, ki, kf, msk, tag):
    P = self.P
    v = 'vector'
    P.op(v, lambda e: e.tensor_scalar(out=ki[:, :n], in0=t, scalar1=1.0 / TWO_PI, scalar2=None, op0=ALU.mult), reads=[tag], writes=[tag + 'ki'])
    P.op(v, lambda e: e.tensor_copy(out=kf[:, :n], in_=ki[:, :n]), reads=[tag + 'ki'], writes=[tag + 'kf'])
    P.op(v, lambda e: e.scalar_tensor_tensor(out=t, in0=kf[:, :n], scalar=-TWO_PI, in1=t, op0=ALU.mult, op1=ALU.add), reads=[tag + 'kf', tag], writes=[tag])
    P.op(v, lambda e: e.tensor_single_scalar(out=msk[:, :n], in_=t, scalar=PI, op=ALU.is_gt), reads=[tag], writes=[tag + 'm'])
    P.op(v, lambda e: e.scalar_tensor_tensor(out=t, in0=msk[:, :n], scalar=-TWO_PI, in1=t, op0=ALU.mult, op1=ALU.add), reads=[tag + 'm', tag], writes=[tag])
    P.op(v, lambda e: e.tensor_single_scalar(out=msk[:, :n], in_=t, scalar=-PI, op=ALU.is_lt), reads=[tag], writes=[tag + 'm'])
    P.op(v, lambda e: e.scalar_tensor_tensor(out=t, in0=msk[:, :n], scalar=TWO_PI, in1=t, op0=ALU.mult, op1=ALU.add), reads=[tag + 'm', tag], writes=[tag])
    P.op(v, lambda e: e.tensor_scalar(out=t, in0=t, scalar1=PI, scalar2=-PI, op0=ALU.min, op1=ALU.max), reads=[tag], writes=[tag])


def cmul_cols(self, o_re, o_im, a_re, a_im, s_re, s_im, tmp, rk, wk):
    P = self.P
    v = 'vector'
    P.op(v, lambda e: e.tensor_scalar(out=tmp, in0=a_im, scalar1=s_im, scalar2=None, op0=ALU.mult), reads=rk, writes=[wk + 't'])
    P.op(v, lambda e: e.scalar_tensor_tensor(out=o_re, in0=a_re, scalar=s_re, in1=tmp, op0=ALU.mult, op1=ALU.subtract), reads=rk + [wk + 't'], writes=[wk + 're'])
    P.op(v, lambda e: e.tensor_scalar(out=tmp, in0=a_re, scalar1=s_im, scalar2=None, op0=ALU.mult), reads=rk + [wk + 're'], writes=[wk + 't'])
    P.op(v, lambda e: e.scalar_tensor_tensor(out=o_im, in0=a_im, scalar=s_re, in1=tmp, op0=ALU.mult, op1=ALU.add), reads=rk + [wk + 't'], writes=[wk + 'im'])


def phase_s5(self):
    P, nc, dr = self.P, self.nc, self.dr
    sb, ps = self.sb, self.ps
    PT = dr['PT']
    v, a, g_ = 'vector', 'scalar', 'gpsimd'
    with ExitStack() as es0:
        Sel = sb(es0, "Sel", [128, 8, 8, 128], BF16)
        bmask = sb(es0, "bmask", [128, 8, 16], F32)
        krow = sb(es0, "krow", [128, 16], F32)
        crow = sb(es0, "crow", [128, NCH + 1], F32)
        lre = sb(es0, "lre", [128, 32], F32)
        lim = sb(es0, "lim", [128, 32], F32)
        dlt = sb(es0, "dlt", [128, 32], F32)
        reD = sb(es0, "reD", [128, 32], F32)
        imD = sb(es0, "imD", [128, 32], F32)
        bre = sb(es0, "bre", [128, 32, 16], F32)
        bim = sb(es0, "bim", [128, 32, 16], F32)
        cre = sb(es0, "cre", [128, 32, 16], F32)
        cim = sb(es0, "cim", [128, 32, 16], F32)
        dsk = sb(es0, "dsk", [128, 4], F32)
        gy = sb(es0, "gy", [128, 4, T], BF16)
        uT = sb(es0, "uT", [128, 2, T], BF16)
        uF = sb(es0, "uF", [128, T], F32)
        ki = sb(es0, "ki", [128, NCH + 1], I32)
        kf = sb(es0, "kf", [128, NCH + 1], F32)
        msk = sb(es0, "msk", [128, NCH + 1], F32)
        P.op(g_, lambda e: e.memset(Sel[:], 0.0), writes=['Sel'])
        for a_ in range(8):
            for b_ in range(8):
                P.op(g_ if (a_ + b_) % 2 else v, lambda e, a_=a_, b_=b_: e.tensor_copy(out=Sel[:, a_, b_, b_ * 16:(b_ + 1) * 16], in_=self.ident_f[:, a_ * 16:(a_ + 1) * 16]),
                     reads=['ident_f', 'Sel'], writes=['Sel'])
        P.op(g_, lambda e: e.memset(bmask[:], 1.0), writes=['bmask'])
        P.op(g_, lambda e: e.affine_select(out=bmask[:], in_=bmask[:], pattern=[[16, 8], [0, 16]], base=15, channel_multiplier=-1, compare_op=ALU.is_ge, fill=0.0),
             reads=['bmask'], writes=['bmask'])
        P.op(g_, lambda e: e.iota(krow[:], pattern=[[1, 16]], base=-7, channel_multiplier=0, allow_small_or_imprecise_dtypes=True), writes=['krow'])
        P.op(g_, lambda e: e.iota(crow[:], pattern=[[1, NCH + 1]], base=0, channel_multiplier=0, allow_small_or_imprecise_dtypes=True), writes=['crow'])
        for nm, t_ in (('s5_lre', lre), ('s5_lim', lim), ('s5_dlt', dlt)):
            P.dma('sync', t_[:], dr[nm], writes=[nm])
        for nm, t_ in (('s5_bre', bre), ('s5_bim', bim), ('s5_cre', cre), ('s5_cim', cim)):
            P.dma('sync', t_[:], dr[nm], writes=[nm])
        P.dma('sync', dsk[:], dr['s5_dsk'], writes=['dsk'])
        P.op(a, lambda e: e.activation(out=dlt[:], in_=dlt[:], func=AF.Exp), reads=['s5_dlt'], writes=['s5_dlt'])
        P.op(v, lambda e: e.tensor_tensor(out=reD[:], in0=lre[:], in1=dlt[:], op=ALU.mult), reads=['s5_lre', 's5_dlt'], writes=['reD'])
        P.op(v, lambda e: e.tensor_tensor(out=imD[:], in0=lim[:], in1=dlt[:], op=ALU.mult), reads=['s5_lim', 's5_dlt'], writes=['imD'])
        self.cfre = sb(es0, "cfre", [128, 32], F32)
        self.cfim = sb(es0, "cfim", [128, 32], F32)
        with ExitStack() as es:
            mg = sb(es, "c_mg", [128, 32], F32)
            an = sb(es, "c_an", [128, 32], F32)
            an2 = sb(es, "c_an2", [128, 32], F32)
            nre = sb(es, "c_nre", [128, 32], F32)
            nim = sb(es, "c_nim", [128, 32], F32)
            den = sb(es, "c_den", [128, 32], F32)
            t1 = sb(es, "c_t1", [128, 32], F32)
            cfre, cfim = self.cfre, self.cfim
            P.op(a, lambda e: e.activation(out=mg[:], in_=reD[:], func=AF.Exp), reads=['reD'], writes=['c_mg'])
            P.op(v, lambda e: e.tensor_copy(out=an[:], in_=imD[:]), reads=['imD'], writes=['c_an'])
            reduce_angle(self, an[:], 32, ki, kf, msk, 'c_an')
            P.op(v, lambda e: e.tensor_scalar(out=an2[:], in0=imD[:], scalar1=PI / 2, scalar2=None, op0=ALU.add), reads=['imD'], writes=['c_an2'])
            reduce_angle(self, an2[:], 32, ki, kf, msk, 'c_an2')
            P.op(a, lambda e: e.activation(out=an[:], in_=an[:], func=AF.Sin), reads=['c_an'], writes=['c_an'])
            P.op(a, lambda e: e.activation(out=an2[:], in_=an2[:], func=AF.Sin), reads=['c_an2'], writes=['c_an2'])
            P.op(v, lambda e: e.tensor_tensor(out=nre[:], in0=mg[:], in1=an2[:], op=ALU.mult), reads=['c_mg', 'c_an2'], writes=['c_nre'])
            P.op(v, lambda e: e.tensor_scalar(out=nre[:], in0=nre[:], scalar1=-1.0, scalar2=None, op0=ALU.add), reads=['c_nre'], writes=['c_nre'])
            P.op(v, lambda e: e.tensor_tensor(out=nim[:], in0=mg[:], in1=an[:], op=ALU.mult), reads=['c_mg', 'c_an'], writes=['c_nim'])
            P.op(v, lambda e: e.tensor_tensor(out=den[:], in0=lre[:], in1=lre[:], op=ALU.mult), reads=['s5_lre'], writes=['c_den'])
            P.op(v, lambda e: e.tensor_tensor(out=t1[:], in0=lim[:], in1=lim[:], op=ALU.mult), reads=['s5_lim'], writes=['c_t1'])
            P.op(v, lambda e: e.tensor_tensor(out=den[:], in0=den[:], in1=t1[:], op=ALU.add), reads=['c_den', 'c_t1'], writes=['c_den'])
            P.op(v, lambda e: e.reciprocal(out=den[:], in_=den[:]), reads=['c_den'], writes=['c_den'])
            P.op(v, lambda e: e.tensor_tensor(out=cfre[:], in0=nre[:], in1=lre[:], op=ALU.mult), reads=['c_nre', 's5_lre'], writes=['cfre'])
            P.op(v, lambda e: e.tensor_tensor(out=t1[:], in0=nim[:], in1=lim[:], op=ALU.mult), reads=['c_nim', 's5_lim', 'c_den'], writes=['c_t1'])
            P.op(v, lambda e: e.tensor_tensor(out=cfre[:], in0=cfre[:], in1=t1[:], op=ALU.add), reads=['cfre', 'c_t1'], writes=['cfre'])
            P.op(v, lambda e: e.tensor_tensor(out=cfre[:], in0=cfre[:], in1=den[:], op=ALU.mult), reads=['cfre', 'c_den'], writes=['cfre'])
            P.op(v, lambda e: e.tensor_tensor(out=cfim[:], in0=nim[:], in1=lre[:], op=ALU.mult), reads=['c_nim', 's5_lre'], writes=['cfim'])
            P.op(v, lambda e: e.tensor_tensor(out=t1[:], in0=nre[:], in1=lim[:], op=ALU.mult), reads=['c_nre', 's5_lim', 'cfre'], writes=['c_t1'])
            P.op(v, lambda e: e.tensor_tensor(out=cfim[:], in0=cfim[:], in1=t1[:], op=ALU.subtract), reads=['cfim', 'c_t1'], writes=['cfim'])
            P.op(v, lambda e: e.tensor_tensor(out=cfim[:], in0=cfim[:], in1=den[:], op=ALU.mult), reads=['cfim', 'c_den'], writes=['cfim'])
            P.barrier()
        cfre, cfim = self.cfre, self.cfim
        with ExitStack() as es:
            mgk = sb(es, "mgk", [128, 16], F32)
            ank = sb(es, "ank", [128, 16], F32)
            ank2 = sb(es, "ank2", [128, 16], F32)
            LPre = sb(es, "LPre", [128, 16], F32)
            LPim = sb(es, "LPim", [128, 16], F32)
            bbre = sb(es, "bbre", [128, 16], F32)
            bbim = sb(es, "bbim", [128, 16], F32)
            ctmp = sb(es, "ctmp", [128, 128], F32)
            Bmre = sb(es, "Bmre", [128, 8, 16], F32)
            Bmim = sb(es, "Bmim", [128, 8, 16], F32)
            Cmre = sb(es, "Cmre", [128, 8, 16], F32)
            Cmim = sb(es, "Cmim", [128, 8, 16], F32)
            WiTre = sb(es, "WiTre", [128, 128], F32)
            WiTim = sb(es, "WiTim", [128, 128], F32)
            Wore = sb(es, "Wore", [128, 128], BF16)
            Woim = sb(es, "Woim", [128, 128], BF16)
            WoFre = sb(es, "WoFre", [128, 128], F32)
            WoFim = sb(es, "WoFim", [128, 128], F32)
            Tin = sb(es, "Tin", [128, 2, 128], BF16)
            Win = sb(es, "Win", [128, 2, 2, 64], BF16)
            th = sb(es, "th", [128, 1], F32)
            rr = sb(es, "rr", [128, 1], F32)
            rtab = sb(es, "rtab", [128, NCH], F32)
            ctab = sb(es, "ctab", [128, NCH + 1], F32)
            stab = sb(es, "stab", [128, NCH + 1], F32)
            Usb = sb(es, "Usb", [128, 2, 2, NCH], BF16)
            Sre = sb(es, "Sre", [128, NCH], F32)
            Sim = sb(es, "Sim", [128, NCH], F32)
            s2re = sb(es, "s2re", [128, NCH], F32)
            s2im = sb(es, "s2im", [128, NCH], F32)
            Zre = sb(es, "Zre", [128, NCH + 1], F32)
            Zim = sb(es, "Zim", [128, NCH + 1], F32)
            Xre = sb(es, "Xre", [128, NCH], BF16)
            Xim = sb(es, "Xim", [128, NCH], BF16)
            xt1 = sb(es, "xt1", [128, NCH], F32)
            Ysb = sb(es, "Ysb", [128, 2, 8, NCH], BF16)
            yT = sb(es, "yT5", [128, 2, T], F32)
            gl_w = sb(es, "gluw", [128, 4, 512], BF16)
            sgm = sb(es, "sgm", [128, 512], F32)
            og = sb(es, "og5", [128, 512], BF16)
            pU = [ps(es, f"pU{i}", [128, 512], F32) for i in range(2)]
            pSr = ps(es, "pSr", [128, 512], F32)
            pSi = ps(es, "pSi", [128, 512], F32)
            pY = [ps(es, f"pY{i}", [128, 512], F32) for i in range(2)]
            pB = ps(es, "pB5", [128, 512], F32)
            P.dma('gpsimd', gl_w[:], dr['s5_glu_w'].rearrange("(kt p) n -> p kt n", p=128), writes=['gluw'])
            P.op(g_, lambda e: e.memset(Zre[:, 0:1], 0.0), writes=['Zre'])
            P.op(g_, lambda e: e.memset(Zim[:, 0:1], 0.0), writes=['Zim'])
            hs = [(0, HALF), (HALF, NCH)]
            for ft in range(4):
                P.dma('sync', uF[:], PT[ft * 128:(ft + 1) * 128, :], writes=['uF'])
                P.op(a, lambda e: e.copy(out=uT[:, 0, :], in_=uF[:]), reads=['uF'], writes=['uT'])
                P.op(v, lambda e: e.tensor_copy(out=uT[:, 1, 0:TC], in_=uF[:, TC - 1::-1]), reads=['uF'], writes=['uT'])
                P.op(v, lambda e: e.tensor_copy(out=uT[:, 1, TC:T], in_=uF[:, T - 1:TC - 1:-1]), reads=['uF'], writes=['uT'])
                for gp4 in range(4):
                    gp = ft * 4 + gp4
                    for d in range(2):
                        for g2 in range(2):
                            gl = gp4 * 2 + g2
                            for hi_, (c0, c1) in enumerate(hs):
                                for j in range(8):
                                    P.op('tensor', lambda e, d=d, gl=gl, j=j, c0=c0, c1=c1, hi_=hi_: e.matmul(
                                        pU[hi_][:, :c1 - c0], lhsT=Sel[:, gl, j, :], rhs=uT[:, d, c0 * 8 + j:c1 * 8:8], start=(j == 0), stop=(j == 7)),
                                        reads=['Sel', 'uT'], writes=[f'PS:pU{hi_}'])
                                eng = a if hi_ == 0 else v
                                if eng == a:
                                    P.op(a, lambda e, d=d, g2=g2, c0=c0, c1=c1, hi_=hi_: e.copy(out=Usb[:, d, g2, c0:c1], in_=pU[hi_][:, :c1 - c0]), reads=[f'PS:pU{hi_}'], writes=[('Usb', d, g2)])
                                else:
                                    P.op(v, lambda e, d=d, g2=g2, c0=c0, c1=c1, hi_=hi_: e.tensor_copy(out=Usb[:, d, g2, c0:c1], in_=pU[hi_][:, :c1 - c0]), reads=[f'PS:pU{hi_}'], writes=[('Usb', d, g2)])
                    for d in range(2):
                        col = d * 16 + gp
                        P.op(v, lambda e, col=col: e.tensor_scalar(out=mgk[:], in0=krow[:], scalar1=reD[:, col:col + 1], scalar2=None, op0=ALU.mult), reads=['krow', 'reD'], writes=['mgk'])
                        P.op(a, lambda e: e.activation(out=mgk[:], in_=mgk[:], func=AF.Exp), reads=['mgk'], writes=['mgk'])
                        P.op(v, lambda e, col=col: e.tensor_scalar(out=ank[:], in0=krow[:], scalar1=imD[:, col:col + 1], scalar2=None, op0=ALU.mult), reads=['krow', 'imD'], writes=['ank'])
                        P.op(v, lambda e: e.tensor_scalar(out=ank2[:], in0=ank[:], scalar1=PI / 2, scalar2=None, op0=ALU.add), reads=['ank'], writes=['ank2'])
                        reduce_angle(self, ank[:], 16, ki, kf, msk, 'ank')
                        reduce_angle(self, ank2[:], 16, ki, kf, msk, 'ank2')
                        P.op(a, lambda e: e.activation(out=ank[:], in_=ank[:], func=AF.Sin), reads=['ank'], writes=['ank'])
                        P.op(a, lambda e: e.activation(out=ank2[:], in_=ank2[:], func=AF.Sin), reads=['ank2'], writes=['ank2'])
                        P.op(v, lambda e: e.tensor_tensor(out=LPre[:], in0=mgk[:], in1=ank2[:], op=ALU.mult), reads=['mgk', 'ank2'], writes=['LPre'])
                        P.op(v, lambda e: e.tensor_tensor(out=LPim[:], in0=mgk[:], in1=ank[:], op=ALU.mult), reads=['mgk', 'ank'], writes=['LPim'])
                        LP = ['LPre', 'LPim']
                        cmul_cols(self, bbre[:], bbim[:], bre[:, col, :], bim[:, col, :], cfre[:, col:col + 1], cfim[:, col:col + 1], ctmp[:, 0:16], ['s5_bre', 's5_bim', 'cfre', 'cfim'], 'bb')
                        for j in range(8):
                            kk = 7 - j
                            cmul_cols(self, Bmre[:, j, :], Bmim[:, j, :], bbre[:], bbim[:], LPre[:, kk:kk + 1], LPim[:, kk:kk + 1], ctmp[:, 0:16], ['bbre', 'bbim'] + LP, 'Bm')
                            kk = 7 + j
                            cmul_cols(self, Cmre[:, j, :], Cmim[:, j, :], cre[:, col, :], cim[:, col, :], LPre[:, kk:kk + 1], LPim[:, kk:kk + 1], ctmp[:, 0:16], ['s5_cre', 's5_cim'] + LP, 'Cm')
                        P.op(v, lambda e: e.tensor_scalar(out=Cmim[:], in0=Cmim[:], scalar1=-1.0, scalar2=None, op0=ALU.mult), reads=['Cmim'], writes=['Cmim'])
                        B2re, B2im = Bmre[:].rearrange("p a b -> p (a b)"), Bmim[:].rearrange("p a b -> p (a b)")
                        C2re, C2im = Cmre[:].rearrange("p a b -> p (a b)"), Cmim[:].rearrange("p a b -> p (a b)")
                        cmul_cols(self, WiTre[:], WiTim[:], B2re, B2im, LPre[:, 14:15], LPim[:, 14:15], ctmp[:], ['Bmre', 'Bmim'] + LP, 'WiT')
                        P.op(v, lambda e: e.tensor_scalar(out=ctmp[:], in0=C2im, scalar1=LPim[:, 8:9], scalar2=None, op0=ALU.mult), reads=['Cmim', 'LPim'], writes=['Wot'])
                        P.op(v, lambda e: e.scalar_tensor_tensor(out=WoFre[:], in0=C2re, scalar=LPre[:, 8:9], in1=ctmp[:], op0=ALU.mult, op1=ALU.add), reads=['Cmre', 'LPre', 'Wot'], writes=['WoFre'])
                        P.op(v, lambda e: e.tensor_scalar(out=ctmp[:], in0=C2re, scalar1=LPim[:, 8:9], scalar2=None, op0=ALU.mult), reads=['Cmre', 'LPim', 'WoFre'], writes=['Wot'])
                        P.op(v, lambda e: e.scalar_tensor_tensor(out=WoFim[:], in0=C2im, scalar=LPre[:, 8:9], in1=ctmp[:], op0=ALU.mult, op1=ALU.subtract), reads=['Cmim', 'LPre', 'Wot'], writes=['WoFim'])
                        P.op(a, lambda e: e.copy(out=Wore[:], in_=WoFre[:]), reads=['WoFre'], writes=['Wore'])
                        P.op(a, lambda e: e.copy(out=Woim[:], in_=WoFim[:]), reads=['WoFim'], writes=['Woim'])
                        for g2 in range(2):
                            r0, r1 = g2 * 64, (g2 + 1) * 64
                            P.op('tensor', lambda e, r0=r0, r1=r1: e.matmul(pB[:, 0:128], lhsT=B2re[r0:r1, :], rhs=C2re[r0:r1, :], start=True, stop=False), reads=['Bmre', 'Cmre'], writes=['PS:pB5'])
                            P.op('tensor', lambda e, r0=r0, r1=r1: e.matmul(pB[:, 0:128], lhsT=B2im[r0:r1, :], rhs=C2im[r0:r1, :], start=False, stop=True), reads=['Bmim', 'Cmim'], writes=['PS:pB5'])
                            P.op(v, lambda e, g2=g2: e.tensor_tensor(out=Tin[:, g2, :], in0=pB[:, 0:128], in1=bmask[:].rearrange("p a b -> p (a b)"), op=ALU.mult), reads=['PS:pB5', 'bmask'], writes=[('Tin', g2)])
                            P.op('tensor', lambda e, r0=r0, r1=r1: e.matmul(pB[:, 128:192], lhsT=WiTre[r0:r1, :], rhs=self.ident_f[r0:r1, r0:r1], start=True, stop=True), reads=['WiTre', 'ident_f'], writes=['PS:pB5'])
                            P.op('tensor', lambda e, r0=r0, r1=r1: e.matmul(pB[:, 192:256], lhsT=WiTim[r0:r1, :], rhs=self.ident_f[r0:r1, r0:r1], start=True, stop=True), reads=['WiTim', 'ident_f'], writes=['PS:pB5'])
                            P.op(a, lambda e, g2=g2: e.copy(out=Win[:, g2, :, :].rearrange("p a b -> p (a b)"), in_=pB[:, 128:256]), reads=['PS:pB5'], writes=[('Win', g2)])
                        P.op(v, lambda e, col=col: e.tensor_scalar(out=th[:], in0=imD[:, col:col + 1], scalar1=8.0, scalar2=None, op0=ALU.mult), reads=['imD'], writes=['th'])
                        reduce_angle(self, th[:], 1, ki, kf, msk, 'th')
                        P.op(a, lambda e, col=col: e.activation(out=rr[:], in_=reD[:, col:col + 1], func=AF.Exp, scale=8.0), reads=['reD'], writes=['rr'])
                        P.op(v, lambda e: e.tensor_scalar(out=rtab[:], in0=crow[:, 0:NCH], scalar1=0.0, scalar2=rr[:], op0=ALU.mult, op1=ALU.add), reads=['crow', 'rr'], writes=['rtab'])
                        P.op(v, lambda e: e.tensor_scalar(out=stab[:], in0=crow[:], scalar1=th[:], scalar2=None, op0=ALU.mult), reads=['crow', 'th'], writes=['stab'])
                        P.op(v, lambda e: e.tensor_scalar(out=ctab[:], in0=stab[:], scalar1=PI / 2, scalar2=None, op0=ALU.add), reads=['stab'], writes=['ctab'])
                        reduce_angle(self, stab[:], NCH + 1, ki, kf, msk, 'stab')
                        reduce_angle(self, ctab[:], NCH + 1, ki, kf, msk, 'ctab')
                        P.op(a, lambda e: e.activation(out=stab[:], in_=stab[:], func=AF.Sin), reads=['stab'], writes=['stab'])
                        P.op(a, lambda e: e.activation(out=ctab[:], in_=ctab[:], func=AF.Sin), reads=['ctab'], writes=['ctab'])
                        for g2 in range(2):
                            r0, r1 = g2 * 64, (g2 + 1) * 64
                            for hi_, (c0, c1) in enumerate(hs):
                                w_ = 272 * hi_
                                P.op('tensor', lambda e, d=d, g2=g2, r0=r0, r1=r1, c0=c0, c1=c1: e.matmul(pSr[r0:r1, 0:c1 - c0], lhsT=Win[:, g2, 0, :], rhs=Usb[:, d, g2, c0:c1], start=True, stop=True),
                                     reads=[('Win', g2), ('Usb', d, g2)], writes=['PS:pSr'])
                                P.op('tensor', lambda e, d=d, g2=g2, r0=r0, r1=r1, c0=c0, c1=c1: e.matmul(pSi[r0:r1, 0:c1 - c0], lhsT=Win[:, g2, 1, :], rhs=Usb[:, d, g2, c0:c1], start=True, stop=True),
                                     reads=[('Win', g2), ('Usb', d, g2)], writes=['PS:pSi'])
                                P.op(a, lambda e, r0=r0, r1=r1, c0=c0, c1=c1: e.copy(out=Sre[r0:r1, c0:c1], in_=pSr[r0:r1, 0:c1 - c0]), reads=['PS:pSr'], writes=['Sre'])
                                P.op(v, lambda e, r0=r0, r1=r1, c0=c0, c1=c1: e.tensor_copy(out=Sim[r0:r1, c0:c1], in_=pSi[r0:r1, 0:c1 - c0]), reads=['PS:pSi'], writes=['Sim'])
                        P.op(v, lambda e: e.tensor_tensor(out=xt1[:], in0=Sim[:], in1=stab[:, 1:NCH + 1], op=ALU.mult), reads=['Sim', 'stab'], writes=['xt1'])
                        P.op(v, lambda e: e.tensor_tensor(out=s2re[:], in0=Sre[:], in1=ctab[:, 1:NCH + 1], op=ALU.mult), reads=['Sre', 'ctab'], writes=['s2re'])
                        P.op(v, lambda e: e.tensor_tensor(out=s2re[:], in0=s2re[:], in1=xt1[:], op=ALU.add), reads=['s2re', 'xt1'], writes=['s2re'])
                        P.op(v, lambda e: e.tensor_tensor(out=xt1[:], in0=Sre[:], in1=stab[:, 1:NCH + 1], op=ALU.mult), reads=['Sre', 'stab', 's2re'], writes=['xt1'])
                        P.op(v, lambda e: e.tensor_tensor(out=s2im[:], in0=Sim[:], in1=ctab[:, 1:NCH + 1], op=ALU.mult), reads=['Sim', 'ctab'], writes=['s2im'])
                        P.op(v, lambda e: e.tensor_tensor(out=s2im[:], in0=s2im[:], in1=xt1[:], op=ALU.subtract), reads=['s2im', 'xt1'], writes=['s2im'])
                        P.op(v, lambda e: e.tensor_tensor_scan(out=Zre[:, 1:NCH + 1], data0=rtab[:], data1=s2re[:], initial=0.0, op0=ALU.mult, op1=ALU.add), reads=['rtab', 's2re'], writes=['Zre'])
                        P.op(v, lambda e: e.tensor_tensor_scan(out=Zim[:, 1:NCH + 1], data0=rtab[:], data1=s2im[:], initial=0.0, op0=ALU.mult, op1=ALU.add), reads=['rtab', 's2im'], writes=['Zim'])
                        P.op(v, lambda e: e.tensor_tensor(out=xt1[:], in0=Zim[:, 0:NCH], in1=stab[:, 0:NCH], op=ALU.mult), reads=['Zim', 'stab', 's2im'], writes=['xt1'])
                        P.op(v, lambda e: e.tensor_tensor(out=s2re[:], in0=Zre[:, 0:NCH], in1=ctab[:, 0:NCH], op=ALU.mult), reads=['Zre', 'ctab'], writes=['s2re'])
                        P.op(v, lambda e: e.tensor_tensor(out=Xre[:], in0=s2re[:], in1=xt1[:], op=ALU.subtract), reads=['s2re', 'xt1'], writes=['Xre'])
                        P.op(v, lambda e: e.tensor_tensor(out=xt1[:], in0=Zre[:, 0:NCH], in1=stab[:, 0:NCH], op=ALU.mult), reads=['Zre', 'stab', 'Xre'], writes=['xt1'])
                        P.op(v, lambda e: e.tensor_tensor(out=s2im[:], in0=Zim[:, 0:NCH], in1=ctab[:, 0:NCH], op=ALU.mult), reads=['Zim', 'ctab'], writes=['s2im'])
                        P.op(v, lambda e: e.tensor_tensor(out=Xim[:], in0=s2im[:], in1=xt1[:], op=ALU.add), reads=['s2im', 'xt1'], writes=['Xim'])
                        for g2 in range(2):
                            r0, r1 = g2 * 64, (g2 + 1) * 64
                            gl = gp4 * 2 + g2
                            for hi_, (c0, c1) in enumerate(hs):
                                py = pY[hi_]
                                pk = f'PS:pY{hi_}'
                                P.op('tensor', lambda e, d=d, g2=g2, c0=c0, c1=c1, py=py: e.matmul(py[:, 0:c1 - c0], lhsT=Tin[:, g2, :], rhs=Usb[:, d, g2, c0:c1], start=True, stop=False),
                                     reads=[('Tin', g2), ('Usb', d, g2)], writes=[pk])
                                P.op('tensor', lambda e, r0=r0, r1=r1, c0=c0, c1=c1, py=py: e.matmul(py[:, 0:c1 - c0], lhsT=Wore[r0:r1, :], rhs=Xre[r0:r1, c0:c1], start=False, stop=False),
                                     reads=['Wore', 'Xre'], writes=[pk])
                                P.op('tensor', lambda e, r0=r0, r1=r1, c0=c0, c1=c1, py=py: e.matmul(py[:, 0:c1 - c0], lhsT=Woim[r0:r1, :], rhs=Xim[r0:r1, c0:c1], start=False, stop=True),
                                     reads=['Woim', 'Xim'], writes=[pk])
                                if hi_ == 0:
                                    P.op(a, lambda e, d=d, gl=gl, c0=c0, c1=c1, py=py: e.copy(out=Ysb[:, d, gl, c0:c1], in_=py[:, 0:c1 - c0]), reads=[pk], writes=[('Ysb', d, gl)])
                                else:
                                    P.op(v, lambda e, d=d, gl=gl, c0=c0, c1=c1, py=py: e.tensor_copy(out=Ysb[:, d, gl, c0:c1], in_=py[:, 0:c1 - c0]), reads=[pk], writes=[('Ysb', d, gl)])
                for d in range(2):
                    for i in range(8):
                        for hi_, (c0, c1) in enumerate(hs):
                            py = pY[hi_]
                            pk = f'PS:pY{hi_}'
                            for gl in range(8):
                                P.op('tensor', lambda e, d=d, gl=gl, i=i, c0=c0, c1=c1, py=py: e.matmul(py[:, 0:c1 - c0], lhsT=Sel[:, i, gl, :], rhs=Ysb[:, d, gl, c0:c1], start=(gl == 0), stop=(gl == 7)),
                                     reads=['Sel', ('Ysb', d, gl)], writes=[pk])
                            if hi_ == 0:
                                P.op(a, lambda e, d=d, i=i, c0=c0, c1=c1, py=py: e.copy(out=yT[:, d, c0 * 8 + i:c1 * 8:8], in_=py[:, 0:c1 - c0]), reads=[pk], writes=['yT5'])
                            else:
                                P.op(v, lambda e, d=d, i=i, c0=c0, c1=c1, py=py: e.tensor_copy(out=yT[:, d, c0 * 8 + i:c1 * 8:8], in_=py[:, 0:c1 - c0]), reads=[pk], writes=['yT5'])
                P.op(v, lambda e: e.tensor_tensor(out=yT[:, 0, 0:TC], in0=yT[:, 0, 0:TC], in1=yT[:, 1, TC - 1::-1], op=ALU.add), reads=['yT5'], writes=['yT5'])
                P.op(v, lambda e: e.tensor_tensor(out=yT[:, 0, TC:T], in0=yT[:, 0, TC:T], in1=yT[:, 1, T - 1:TC - 1:-1], op=ALU.add), reads=['yT5'], writes=['yT5'])
                P.op(v, lambda e, ft=ft: e.scalar_tensor_tensor(out=yT[:, 0, :], in0=uF[:], scalar=dsk[:, ft:ft + 1], in1=yT[:, 0, :], op0=ALU.mult, op1=ALU.add), reads=['uF', 'dsk', 'yT5'], writes=['yT5'])
                P.op(a, lambda e: e.activation(out=yT[:, 1, :], in_=yT[:, 0, :], func=AF.Square), reads=['yT5'], writes=['yT5b'])
                P.op(v, lambda e: e.tensor_scalar(out=yT[:, 1, :], in0=yT[:, 1, :], scalar1=0.044715, scalar2=1.0, op0=ALU.mult, op1=ALU.add), reads=['yT5b'], writes=['yT5b'])
                P.op(v, lambda e: e.tensor_tensor(out=yT[:, 1, :], in0=yT[:, 1, :], in1=yT[:, 0, :], op=ALU.mult), reads=['yT5b', 'yT5'], writes=['yT5b'])
                P.op(a, lambda e: e.activation(out=yT[:, 1, :], in_=yT[:, 1, :], func=AF.Sigmoid, scale=1.5957691216057308), reads=['yT5b'], writes=['yT5b'])
                P.op(v, lambda e, ft=ft: e.tensor_tensor(out=gy[:, ft, :], in0=yT[:, 1, :], in1=yT[:, 0, :], op=ALU.mult), reads=['yT5b', 'yT5'], writes=[('gy', ft)])
            it = 0
            for fo in range(4):
                for (tok0, nt, m) in self.blocks():
                    py = pY[it % 2]
                    pk = f'PS:pY{it % 2}'
                    it += 1
                    for fi in range(4):
                        P.op('tensor', lambda e, fo=fo, fi=fi, tok0=tok0, nt=nt, py=py: e.matmul(py[:, :nt], lhsT=gl_w[:, fi, fo * 128:(fo + 1) * 128], rhs=gy[:, fi, tok0:tok0 + nt], start=(fi == 0), stop=(fi == 3)),
                             reads=['gluw'] + [('gy', f) for f in range(4)], writes=[pk])
                    P.op(a, lambda e, nt=nt, py=py: e.activation(out=sgm[:, :nt], in_=py[:, :nt], func=AF.Sigmoid), reads=[pk], writes=['sgm'])
                    P.op(v, lambda e, fo=fo, tok0=tok0, nt=nt: e.tensor_tensor(out=og[:, :nt], in0=sgm[:, :nt], in1=gy[:, fo, tok0:tok0 + nt], op=ALU.mult), reads=['sgm', ('gy', fo)], writes=['og5'])
                    P.dma('gpsimd', dr['YT'][fo * 128:(fo + 1) * 128, tok0:tok0 + nt], og[:, :nt], reads=['og5'], writes=[('YT5', fo, tok0)])
            P.barrier()


B.phase_s5 = phase_s5


R_QM, R_KM, R_VM, R_OM, R_IG, R_FG = 4128, 4640, 5152, 6176, 7200, 7208


def phase_mlstm(self):
    P, nc, dr = self.P, self.nc, self.dr
    sb, ps = self.sb, self.ps
    PT = dr['PT']
    v, a, g_ = 'vector', 'scalar', 'gpsimd'
    with ExitStack() as es0:
        gt_tok = sb(es0, "m_gt", [128, 34, 40], F32)
        with ExitStack() as es:
            G = sb(es, "m_G", [128, T], F32)
            bcol = sb(es, "m_bcol", [128, 1], F32)
            ptr = ps(es, "m_ptr", [128, 512], F32)
            P.op(g_, lambda e: e.memset(G[:], 0.0), writes=['m_G'])
            P.op(g_, lambda e: e.memset(bcol[:], 0.0), writes=['m_bcol'])
            P.dma('sync', G[0:8, :], PT[R_IG:R_IG + 8, :], writes=['m_G'])
            P.dma('sync', G[32:40, :], PT[R_FG:R_FG + 8, :], writes=['m_G'])
            P.dma('sync', bcol[0:8, :], dr['ml_ib'], writes=['m_bcol'])
            P.dma('sync', bcol[32:40, :], dr['ml_fb'], writes=['m_bcol'])
            P.op(a, lambda e: e.activation(out=G[0:32, :], in_=G[0:32, :], func=AF.Exp, bias=bcol[0:32, :]), reads=['m_G', 'm_bcol'], writes=['m_G'])
            P.op(v, lambda e: e.tensor_scalar(out=bcol[32:64, :], in0=bcol[32:64, :], scalar1=-1.0, scalar2=None, op0=ALU.mult), reads=['m_bcol', 'm_G'], writes=['m_bcol'])
            P.op(a, lambda e: e.activation(out=G[32:64, :], in_=G[32:64, :], func=AF.Exp, scale=-1.0, bias=bcol[32:64, :]), reads=['m_G', 'm_bcol'], writes=['m_G'])
            P.op(a, lambda e: e.activation(out=G[32:64, :], in_=G[32:64, :], func=AF.Ln, bias=self.one_col[32:64, :]), reads=['m_G', 'one_col'], writes=['m_G'])
            P.op(v, lambda e: e.tensor_scalar(out=G[32:64, :], in0=G[32:64, :], scalar1=-1.0, scalar2=None, op0=ALU.mult), reads=['m_G'], writes=['m_G'])
            for c in range(34):
                P.op('tensor', lambda e, c=c: e.matmul(ptr[:, 0:40], lhsT=G[:, c * 128:(c + 1) * 128], rhs=self.ident_f[:, 0:40], start=True, stop=True), reads=['m_G', 'ident_f'], writes=['PS:m_ptr'])
                P.op(v, lambda e, c=c: e.tensor_copy(out=gt_tok[:, c, :], in_=ptr[:, 0:40]), reads=['PS:m_ptr'], writes=['m_gt'])
            P.barrier()
        with ExitStack() as es:
            xin = [sb(es, f"m_xin{i}", [128, 16, 512], F32) for i in range(2)]
            cs = sb(es, "m_cs", [128, 16, 512], BF16)
            tokst = [sb(es, f"m_tokst{i}", [128, 1536], BF16) for i in range(2)]
            ptA = ps(es, "m_ptA", [128, 8, 128], BF16)
            ptB = ps(es, "m_ptB", [128, 8, 128], BF16)
            ci = 0
            for bi, (tok0, nt, m) in enumerate(self.blocks()):
                x = xin[bi % 2]
                xk = f'm_xin{bi % 2}'
                P.dma('sync', x[:, 0:8, :nt], PT[R_QM:R_QM + 1024, :].rearrange("(f p) t -> p f t", p=128)[:, :, tok0:tok0 + nt], writes=[xk])
                P.dma('sync', x[:, 8:16, :nt], PT[R_VM:R_VM + 1024, :].rearrange("(f p) t -> p f t", p=128)[:, :, tok0:tok0 + nt], writes=[xk])
                P.op(a, lambda e, x=x, nt=nt: e.copy(out=cs[:, 0:4, :nt], in_=x[:, 0:4, :nt]), reads=[xk], writes=['m_cs'])
                P.op(a, lambda e, x=x, nt=nt: e.activation(out=cs[:, 4:8, :nt], in_=x[:, 4:8, :nt], func=AF.Copy, scale=128 ** -0.5), reads=[xk], writes=['m_cs'])
                P.op(v, lambda e, x=x, nt=nt: e.tensor_copy(out=cs[:, 8:16, :nt], in_=x[:, 8:16, :nt]), reads=[xk], writes=['m_cs'])
                P.dma('gpsimd', dr['BCT'][0:1024, :].rearrange("(f p) t -> p f t", p=128)[:, :, tok0:tok0 + nt], cs[:, 0:8, :nt], reads=['m_cs'], writes=[('BCT', tok0)])
                for ch in range(nt // 128):
                    tk = tokst[ci % 2]
                    tkk = f'm_tokst{ci % 2}'
                    ci += 1
                    for f in range(8):
                        P.op('tensor', lambda e, f=f, ch=ch: e.transpose(out=ptA[:, f, :], in_=cs[:, 8 + f, ch * 128:(ch + 1) * 128], identity=self.ident_b[:]), reads=['m_cs', 'ident_b'], writes=['PS:m_ptA'])
                    for f in range(4):
                        P.op('tensor', lambda e, f=f, ch=ch: e.transpose(out=ptB[:, f, :], in_=cs[:, 4 + f, ch * 128:(ch + 1) * 128], identity=self.ident_b[:]), reads=['m_cs', 'ident_b'], writes=['PS:m_ptB'])
                    P.op(v, lambda e, tk=tk: e.tensor_copy(out=tk[:, 0:1024], in_=ptA[:].rearrange("p a b -> p (a b)")), reads=['PS:m_ptA'], writes=[tkk])
                    P.op(a, lambda e, tk=tk: e.copy(out=tk[:, 1024:1536], in_=ptB[:, 0:4, :].rearrange("p a b -> p (a b)")), reads=['PS:m_ptB'], writes=[tkk])
                    P.dma('gpsimd', dr['TOK'][tok0 + ch * 128:tok0 + (ch + 1) * 128, 0:1536], tk[:], reads=[tkk], writes=[('TOK', tok0 + ch * 128)])
            P.barrier()
        with ExitStack() as es:
            S_f = sb(es, "m_Sf", [128, 4, 257], F32)
            S_b = sb(es, "m_Sb", [128, 4, 257], BF16)
            tok = [sb(es, f"m_tok{i}", [128, 1536], BF16) for i in range(2)]
            qk = [sb(es, f"m_qk{i}", [128, 8, 128], BF16) for i in range(2)]
            Lm = [sb(es, f"m_Lm{i}", [128, 128], F32) for i in range(2)]
            Dm = [sb(es, f"m_Dm{i}", [128, 128], F32) for i in range(2)]
            WT = [sb(es, f"m_WT{i}", [128, 128], BF16) for i in range(2)]
            v1 = [sb(es, f"m_v1_{i}", [128, 257], BF16) for i in range(2)]
            v2 = [sb(es, f"m_v2_{i}", [128, 257], BF16) for i in range(2)]
            yA = [sb(es, f"m_yA{i}", [128, 257], F32) for i in range(2)]
            num = [sb(es, f"m_num{i}", [128, 257], F32) for i in range(2)]
            den = [sb(es, f"m_den{i}", [128, 1], F32) for i in range(2)]
            hsb = [sb(es, f"m_hsb{i}", [128, 1024], F32) for i in range(2)]
            hfl = sb(es, "m_hfl", [128, 1024], F32)
            sqj = sb(es, "m_sqj", [128, 256], F32)
            ss = sb(es, "m_ss", [128, 4], F32)
            hn = sb(es, "m_hn", [128, 1024], BF16)
            ecum = sb(es, "m_ecum", [128, 4], F32)
            cdec = sb(es, "m_cdec", [128, 4], F32)
            nwB = sb(es, "m_nwB", [128, 1024], F32)
            ytb = sb(es, "m_ytb", [128, 8, 512], BF16)
            oT = sb(es, "m_oT", [128, 8, 512], F32)
            ot = sb(es, "m_ot", [128, 8, 512], BF16)
            pX = [ps(es, f"m_pX{i}", [128, 512], F32) for i in range(2)]
            pyA = [ps(es, f"m_pyA{i}", [128, 512], F32) for i in range(2)]
            pyB = [ps(es, f"m_pyB{i}", [128, 512], F32) for i in range(2)]
            pS = ps(es, "m_pS", [128, 512], F32)
            ptT = ps(es, "m_ptT", [128, 8, 128], BF16)
            P.dma('sync', nwB[:], dr['ml_nw'].partition_broadcast(128), writes=['m_nwB'])
            li = 0
            for d in range(2):
                MASK = self.MGT if d == 0 else self.MLT
                TRI = self.TLE if d == 0 else self.TGE
                mk, tk_ = ('MGT', 'TLE') if d == 0 else ('MLT', 'TGE')
                ecol = 127 if d == 0 else 0
                P.op(g_, lambda e: e.memset(S_f[:], 0.0), writes=[('m_Sf', h) for h in range(4)])
                P.op(g_, lambda e: e.memset(S_b[:], 0.0), writes=[('m_Sb', h) for h in range(4)])
                for c in chunk_order(d):
                    t0 = c * 128
                    tb, tbk = tok[li % 2], f'm_tok{li % 2}'
                    qb, qbk = qk[li % 2], f'm_qk{li % 2}'
                    hs_, hsk = hsb[li % 2], f'm_hsb{li % 2}'
                    li += 1
                    P.dma('sync', tb[:], dr['TOK'][t0:t0 + 128, 0:1536], writes=[tbk])
                    P.dma('sync', qb[:], dr['BCT'][0:1024, :].rearrange("(f p) t -> p f t", p=128)[:, :, t0:t0 + 128], writes=[qbk])
                    lac = gt_tok[:, c, 32 + d * 4:32 + (d + 1) * 4]
                    P.op('tensor', lambda e, lac=lac: e.matmul(pS[:, 300:304], lhsT=TRI[:], rhs=lac, start=True, stop=True), reads=[tk_, 'm_gt'], writes=['PS:m_pS'])
                    P.op('tensor', lambda e, lac=lac: e.matmul(pS[:, 320:324], lhsT=self.ones_f[:], rhs=lac, start=True, stop=True), reads=['ones_f', 'm_gt'], writes=['PS:m_pS'])
                    P.op(a, lambda e: e.activation(out=ecum[:], in_=pS[:, 300:304], func=AF.Exp), reads=['PS:m_pS'], writes=['m_ecum'])
                    P.op(a, lambda e: e.activation(out=cdec[:], in_=pS[:, 320:324], func=AF.Exp), reads=['PS:m_pS'], writes=['m_cdec'])
                    for h in range(4):
                        r = h % 2
                        X, Xk = pX[r], f'PS:m_pX{r}'
                        A_, Ak = pyA[r], f'PS:m_pyA{r}'
                        B_, Bk = pyB[r], f'PS:m_pyB{r}'
                        gcol = gt_tok[:, c, 32 + d * 4 + h:32 + d * 4 + h + 1]
                        icol = gt_tok[:, c, d * 4 + h:d * 4 + h + 1]
                        P.op(v, lambda e, r=r, gcol=gcol: e.tensor_scalar(out=Lm[r][:], in0=MASK[:], scalar1=gcol, scalar2=None, op0=ALU.mult), reads=[mk, 'm_gt'], writes=[f'm_Lm{r}'])
                        P.op('tensor', lambda e, r=r, X=X: e.matmul(X[:, 0:128], lhsT=Lm[r][:], rhs=TRI[:], start=True, stop=False), reads=[f'm_Lm{r}', tk_], writes=[Xk])
                        P.op('tensor', lambda e, r=r, X=X: e.matmul(X[:, 0:128], lhsT=self.negI[:], rhs=MASK[:], start=False, stop=True), reads=['negI', mk], writes=[Xk])
                        P.op('tensor', lambda e, h=h, X=X, qb=qb: e.matmul(X[:, 128:256], lhsT=qb[:, 4 + h, :], rhs=qb[:, h, :], start=True, stop=True), reads=[qbk], writes=[Xk])
                        P.op(a, lambda e, r=r, X=X: e.activation(out=Dm[r][:], in_=X[:, 0:128], func=AF.Exp), reads=[Xk], writes=[f'm_Dm{r}'])
                        P.op(v, lambda e, r=r, X=X: e.tensor_tensor(out=WT[r][:], in0=Dm[r][:], in1=X[:, 128:256], op=ALU.mult), reads=[f'm_Dm{r}', Xk], writes=[f'm_WT{r}'])
                        P.op(g_, lambda e, r=r, h=h, tb=tb, icol=icol: e.tensor_scalar(out=v1[r][:, 0:256], in0=tb[:, h * 256:(h + 1) * 256], scalar1=icol, scalar2=None, op0=ALU.mult), reads=[tbk, 'm_gt'], writes=[f'm_v1_{r}'])
                        P.op(g_, lambda e, r=r, icol=icol: e.tensor_copy(out=v1[r][:, 256:257], in_=icol), reads=['m_gt'], writes=[f'm_v1_{r}'])
                        P.op(g_, lambda e, r=r: e.tensor_scalar(out=v2[r][:], in0=v1[r][:], scalar1=Dm[r][:, ecol:ecol + 1], scalar2=None, op0=ALU.mult), reads=[f'm_v1_{r}', f'm_Dm{r}'], writes=[f'm_v2_{r}'])
                        P.op('tensor', lambda e, r=r, A_=A_: e.matmul(A_[:, 0:257], lhsT=WT[r][:], rhs=v1[r][:], start=True, stop=True), reads=[f'm_WT{r}', f'm_v1_{r}'], writes=[Ak])
                        P.op('tensor', lambda e, h=h, B_=B_, qb=qb: e.matmul(B_[:, 0:257], lhsT=qb[:, h, :], rhs=S_b[:, h, :], start=True, stop=True), reads=[qbk, ('m_Sb', h)], writes=[Bk])
                        P.op('tensor', lambda e, r=r, h=h, tb=tb: e.matmul(pS[:, 0:257], lhsT=tb[:, 1024 + h * 128:1024 + (h + 1) * 128], rhs=v2[r][:], start=True, stop=True), reads=[tbk, f'm_v2_{r}'], writes=['PS:m_pS'])
                        P.op(a, lambda e, r=r, A_=A_: e.copy(out=yA[r][:], in_=A_[:, 0:257]), reads=[Ak], writes=[f'm_yA{r}'])
                        P.op(v, lambda e, r=r, h=h, B_=B_: e.scalar_tensor_tensor(out=num[r][:], in0=B_[:, 0:257], scalar=ecum[:, h:h + 1], in1=yA[r][:], op0=ALU.mult, op1=ALU.add), reads=[Bk, 'm_ecum', f'm_yA{r}'], writes=[f'm_num{r}'])
                        P.op(v, lambda e, h=h: e.scalar_tensor_tensor(out=S_f[:, h, :], in0=S_f[:, h, :], scalar=cdec[:, h:h + 1], in1=pS[:, 0:257], op0=ALU.mult, op1=ALU.add), reads=[('m_Sf', h), 'm_cdec', 'PS:m_pS'], writes=[('m_Sf', h)])
                        P.op(a, lambda e, h=h: e.copy(out=S_b[:, h, :], in_=S_f[:, h, :]), reads=[('m_Sf', h)], writes=[('m_Sb', h)])
                        P.op(a, lambda e, r=r: e.activation(out=den[r][:], in_=num[r][:, 256:257], func=AF.Abs), reads=[f'm_num{r}'], writes=[f'm_den{r}'])
                        P.op(v, lambda e, r=r: e.tensor_scalar(out=den[r][:], in0=den[r][:], scalar1=1.0, scalar2=None, op0=ALU.max), reads=[f'm_den{r}'], writes=[f'm_den{r}'])
                        P.op(v, lambda e, r=r: e.reciprocal(out=den[r][:], in_=den[r][:]), reads=[f'm_den{r}'], writes=[f'm_den{r}'])
                        P.op(v, lambda e, r=r, h=h, hs_=hs_: e.tensor_scalar(out=hs_[:, h * 256:(h + 1) * 256], in0=num[r][:, 0:256], scalar1=den[r][:], scalar2=None, op0=ALU.mult), reads=[f'm_num{r}', f'm_den{r}'], writes=[hsk])
                    if d == 0:
                        P.dma('gpsimd', dr['YF'][t0:t0 + 128, 0:1024], hs_[:], reads=[hsk], writes=[('YF', c)])
                    else:
                        P.dma('sync', hfl[:], dr['YF'][t0:t0 + 128, 0:1024], reads=[('YF', c)], writes=['m_hfl'])
                        P.op(v, lambda e, hs_=hs_: e.tensor_tensor(out=hfl[:], in0=hfl[:], in1=hs_[:], op=ALU.add), reads=['m_hfl', hsk], writes=['m_hfl'])
                        for h in range(4):
                            P.op(a, lambda e, h=h: e.activation(out=sqj[:], in_=hfl[:, h * 256:(h + 1) * 256], func=AF.Square, accum_out=ss[:, h:h + 1]), reads=['m_hfl'], writes=['m_sqj', 'm_ss'])
                        P.op(a, lambda e: e.activation(out=ss[:], in_=ss[:], func=AF.Sqrt, scale=1.0 / 256, bias=self.eps_col[:]), reads=['m_ss', 'eps_col'], writes=['m_ss'])
                        P.op(v, lambda e: e.reciprocal(out=ss[:], in_=ss[:]), reads=['m_ss'], writes=['m_ss'])
                        for h in range(4):
                            P.op(v, lambda e, h=h: e.scalar_tensor_tensor(out=hn[:, h * 256:(h + 1) * 256], in0=hfl[:, h * 256:(h + 1) * 256], scalar=ss[:, h:h + 1], in1=nwB[:, h * 256:(h + 1) * 256], op0=ALU.mult, op1=ALU.mult),
                                 reads=['m_hfl', 'm_ss', 'm_nwB'], writes=['m_hn'])
                        if c < 2:
                            tok0, nt, ch = 0, TC, c
                        else:
                            tok0, nt, ch = TC + ((c - 2) // 4) * 512, 512, (c - 2) % 4
                        for f in range(8):
                            P.op('tensor', lambda e, f=f: e.transpose(out=ptT[:, f, :], in_=hn[:, f * 128:(f + 1) * 128], identity=self.ident_b[:]), reads=['m_hn', 'ident_b'], writes=['PS:m_ptT'])
                        P.op(v, lambda e, ch=ch: e.tensor_copy(out=ytb[:, :, ch * 128:(ch + 1) * 128], in_=ptT[:]), reads=['PS:m_ptT'], writes=['m_ytb'])
                        if ch == 0:
                            P.dma('sync', oT[:, :, :nt], PT[R_OM:R_OM + 1024, :].rearrange("(f p) t -> p f t", p=128)[:, :, tok0:tok0 + nt], writes=['m_oT'])
                            P.op(a, lambda e, nt=nt: e.activation(out=oT[:, :, :nt], in_=oT[:, :, :nt], func=AF.Sigmoid), reads=['m_oT'], writes=['m_oT'])
                            P.op(v, lambda e, nt=nt: e.tensor_tensor(out=ot[:, :, :nt], in0=ytb[:, :, :nt], in1=oT[:, :, :nt], op=ALU.mult), reads=['m_ytb', 'm_oT'], writes=['m_ot'])
                            P.dma('gpsimd', dr['YT'][1024:2048, :].rearrange("(f p) t -> p f t", p=128)[:, :, tok0:tok0 + nt], ot[:, :, :nt], reads=['m_ot'], writes=[('YT', 'ml', tok0)])
            P.barrier()


B.phase_mlstm = phase_mlstm


R_QKV, R_ZG, R_BETA, R_A = 0, 3072, 4096, 4112


def phase_gdn(self):
    P, nc, dr = self.P, self.nc, self.dr
    sb, ps = self.sb, self.ps
    PT = dr['PT']
    v, a, g_ = 'vector', 'scalar', 'gpsimd'
    with ExitStack() as es0:
        gt_tok = sb(es0, "g_gt", [128, 34, 48], F32)
        with ExitStack() as es:
            G = sb(es, "g_G", [128, T], F32)
            bcol = sb(es, "g_bcol", [128, 1], F32)
            acol = sb(es, "g_acol", [128, 1], F32)
            ptr = ps(es, "g_ptr", [128, 512], F32)
            P.op(g_, lambda e: e.memset(G[:], 0.0), writes=['g_G'])
            P.op(g_, lambda e: e.memset(bcol[:], 0.0), writes=['g_bcol'])
            P.op(g_, lambda e: e.memset(acol[:], 0.0), writes=['g_acol'])
            P.dma('sync', G[0:16, :], PT[R_BETA:R_BETA + 16, :], writes=['g_G'])
            P.dma('sync', G[32:48, :], PT[R_A:R_A + 16, :], writes=['g_G'])
            P.dma('sync', bcol[32:48, :], dr['gdn_dtb'], writes=['g_bcol'])
            P.dma('sync', acol[32:48, :], dr['gdn_alog'], writes=['g_acol'])
            P.op(a, lambda e: e.activation(out=G[0:32, :], in_=G[0:32, :], func=AF.Sigmoid), reads=['g_G'], writes=['g_G'])
            P.op(a, lambda e: e.activation(out=acol[32:64, :], in_=acol[32:64, :], func=AF.Exp), reads=['g_acol'], writes=['g_acol'])
            P.op(v, lambda e: e.tensor_scalar(out=acol[32:64, :], in0=acol[32:64, :], scalar1=-1.0, scalar2=None, op0=ALU.mult), reads=['g_acol'], writes=['g_acol'])
            P.op(a, lambda e: e.activation(out=G[32:64, :], in_=G[32:64, :], func=AF.Exp, bias=bcol[32:64, :]), reads=['g_G', 'g_bcol'], writes=['g_G'])
            P.op(a, lambda e: e.activation(out=G[32:64, :], in_=G[32:64, :], func=AF.Ln, bias=self.one_col[32:64, :]), reads=['g_G', 'one_col'], writes=['g_G'])
            P.op(v, lambda e: e.tensor_scalar(out=G[32:64, :], in0=G[32:64, :], scalar1=acol[32:64, :], scalar2=None, op0=ALU.mult), reads=['g_G', 'g_acol'], writes=['g_G'])
            for c in range(34):
                P.op('tensor', lambda e, c=c: e.matmul(ptr[:, 0:48], lhsT=G[:, c * 128:(c + 1) * 128], rhs=self.ident_f[:, 0:48], start=True, stop=True), reads=['g_G', 'ident_f'], writes=['PS:g_ptr'])
                P.op(v, lambda e, c=c: e.tensor_copy(out=gt_tok[:, c, :], in_=ptr[:, 0:48]), reads=['PS:g_ptr'], writes=['g_gt'])
            P.barrier()
        with ExitStack() as es:
            xin = [sb(es, f"g_xin{i}", [128, 24, 514], F32) for i in range(2)]
            cs = sb(es, "g_cs", [128, 24, 512], BF16)
            cf = sb(es, "g_cf", [128, 512], F32)
            sqb = sb(es, "g_sqb", [128, 512], BF16)
            rn = sb(es, "g_rn", [128, 512], F32)
            acc = [sb(es, f"g_cacc{i}", [128, 512], F32) for i in range(2)]
            cw = sb(es, "g_cw", [128, 24, 3], F32)
            tokst = [sb(es, f"g_tokst{i}", [128, 2048], BF16) for i in range(2)]
            ptA = ps(es, "g_ptA", [128, 8, 128], BF16)
            ptB = ps(es, "g_ptB", [128, 8, 128], BF16)
            pss = ps(es, "g_pss", [128, 512], F32)
            P.dma('sync', cw[:], dr['gdn_cw'], writes=['g_cw'])
            src = PT[0:3072, :].rearrange("(f p) t -> p f t", p=128)
            ci = 0
            for bi, (tok0, nt, m) in enumerate(self.blocks()):
                x = xin[bi % 2]
                xk = f'g_xin{bi % 2}'
                seq0, seq1 = (0, TC) if m == 1 else (TC, T)
                lo = max(tok0 - 1, seq0)
                hi = min(tok0 + nt + 1, seq1)
                P.dma('sync', x[:, 0:12, lo - (tok0 - 1):hi - (tok0 - 1)], src[:, 0:12, lo:hi], writes=[xk])
                P.dma('sync', x[:, 12:24, lo - (tok0 - 1):hi - (tok0 - 1)], src[:, 12:24, lo:hi], writes=[xk])
                if lo > tok0 - 1:
                    P.op(g_, lambda e, x=x: e.memset(x[:, :, 0:1], 0.0), writes=[xk])
                if hi < tok0 + nt + 1:
                    P.op(g_, lambda e, x=x, nt=nt: e.memset(x[:, :, nt + 1:nt + 2], 0.0), writes=[xk])
                for f in range(24):
                    ac = acc[f % 2]
                    ak = f'g_cacc{f % 2}'
                    P.op(v, lambda e, x=x, ac=ac, f=f, nt=nt: e.tensor_scalar(out=ac[:, :nt], in0=x[:, f, 0:nt], scalar1=cw[:, f, 0:1], scalar2=None, op0=ALU.mult), reads=[xk, 'g_cw'], writes=[ak])
                    P.op(v, lambda e, x=x, ac=ac, f=f, nt=nt: e.scalar_tensor_tensor(out=ac[:, :nt], in0=x[:, f, 1:nt + 1], scalar=cw[:, f, 1:2], in1=ac[:, :nt], op0=ALU.mult, op1=ALU.add), reads=[xk, 'g_cw', ak], writes=[ak])
                    P.op(v, lambda e, x=x, ac=ac, f=f, nt=nt: e.scalar_tensor_tensor(out=ac[:, :nt], in0=x[:, f, 2:nt + 2], scalar=cw[:, f, 2:3], in1=ac[:, :nt], op0=ALU.mult, op1=ALU.add), reads=[xk, 'g_cw', ak], writes=[ak])
                    if f >= 16:
                        P.op(a, lambda e, ac=ac, f=f, nt=nt: e.activation(out=cs[:, f, :nt], in_=ac[:, :nt], func=AF.Silu), reads=[ak], writes=[('g_cs', f)])
                    else:
                        P.op(a, lambda e, ac=ac, nt=nt: e.activation(out=cf[:, :nt], in_=ac[:, :nt], func=AF.Silu), reads=[ak], writes=['g_cf'])
                        P.op(a, lambda e, nt=nt: e.activation(out=sqb[:, :nt], in_=cf[:, :nt], func=AF.Square), reads=['g_cf'], writes=['g_sqb'])
                        P.op('tensor', lambda e, nt=nt: e.matmul(pss[:, :nt], lhsT=self.ones_b[:], rhs=sqb[:, :nt], start=True, stop=True), reads=['g_sqb', 'ones_b'], writes=['PS:g_pss'])
                        P.op(a, lambda e, nt=nt: e.activation(out=rn[:, :nt], in_=pss[:, :nt], func=AF.Sqrt, bias=self.eps_col[:]), reads=['PS:g_pss', 'eps_col'], writes=['g_rn'])
                        P.op(v, lambda e, nt=nt: e.reciprocal(out=rn[:, :nt], in_=rn[:, :nt]), reads=['g_rn'], writes=['g_rn'])
                        sc = 128 ** -0.5 if f < 8 else 1.0
                        P.op(v, lambda e, f=f, nt=nt, sc=sc: e.scalar_tensor_tensor(out=cs[:, f, :nt], in0=cf[:, :nt], scalar=sc, in1=rn[:, :nt], op0=ALU.mult, op1=ALU.mult), reads=['g_cf', 'g_rn'], writes=[('g_cs', f)])
                P.dma('gpsimd', dr['BCT'].rearrange("(f p) t -> p f t", p=128)[:, :, tok0:tok0 + nt], cs[:, 0:16, :nt],
                      reads=[('g_cs', f) for f in range(16)], writes=[('BCT', tok0)])
                for ch in range(nt // 128):
                    tk = tokst[ci % 2]
                    tkk = f'g_tokst{ci % 2}'
                    ci += 1
                    for f in range(8):
                        P.op('tensor', lambda e, f=f, ch=ch: e.transpose(out=ptA[:, f, :], in_=cs[:, 16 + f, ch * 128:(ch + 1) * 128], identity=self.ident_b[:]), reads=[('g_cs', 16 + f), 'ident_b'], writes=['PS:g_ptA'])
                    for f in range(8):
                        P.op('tensor', lambda e, f=f, ch=ch: e.transpose(out=ptB[:, f, :], in_=cs[:, 8 + f, ch * 128:(ch + 1) * 128], identity=self.ident_b[:]), reads=[('g_cs', 8 + f), 'ident_b'], writes=['PS:g_ptB'])
                    P.op(v, lambda e, tk=tk: e.tensor_copy(out=tk[:, 0:1024], in_=ptA[:].rearrange("p a b -> p (a b)")), reads=['PS:g_ptA'], writes=[tkk])
                    P.op(a, lambda e, tk=tk: e.copy(out=tk[:, 1024:2048], in_=ptB[:].rearrange("p a b -> p (a b)")), reads=['PS:g_ptB'], writes=[tkk])
                    P.dma('gpsimd', dr['TOK'][tok0 + ch * 128:tok0 + (ch + 1) * 128, :], tk[:], reads=[tkk], writes=[('TOK', tok0 + ch * 128)])
            P.barrier()
        with ExitStack() as es:
            S_f = sb(es, "g_Sf", [128, 8, 128], F32)
            S_b = sb(es, "g_Sb", [128, 8, 128], BF16)
            tok = [sb(es, f"g_tok{i}", [128, 2048], BF16) for i in range(2)]
            qk = [sb(es, f"g_qk{i}", [128, 16, 128], BF16) for i in range(2)]
            Lm = [sb(es, f"g_Lm{i}", [128, 128], F32) for i in range(2)]
            Dm = [sb(es, f"g_Dm{i}", [128, 128], F32) for i in range(2)]
            WT = [sb(es, f"g_WT{i}", [128, 128], BF16) for i in range(2)]
            tX = [sb(es, f"g_tX{i}", [128, 128], F32) for i in range(2)]
            Xp = [[sb(es, f"g_Xp{r}{i}", [128, 128], F32) for i in range(2)] for r in range(2)]
            Np = [[sb(es, f"g_Np{r}{i}", [128, 128], F32) for i in range(2)] for r in range(2)]
            rr = [[sb(es, f"g_rr{r}{i}", [128, 256], F32) for i in range(2)] for r in range(2)]
            wT = [sb(es, f"g_wT{i}", [128, 128], F32) for i in range(2)]
            vn = [sb(es, f"g_vn{i}", [128, 128], F32) for i in range(2)]
            v1 = [sb(es, f"g_v1{i}", [128, 128], BF16) for i in range(2)]
            v2 = [sb(es, f"g_v2{i}", [128, 128], BF16) for i in range(2)]
            yA = [sb(es, f"g_yA{i}", [128, 128], F32) for i in range(2)]
            osb = [sb(es, f"g_osb{i}", [128, 1024], F32) for i in range(2)]
            ofl = sb(es, "g_ofl", [128, 1024], F32)
            sqj = sb(es, "g_sqj", [128, 128], F32)
            ss = sb(es, "g_ss", [128, 8], F32)
            hn = sb(es, "g_hn", [128, 1024], BF16)
            ecum = sb(es, "g_ecum", [128, 8], F32)
            cdec = sb(es, "g_cdec", [128, 8], F32)
            nwB = sb(es, "g_nwB", [128, 128], F32)
            ytb = sb(es, "g_ytb", [128, 8, 512], BF16)
            zT = sb(es, "g_zT", [128, 8, 512], F32)
            ot = sb(es, "g_ot", [128, 8, 512], BF16)
            pX = [ps(es, f"g_pX{i}", [128, 512], F32) for i in range(2)]
            pN = [ps(es, f"g_pN{i}", [128, 512], F32) for i in range(2)]
            pA = [ps(es, f"g_pA{i}", [128, 512], F32) for i in range(2)]
            ptT = ps(es, "g_ptT", [128, 8, 128], BF16)
            P.dma('sync', nwB[:], dr['gdn_nw'].partition_broadcast(128), writes=['g_nwB'])
            li = 0
            for d in range(2):
                MASK = self.MGT if d == 0 else self.MLT
                TRI = self.TLE if d == 0 else self.TGE
                STR = self.MLT if d == 0 else self.MGT
                mk, tk_, sk_ = ('MGT', 'TLE', 'MLT') if d == 0 else ('MLT', 'TGE', 'MGT')
                ecol = 127 if d == 0 else 0
                P.op(g_, lambda e: e.memset(S_f[:], 0.0), writes=[('g_Sf', h) for h in range(8)])
                P.op(g_, lambda e: e.memset(S_b[:], 0.0), writes=[('g_Sb', h) for h in range(8)])
                for c in chunk_order(d):
                    t0 = c * 128
                    tb, tbk = tok[li % 2], f'g_tok{li % 2}'
                    qb, qbk = qk[li % 2], f'g_qk{li % 2}'
                    os_, osk = osb[li % 2], f'g_osb{li % 2}'
                    li += 1
                    P.dma('sync', tb[:], dr['TOK'][t0:t0 + 128, :], writes=[tbk])
                    P.dma('sync', qb[:], dr['BCT'].rearrange("(f p) t -> p f t", p=128)[:, :, t0:t0 + 128], writes=[qbk])
                    lac = gt_tok[:, c, 32 + d * 8:32 + (d + 1) * 8]
                    P.op('tensor', lambda e, lac=lac: e.matmul(pX[0][:, 400:408], lhsT=TRI[:], rhs=lac, start=True, stop=True), reads=[tk_, 'g_gt'], writes=['PS:g_pX0'])
                    P.op('tensor', lambda e, lac=lac: e.matmul(pX[0][:, 420:428], lhsT=self.ones_f[:], rhs=lac, start=True, stop=True), reads=['ones_f', 'g_gt'], writes=['PS:g_pX0'])
                    P.op(a, lambda e: e.activation(out=ecum[:], in_=pX[0][:, 400:408], func=AF.Exp), reads=['PS:g_pX0'], writes=['g_ecum'])
                    P.op(a, lambda e: e.activation(out=cdec[:], in_=pX[0][:, 420:428], func=AF.Exp), reads=['PS:g_pX0'], writes=['g_cdec'])
                    for hp in range(4):
                        hs = (2 * hp, 2 * hp + 1)
                        HV = {}
                        for r, h in enumerate(hs):
                            HV[r] = dict(
                                gcol=gt_tok[:, c, 32 + d * 8 + h:32 + d * 8 + h + 1],
                                bcol=gt_tok[:, c, d * 8 + h:d * 8 + h + 1],
                                kT=qb[:, 8 + h, :], qT=qb[:, h, :],
                                vtok=tb[:, h * 128:(h + 1) * 128], ktok=tb[:, 1024 + h * 128:1024 + (h + 1) * 128])
                        for r, h in enumerate(hs):
                            H = HV[r]
                            P.op(v, lambda e, r=r, H=H: e.tensor_scalar(out=Lm[r][:], in0=MASK[:], scalar1=H['gcol'], scalar2=None, op0=ALU.mult), reads=[mk, 'g_gt'], writes=[f'g_Lm{r}'])
                        for r, h in enumerate(hs):
                            H = HV[r]
                            P.op('tensor', lambda e, r=r: e.matmul(pX[r][:, 0:128], lhsT=Lm[r][:], rhs=TRI[:], start=True, stop=False), reads=[f'g_Lm{r}', tk_], writes=[f'PS:g_pX{r}'])
                            P.op('tensor', lambda e, r=r: e.matmul(pX[r][:, 0:128], lhsT=self.negI[:], rhs=MASK[:], start=False, stop=True), reads=['negI', mk], writes=[f'PS:g_pX{r}'])
                            P.op('tensor', lambda e, r=r, H=H: e.matmul(pX[r][:, 128:256], lhsT=H['kT'], rhs=H['kT'], start=True, stop=True), reads=[qbk], writes=[f'PS:g_pX{r}'])
                            P.op('tensor', lambda e, r=r, H=H: e.matmul(pX[r][:, 256:384], lhsT=H['kT'], rhs=H['qT'], start=True, stop=True), reads=[qbk], writes=[f'PS:g_pX{r}'])
                        for r, h in enumerate(hs):
                            H = HV[r]
                            P.op(a, lambda e, r=r: e.activation(out=Dm[r][:], in_=pX[r][:, 0:128], func=AF.Exp), reads=[f'PS:g_pX{r}'], writes=[f'g_Dm{r}'])
                            P.op(v, lambda e, r=r: e.tensor_tensor(out=tX[r][:], in0=Dm[r][:], in1=pX[r][:, 128:256], op=ALU.mult), reads=[f'g_Dm{r}', f'PS:g_pX{r}'], writes=[f'g_tX{r}'])
                            P.op(v, lambda e, r=r: e.tensor_tensor(out=WT[r][:], in0=Dm[r][:], in1=pX[r][:, 256:384], op=ALU.mult), reads=[f'g_Dm{r}', f'PS:g_pX{r}'], writes=[f'g_WT{r}'])
                            P.op(v, lambda e, r=r, H=H: e.scalar_tensor_tensor(out=Xp[r][0][:], in0=tX[r][:], scalar=H['bcol'], in1=STR[:], op0=ALU.mult, op1=ALU.mult), reads=[f'g_tX{r}', 'g_gt', sk_], writes=[f'g_Xp{r}0'])
                            P.op(g_, lambda e, r=r, H=H: e.tensor_copy(out=rr[r][0][:, 0:128], in_=H['vtok']), reads=[tbk], writes=[f'g_rr{r}0'])
                            P.op(g_, lambda e, r=r, H=H, h=h: e.tensor_scalar(out=rr[r][0][:, 128:256], in0=H['ktok'], scalar1=ecum[:, h:h + 1], scalar2=None, op0=ALU.mult), reads=[tbk, 'g_ecum'], writes=[f'g_rr{r}0'])
                        for r, h in enumerate(hs):
                            P.op('tensor', lambda e, r=r: e.matmul(pN[r][:, 0:128], lhsT=Xp[r][0][:], rhs=self.ident_f[:], start=True, stop=True), reads=[f'g_Xp{r}0', 'ident_f'], writes=[f'PS:g_pN{r}'])
                            P.op('tensor', lambda e, r=r: e.matmul(pX[r][:, 0:256], lhsT=Xp[r][0][:], rhs=rr[r][0][:], start=True, stop=True), reads=[f'g_Xp{r}0', f'g_rr{r}0'], writes=[f'PS:g_pX{r}'])
                        for r, h in enumerate(hs):
                            P.op(a, lambda e, r=r: e.copy(out=Np[r][0][:], in_=pN[r][:, 0:128]), reads=[f'PS:g_pN{r}'], writes=[f'g_Np{r}0'])
                            P.op(v, lambda e, r=r: e.tensor_tensor(out=rr[r][1][:], in0=rr[r][0][:], in1=pX[r][:, 0:256], op=ALU.subtract), reads=[f'g_rr{r}0', f'PS:g_pX{r}'], writes=[f'g_rr{r}1'])
                        cur = 1
                        xi = 0
                        for lev in range(6):
                            nx = 1 - xi
                            for r, h in enumerate(hs):
                                P.op('tensor', lambda e, r=r, xi=xi: e.matmul(pN[r][:, 0:128], lhsT=Np[r][xi][:], rhs=Xp[r][xi][:], start=True, stop=True), reads=[f'g_Np{r}{xi}', f'g_Xp{r}{xi}'], writes=[f'PS:g_pN{r}'])
                                if lev < 5:
                                    P.op('tensor', lambda e, r=r, xi=xi: e.matmul(pN[r][:, 128:256], lhsT=Xp[r][xi][:], rhs=Np[r][xi][:], start=True, stop=True), reads=[f'g_Np{r}{xi}', f'g_Xp{r}{xi}'], writes=[f'PS:g_pN{r}'])
                            for r, h in enumerate(hs):
                                P.op(a, lambda e, r=r, nx=nx: e.copy(out=Xp[r][nx][:], in_=pN[r][:, 0:128]), reads=[f'PS:g_pN{r}'], writes=[f'g_Xp{r}{nx}'])
                                if lev < 5:
                                    P.op(v, lambda e, r=r, nx=nx: e.tensor_copy(out=Np[r][nx][:], in_=pN[r][:, 128:256]), reads=[f'PS:g_pN{r}'], writes=[f'g_Np{r}{nx}'])
                            for r, h in enumerate(hs):
                                P.op('tensor', lambda e, r=r, nx=nx, cur=cur: e.matmul(pX[r][:, 0:256], lhsT=Xp[r][nx][:], rhs=rr[r][cur][:], start=True, stop=True), reads=[f'g_Xp{r}{nx}', f'g_rr{r}{cur}'], writes=[f'PS:g_pX{r}'])
                            for r, h in enumerate(hs):
                                P.op(v, lambda e, r=r, cur=cur: e.tensor_tensor(out=rr[r][1 - cur][:], in0=rr[r][cur][:], in1=pX[r][:, 0:256], op=ALU.add), reads=[f'g_rr{r}{cur}', f'PS:g_pX{r}'], writes=[f'g_rr{r}{1 - cur}'])
                            cur = 1 - cur
                            xi = nx
                        for r, h in enumerate(hs):
                            R_ = rr[r][cur]
                            Rk = f'g_rr{r}{cur}'
                            P.op('tensor', lambda e, r=r, R_=R_: e.matmul(pN[r][:, 256:384], lhsT=R_[:, 128:256], rhs=self.ident_f[:], start=True, stop=True), reads=[Rk, 'ident_f'], writes=[f'PS:g_pN{r}'])
                        for r, h in enumerate(hs):
                            P.op(a, lambda e, r=r: e.copy(out=wT[r][:], in_=pN[r][:, 256:384]), reads=[f'PS:g_pN{r}'], writes=[f'g_wT{r}'])
                        for r, h in enumerate(hs):
                            P.op('tensor', lambda e, r=r, h=h: e.matmul(pA[r][:, 128:256], lhsT=wT[r][:], rhs=S_f[:, h, :], start=True, stop=True), reads=[f'g_wT{r}', ('g_Sf', h)], writes=[f'PS:g_pA{r}'])
                        for r, h in enumerate(hs):
                            H = HV[r]
                            R_ = rr[r][cur]
                            Rk = f'g_rr{r}{cur}'
                            P.op(v, lambda e, r=r, R_=R_: e.scalar_tensor_tensor(out=vn[r][:], in0=pA[r][:, 128:256], scalar=-1.0, in1=R_[:, 0:128], op0=ALU.mult, op1=ALU.add), reads=[f'PS:g_pA{r}', Rk], writes=[f'g_vn{r}'])
                            P.op(v, lambda e, r=r, H=H: e.tensor_scalar(out=v1[r][:], in0=vn[r][:], scalar1=H['bcol'], scalar2=None, op0=ALU.mult), reads=[f'g_vn{r}', 'g_gt'], writes=[f'g_v1{r}'])
                            P.op(g_, lambda e, r=r: e.tensor_scalar(out=v2[r][:], in0=v1[r][:], scalar1=Dm[r][:, ecol:ecol + 1], scalar2=None, op0=ALU.mult), reads=[f'g_v1{r}', f'g_Dm{r}'], writes=[f'g_v2{r}'])
                        for r, h in enumerate(hs):
                            H = HV[r]
                            P.op('tensor', lambda e, r=r: e.matmul(pA[r][:, 0:128], lhsT=WT[r][:], rhs=v1[r][:], start=True, stop=True), reads=[f'g_WT{r}', f'g_v1{r}'], writes=[f'PS:g_pA{r}'])
                            P.op('tensor', lambda e, r=r, h=h, H=H: e.matmul(pA[r][:, 256:384], lhsT=H['qT'], rhs=S_b[:, h, :], start=True, stop=True), reads=[qbk, ('g_Sb', h)], writes=[f'PS:g_pA{r}'])
                            P.op('tensor', lambda e, r=r, H=H: e.matmul(pA[r][:, 384:512], lhsT=H['ktok'], rhs=v2[r][:], start=True, stop=True), reads=[tbk, f'g_v2{r}'], writes=[f'PS:g_pA{r}'])
                        for r, h in enumerate(hs):
                            P.op(a, lambda e, r=r: e.copy(out=yA[r][:], in_=pA[r][:, 0:128]), reads=[f'PS:g_pA{r}'], writes=[f'g_yA{r}'])
                            P.op(v, lambda e, r=r, h=h, os_=os_: e.scalar_tensor_tensor(out=os_[:, h * 128:(h + 1) * 128], in0=pA[r][:, 256:384], scalar=ecum[:, h:h + 1], in1=yA[r][:], op0=ALU.mult, op1=ALU.add), reads=[f'PS:g_pA{r}', 'g_ecum', f'g_yA{r}'], writes=[osk])
                            P.op(v, lambda e, r=r, h=h: e.scalar_tensor_tensor(out=S_f[:, h, :], in0=S_f[:, h, :], scalar=cdec[:, h:h + 1], in1=pA[r][:, 384:512], op0=ALU.mult, op1=ALU.add), reads=[('g_Sf', h), 'g_cdec', f'PS:g_pA{r}'], writes=[('g_Sf', h)])
                            P.op(a, lambda e, h=h: e.copy(out=S_b[:, h, :], in_=S_f[:, h, :]), reads=[('g_Sf', h)], writes=[('g_Sb', h)])
                    if d == 0:
                        P.dma('gpsimd', dr['YF'][t0:t0 + 128, 1024:2048], os_[:], reads=[osk], writes=[('YFg', c)])
                    else:
                        P.dma('sync', ofl[:], dr['YF'][t0:t0 + 128, 1024:2048], reads=[('YFg', c)], writes=['g_ofl'])
                        P.op(v, lambda e, os_=os_: e.tensor_tensor(out=ofl[:], in0=ofl[:], in1=os_[:], op=ALU.add), reads=['g_ofl', osk], writes=['g_ofl'])
                        for h in range(8):
                            P.op(a, lambda e, h=h: e.activation(out=sqj[:], in_=ofl[:, h * 128:(h + 1) * 128], func=AF.Square, accum_out=ss[:, h:h + 1]), reads=['g_ofl'], writes=['g_sqj', 'g_ss'])
                        P.op(a, lambda e: e.activation(out=ss[:], in_=ss[:], func=AF.Sqrt, scale=1.0 / 128, bias=self.eps_col[:]), reads=['g_ss', 'eps_col'], writes=['g_ss'])
                        P.op(v, lambda e: e.reciprocal(out=ss[:], in_=ss[:]), reads=['g_ss'], writes=['g_ss'])
                        for h in range(8):
                            P.op(v, lambda e, h=h: e.scalar_tensor_tensor(out=hn[:, h * 128:(h + 1) * 128], in0=ofl[:, h * 128:(h + 1) * 128], scalar=ss[:, h:h + 1], in1=nwB[:], op0=ALU.mult, op1=ALU.mult),
                                 reads=['g_ofl', 'g_ss', 'g_nwB'], writes=['g_hn'])
                        if c < 2:
                            tok0, nt, ch = 0, TC, c
                        else:
                            tok0, nt, ch = TC + ((c - 2) // 4) * 512, 512, (c - 2) % 4
                        for f in range(8):
                            P.op('tensor', lambda e, f=f: e.transpose(out=ptT[:, f, :], in_=hn[:, f * 128:(f + 1) * 128], identity=self.ident_b[:]), reads=['g_hn', 'ident_b'], writes=['PS:g_ptT'])
                        P.op(v, lambda e, ch=ch: e.tensor_copy(out=ytb[:, :, ch * 128:(ch + 1) * 128], in_=ptT[:]), reads=['PS:g_ptT'], writes=['g_ytb'])
                        if ch == 0:
                            P.dma('sync', zT[:, :, :nt], PT[R_ZG:R_ZG + 1024, :].rearrange("(f p) t -> p f t", p=128)[:, :, tok0:tok0 + nt], writes=['g_zT'])
                            P.op(a, lambda e, nt=nt: e.activation(out=zT[:, :, :nt], in_=zT[:, :, :nt], func=AF.Silu), reads=['g_zT'], writes=['g_zT'])
                            P.op(v, lambda e, nt=nt: e.tensor_tensor(out=ot[:, :, :nt], in0=ytb[:, :, :nt], in1=zT[:, :, :nt], op=ALU.mult), reads=['g_ytb', 'g_zT'], writes=['g_ot'])
                            P.dma('gpsimd', dr['YT'][0:1024, :].rearrange("(f p) t -> p f t", p=128)[:, :, tok0:tok0 + nt], ot[:, :, :nt], reads=['g_ot'], writes=[('YT', 'gdn', tok0)])
            P.barrier()


B.phase_gdn = phase_gdn


def phase_final(self, src):
    P, nc, dr = self.P, self.nc, self.dr
    sb, ps = self.sb, self.ps
    with ExitStack() as es:
        xt = [sb(es, f"fin_x{i}", [128, KC, 512], F32) for i in range(2)]
        sq = sb(es, "fin_sq", [128, KC, 512], BF16)
        rstd = sb(es, "fin_rstd", [128, 512], F32)
        fw = sb(es, "fin_w", [128, KC], F32)
        pss = ps(es, "fin_pss", [128, 512], F32)
        P.dma('sync', fw[:], dr['fnwT'], writes=['fin_w'])
        for bi, (tok0, nt, m) in enumerate(self.blocks()[1:]):
            x, xk = xt[bi % 2], f'fin_x{bi % 2}'
            P.dma('sync', x[:, :, :nt], src.rearrange("(kc p) t -> p kc t", p=128)[:, :, tok0:tok0 + nt], writes=[xk])
            P.op('scalar', lambda e, x=x, nt=nt: e.activation(out=sq[:, :, :nt], in_=x[:, :, :nt], func=AF.Square), reads=[xk], writes=['fin_sq'])
            for kc in range(KC):
                P.op('tensor', lambda e, kc=kc, nt=nt: e.matmul(pss[:, :nt], lhsT=self.ones_b[:], rhs=sq[:, kc, :nt], start=(kc == 0), stop=(kc == KC - 1)), reads=['fin_sq', 'ones_b'], writes=['PS:fin_pss'])
            P.op('scalar', lambda e, nt=nt: e.activation(out=rstd[:, :nt], in_=pss[:, :nt], func=AF.Sqrt, scale=1.0 / D, bias=self.eps_col[:]), reads=['PS:fin_pss', 'eps_col'], writes=['fin_rstd'])
            P.op('vector', lambda e, nt=nt: e.reciprocal(out=rstd[:, :nt], in_=rstd[:, :nt]), reads=['fin_rstd'], writes=['fin_rstd'])
            for kc in range(KC):
                P.op('vector', lambda e, kc=kc, x=x, nt=nt: e.scalar_tensor_tensor(out=x[:, kc, :nt], in0=x[:, kc, :nt], scalar=fw[:, kc:kc + 1], in1=rstd[:, :nt], op0=ALU.mult, op1=ALU.mult),
                     reads=[xk, 'fin_w', 'fin_rstd'], writes=[xk])
            P.dma('gpsimd', dr['outT'].rearrange("(kc p) t -> p kc t", p=128)[:, :, tok0 - TC:tok0 - TC + nt], x[:, :, :nt], reads=[xk], writes=[('outT', tok0)])
        P.barrier()


B.phase_final = phase_final
```

```python
import numpy as np
from contextlib import ExitStack
import concourse.bass as bass
import concourse.mybir as mybir
from concourse.bass_utils import run_bass_kernel_spmd

F32 = mybir.dt.float32
BF16 = mybir.dt.bfloat16
I32 = mybir.dt.int32
AF = mybir.ActivationFunctionType
ALU = mybir.AluOpType
AX = mybir.AxisListType

D = 2048
KC = 16
TC = 256
TL = 4096
T = TC + TL
FFN = 5504
EPS = 1e-6
EVEN_IN = 4656
ODD_IN = 7216
NEG = -30000.0

SEM_LIMIT = 4000
DMA_SLOT_LIMIT = 1200
DMA_POOL = 6


class Prog:
    ENG = ('sync', 'scalar', 'vector', 'gpsimd', 'tensor')

    def __init__(self, nc, es):
        self.nc = nc
        self.es = es
        self.e = dict(sync=nc.sync, scalar=nc.scalar, vector=nc.vector,
                      gpsimd=nc.gpsimd, tensor=nc.tensor)
        self.semh = []
        self.cur = {}
        self.cnt = {}
        self.known = {e: {} for e in self.ENG}
        self.lastw = {}
        self.readers = {}
        self.pool = {}
        self.pidx = {}
        self.nins = {e: 0 for e in self.ENG}
        for e in self.ENG:
            self._fresh(e)

    def _newsem(self, name):
        h = self.es.enter_context(self.nc.semaphore(name))
        self.semh.append(h)
        return len(self.semh) - 1

    def _fresh(self, e):
        self.cur[e] = self._newsem(f"s_{e}_{len(self.semh)}")
        self.cnt[e] = 0

    def _deps(self, reads, writes):
        d = {}

        def add(tok):
            if tok is None:
                return
            sk, val, pe = tok
            if sk not in d or d[sk][0] < val:
                d[sk] = (val, pe)
        for k in reads:
            add(self.lastw.get(k))
        for k in writes:
            add(self.lastw.get(k))
            for t in self.readers.get(k, ()):
                add(t)
        return d

    def _update(self, reads, writes, tok):
        for k in reads:
            self.readers.setdefault(k, []).append(tok)
        for k in writes:
            self.lastw[k] = tok
            self.readers[k] = []

    def _wait(self, eng, sk, val):
        if self.known[eng].get(sk, 0) >= val:
            return
        self.e[eng].wait_ge(self.semh[sk], val)
        self.known[eng][sk] = val
        self.nins[eng] += 1

    def _waits(self, eng, deps):
        for sk, (val, pe) in deps.items():
            if pe == 'tensor' and eng == 'tensor':
                continue
            self._wait(eng, sk, val)

    @staticmethod
    def _excl(reads, writes):
        ex = [k for k in reads if isinstance(k, str) and k.startswith('PS:')]
        if ex:
            reads = [k for k in reads if k not in ex]
            writes = list(writes) + ex
        return reads, writes

    def op(self, eng, fn, reads=(), writes=()):
        reads, writes = self._excl(reads, writes)
        deps = self._deps(reads, writes)
        self._waits(eng, deps)
        ins = fn(self.e[eng])
        if self.cnt[eng] >= SEM_LIMIT:
            self._fresh(eng)
        self.cnt[eng] += 1
        ins.then_inc(self.semh[self.cur[eng]], 1)
        tok = (self.cur[eng], self.cnt[eng], eng)
        self._update(reads, writes, tok)
        self.nins[eng] += 1
        return tok

    def dma(self, q, out, in_, reads=(), writes=(), **kw):
        deps = self._deps(reads, writes)
        self._waits(q, deps)
        E = self.e[q]
        if q not in self.pool:
            self.pool[q] = [[self._newsem(f"d_{q}_{i}_{len(self.semh)}"), 0] for i in range(DMA_POOL)]
            self.pidx[q] = 0
        i = self.pidx[q] % DMA_POOL
        self.pidx[q] += 1
        slot = self.pool[q][i]
        if slot[1] >= DMA_SLOT_LIMIT:
            self._wait(q, slot[0], 16 * slot[1])
            slot[0] = self._newsem(f"d_{q}_{i}_{len(self.semh)}")
            slot[1] = 0
        sk, n = slot
        if n > 0:
            self._wait(q, sk, 16 * n)
        ins = E.dma_start(out=out, in_=in_, **kw)
        ins.then_inc(self.semh[sk], 16)
        slot[1] = n + 1
        tok = (sk, 16 * (n + 1), 'dma')
        self._update(reads, writes, tok)
        self.nins[q] += 1
        return tok

    def barrier(self):
        toks = []
        for q, slots in self.pool.items():
            for sk, n in slots:
                if n > 0:
                    toks.append((sk, 16 * n))
        for e in self.ENG:
            if self.cnt[e] > 0:
                toks.append((self.cur[e], self.cnt[e]))
        for e in self.ENG:
            for sk, val in toks:
                self._wait(e, sk, val)
        self.lastw.clear()
        self.readers.clear()

    def finish(self):
        self.barrier()


class B:
    def __init__(self, stage=99, dbg=(), sub=99):
        self.sub = sub
        self.stage = stage
        self.dbg = set(dbg)
        self.nc = bass.Bass("TRN2", target_bir_lowering=False)
        self.dr = {}

    def din(self, name, shape, dt=F32):
        self.dr[name] = self.nc.dram_tensor(name, list(shape), dt, kind="ExternalInput").ap()
        return self.dr[name]

    def dout(self, name, shape, dt=F32):
        self.dr[name] = self.nc.dram_tensor(name, list(shape), dt, kind="ExternalOutput").ap()
        return self.dr[name]

    def dscr(self, name, shape, dt=F32):
        kind = "ExternalOutput" if name in self.dbg else "Internal"
        self.dr[name] = self.nc.dram_tensor(name, list(shape), dt, kind=kind).ap()
        return self.dr[name]

    def _uniq(self, name):
        self._names = getattr(self, '_names', {})
        n = self._names.get(name, 0)
        self._names[name] = n + 1
        return name if n == 0 else f"{name}_u{n}"

    def sb(self, es, name, shape, dt=F32):
        return es.enter_context(self.nc.sbuf_tensor(self._uniq(name), list(shape), dt))

    def ps(self, es, name, shape, dt=F32):
        return es.enter_context(self.nc.psum_tensor(self._uniq(name), list(shape), dt))

    def build(self):
        nc = self.nc
        din = self.din
        din("xT", [D, T])
        din("ccT", [128, KC, 2])
        din("mod_w", [2, D, 6 * D])
        din("mod_bT", [2, 128, 96])
        din("nmwT", [2, 128, KC])
        din("nfwT", [2, 128, KC])
        din("ev_in_w", [D, EVEN_IN])
        din("od_in_w", [D, ODD_IN])
        self.dscr("XS", [D, T + 64])
        self.dscr("PT", [ODD_IN, T])
        self.dscr("WB", [D, ODD_IN], BF16)
        din("ssd_cw", [128, 20, 3])
        din("ssd_cb", [128, 20])
        din("ssd_dtb", [48, 1])
        din("ssd_alog", [48, 1])
        din("ssd_d", [1, 24])
        din("ssd_nw", [128, 12])
        self.dscr("TOK", [T, 2048], BF16)
        self.dscr("BCT", [2048, T], BF16)
        self.dscr("YF", [T, 2048])
        self.dscr("YT", [D, T + 64], BF16)
        for nm in ("s5_lre", "s5_lim", "s5_dlt"):
            din(nm, [128, 32])
        for nm in ("s5_bre", "s5_bim", "s5_cre", "s5_cim"):
            din(nm, [128, 32, 16])
        din("s5_dsk", [128, 4])
        din("s5_glu_w", [512, 512])
        din("fnwT", [128, KC])
        din("gdn_dtb", [16, 1])
        din("gdn_alog", [16, 1])
        din("gdn_cw", [128, 24, 3])
        din("gdn_nw", [1, 128])
        din("ml_ib", [8, 1])
        din("ml_fb", [8, 1])
        din("ml_nw", [1, 1024])
        din("ev_out_w", [D, D])
        din("od_out_w", [D, D])
        din("ffn_up_w", [2, D, 2 * FFN])
        din("ffn_down_w", [2, FFN, D])
        din("ffn_cw", [2, 128, 43, 9])
        self.dscr("WO", [D, D], BF16)
        self.dscr("WU", [D, 2 * FFN], BF16)
        self.dscr("WD", [FFN, D], BF16)
        self.dscr("XS2", [D, T + 64])
        self.dout("outT", [D, TL // 2])
        din("halo_mask", [128, 2])
        if 'YTin' in self.dbg:
            din('YTin', [D, T])
        if 'XSin' in self.dbg:
            din('XSin', [D, T])
        dshapes = {'dbg_mod': [128, 96, 2], 'dbg_h': [D, T], 'dbg_gates': [2, 128, 34 * 48]}
        for k in self.dbg:
            if k in dshapes:
                self.dout(k, dshapes[k])
        with ExitStack() as es:
            self.P = Prog(nc, es)
            self.consts(es)
            for layer in range(2):
                if 'XSin' in self.dbg:
                    if layer == 0:
                        for r in range(0, D, 512):
                            self.P.dma('gpsimd', self.dr['XS'][r:r + 512, :], self.dr['XSin'][r:r + 512, :], writes=['xsin'])
                        self.P.barrier()
                        continue
                self.layer(layer)
                if self.stage <= layer * 10 + 9:
                    break
            self.P.finish()
        return nc

    def consts(self, es):
        P = self.P
        sb = self.sb
        self.ident_b = sb(es, "ident_b", [128, 128], BF16)
        self.ident_f = sb(es, "ident_f", [128, 128], F32)
        self.ones_b = sb(es, "ones_b", [128, 128], BF16)
        self.ones_f = sb(es, "ones_f", [128, 128], F32)
        self.negI = sb(es, "negI", [128, 128], F32)
        self.MGT = sb(es, "MGT", [128, 128], F32)
        self.MLT = sb(es, "MLT", [128, 128], F32)
        self.TLE = sb(es, "TLE", [128, 128], F32)
        self.TGE = sb(es, "TGE", [128, 128], F32)
        g = 'gpsimd'
        P.op(g, lambda e: e.memset(self.ones_f[:], 1.0), writes=['ones_f'])
        P.op(g, lambda e: e.memset(self.ones_b[:], 1.0), writes=['ones_b'])

        def sel(out, cm, step, cmp, key, fill=0.0, src=None):
            src = self.ones_f if src is None else src
            P.op(g, lambda e: e.affine_select(out=out[:], in_=src[:], pattern=[[step, 128]], base=0,
                                              channel_multiplier=cm, compare_op=cmp, fill=fill),
                 reads=['ones_f'], writes=[key])
        sel(self.ident_f, 1, -1, ALU.is_equal, 'ident_f')
        sel(self.MGT, 1, -1, ALU.is_gt, 'MGT')
        sel(self.MLT, -1, 1, ALU.is_gt, 'MLT')
        sel(self.TLE, -1, 1, ALU.is_ge, 'TLE')
        sel(self.TGE, 1, -1, ALU.is_ge, 'TGE')
        P.op(g, lambda e: e.tensor_copy(out=self.ident_b[:], in_=self.ident_f[:]), reads=['ident_f'], writes=['ident_b'])
        P.op(g, lambda e: e.tensor_scalar(out=self.negI[:], in0=self.ident_f[:], scalar1=NEG, scalar2=None, op0=ALU.mult),
             reads=['ident_f'], writes=['negI'])
        self.modT = sb(es, "modT", [128, 96, 2], F32)
        self.s1 = sb(es, "s1", [128, KC, 2], F32)
        self.s2 = sb(es, "s2", [128, KC, 2], F32)
        self.eps_col = sb(es, "eps_col", [128, 1], F32)
        P.op(g, lambda e: e.memset(self.eps_col[:], EPS), writes=['eps_col'])
        self.one_col = sb(es, "one_col", [128, 1], F32)
        P.op(g, lambda e: e.memset(self.one_col[:], 1.0), writes=['one_col'])
        self.half = {'sync': self.nc.sync.partition_id() // 4, 'gpsimd': self.nc.gpsimd.partition_id() // 4}
        self.hmask = sb(es, "hmask", [128, 2], F32)
        P.dma('sync', self.hmask[:], self.dr['halo_mask'], writes=['hmask'])
        zpad = sb(es, "zpad", [128, KC, 64], F32)
        zpadb = sb(es, "zpadb", [128, KC, 64], BF16)
        P.op(g, lambda e: e.memset(zpad[:], 0.0), writes=['zpad'])
        P.op(g, lambda e: e.memset(zpadb[:], 0.0), writes=['zpadb'])
        for nm in ('XS', 'XS2'):
            P.dma('sync', self.dr[nm].rearrange("(kc p) t -> p kc t", p=128)[:, :, T:T + 64], zpad[:], reads=['zpad'], writes=[nm + 'pad'])
        P.dma('sync', self.dr['YT'].rearrange("(kc p) t -> p kc t", p=128)[:, :, T:T + 64], zpadb[:], reads=['zpadb'], writes=['YTpad'])
        P.barrier()

    def dyn(self, q, static):
        return self.half[q] * 2048 + static

    def layer(self, layer):
        import os
        if os.environ.get('SKIP12'):
            self.phase_ssd()
            return
        self.phase_mod(layer)
        if self.stage <= layer * 10 + 1:
            return
        self.phase_inproj(layer)
        if self.stage <= layer * 10 + 2:
            return
        if layer == 0 and 'YTin' not in self.dbg:
            import os
            if not os.environ.get('NOS5'):
                self.phase_s5()
            if self.stage <= layer * 10 + 2 or os.environ.get('NOSSD'):
                return
            self.phase_ssd()
            if self.stage <= layer * 10 + 3:
                return
        if layer == 1 and 'YTin' not in self.dbg:
            import os
            if not os.environ.get('NOGDN'):
                self.phase_gdn()
            if not os.environ.get('NOML'):
                self.phase_mlstm()
            if self.stage <= layer * 10 + 3:
                return
        if 'YTin' in self.dbg:
            self.P.dma('gpsimd', self.dr['YT'], self.dr['YTin'], writes=['ytin'])
            self.P.barrier()
        src = self.dr['xT'] if layer == 0 else self.dr['XS']
        self.phase_outproj(layer, src, self.dr['XS2'])
        if self.stage <= layer * 10 + 4:
            return
        self.phase_ffn(layer, self.dr['XS2'], self.dr['XS'])
        if layer == 1:
            self.phase_final(self.dr['XS'])

    def phase_mod(self, layer):
        P, nc = self.P, self.nc
        dr = self.dr
        with ExitStack() as es:
            sb, ps = self.sb, self.ps
            PW = 768
            wt = [sb(es, f"modw{i}", [128, KC, PW], F32) for i in range(2)]
            cc = sb(es, "cc", [128, KC, 2], F32)
            scc = sb(es, "scc", [128, KC, 2], F32)
            mb = sb(es, "mb", [128, 96], F32)
            nmw = sb(es, "nmw", [128, KC], F32)
            nfw = sb(es, "nfw", [128, KC], F32)
            pm = ps(es, "pm", [128, 96, 2], F32)
            P.dma('sync', cc[:], dr["ccT"], writes=['cc'])
            P.dma('sync', mb[:], dr["mod_bT"][layer], writes=['mb'])
            P.dma('sync', nmw[:], dr["nmwT"][layer], writes=['nmw'])
            P.dma('sync', nfw[:], dr["nfwT"][layer], writes=['nfw'])
            P.op('scalar', lambda e: e.activation(out=scc[:], in_=cc[:], func=AF.Silu), reads=['cc'], writes=['scc'])
            wsrc = dr["mod_w"][layer].rearrange("(kc p) n -> p kc n", p=128)
            for pn in range(16):
                w = wt[pn % 2]
                wk = f'modw{pn % 2}'
                P.dma('sync' if pn % 2 == 0 else 'gpsimd', w[:], wsrc[:, :, pn * PW:(pn + 1) * PW], writes=[wk])
                for jj in range(6):
                    j = pn * 6 + jj
                    for kc in range(KC):
                        P.op('tensor', lambda e, w=w, jj=jj, kc=kc, j=j: e.matmul(
                            pm[:, j, :], lhsT=w[:, kc, jj * 128:(jj + 1) * 128], rhs=scc[:, kc, :],
                            start=(kc == 0), stop=(kc == KC - 1)), reads=[wk, 'scc'], writes=['PS:pm'])
            for m in range(2):
                P.op('vector', lambda e, m=m: e.tensor_tensor(out=self.modT[:, :, m], in0=pm[:, :, m], in1=mb[:], op=ALU.add),
                     reads=['PS:pm', 'mb'], writes=['modT'])
            for m in range(2):
                P.op('vector', lambda e, m=m: e.scalar_tensor_tensor(
                    out=self.s1[:, :, m], in0=self.modT[:, 16:32, m], scalar=1.0, in1=nmw[:], op0=ALU.add, op1=ALU.mult),
                    reads=['modT', 'nmw'], writes=['s1'])
                P.op('vector', lambda e, m=m: e.scalar_tensor_tensor(
                    out=self.s2[:, :, m], in0=self.modT[:, 64:80, m], scalar=1.0, in1=nfw[:], op0=ALU.add, op1=ALU.mult),
                    reads=['modT', 'nfw'], writes=['s2'])
            if 'dbg_mod' in self.dbg:
                P.dma('sync', dr['dbg_mod'], self.modT[:], reads=['modT'], writes=['dbg_mod'])
            P.barrier()

    def blocks(self):
        return [(0, TC, 1)] + [(TC + i * 512, 512, 0) for i in range(8)]

    def norm_block(self, src, tok0, nt, m, scale_t, shift_lo, bufs, keys):
        P = self.P
        xt, sq, hT, rstd, tmp, pss = bufs['xt'], bufs['sq'], bufs['hT'], bufs['rstd'], bufs['tmp'], bufs['pss']
        kx, ksq, kh, kr, kt, kp = keys
        P.dma('sync', xt[:, :, :nt], src.rearrange("(kc p) t -> p kc t", p=128)[:, :, tok0:tok0 + nt], writes=[kx])
        P.op('scalar', lambda e: e.activation(out=sq[:, :, :nt], in_=xt[:, :, :nt], func=AF.Square), reads=[kx], writes=[ksq])
        for kc in range(KC):
            P.op('tensor', lambda e, kc=kc: e.matmul(pss[:, :nt], lhsT=self.ones_b[:], rhs=sq[:, kc, :nt],
                                                     start=(kc == 0), stop=(kc == KC - 1)), reads=[ksq, 'ones_b'], writes=[kp])
        P.op('scalar', lambda e: e.activation(out=rstd[:, :nt], in_=pss[:, :nt], func=AF.Sqrt, scale=1.0 / D, bias=self.eps_col[:]),
             reads=[kp, 'eps_col'], writes=[kr])
        P.op('vector', lambda e: e.reciprocal(out=rstd[:, :nt], in_=rstd[:, :nt]), reads=[kr], writes=[kr])
        for kc in range(KC):
            tk = kt + str(kc % 2)
            t = tmp[kc % 2]
            P.op('vector', lambda e, kc=kc, t=t: e.tensor_tensor(out=t[:, :nt], in0=xt[:, kc, :nt], in1=rstd[:, :nt], op=ALU.mult),
                 reads=[kx, kr], writes=[tk])
            P.op('scalar', lambda e, kc=kc, t=t: e.activation(out=hT[:, kc, :nt], in_=t[:, :nt], func=AF.Identity,
                                                              scale=scale_t[:, kc, m:m + 1], bias=self.modT[:, shift_lo + kc, m:m + 1]),
                 reads=[tk, 'modT', 's1', 's2'], writes=[kh + str(kc)])

    def phase_inproj(self, layer):
        P, nc, dr = self.P, self.nc, self.dr
        nin = EVEN_IN if layer == 0 else ODD_IN
        wsrc = dr["ev_in_w"] if layer == 0 else dr["od_in_w"]
        WB = dr["WB"]
        for r in range(0, D, 256):
            P.dma('gpsimd', WB[r:r + 256, :nin], wsrc[r:r + 256, :], writes=[('WB', r)])
        src = dr["xT"] if layer == 0 else dr["XS"]
        ntile = [(c0, min(128, nin - c0)) for c0 in range(0, nin, 128)]
        PWT = 4
        with ExitStack() as es:
            sb, ps = self.sb, self.ps
            bufs = dict(xt=sb(es, "xt", [128, KC, 512], F32), sq=sb(es, "sq", [128, KC, 512], BF16),
                        hT=sb(es, "hT", [128, KC, 512], BF16), rstd=sb(es, "rstd", [128, 512], F32),
                        tmp=[sb(es, f"ntmp{i}", [128, 512], F32) for i in range(2)],
                        pss=ps(es, "pss", [128, 512], F32))
            wb = [sb(es, f"wb{i}", [128, KC, PWT * 128], BF16) for i in range(2)]
            stg = [sb(es, f"stg{i}", [128, 512], F32) for i in range(3)]
            pacc = [ps(es, f"pacc{i}", [128, 512], F32) for i in range(4)]
            wv = WB.rearrange("(kc p) n -> p kc n", p=128)
            hkeys = ['hT' + str(kc) for kc in range(KC)]
            it = 0
            pi = 0
            for (tok0, nt, m) in self.blocks():
                self.norm_block(src, tok0, nt, m, self.s1, 0, bufs, ('xt', 'sq', 'hT', 'rstd', 'ntmp', 'PS:pss'))
                if 'dbg_h' in self.dbg:
                    hf = bufs['xt']
                    P.op('vector', lambda e: e.tensor_copy(out=hf[:, :, :nt], in_=bufs['hT'][:, :, :nt]), reads=hkeys + ['xt'], writes=['xt'])
                    P.dma('sync', dr['dbg_h'].rearrange("(kc p) t -> p kc t", p=128)[:, :, tok0:tok0 + nt], hf[:, :, :nt], reads=['xt'], writes=['dbg_h'])
                for p0 in range(0, len(ntile), PWT):
                    tiles = ntile[p0:p0 + PWT]
                    c0 = tiles[0][0]
                    cw = sum(t[1] for t in tiles)
                    w = wb[pi % 2]
                    wk = f'wb{pi % 2}'
                    pi += 1
                    P.dma('sync', w[:, :, :cw], wv[:, :, c0:c0 + cw], reads=[('WB', r) for r in range(0, D, 256)], writes=[wk])
                    for (tc0, tw) in tiles:
                        pa = pacc[it % 4]
                        pk = f'PS:pacc{it % 4}'
                        st = stg[it % 3]
                        sk = f'stg{it % 3}'
                        for kc in range(KC):
                            P.op('tensor', lambda e, kc=kc, pa=pa, w=w, tc0=tc0, tw=tw, c0=c0: e.matmul(
                                pa[:tw, :nt], lhsT=w[:, kc, tc0 - c0:tc0 - c0 + tw], rhs=bufs['hT'][:, kc, :nt],
                                start=(kc == 0), stop=(kc == KC - 1)), reads=[wk, hkeys[kc]], writes=[pk])
                        eng = 'scalar' if it % 2 == 0 else 'vector'
                        if eng == 'scalar':
                            P.op('scalar', lambda e, pa=pa, st=st, tw=tw: e.copy(out=st[:tw, :nt], in_=pa[:tw, :nt]), reads=[pk], writes=[sk])
                        else:
                            P.op('vector', lambda e, pa=pa, st=st, tw=tw: e.tensor_copy(out=st[:tw, :nt], in_=pa[:tw, :nt]), reads=[pk], writes=[sk])
                        P.dma('gpsimd', dr['PT'][tc0:tc0 + tw, tok0:tok0 + nt], st[:tw, :nt], reads=[sk], writes=[('PT', tc0, tok0)])
                        it += 1
            P.barrier()


def _ssd_methods():
    pass


def build_program(stage=99, dbg=()):
    b = B(stage, dbg)
    for name in dbg:
        pass
    return b


def make_inputs_small(inputs, b):
    f = np.float32
    x = np.asarray(inputs['x'], f)
    ctx = np.asarray(inputs['ctx'], f)
    c = np.asarray(inputs['c'], f)
    c_ctx = np.asarray(inputs['c_ctx'], f)
    xT = np.ascontiguousarray(np.concatenate([ctx[b], x[b]], axis=0).T)
    cc = np.stack([c[b], c_ctx], axis=0)
    ccT = np.ascontiguousarray(cc.reshape(2, KC, 128).transpose(2, 1, 0))
    return {"xT": xT, "ccT": ccT}


def halo_mask(core):
    top, bot = (0.0, 1.0) if core < 4 else (1.0, 0.0)
    return np.ascontiguousarray(np.tile(np.array([[top, bot]], np.float32), (128, 1)))


def make_inputs(inputs, b):
    f = np.float32
    x = np.asarray(inputs['x'], f)
    ctx = np.asarray(inputs['ctx'], f)
    c = np.asarray(inputs['c'], f)
    c_ctx = np.asarray(inputs['c_ctx'], f)
    xT = np.ascontiguousarray(np.concatenate([ctx[b], x[b]], axis=0).T)
    cc = np.stack([c[b], c_ctx], axis=0)
    ccT = np.ascontiguousarray(cc.reshape(2, KC, 128).transpose(2, 1, 0))
    mod_b = np.asarray(inputs['mod_b'], f)
    m = {
        "xT": xT, "ccT": ccT, "halo_mask": halo_mask(0),
        "mod_w": np.ascontiguousarray(np.asarray(inputs['mod_w'], f)),
        "mod_bT": np.ascontiguousarray(mod_b.reshape(2, 96, 128).transpose(0, 2, 1)),
        "nmwT": np.ascontiguousarray(np.asarray(inputs['norm_mix_w'], f).reshape(2, KC, 128).transpose(0, 2, 1)),
        "nfwT": np.ascontiguousarray(np.asarray(inputs['norm_ffn_w'], f).reshape(2, KC, 128).transpose(0, 2, 1)),
        "ev_in_w": np.ascontiguousarray(np.asarray(inputs['ev_in_w'], f)[0]),
        "od_in_w": np.ascontiguousarray(np.asarray(inputs['od_in_w'], f)[0]),
    }
    m["ev_out_w"] = np.ascontiguousarray(np.asarray(inputs['ev_out_w'], f)[0])
    m["od_out_w"] = np.ascontiguousarray(np.asarray(inputs['od_out_w'], f)[0])
    m["ffn_up_w"] = np.ascontiguousarray(np.asarray(inputs['ffn_up_w'], f))
    m["ffn_down_w"] = np.ascontiguousarray(np.asarray(inputs['ffn_down_w'], f))
    fcw = np.asarray(inputs['ffn_conv_w'], f)
    fcw = np.concatenate([fcw.reshape(2, 9, FFN), np.zeros((2, 9, 43 * 128 - FFN), f)], axis=2)
    m["ffn_cw"] = np.ascontiguousarray(fcw.reshape(2, 9, 43, 128).transpose(0, 3, 2, 1))
    m["fnwT"] = np.ascontiguousarray(np.asarray(inputs['final_norm_w'], f).reshape(KC, 128).T)
    m["gdn_dtb"] = np.ascontiguousarray(np.asarray(inputs['gdn_dt_bias'], f)[0].reshape(16, 1))
    m["gdn_alog"] = np.ascontiguousarray(np.asarray(inputs['gdn_a_log'], f)[0].reshape(16, 1))
    gcw = np.asarray(inputs['gdn_conv_w'], f)[0]
    m["gdn_cw"] = np.ascontiguousarray(gcw.reshape(3, 24, 128).transpose(2, 1, 0))
    m["gdn_nw"] = np.ascontiguousarray(np.asarray(inputs['gdn_norm_w'], f)[0].reshape(1, 128))
    m["ml_ib"] = np.ascontiguousarray(np.asarray(inputs['mlstm_igate_b'], f)[0].reshape(8, 1))
    m["ml_fb"] = np.ascontiguousarray(np.asarray(inputs['mlstm_fgate_b'], f)[0].reshape(8, 1))
    m["ml_nw"] = np.ascontiguousarray(np.asarray(inputs['mlstm_norm_w'], f)[0].reshape(1, 1024))
    def pair(x):
        sh = x.shape[3:]
        x = x.reshape((2, 16, 2, 64) + sh)
        x = np.moveaxis(x, (2, 3), (0, 1))
        return np.ascontiguousarray(x.reshape((128, 32) + sh))
    m["s5_lre"] = pair(np.asarray(inputs['s5_lam_re'], f)[0])
    m["s5_lim"] = pair(np.asarray(inputs['s5_lam_im'], f)[0])
    m["s5_dlt"] = pair(np.repeat(np.asarray(inputs['s5_log_step'], f)[0][:, :, None], 64, axis=2))
    m["s5_bre"] = pair(np.asarray(inputs['s5_b_re'], f)[0])
    m["s5_bim"] = pair(np.asarray(inputs['s5_b_im'], f)[0])
    m["s5_cre"] = pair(np.asarray(inputs['s5_c_re'], f)[0].transpose(0, 1, 3, 2))
    m["s5_cim"] = pair(np.asarray(inputs['s5_c_im'], f)[0].transpose(0, 1, 3, 2))
    m["s5_dsk"] = np.ascontiguousarray(np.asarray(inputs['s5_d'], f)[0].reshape(4, 128).T)
    m["s5_glu_w"] = np.ascontiguousarray(np.asarray(inputs['s5_glu_w'], f)[0])
    cw = np.asarray(inputs['ssd_conv_w'], f)[0]
    m["ssd_cw"] = np.ascontiguousarray(cw.reshape(3, 20, 128).transpose(2, 1, 0))
    m["ssd_cb"] = np.ascontiguousarray(np.asarray(inputs['ssd_conv_b'], f)[0].reshape(20, 128).T)
    m["ssd_dtb"] = np.ascontiguousarray(np.asarray(inputs['ssd_dt_bias'], f)[0].reshape(48, 1))
    m["ssd_alog"] = np.ascontiguousarray(np.asarray(inputs['ssd_a_log'], f)[0].reshape(48, 1))
    m["ssd_d"] = np.ascontiguousarray(np.asarray(inputs['ssd_d'], f)[0].reshape(1, 24))
    m["ssd_nw"] = np.ascontiguousarray(np.asarray(inputs['ssd_norm_w'], f)[0].reshape(12, 128).T)
    return m


def kernel(**inputs):
    b = B()
    nc = b.build()
    n = 8
    shared = make_inputs(inputs, 0)
    in_maps = []
    for i in range(n):
        mi = dict(shared)
        if i % 4 != 0:
            pi = make_inputs_small(inputs, i % 4)
            mi.update(pi)
        mi["halo_mask"] = halo_mask(i)
        in_maps.append(mi)
    res = run_bass_kernel_spmd(nc, in_maps, core_ids=list(range(n)))
    out = np.stack([np.concatenate([res.results[i]["outT"].T, res.results[i + 4]["outT"].T], axis=0) for i in range(4)], axis=0)
    return out.astype(np.float32)


def chunk_order(d):
    if d == 0:
        return list(range(34))
    return [1, 0] + list(range(33, 1, -1))


def phase_ssd(self):
    import os
    P, nc, dr = self.P, self.nc, self.dr
    sb, ps = self.sb, self.ps
    PT = dr['PT']
    with ExitStack() as es0:
        dt_tok = sb(es0, "dt_tok", [128, 34, 48], F32)
        la_tok = sb(es0, "la_tok", [128, 34, 48], F32)
        with ExitStack() as es:
            dtr = sb(es, "dtr", [128, T], F32)
            laT = sb(es, "laT", [128, T], F32)
            P.op('gpsimd', lambda e: e.memset(dtr[:], 0.0), writes=['dtr'])
            P.op('gpsimd', lambda e: e.memset(laT[:], 0.0), writes=['laT'])
            dtb = sb(es, "dtb", [48, 1], F32)
            alog = sb(es, "alog", [48, 1], F32)
            nega = sb(es, "nega", [48, 1], F32)
            ptr = ps(es, "ptr_g", [128, 512], F32)
            if os.environ.get('SKIP12') or os.environ.get('DTRMEM'):
                P.op('gpsimd', lambda e: e.memset(dtr[:48, :], 0.5), writes=['dtr'])
            else:
                P.dma('sync', dtr[:48, :], PT[4608:4656, :], writes=['dtr'])
            P.dma('sync', dtb[:], dr['ssd_dtb'], writes=['dtb'])
            P.dma('sync', alog[:], dr['ssd_alog'], writes=['alog'])
            P.op('scalar', lambda e: e.activation(out=nega[:], in_=alog[:], func=AF.Exp), reads=['alog'], writes=['nega'])
            P.op('vector', lambda e: e.tensor_scalar(out=nega[:], in0=nega[:], scalar1=-1.0, scalar2=None, op0=ALU.mult), reads=['nega'], writes=['nega'])
            P.op('scalar', lambda e: e.activation(out=dtr[:48, :], in_=dtr[:48, :], func=AF.Exp, bias=dtb[:]), reads=['dtr', 'dtb'], writes=['dtr'])
            P.op('scalar', lambda e: e.activation(out=dtr[:48, :], in_=dtr[:48, :], func=AF.Ln, bias=self.one_col[:48, :]), reads=['dtr', 'one_col'], writes=['dtr'])
            P.op('vector', lambda e: e.tensor_scalar(out=laT[:48, :], in0=dtr[:48, :], scalar1=nega[:], scalar2=None, op0=ALU.mult), reads=['dtr', 'nega'], writes=['laT'])
            import os
            CUT = int(os.environ.get('CUT', '99'))
            if CUT <= 2:
                P.op('gpsimd', lambda e: e.memset(dt_tok[:], 0.05), writes=['dt_tok'])
                P.op('gpsimd', lambda e: e.memset(la_tok[:], -0.01), writes=['la_tok'])
            VV = os.environ.get('VV', 'ABCD')
            for c in range((34 if CUT > 3 else 1) if CUT > 2 else 0):
                if 'A' in VV:
                    P.op('tensor', lambda e, c=c: e.matmul(ptr[:, 0:48], lhsT=dtr[:, c * 128:(c + 1) * 128], rhs=self.ident_f[:, :48], start=True, stop=True),
                         reads=['dtr', 'ident_f'], writes=['PS:ptr_g'])
                if 'B' in VV:
                    P.op('tensor', lambda e, c=c: e.matmul(ptr[:, 64:112], lhsT=laT[:, c * 128:(c + 1) * 128], rhs=self.ident_f[:, :48], start=True, stop=True),
                         reads=['laT', 'ident_f'], writes=['PS:ptr_g'])
                if 'C' in VV:
                    P.op('vector', lambda e, c=c: e.tensor_copy(out=dt_tok[:, c, :], in_=ptr[:, 0:48]), reads=['PS:ptr_g'], writes=['dt_tok'])
                if 'D' in VV:
                    P.op('scalar', lambda e, c=c: e.copy(out=la_tok[:, c, :], in_=ptr[:, 64:112]), reads=['PS:ptr_g'], writes=['la_tok'])
            P.barrier()
        if self.sub <= 1:
            return
        with ExitStack() as es:
            xin = [sb(es, f"xin{i}", [128, 20, 514], F32) for i in range(2)]
            cs = sb(es, "cs", [128, 20, 512], BF16)
            acc = [sb(es, f"cacc{i}", [128, 512], F32) for i in range(2)]
            cw = sb(es, "cw", [128, 20, 3], F32)
            cb = sb(es, "cb", [128, 20], F32)
            tokst = [sb(es, f"tokst{i}", [128, 2048], BF16) for i in range(2)]
            ptA = ps(es, "ptA", [128, 8, 128], BF16)
            ptB = ps(es, "ptB", [128, 8, 128], BF16)
            P.dma('sync', cw[:], dr['ssd_cw'], writes=['cw'])
            P.dma('sync', cb[:], dr['ssd_cb'], writes=['cb'])
            src = PT[2048:4608, :].rearrange("(f p) t -> p f t", p=128)
            ci = 0
            for bi, (tok0, nt, m) in enumerate(self.blocks()):
                x = xin[bi % 2]
                xk = f'xin{bi % 2}'
                seq0, seq1 = (0, TC) if m == 1 else (TC, T)
                lo = max(tok0 - 1, seq0)
                hi = min(tok0 + nt + 1, seq1)
                P.dma('sync', x[:, :, lo - (tok0 - 1):hi - (tok0 - 1)], src[:, :, lo:hi], writes=[xk])
                if lo > tok0 - 1:
                    P.op('gpsimd', lambda e, x=x: e.memset(x[:, :, 0:1], 0.0), writes=[xk])
                if hi < tok0 + nt + 1:
                    P.op('gpsimd', lambda e, x=x, nt=nt: e.memset(x[:, :, nt + 1:nt + 2], 0.0), writes=[xk])
                for f in range(20):
                    a = acc[f % 2]
                    ak = f'cacc{f % 2}'
                    P.op('vector', lambda e, x=x, a=a, f=f, nt=nt: e.tensor_scalar(out=a[:, :nt], in0=x[:, f, 0:nt], scalar1=cw[:, f, 0:1], scalar2=None, op0=ALU.mult),
                         reads=[xk, 'cw'], writes=[ak])
                    P.op('vector', lambda e, x=x, a=a, f=f, nt=nt: e.scalar_tensor_tensor(out=a[:, :nt], in0=x[:, f, 1:nt + 1], scalar=cw[:, f, 1:2], in1=a[:, :nt], op0=ALU.mult, op1=ALU.add),
                         reads=[xk, 'cw', ak], writes=[ak])
                    P.op('vector', lambda e, x=x, a=a, f=f, nt=nt: e.scalar_tensor_tensor(out=a[:, :nt], in0=x[:, f, 2:nt + 2], scalar=cw[:, f, 2:3], in1=a[:, :nt], op0=ALU.mult, op1=ALU.add),
                         reads=[xk, 'cw', ak], writes=[ak])
                    P.op('scalar', lambda e, a=a, f=f, nt=nt: e.activation(out=cs[:, f, :nt], in_=a[:, :nt], func=AF.Silu, bias=cb[:, f:f + 1]),
                         reads=[ak, 'cb'], writes=[('cs', f)])
                P.dma('gpsimd', dr['BCT'][0:1024, :].rearrange("(f p) t -> p f t", p=128)[:, :, tok0:tok0 + nt], cs[:, 12:20, :nt],
                      reads=[('cs', f) for f in range(12, 20)], writes=[('BCT', tok0)])
                for ch in range(nt // 128):
                    tk = tokst[ci % 2]
                    tkk = f'tokst{ci % 2}'
                    ci += 1
                    for f in range(16):
                        pt_ = ptA if f < 8 else ptB
                        P.op('tensor', lambda e, f=f, ch=ch, pt_=pt_: e.transpose(out=pt_[:, f % 8, :], in_=cs[:, f, ch * 128:(ch + 1) * 128], identity=self.ident_b[:]),
                             reads=[('cs', f), 'ident_b'], writes=['PS:ptA' if f < 8 else 'PS:ptB'])
                    P.op('vector', lambda e, tk=tk: e.tensor_copy(out=tk[:, 0:1024], in_=ptA[:].rearrange("p a b -> p (a b)")), reads=['PS:ptA'], writes=[tkk])
                    P.op('scalar', lambda e, tk=tk: e.copy(out=tk[:, 1024:2048], in_=ptB[:].rearrange("p a b -> p (a b)")), reads=['PS:ptB'], writes=[tkk])
                    P.dma('gpsimd', dr['TOK'][tok0 + ch * 128:tok0 + (ch + 1) * 128, :], tk[:], reads=[tkk], writes=[('TOK', tok0 + ch * 128)])
            P.barrier()
        if 'dbg_gates' in self.dbg:
            P.dma('sync', dr['dbg_gates'][0], dt_tok[:].rearrange("p a b -> p (a b)"), reads=[], writes=['dbg_gates'])
            P.dma('sync', dr['dbg_gates'][1], la_tok[:].rearrange("p a b -> p (a b)"), reads=[], writes=['dbg_gates'])
            P.barrier()
        if self.sub <= 2:
            return
        with ExitStack() as es:
            S_f = sb(es, "S_f", [128, 24, 64], F32)
            S_b = sb(es, "S_b", [128, 24, 64], BF16)
            tok = [sb(es, f"tok{i}", [128, 2048], BF16) for i in range(2)]
            bct = [sb(es, f"bct{i}", [128, 8, 128], BF16) for i in range(2)]
            Lm = [sb(es, f"Lm{i}", [128, 128], F32) for i in range(8)]
            DmAll = sb(es, "DmAll", [128, 2, 24, 128], F32)
            WT = [sb(es, f"WT{i}", [128, 128], BF16) for i in range(6)]
            v1 = [sb(es, f"v1_{i}", [128, 64], BF16) for i in range(6)]
            v2 = [sb(es, f"v2_{i}", [128, 64], BF16) for i in range(6)]
            yA = sb(es, "yA", [128, 384], F32)
            ysb = [sb(es, f"ysb{i}", [128, 1536], F32) for i in range(2)]
            yfl = sb(es, "yfl", [128, 1536], F32)
            ytk = sb(es, "ytk", [128, 1536], BF16)
            ecum = sb(es, "ecum", [128, 24], F32)
            cdec = sb(es, "cdec", [128, 24], F32)
            dB = sb(es, "dB", [128, 24], F32)
            nw = sb(es, "ssd_nw_sb", [128, 12], F32)
            ytb = sb(es, "ytb", [128, 12, 512], BF16)
            zT = sb(es, "zT", [128, 12, 512], F32)
            yg = sb(es, "yg", [128, 12, 512], F32)
            sqg = sb(es, "sqg", [128, 12, 512], BF16)
            rstd = sb(es, "rstd_g", [128, 512], F32)
            ot = sb(es, "ot", [128, 12, 512], BF16)
            psegA = ps(es, "psegA", [128, 4, 128], F32)
            psegB = ps(es, "psegB", [128, 4, 128], F32)
            pyA = ps(es, "pyA", [128, 512], F32)
            pyB = ps(es, "pyB", [128, 512], F32)
            pS = ps(es, "pS", [128, 512], F32)
            ptT = ps(es, "ptT", [128, 8, 128], BF16)
            ptU = ps(es, "ptU", [128, 8, 128], BF16)
            pss = ps(es, "pss_g", [128, 512], F32)
            P.dma('sync', dB[:], dr['ssd_d'].partition_broadcast(128), writes=['dB'])
            P.dma('sync', nw[:], dr['ssd_nw'], writes=['ssd_nw'])
            seg = lambda r: (psegA[:, r, :] if r < 4 else psegB[:, r - 4, :])
            segk = lambda r: ('PS:psegA' if r < 4 else 'PS:psegB')
            li = 0
            for d in range(2):
                MASK = self.MGT if d == 0 else self.MLT
                TRI = self.TLE if d == 0 else self.TGE
                mk, tk_ = ('MGT', 'TLE') if d == 0 else ('MLT', 'TGE')
                ecol = 127 if d == 0 else 0
                P.op('gpsimd', lambda e: e.memset(S_f[:], 0.0), writes=[('S_f', h) for h in range(24)])
                P.op('gpsimd', lambda e: e.memset(S_b[:], 0.0), writes=[('S_b', h) for h in range(24)])
                order = chunk_order(d)

                def stageA(c, par):
                    t0 = c * 128
                    P.dma('sync', tok[par][:], dr['TOK'][t0:t0 + 128, :], writes=[f'tok{par}'])
                    P.dma('sync', bct[par][:], dr['BCT'][0:1024, :].rearrange("(f p) t -> p f t", p=128)[:, :, t0:t0 + 128], writes=[f'bct{par}'])
                    for b in range(6):
                        bank, bk = (psegA, 'PS:psegA') if b % 2 == 0 else (psegB, 'PS:psegB')
                        for r4 in range(4):
                            h = b * 4 + r4
                            L = Lm[(b % 2) * 4 + r4]
                            Lk = f'Lm{(b % 2) * 4 + r4}'
                            P.op('vector', lambda e, L=L, h=h, c=c: e.tensor_scalar(out=L[:], in0=MASK[:], scalar1=la_tok[:, c, d * 24 + h:d * 24 + h + 1], scalar2=None, op0=ALU.mult),
                                 reads=[mk, 'la_tok'], writes=[Lk])
                        for r4 in range(4):
                            L = Lm[(b % 2) * 4 + r4]
                            Lk = f'Lm{(b % 2) * 4 + r4}'
                            P.op('tensor', lambda e, L=L, bank=bank, r4=r4: e.matmul(bank[:, r4, :], lhsT=L[:], rhs=TRI[:], start=True, stop=False), reads=[Lk, tk_], writes=[bk])
                            P.op('tensor', lambda e, bank=bank, r4=r4: e.matmul(bank[:, r4, :], lhsT=self.negI[:], rhs=MASK[:], start=False, stop=True), reads=['negI', mk], writes=[bk])
                        for r4 in range(4):
                            h = b * 4 + r4
                            P.op('scalar', lambda e, bank=bank, r4=r4, h=h, par=par: e.activation(out=DmAll[:, par, h, :], in_=bank[:, r4, :], func=AF.Exp), reads=[bk], writes=[('DmAll', par, h)])

                stageA(order[0], li % 2)
                for oi, c in enumerate(order):
                    t0 = c * 128
                    par = li % 2
                    tb = tok[par]
                    tbk = f'tok{par}'
                    bc = bct[par]
                    bck = f'bct{par}'
                    ys = ysb[par]
                    ysk = f'ysb{par}'
                    li += 1
                    if oi + 1 < len(order):
                        stageA(order[oi + 1], li % 2)
                    lac = la_tok[:, c, d * 24:(d + 1) * 24]
                    P.op('tensor', lambda e, lac=lac: e.matmul(pss[:, 128:152], lhsT=TRI[:], rhs=lac, start=True, stop=True),
                         reads=[tk_, 'la_tok'], writes=['PS:pss_g'])
                    P.op('tensor', lambda e, lac=lac: e.matmul(pss[:, 160:184], lhsT=self.ones_f[:], rhs=lac, start=True, stop=True),
                         reads=['ones_f', 'la_tok'], writes=['PS:pss_g'])
                    P.op('scalar', lambda e: e.activation(out=ecum[:], in_=pss[:, 128:152], func=AF.Exp), reads=['PS:pss_g'], writes=['ecum'])
                    P.op('scalar', lambda e: e.activation(out=cdec[:], in_=pss[:, 160:184], func=AF.Exp), reads=['PS:pss_g'], writes=['cdec'])
                    for g in range(4):
                        hs = [g * 6 + r for r in range(6)]
                        P.op('tensor', lambda e, g=g, bc=bc: e.matmul(pss[:, 0:128], lhsT=bc[:, g, :], rhs=bc[:, 4 + g, :], start=True, stop=True),
                             reads=[bck], writes=['PS:pss_g'])
                        for r, h in enumerate(hs):
                            P.op('vector', lambda e, r=r, h=h, par=par: e.tensor_tensor(out=WT[r][:], in0=DmAll[:, par, h, :], in1=pss[:, 0:128], op=ALU.mult),
                                 reads=[('DmAll', par, h), 'PS:pss_g'], writes=[f'WT{r}'])
                            P.op('gpsimd', lambda e, r=r, h=h, c=c, tb=tb: e.tensor_scalar(out=v1[r][:], in0=tb[:, h * 64:(h + 1) * 64], scalar1=dt_tok[:, c, d * 24 + h:d * 24 + h + 1], scalar2=None, op0=ALU.mult),
                                 reads=[tbk, 'dt_tok'], writes=[f'v1_{r}'])
                            P.op('gpsimd', lambda e, r=r, h=h, par=par: e.tensor_scalar(out=v2[r][:], in0=v1[r][:], scalar1=DmAll[:, par, h, ecol:ecol + 1], scalar2=None, op0=ALU.mult),
                                 reads=[f'v1_{r}', ('DmAll', par, h)], writes=[f'v2_{r}'])
                        for r, h in enumerate(hs):
                            P.op('tensor', lambda e, r=r: e.matmul(pyA[:, r * 64:(r + 1) * 64], lhsT=WT[r][:], rhs=v1[r][:], start=True, stop=True),
                                 reads=[f'WT{r}', f'v1_{r}'], writes=['PS:pyA'])
                            P.op('tensor', lambda e, r=r, h=h, g=g, bc=bc: e.matmul(pyB[:, r * 64:(r + 1) * 64], lhsT=bc[:, 4 + g, :], rhs=S_b[:, h, :], start=True, stop=True),
                                 reads=[bck, ('S_b', h)], writes=['PS:pyB'])
                            P.op('tensor', lambda e, r=r, g=g, tb=tb: e.matmul(pS[:, r * 64:(r + 1) * 64], lhsT=tb[:, 1536 + g * 128:1536 + (g + 1) * 128], rhs=v2[r][:], start=True, stop=True),
                                 reads=[tbk, f'v2_{r}'], writes=['PS:pS'])
                        P.op('scalar', lambda e: e.copy(out=yA[:], in_=pyA[:, 0:384]), reads=['PS:pyA'], writes=['yA'])
                        for r, h in enumerate(hs):
                            P.op('vector', lambda e, r=r, h=h, ys=ys: e.scalar_tensor_tensor(out=ys[:, h * 64:(h + 1) * 64], in0=pyB[:, r * 64:(r + 1) * 64], scalar=ecum[:, h:h + 1], in1=yA[:, r * 64:(r + 1) * 64], op0=ALU.mult, op1=ALU.add),
                                 reads=['PS:pyB', 'ecum', 'yA'], writes=[ysk])
                            P.op('vector', lambda e, r=r, h=h: e.scalar_tensor_tensor(out=S_f[:, h, :], in0=S_f[:, h, :], scalar=cdec[:, h:h + 1], in1=pS[:, r * 64:(r + 1) * 64], op0=ALU.mult, op1=ALU.add),
                                 reads=[('S_f', h), 'cdec', 'PS:pS'], writes=[('S_f', h)])
                            P.op('scalar', lambda e, h=h: e.copy(out=S_b[:, h, :], in_=S_f[:, h, :]), reads=[('S_f', h)], writes=[('S_b', h)])
                    if d == 0:
                        P.dma('gpsimd', dr['YF'][t0:t0 + 128, 0:1536], ys[:], reads=[ysk], writes=[('YF', c)])
                    else:
                        P.dma('sync', yfl[:], dr['YF'][t0:t0 + 128, 0:1536], reads=[('YF', c)], writes=['yfl'])
                        P.op('vector', lambda e, ys=ys: e.tensor_tensor(out=yfl[:], in0=yfl[:], in1=ys[:], op=ALU.add), reads=['yfl', ysk], writes=['yfl'])
                        for h in range(24):
                            P.op('vector', lambda e, h=h, tb=tb: e.scalar_tensor_tensor(out=ytk[:, h * 64:(h + 1) * 64], in0=tb[:, h * 64:(h + 1) * 64], scalar=dB[:, h:h + 1], in1=yfl[:, h * 64:(h + 1) * 64], op0=ALU.mult, op1=ALU.add),
                                 reads=[tbk, 'dB', 'yfl'], writes=['ytk'])
                        if c < 2:
                            tok0, nt, ch = 0, TC, c
                        else:
                            tok0, nt, ch = TC + ((c - 2) // 4) * 512, 512, (c - 2) % 4
                        for f in range(12):
                            pt_ = ptT if f < 8 else ptU
                            P.op('tensor', lambda e, f=f, pt_=pt_: e.transpose(out=pt_[:, f % 8, :], in_=ytk[:, f * 128:(f + 1) * 128], identity=self.ident_b[:]),
                                 reads=['ytk', 'ident_b'], writes=['PS:ptT' if f < 8 else 'PS:ptU'])
                        P.op('vector', lambda e, ch=ch: e.tensor_copy(out=ytb[:, 0:8, ch * 128:(ch + 1) * 128], in_=ptT[:]), reads=['PS:ptT'], writes=['ytb'])
                        P.op('scalar', lambda e, ch=ch: e.copy(out=ytb[:, 8:12, ch * 128:(ch + 1) * 128], in_=ptU[:, 0:4, :]), reads=['PS:ptU'], writes=['ytb'])
                        if ch == 0:
                            P.dma('sync', zT[:, :, :nt], PT[512:2048, :].rearrange("(f p) t -> p f t", p=128)[:, :, tok0:tok0 + nt], writes=['zT'])
                            P.op('scalar', lambda e, nt=nt: e.activation(out=zT[:, :, :nt], in_=zT[:, :, :nt], func=AF.Silu), reads=['zT'], writes=['zT'])
                            P.op('vector', lambda e, nt=nt: e.tensor_tensor(out=yg[:, :, :nt], in0=ytb[:, :, :nt], in1=zT[:, :, :nt], op=ALU.mult), reads=['ytb', 'zT'], writes=['yg'])
                            P.op('scalar', lambda e, nt=nt: e.activation(out=sqg[:, :, :nt], in_=yg[:, :, :nt], func=AF.Square), reads=['yg'], writes=['sqg'])
                            for gq in range(4):
                                for i3 in range(3):
                                    P.op('tensor', lambda e, gq=gq, i3=i3, nt=nt: e.matmul(pss[:, :nt], lhsT=self.ones_b[:], rhs=sqg[:, gq * 3 + i3, :nt], start=(i3 == 0), stop=(i3 == 2)),
                                         reads=['sqg', 'ones_b'], writes=['PS:pss_g'])
                                P.op('scalar', lambda e, nt=nt: e.activation(out=rstd[:, :nt], in_=pss[:, :nt], func=AF.Sqrt, scale=1.0 / 384, bias=self.eps_col[:]),
                                     reads=['PS:pss_g', 'eps_col'], writes=['rstd_g'])
                                P.op('vector', lambda e, nt=nt: e.reciprocal(out=rstd[:, :nt], in_=rstd[:, :nt]), reads=['rstd_g'], writes=['rstd_g'])
                                for i3 in range(3):
                                    f = gq * 3 + i3
                                    P.op('vector', lambda e, f=f, nt=nt: e.scalar_tensor_tensor(out=ot[:, f, :nt], in0=yg[:, f, :nt], scalar=nw[:, f:f + 1], in1=rstd[:, :nt], op0=ALU.mult, op1=ALU.mult),
                                         reads=['yg', 'ssd_nw', 'rstd_g'], writes=['ot'])
                            P.dma('gpsimd', dr['YT'][512:2048, :].rearrange("(f p) t -> p f t", p=128)[:, :, tok0:tok0 + nt], ot[:, :, :nt], reads=['ot'], writes=[('YT', 'ssd', tok0)])
            P.barrier()


B.phase_ssd = phase_ssd


def norm_cols(self, xt, sq, hT, rstd, tmp, pss, c0, n, m, scale_t, shift_lo, tag):
    P = self.P
    kx, ksq, kr, kp = tag + 'xt', tag + 'sq', tag + 'rstd', 'PS:' + tag + 'pss'
    P.op('scalar', lambda e: e.activation(out=sq[:, :, c0:c0 + n], in_=xt[:, :, c0:c0 + n], func=AF.Square), reads=[kx], writes=[ksq])
    for kc in range(KC):
        P.op('tensor', lambda e, kc=kc: e.matmul(pss[:, :n], lhsT=self.ones_b[:], rhs=sq[:, kc, c0:c0 + n],
                                                 start=(kc == 0), stop=(kc == KC - 1)), reads=[ksq, 'ones_b'], writes=[kp])
    P.op('scalar', lambda e: e.activation(out=rstd[:, :n], in_=pss[:, :n], func=AF.Sqrt, scale=1.0 / D, bias=self.eps_col[:]),
         reads=[kp, 'eps_col'], writes=[kr])
    P.op('vector', lambda e: e.reciprocal(out=rstd[:, :n], in_=rstd[:, :n]), reads=[kr], writes=[kr])
    for kc in range(KC):
        tk = tag + 'ntmp' + str(kc % 2)
        t = tmp[kc % 2]
        P.op('vector', lambda e, kc=kc, t=t: e.tensor_tensor(out=t[:, :n], in0=xt[:, kc, c0:c0 + n], in1=rstd[:, :n], op=ALU.mult),
             reads=[kx, kr], writes=[tk])
        P.op('scalar', lambda e, kc=kc, t=t: e.activation(out=hT[:, kc, c0:c0 + n], in_=t[:, :n], func=AF.Identity,
                                                          scale=scale_t[:, kc, m:m + 1], bias=self.modT[:, shift_lo + kc, m:m + 1]),
             reads=[tk, 'modT', 's1', 's2'], writes=[tag + 'hT'])


def phase_outproj(self, layer, src, dst):
    P, nc, dr = self.P, self.nc, self.dr
    sb, ps = self.sb, self.ps
    wsrc = dr['ev_out_w'] if layer == 0 else dr['od_out_w']
    WO = dr['WO']
    for r in range(0, D, 512):
        P.dma('gpsimd', WO[r:r + 512, :], wsrc[r:r + 512, :], writes=[('WO', r)])
    split = (layer == 1)
    if split:
        blocks = [(TC - 64 + i * 512, 512, 0) for i in range(4)] + [(TC - 64 + 2048, 128, 0)]
    else:
        blocks = self.blocks()
    with ExitStack() as es:
        W = sb(es, "wo_sb", [128, KC, D], BF16)
        xt = [sb(es, f"ox{i}", [128, KC, 512], F32) for i in range(2)]
        yt = [sb(es, f"oy{i}", [128, KC, 512], BF16) for i in range(2)]
        pacc = [ps(es, f"opacc{i}", [128, 512], F32) for i in range(4)]
        P.dma('sync', W[:], WO.rearrange("(kc p) n -> p kc n", p=128), reads=[('WO', r) for r in range(0, D, 512)], writes=['wo_sb'])
        it = 0
        for bi, (tok0, nt, m) in enumerate(blocks):
            x, xk = xt[bi % 2], f'ox{bi % 2}'
            y, yk = yt[bi % 2], f'oy{bi % 2}'
            ts_ = self.dyn('sync', tok0) if split else tok0
            tg_ = self.dyn('gpsimd', tok0) if split else tok0
            P.dma('sync', x[:, :, :nt], src.rearrange("(kc p) t -> p kc t", p=128)[:, :, (bass.ds(ts_, nt) if split else slice(ts_, ts_ + nt))], writes=[xk])
            P.dma('sync', y[:, :, :nt], dr['YT'].rearrange("(kc p) t -> p kc t", p=128)[:, :, (bass.ds(ts_, nt) if split else slice(ts_, ts_ + nt))], writes=[yk])
            for d in range(KC):
                pa, pk = pacc[it % 4], f'PS:opacc{it % 4}'
                it += 1
                for kc in range(KC):
                    P.op('tensor', lambda e, kc=kc, d=d, pa=pa, y=y, nt=nt: e.matmul(pa[:, :nt], lhsT=W[:, kc, d * 128:(d + 1) * 128], rhs=y[:, kc, :nt],
                                                                                   start=(kc == 0), stop=(kc == KC - 1)), reads=['wo_sb', yk], writes=[pk])
                P.op('vector', lambda e, d=d, pa=pa, x=x, nt=nt, m=m: e.scalar_tensor_tensor(out=x[:, d, :nt], in0=pa[:, :nt], scalar=self.modT[:, 32 + d, m:m + 1], in1=x[:, d, :nt], op0=ALU.mult, op1=ALU.add),
                     reads=[pk, 'modT', xk], writes=[xk])
            P.dma('gpsimd', dst.rearrange("(kc p) t -> p kc t", p=128)[:, :, (bass.ds(tg_, nt) if split else slice(tg_, tg_ + nt))], x[:, :, :nt], reads=[xk], writes=[('X1', tok0)])
        P.barrier()


def phase_ffn(self, layer, src, dst):
    P, nc, dr = self.P, self.nc, self.dr
    sb, ps = self.sb, self.ps
    WU, WD = dr['WU'], dr['WD']
    for r in range(0, D, 128):
        P.dma('gpsimd', WU[r:r + 128, :], dr['ffn_up_w'][layer][r:r + 128, :], writes=[('WU', r)])
    for r in range(0, FFN, 512):
        r1 = min(r + 512, FFN)
        P.dma('gpsimd', WD[r:r1, :], dr['ffn_down_w'][layer][r:r1, :], writes=[('WD', r)])
    wu_keys = [('WU', r) for r in range(0, D, 128)]
    wd_keys = [('WD', r) for r in range(0, FFN, 512)]
    NF = 43
    split = (layer == 1)
    blocks = [(TC + i * 512, 512, 0) for i in range(4)] if split else self.blocks()
    with ExitStack() as es0:
        gT = sb(es0, "gT", [128, NF, 512], BF16)
        xt = sb(es0, "fx", [128, KC, 640], F32)
        cwt = sb(es0, "fcw", [128, NF, 9], F32)
        P.dma('sync', cwt[:], dr['ffn_cw'][layer], writes=['fcw'])
        for bi, (tok0, nt, m) in enumerate(blocks):
            if split:
                lo, hi, off = tok0 - 64, tok0 + nt + 64, 0
            elif m == 1:
                lo, hi, off = 0, TC, 0
            else:
                lo = max(tok0 - 64, TC)
                hi = min(tok0 + nt + 64, T)
                off = lo - (tok0 - 64)
            nw = hi - lo
            with ExitStack() as es:
                sq = sb(es, "fsq", [128, KC, 640], BF16)
                hT = sb(es, "fh", [128, KC, 640], BF16)
                rstd = sb(es, "frstd", [128, 512], F32)
                tmp = [sb(es, f"ftmp{i}", [128, 512], F32) for i in range(2)]
                wa = [sb(es, f"fwa{i}", [128, KC, 256], BF16) for i in range(2)]
                wv = [sb(es, f"fwv{i}", [128, KC, 256], BF16) for i in range(2)]
                asb = [sb(es, f"fasb{i}", [128, 10, 66], F32) for i in range(2)]
                acc = [sb(es, f"facc{i}", [128, 8, 64], F32) for i in range(2)]
                sg = [sb(es, f"fsg{i}", [128, 512], F32) for i in range(2)]
                pss = ps(es, "fpss", [128, 512], F32)
                pa0 = [ps(es, f"fpa0_{i}", [128, 512], F32) for i in range(2)]
                pa1 = [ps(es, f"fpa1_{i}", [128, 512], F32) for i in range(2)]
                pv = [ps(es, f"fpv{i}", [128, 512], F32) for i in range(2)]
                lo_ = self.dyn('sync', lo) if split else lo
                P.dma('sync', xt[:, :, off:off + nw], src.rearrange("(kc p) t -> p kc t", p=128)[:, :, (bass.ds(lo_, nw) if split else slice(lo_, lo_ + nw))], writes=['fxt'])
                c = off
                while c < off + nw:
                    n = min(320, off + nw - c)
                    norm_cols(self, xt, sq, hT, rstd, tmp, pss, c, n, m, self.s2, 48, 'f')
                    c += n
                for i in range(2):
                    P.op('gpsimd', lambda e, i=i: e.memset(asb[i][:], 0.0), writes=[f'fasb{i}'])
                wuv = WU.rearrange("(kc p) n -> p kc n", p=128)
                for fp in range(0, NF, 2):
                    nf = min(2, NF - fp)
                    wi = (fp // 2) % 2
                    P.dma('sync', wa[wi][:, :, :nf * 128], wuv[:, :, fp * 128:(fp + nf) * 128], reads=wu_keys, writes=[f'fwa{wi}'])
                    P.dma('sync', wv[wi][:, :, :nf * 128], wuv[:, :, FFN + fp * 128:FFN + (fp + nf) * 128], reads=wu_keys, writes=[f'fwv{wi}'])
                    for ff in range(nf):
                        f = fp + ff
                        b2 = f % 2
                        A, Ak = asb[b2], f'fasb{b2}'
                        if m == 1:
                            P0, P0k = pa0[b2], f'PS:fpa0_{b2}'
                            for kc in range(KC):
                                P.op('tensor', lambda e, kc=kc, ff=ff, wi=wi, P0=P0: e.matmul(P0[:, :256], lhsT=wa[wi][:, kc, ff * 128:(ff + 1) * 128], rhs=hT[:, kc, 0:256], start=(kc == 0), stop=(kc == KC - 1)),
                                     reads=[f'fwa{wi}', 'fhT'], writes=[P0k])
                            Av = A[:].rearrange("p a b -> p (a b)")
                            P.op('scalar', lambda e, Av=Av, P0=P0: e.copy(out=Av[:, 1:257], in_=P0[:, :256]), reads=[P0k], writes=[Ak])
                            ac, ack = acc[b2][:].rearrange("p a b -> p (a b)"), f'facc{b2}'
                            P.op('vector', lambda e, Av=Av, ac=ac, f=f: e.tensor_scalar(out=ac[:, :256], in0=Av[:, 0:256], scalar1=cwt[:, f, 3:4], scalar2=None, op0=ALU.mult), reads=[Ak, 'fcw'], writes=[ack])
                            for dx in (1, 2):
                                P.op('vector', lambda e, Av=Av, ac=ac, f=f, dx=dx: e.scalar_tensor_tensor(out=ac[:, :256], in0=Av[:, dx:dx + 256], scalar=cwt[:, f, 3 + dx:4 + dx], in1=ac[:, :256], op0=ALU.mult, op1=ALU.add),
                                     reads=[Ak, 'fcw', ack], writes=[ack])
                            vlo = 0
                        else:
                            P0, P0k = pa0[b2], f'PS:fpa0_{b2}'
                            P1, P1k = pa1[b2], f'PS:fpa1_{b2}'
                            h0 = min(320, nw)
                            h1 = nw - h0
                            for kc in range(KC):
                                P.op('tensor', lambda e, kc=kc, ff=ff, wi=wi, P0=P0, h0=h0: e.matmul(P0[:, :h0], lhsT=wa[wi][:, kc, ff * 128:(ff + 1) * 128], rhs=hT[:, kc, off:off + h0], start=(kc == 0), stop=(kc == KC - 1)),
                                     reads=[f'fwa{wi}', 'fhT'], writes=[P0k])
                            for kc in range(KC):
                                P.op('tensor', lambda e, kc=kc, ff=ff, wi=wi, P1=P1, h0=h0, h1=h1: e.matmul(P1[:, :h1], lhsT=wa[wi][:, kc, ff * 128:(ff + 1) * 128], rhs=hT[:, kc, off + h0:off + h0 + h1], start=(kc == 0), stop=(kc == KC - 1)),
                                     reads=[f'fwa{wi}', 'fhT'], writes=[P1k])
                            r_off = off // 64
                            P.op('scalar', lambda e, A=A, P0=P0, h0=h0, r_off=r_off: e.copy(out=A[:, r_off:r_off + h0 // 64, 1:65], in_=P0[:, :h0].rearrange("p (a b) -> p a b", b=64)), reads=[P0k], writes=[Ak])
                            P.op('scalar', lambda e, A=A, P1=P1, h0=h0, h1=h1, r_off=r_off: e.copy(out=A[:, r_off + h0 // 64:r_off + (h0 + h1) // 64, 1:65], in_=P1[:, :h1].rearrange("p (a b) -> p a b", b=64)), reads=[P1k], writes=[Ak])
                            if split and bi == 0:
                                P.op('gpsimd', lambda e, A=A: e.tensor_scalar(out=A[:, 0, :], in0=A[:, 0, :], scalar1=self.hmask[:, 0:1], scalar2=None, op0=ALU.mult), reads=[Ak, 'hmask'], writes=[Ak])
                            if split and bi == 3:
                                P.op('gpsimd', lambda e, A=A: e.tensor_scalar(out=A[:, 9, :], in0=A[:, 9, :], scalar1=self.hmask[:, 1:2], scalar2=None, op0=ALU.mult), reads=[Ak, 'hmask'], writes=[Ak])
                            ac3, ack = acc[b2], f'facc{b2}'
                            first = True
                            for dy in range(3):
                                for dx in range(3):
                                    tap = dy * 3 + dx
                                    if first:
                                        P.op('vector', lambda e, A=A, ac3=ac3, f=f, dy=dy, dx=dx, tap=tap: e.tensor_scalar(out=ac3[:], in0=A[:, dy:dy + 8, dx:dx + 64], scalar1=cwt[:, f, tap:tap + 1], scalar2=None, op0=ALU.mult),
                                             reads=[Ak, 'fcw'], writes=[ack])
                                        first = False
                                    else:
                                        P.op('vector', lambda e, A=A, ac3=ac3, f=f, dy=dy, dx=dx, tap=tap: e.scalar_tensor_tensor(out=ac3[:], in0=A[:, dy:dy + 8, dx:dx + 64], scalar=cwt[:, f, tap:tap + 1], in1=ac3[:], op0=ALU.mult, op1=ALU.add),
                                             reads=[Ak, 'fcw', ack], writes=[ack])
                            ac = ac3[:].rearrange("p a b -> p (a b)")
                            vlo = 64
                        PV, PVk = pv[b2], f'PS:fpv{b2}'
                        for kc in range(KC):
                            P.op('tensor', lambda e, kc=kc, ff=ff, wi=wi, PV=PV, vlo=vlo, nt=nt: e.matmul(PV[:, :nt], lhsT=wv[wi][:, kc, ff * 128:(ff + 1) * 128], rhs=hT[:, kc, vlo:vlo + nt], start=(kc == 0), stop=(kc == KC - 1)),
                                 reads=[f'fwv{wi}', 'fhT'], writes=[PVk])
                        S, Sk = sg[b2], f'fsg{b2}'
                        P.op('scalar', lambda e, S=S, ac=ac, nt=nt: e.activation(out=S[:, :nt], in_=ac[:, :nt], func=AF.Silu), reads=[ack], writes=[Sk])
                        P.op('vector', lambda e, S=S, PV=PV, f=f, nt=nt: e.tensor_tensor(out=gT[:, f, :nt], in0=S[:, :nt], in1=PV[:, :nt], op=ALU.mult), reads=[Sk, PVk], writes=[('gT', f)])
                P.barrier()
            with ExitStack() as es:
                wd = [sb(es, f"fwd{i}", [128, 1024], BF16) for i in range(3)]
                pacc = [ps(es, f"fdacc{i}", [128, 512], F32) for i in range(8)]
                vlo = 0 if m == 1 else 64
                for half in range(2):
                    for f in range(NF):
                        w, wk = wd[f % 3], f'fwd{f % 3}'
                        P.dma('sync', w[:], WD[f * 128:(f + 1) * 128, half * 1024:(half + 1) * 1024], reads=wd_keys, writes=[wk])
                        for d in range(8):
                            P.op('tensor', lambda e, d=d, f=f, w=w, nt=nt: e.matmul(pacc[d][:, :nt], lhsT=w[:, d * 128:(d + 1) * 128], rhs=gT[:, f, :nt], start=(f == 0), stop=(f == NF - 1)),
                                 reads=[wk, ('gT', f)], writes=[f'PS:fdacc{d}'])
                    for d in range(8):
                        dd = half * 8 + d
                        P.op('vector', lambda e, d=d, dd=dd, nt=nt, m=m, vlo=vlo: e.scalar_tensor_tensor(out=xt[:, dd, vlo:vlo + nt], in0=pacc[d][:, :nt], scalar=self.modT[:, 80 + dd, m:m + 1], in1=xt[:, dd, vlo:vlo + nt], op0=ALU.mult, op1=ALU.add),
                             reads=[f'PS:fdacc{d}', 'modT', 'fxt'], writes=['fxt'])
                tg_ = self.dyn('gpsimd', tok0) if split else tok0
                P.dma('gpsimd', dst.rearrange("(kc p) t -> p kc t", p=128)[:, :, (bass.ds(tg_, nt) if split else slice(tg_, tg_ + nt))], xt[:, :, vlo:vlo + nt], reads=['fxt'], writes=[('X2', tok0)])
                P.barrier()


B.phase_outproj = phase_outproj
B.phase_ffn = phase_ffn


NCH = T // 8
HALF = NCH // 2
TWO_PI = 6.283185307179586
PI = 3.141592653589793


def reduce_angle(self, t, n, ki, kf, msk, tag):
    P = self.P
    v = 'vector'
    P.op(v, lambda e: e.tensor_scalar(out=ki[:, :n], in0=t, scalar1=1.0 / TWO_PI, scalar2=None, op0=ALU.mult), reads=[tag], writes=[tag + 'ki'])
    P.op(v, lambda e: e.tensor_copy(out=kf[:, :n], in_=ki[:, :n]), reads=[tag + 'ki'], writes=[tag + 'kf'])
    P.op(v, lambda e: e.scalar_tensor_tensor(out=t, in0=kf[:, :n], scalar=-TWO_PI, in1=t, op0=ALU.mult, op1=ALU.add), reads=[tag + 'kf', tag], writes=[tag])
    P.op(v, lambda e: e.tensor_single_scalar(out=msk[:, :n], in_=t, scalar=PI, op=ALU.is_gt), reads=[tag], writes=[tag + 'm'])
    P.op(v, lambda e: e.scalar_tensor_tensor(out=t, in0=msk[:, :n], scalar=-TWO_PI, in1=t, op0=ALU.mult, op1=ALU.add), reads=[tag + 'm', tag], writes=[tag])
    P.op(v, lambda e: e.tensor_single_scalar(out=msk[:, :n], in_=t, scalar=-PI, op=ALU.is_lt), reads=[tag], writes=[tag + 'm'])
    P.op(v, lambda e: e.scalar_tensor_tensor(out=t, in0=msk[:, :n], scalar=TWO_PI, in1=t, op0=ALU.mult, op1=ALU.add), reads=[tag + 'm', tag], writes=[tag])
    P.op(v, lambda e: e.tensor_scalar(out=t, in0=t, scalar1=PI, scalar2=-PI, op0=ALU.min, op1=ALU.max), reads=[tag], writes=[tag])


def cmul_cols(self, o_re, o_im, a_re, a_im, s_re, s_im, tmp, rk, wk):
    P = self.P
    v = 'vector'
    P.op(v, lambda e: e.tensor_scalar(out=tmp, in0=a_im, scalar1=s_im, scalar2=None, op0=ALU.mult), reads=rk, writes=[wk + 't'])
    P.op(v, lambda e: e.scalar_tensor_tensor(out=o_re, in0=a_re, scalar=s_re, in1=tmp, op0=ALU.mult, op1=ALU.subtract), reads=rk + [wk + 't'], writes=[wk + 're'])
    P.op(v, lambda e: e.tensor_scalar(out=tmp, in0=a_re, scalar1=s_im, scalar2=None, op0=ALU.mult), reads=rk + [wk + 're'], writes=[wk + 't'])
    P.op(v, lambda e: e.scalar_tensor_tensor(out=o_im, in0=a_im, scalar=s_re, in1=tmp, op0=ALU.mult, op1=ALU.add), reads=rk + [wk + 't'], writes=[wk + 'im'])


def phase_s5(self):
    P, nc, dr = self.P, self.nc, self.dr
    sb, ps = self.sb, self.ps
    PT = dr['PT']
    v, a, g_ = 'vector', 'scalar', 'gpsimd'
    with ExitStack() as es0:
        Sel = sb(es0, "Sel", [128, 8, 8, 128], BF16)
        bmask = sb(es0, "bmask", [128, 8, 16], F32)
        krow = sb(es0, "krow", [128, 16], F32)
        crow = sb(es0, "crow", [128, NCH + 1], F32)
        lre = sb(es0, "lre", [128, 32], F32)
        lim = sb(es0, "lim", [128, 32], F32)
        dlt = sb(es0, "dlt", [128, 32], F32)
        reD = sb(es0, "reD", [128, 32], F32)
        imD = sb(es0, "imD", [128, 32], F32)
        bre = sb(es0, "bre", [128, 32, 16], F32)
        bim = sb(es0, "bim", [128, 32, 16], F32)
        cre = sb(es0, "cre", [128, 32, 16], F32)
        cim = sb(es0, "cim", [128, 32, 16], F32)
        dsk = sb(es0, "dsk", [128, 4], F32)
        gy = sb(es0, "gy", [128, 4, T], BF16)
        uT = sb(es0, "uT", [128, 2, T], BF16)
        uF = sb(es0, "uF", [128, T], F32)
        ki = sb(es0, "ki", [128, NCH + 1], I32)
        kf = sb(es0, "kf", [128, NCH + 1], F32)
        msk = sb(es0, "msk", [128, NCH + 1], F32)
        P.op(g_, lambda e: e.memset(Sel[:], 0.0), writes=['Sel'])
        for a_ in range(8):
            for b_ in range(8):
                P.op(g_ if (a_ + b_) % 2 else v, lambda e, a_=a_, b_=b_: e.tensor_copy(out=Sel[:, a_, b_, b_ * 16:(b_ + 1) * 16], in_=self.ident_f[:, a_ * 16:(a_ + 1) * 16]),
                     reads=['ident_f', 'Sel'], writes=['Sel'])
        P.op(g_, lambda e: e.memset(bmask[:], 1.0), writes=['bmask'])
        P.op(g_, lambda e: e.affine_select(out=bmask[:], in_=bmask[:], pattern=[[16, 8], [0, 16]], base=15, channel_multiplier=-1, compare_op=ALU.is_ge, fill=0.0),
             reads=['bmask'], writes=['bmask'])
        P.op(g_, lambda e: e.iota(krow[:], pattern=[[1, 16]], base=-7, channel_multiplier=0, allow_small_or_imprecise_dtypes=True), writes=['krow'])
        P.op(g_, lambda e: e.iota(crow[:], pattern=[[1, NCH + 1]], base=0, channel_multiplier=0, allow_small_or_imprecise_dtypes=True), writes=['crow'])
        for nm, t_ in (('s5_lre', lre), ('s5_lim', lim), ('s5_dlt', dlt)):
            P.dma('sync', t_[:], dr[nm], writes=[nm])
        for nm, t_ in (('s5_bre', bre), ('s5_bim', bim), ('s5_cre', cre), ('s5_cim', cim)):
            P.dma('sync', t_[:], dr[nm], writes=[nm])
        P.dma('sync', dsk[:], dr['s5_dsk'], writes=['dsk'])
        P.op(a, lambda e: e.activation(out=dlt[:], in_=dlt[:], func=AF.Exp), reads=['s5_dlt'], writes=['s5_dlt'])
        P.op(v, lambda e: e.tensor_tensor(out=reD[:], in0=lre[:], in1=dlt[:], op=ALU.mult), reads=['s5_lre', 's5_dlt'], writes=['reD'])
        P.op(v, lambda e: e.tensor_tensor(out=imD[:], in0=lim[:], in1=dlt[:], op=ALU.mult), reads=['s5_lim', 's5_dlt'], writes=['imD'])
        self.cfre = sb(es0, "cfre", [128, 32], F32)
        self.cfim = sb(es0, "cfim", [128, 32], F32)
        with ExitStack() as es:
            mg = sb(es, "c_mg", [128, 32], F32)
            an = sb(es, "c_an", [128, 32], F32)
            an2 = sb(es, "c_an2", [128, 32], F32)
            nre = sb(es, "c_nre", [128, 32], F32)
            nim = sb(es, "c_nim", [128, 32], F32)
            den = sb(es, "c_den", [128, 32], F32)
            t1 = sb(es, "c_t1", [128, 32], F32)
            cfre, cfim = self.cfre, self.cfim
            P.op(a, lambda e: e.activation(out=mg[:], in_=reD[:], func=AF.Exp), reads=['reD'], writes=['c_mg'])
            P.op(v, lambda e: e.tensor_copy(out=an[:], in_=imD[:]), reads=['imD'], writes=['c_an'])
            reduce_angle(self, an[:], 32, ki, kf, msk, 'c_an')
            P.op(v, lambda e: e.tensor_scalar(out=an2[:], in0=imD[:], scalar1=PI / 2, scalar2=None, op0=ALU.add), reads=['imD'], writes=['c_an2'])
            reduce_angle(self, an2[:], 32, ki, kf, msk, 'c_an2')
            P.op(a, lambda e: e.activation(out=an[:], in_=an[:], func=AF.Sin), reads=['c_an'], writes=['c_an'])
            P.op(a, lambda e: e.activation(out=an2[:], in_=an2[:], func=AF.Sin), reads=['c_an2'], writes=['c_an2'])
            P.op(v, lambda e: e.tensor_tensor(out=nre[:], in0=mg[:], in1=an2[:], op=ALU.mult), reads=['c_mg', 'c_an2'], writes=['c_nre'])
            P.op(v, lambda e: e.tensor_scalar(out=nre[:], in0=nre[:], scalar1=-1.0, scalar2=None, op0=ALU.add), reads=['c_nre'], writes=['c_nre'])
            P.op(v, lambda e: e.tensor_tensor(out=nim[:], in0=mg[:], in1=an[:], op=ALU.mult), reads=['c_mg', 'c_an'], writes=['c_nim'])
            P.op(v, lambda e: e.tensor_tensor(out=den[:], in0=lre[:], in1=lre[:], op=ALU.mult), reads=['s5_lre'], writes=['c_den'])
            P.op(v, lambda e: e.tensor_tensor(out=t1[:], in0=lim[:], in1=lim[:], op=ALU.mult), reads=['s5_lim'], writes=['c_t1'])
            P.op(v, lambda e: e.tensor_tensor(out=den[:], in0=den[:], in1=t1[:], op=ALU.add), reads=['c_den', 'c_t1'], writes=['c_den'])
            P.op(v, lambda e: e.reciprocal(out=den[:], in_=den[:]), reads=['c_den'], writes=['c_den'])
            P.op(v, lambda e: e.tensor_tensor(out=cfre[:], in0=nre[:], in1=lre[:], op=ALU.mult), reads=['c_nre', 's5_lre'], writes=['cfre'])
            P.op(v, lambda e: e.tensor_tensor(out=t1[:], in0=nim[:], in1=lim[:], op=ALU.mult), reads=['c_nim', 's5_lim', 'c_den'], writes=['c_t1'])
            P.op(v, lambda e: e.tensor_tensor(out=cfre[:], in0=cfre[:], in1=t1[:], op=ALU.add), reads=['cfre', 'c_t1'], writes=['cfre'])
            P.op(v, lambda e: e.tensor_tensor(out=cfre[:], in0=cfre[:], in1=den[:], op=ALU.mult), reads=['cfre', 'c_den'], writes=['cfre'])
            P.op(v, lambda e: e.tensor_tensor(out=cfim[:], in0=nim[:], in1=lre[:], op=ALU.mult), reads=['c_nim', 's5_lre'], writes=['cfim'])
            P.op(v, lambda e: e.tensor_tensor(out=t1[:], in0=nre[:], in1=lim[:], op=ALU.mult), reads=['c_nre', 's5_lim', 'cfre'], writes=['c_t1'])
            P.op(v, lambda e: e.tensor_tensor(out=cfim[:], in0=cfim[:], in1=t1[:], op=ALU.subtract), reads=['cfim', 'c_t1'], writes=['cfim'])
            P.op(v, lambda e: e.tensor_tensor(out=cfim[:], in0=cfim[:], in1=den[:], op=ALU.mult), reads=['cfim', 'c_den'], writes=['cfim'])
            P.barrier()
        cfre, cfim = self.cfre, self.cfim
        with ExitStack() as es:
            mgk = sb(es, "mgk", [128, 16], F32)
            ank = sb(es, "ank", [128, 16], F32)
            ank2 = sb(es, "ank2", [128, 16], F32)
            LPre = sb(es, "LPre", [128, 16], F32)
            LPim = sb(es, "LPim", [128, 16], F32)
            bbre = sb(es, "bbre", [128, 16], F32)
            bbim = sb(es, "bbim", [128, 16], F32)
            ctmp = sb(es, "ctmp", [128, 128], F32)
            Bmre = sb(es, "Bmre", [128, 8, 16], F32)
            Bmim = sb(es, "Bmim", [128, 8, 16], F32)
            Cmre = sb(es, "Cmre", [128, 8, 16], F32)
            Cmim = sb(es, "Cmim", [128, 8, 16], F32)
            WiTre = sb(es, "WiTre", [128, 128], F32)
            WiTim = sb(es, "WiTim", [128, 128], F32)
            Wore = sb(es, "Wore", [128, 128], BF16)
            Woim = sb(es, "Woim", [128, 128], BF16)
            WoFre = sb(es, "WoFre", [128, 128], F32)
            WoFim = sb(es, "WoFim", [128, 128], F32)
            Tin = sb(es, "Tin", [128, 2, 128], BF16)
            Win = sb(es, "Win", [128, 2, 2, 64], BF16)
            th = sb(es, "th", [128, 1], F32)
            rr = sb(es, "rr", [128, 1], F32)
            rtab = sb(es, "rtab", [128, NCH], F32)
            ctab = sb(es, "ctab", [128, NCH + 1], F32)
            stab = sb(es, "stab", [128, NCH + 1], F32)
            Usb = sb(es, "Usb", [128, 2, 2, NCH], BF16)
            Sre = sb(es, "Sre", [128, NCH], F32)
            Sim = sb(es, "Sim", [128, NCH], F32)
            s2re = sb(es, "s2re", [128, NCH], F32)
            s2im = sb(es, "s2im", [128, NCH], F32)
            Zre = sb(es, "Zre", [128, NCH + 1], F32)
            Zim = sb(es, "Zim", [128, NCH + 1], F32)
            Xre = sb(es, "Xre", [128, NCH], BF16)
            Xim = sb(es, "Xim", [128, NCH], BF16)
            xt1 = sb(es, "xt1", [128, NCH], F32)
            Ysb = sb(es, "Ysb", [128, 2, 8, NCH], BF16)
            yT = sb(es, "yT5", [128, 2, T], F32)
            gl_w = sb(es, "gluw", [128, 4, 512], BF16)
            sgm = sb(es, "sgm", [128, 512], F32)
            og = sb(es, "og5", [128, 512], BF16)
            pU = [ps(es, f"pU{i}", [128, 512], F32) for i in range(2)]
            pSr = ps(es, "pSr", [128, 512], F32)
            pSi = ps(es, "pSi", [128, 512], F32)
            pY = [ps(es, f"pY{i}", [128, 512], F32) for i in range(2)]
            pB = ps(es, "pB5", [128, 512], F32)
            P.dma('gpsimd', gl_w[:], dr['s5_glu_w'].rearrange("(kt p) n -> p kt n", p=128), writes=['gluw'])
            P.op(g_, lambda e: e.memset(Zre[:, 0:1], 0.0), writes=['Zre'])
            P.op(g_, lambda e: e.memset(Zim[:, 0:1], 0.0), writes=['Zim'])
            hs = [(0, HALF), (HALF, NCH)]
            for ft in range(4):
                P.dma('sync', uF[:], PT[ft * 128:(ft + 1) * 128, :], writes=['uF'])
                P.op(a, lambda e: e.copy(out=uT[:, 0, :], in_=uF[:]), reads=['uF'], writes=['uT'])
                P.op(v, lambda e: e.tensor_copy(out=uT[:, 1, 0:TC], in_=uF[:, TC - 1::-1]), reads=['uF'], writes=['uT'])
                P.op(v, lambda e: e.tensor_copy(out=uT[:, 1, TC:T], in_=uF[:, T - 1:TC - 1:-1]), reads=['uF'], writes=['uT'])
                for gp4 in range(4):
                    gp = ft * 4 + gp4
                    for d in range(2):
                        for g2 in range(2):
                            gl = gp4 * 2 + g2
                            for hi_, (c0, c1) in enumerate(hs):
                                for j in range(8):
                                    P.op('tensor', lambda e, d=d, gl=gl, j=j, c0=c0, c1=c1, hi_=hi_: e.matmul(
                                        pU[hi_][:, :c1 - c0], lhsT=Sel[:, gl, j, :], rhs=uT[:, d, c0 * 8 + j:c1 * 8:8], start=(j == 0), stop=(j == 7)),
                                        reads=['Sel', 'uT'], writes=[f'PS:pU{hi_}'])
                                eng = a if hi_ == 0 else v
                                if eng == a:
                                    P.op(a, lambda e, d=d, g2=g2, c0=c0, c1=c1, hi_=hi_: e.copy(out=Usb[:, d, g2, c0:c1], in_=pU[hi_][:, :c1 - c0]), reads=[f'PS:pU{hi_}'], writes=[('Usb', d, g2)])
                                else:
                                    P.op(v, lambda e, d=d, g2=g2, c0=c0, c1=c1, hi_=hi_: e.tensor_copy(out=Usb[:, d, g2, c0:c1], in_=pU[hi_][:, :c1 - c0]), reads=[f'PS:pU{hi_}'], writes=[('Usb', d, g2)])
                    for d in range(2):
                        col = d * 16 + gp
                        P.op(v, lambda e, col=col: e.tensor_scalar(out=mgk[:], in0=krow[:], scalar1=reD[:, col:col + 1], scalar2=None, op0=ALU.mult), reads=['krow', 'reD'], writes=['mgk'])
                        P.op(a, lambda e: e.activation(out=mgk[:], in_=mgk[:], func=AF.Exp), reads=['mgk'], writes=['mgk'])
                        P.op(v, lambda e, col=col: e.tensor_scalar(out=ank[:], in0=krow[:], scalar1=imD[:, col:col + 1], scalar2=None, op0=ALU.mult), reads=['krow', 'imD'], writes=['ank'])
                        P.op(v, lambda e: e.tensor_scalar(out=ank2[:], in0=ank[:], scalar1=PI / 2, scalar2=None, op0=ALU.add), reads=['ank'], writes=['ank2'])
                        reduce_angle(self, ank[:], 16, ki, kf, msk, 'ank')
                        reduce_angle(self, ank2[:], 16, ki, kf, msk, 'ank2')
                        P.op(a, lambda e: e.activation(out=ank[:], in_=ank[:], func=AF.Sin), reads=['ank'], writes=['ank'])
                        P.op(a, lambda e: e.activation(out=ank2[:], in_=ank2[:], func=AF.Sin), reads=['ank2'], writes=['ank2'])
                        P.op(v, lambda e: e.tensor_tensor(out=LPre[:], in0=mgk[:], in1=ank2[:], op=ALU.mult), reads=['mgk', 'ank2'], writes=['LPre'])
                        P.op(v, lambda e: e.tensor_tensor(out=LPim[:], in0=mgk[:], in1=ank[:], op=ALU.mult), reads=['mgk', 'ank'], writes=['LPim'])
                        LP = ['LPre', 'LPim']
                        cmul_cols(self, bbre[:], bbim[:], bre[:, col, :], bim[:, col, :], cfre[:, col:col + 1], cfim[:, col:col + 1], ctmp[:, 0:16], ['s5_bre', 's5_bim', 'cfre', 'cfim'], 'bb')
                        for j in range(8):
                            kk = 7 - j
                            cmul_cols(self, Bmre[:, j, :], Bmim[:, j, :], bbre[:], bbim[:], LPre[:, kk:kk + 1], LPim[:, kk:kk + 1], ctmp[:, 0:16], ['bbre', 'bbim'] + LP, 'Bm')
                            kk = 7 + j
                            cmul_cols(self, Cmre[:, j, :], Cmim[:, j, :], cre[:, col, :], cim[:, col, :], LPre[:, kk:kk + 1], LPim[:, kk:kk + 1], ctmp[:, 0:16], ['s5_cre', 's5_cim'] + LP, 'Cm')
                        P.op(v, lambda e: e.tensor_scalar(out=Cmim[:], in0=Cmim[:], scalar1=-1.0, scalar2=None, op0=ALU.mult), reads=['Cmim'], writes=['Cmim'])
                        B2re, B2im = Bmre[:].rearrange("p a b -> p (a b)"), Bmim[:].rearrange("p a b -> p (a b)")
                        C2re, C2im = Cmre[:].rearrange("p a b -> p (a b)"), Cmim[:].rearrange("p a b -> p (a b)")
                        cmul_cols(self, WiTre[:], WiTim[:], B2re, B2im, LPre[:, 14:15], LPim[:, 14:15], ctmp[:], ['Bmre', 'Bmim'] + LP, 'WiT')
                        P.op(v, lambda e: e.tensor_scalar(out=ctmp[:], in0=C2im, scalar1=LPim[:, 8:9], scalar2=None, op0=ALU.mult), reads=['Cmim', 'LPim'], writes=['Wot'])
                        P.op(v, lambda e: e.scalar_tensor_tensor(out=WoFre[:], in0=C2re, scalar=LPre[:, 8:9], in1=ctmp[:], op0=ALU.mult, op1=ALU.add), reads=['Cmre', 'LPre', 'Wot'], writes=['WoFre'])
                        P.op(v, lambda e: e.tensor_scalar(out=ctmp[:], in0=C2re, scalar1=LPim[:, 8:9], scalar2=None, op0=ALU.mult), reads=['Cmre', 'LPim', 'WoFre'], writes=['Wot'])
                        P.op(v, lambda e: e.scalar_tensor_tensor(out=WoFim[:], in0=C2im, scalar=LPre[:, 8:9], in1=ctmp[:], op0=ALU.mult, op1=ALU.subtract), reads=['Cmim', 'LPre', 'Wot'], writes=['WoFim'])
                        P.op(a, lambda e: e.copy(out=Wore[:], in_=WoFre[:]), reads=['WoFre'], writes=['Wore'])
                        P.op(a, lambda e: e.copy(out=Woim[:], in_=WoFim[:]), reads=['WoFim'], writes=['Woim'])
                        for g2 in range(2):
                            r0, r1 = g2 * 64, (g2 + 1) * 64
                            P.op('tensor', lambda e, r0=r0, r1=r1: e.matmul(pB[:, 0:128], lhsT=B2re[r0:r1, :], rhs=C2re[r0:r1, :], start=True, stop=False), reads=['Bmre', 'Cmre'], writes=['PS:pB5'])
                            P.op('tensor', lambda e, r0=r0, r1=r1: e.matmul(pB[:, 0:128], lhsT=B2im[r0:r1, :], rhs=C2im[r0:r1, :], start=False, stop=True), reads=['Bmim', 'Cmim'], writes=['PS:pB5'])
                            P.op(v, lambda e, g2=g2: e.tensor_tensor(out=Tin[:, g2, :], in0=pB[:, 0:128], in1=bmask[:].rearrange("p a b -> p (a b)"), op=ALU.mult), reads=['PS:pB5', 'bmask'], writes=[('Tin', g2)])
                            P.op('tensor', lambda e, r0=r0, r1=r1: e.matmul(pB[:, 128:192], lhsT=WiTre[r0:r1, :], rhs=self.ident_f[r0:r1, r0:r1], start=True, stop=True), reads=['WiTre', 'ident_f'], writes=['PS:pB5'])
                            P.op('tensor', lambda e, r0=r0, r1=r1: e.matmul(pB[:, 192:256], lhsT=WiTim[r0:r1, :], rhs=self.ident_f[r0:r1, r0:r1], start=True, stop=True), reads=['WiTim', 'ident_f'], writes=['PS:pB5'])
                            P.op(a, lambda e, g2=g2: e.copy(out=Win[:, g2, :, :].rearrange("p a b -> p (a b)"), in_=pB[:, 128:256]), reads=['PS:pB5'], writes=[('Win', g2)])
                        P.op(v, lambda e, col=col: e.tensor_scalar(out=th[:], in0=imD[:, col:col + 1], scalar1=8.0, scalar2=None, op0=ALU.mult), reads=['imD'], writes=['th'])
                        reduce_angle(self, th[:], 1, ki, kf, msk, 'th')
                        P.op(a, lambda e, col=col: e.activation(out=rr[:], in_=reD[:, col:col + 1], func=AF.Exp, scale=8.0), reads=['reD'], writes=['rr'])
                        P.op(v, lambda e: e.tensor_scalar(out=rtab[:], in0=crow[:, 0:NCH], scalar1=0.0, scalar2=rr[:], op0=ALU.mult, op1=ALU.add), reads=['crow', 'rr'], writes=['rtab'])
                        P.op(v, lambda e: e.tensor_scalar(out=stab[:], in0=crow[:], scalar1=th[:], scalar2=None, op0=ALU.mult), reads=['crow', 'th'], writes=['stab'])
                        P.op(v, lambda e: e.tensor_scalar(out=ctab[:], in0=stab[:], scalar1=PI / 2, scalar2=None, op0=ALU.add), reads=['stab'], writes=['ctab'])
                        reduce_angle(self, stab[:], NCH + 1, ki, kf, msk, 'stab')
                        reduce_angle(self, ctab[:], NCH + 1, ki, kf, msk, 'ctab')
                        P.op(a, lambda e: e.activation(out=stab[:], in_=stab[:], func=AF.Sin), reads=['stab'], writes=['stab'])
                        P.op(a, lambda e: e.activation(out=ctab[:], in_=ctab[:], func=AF.Sin), reads=['ctab'], writes=['ctab'])
                        for g2 in range(2):
                            r0, r1 = g2 * 64, (g2 + 1) * 64
                            for hi_, (c0, c1) in enumerate(hs):
                                w_ = 272 * hi_
                                P.op('tensor', lambda e, d=d, g2=g2, r0=r0, r1=r1, c0=c0, c1=c1: e.matmul(pSr[r0:r1, 0:c1 - c0], lhsT=Win[:, g2, 0, :], rhs=Usb[:, d, g2, c0:c1], start=True, stop=True),
                                     reads=[('Win', g2), ('Usb', d, g2)], writes=['PS:pSr'])
                                P.op('tensor', lambda e, d=d, g2=g2, r0=r0, r1=r1, c0=c0, c1=c1: e.matmul(pSi[r0:r1, 0:c1 - c0], lhsT=Win[:, g2, 1, :], rhs=Usb[:, d, g2, c0:c1], start=True, stop=True),
                                     reads=[('Win', g2), ('Usb', d, g2)], writes=['PS:pSi'])
                                P.op(a, lambda e, r0=r0, r1=r1, c0=c0, c1=c1: e.copy(out=Sre[r0:r1, c0:c1], in_=pSr[r0:r1, 0:c1 - c0]), reads=['PS:pSr'], writes=['Sre'])
                                P.op(v, lambda e, r0=r0, r1=r1, c0=c0, c1=c1: e.tensor_copy(out=Sim[r0:r1, c0:c1], in_=pSi[r0:r1, 0:c1 - c0]), reads=['PS:pSi'], writes=['Sim'])
                        P.op(v, lambda e: e.tensor_tensor(out=xt1[:], in0=Sim[:], in1=stab[:, 1:NCH + 1], op=ALU.mult), reads=['Sim', 'stab'], writes=['xt1'])
                        P.op(v, lambda e: e.tensor_tensor(out=s2re[:], in0=Sre[:], in1=ctab[:, 1:NCH + 1], op=ALU.mult), reads=['Sre', 'ctab'], writes=['s2re'])
                        P.op(v, lambda e: e.tensor_tensor(out=s2re[:], in0=s2re[:], in1=xt1[:], op=ALU.add), reads=['s2re', 'xt1'], writes=['s2re'])
                        P.op(v, lambda e: e.tensor_tensor(out=xt1[:], in0=Sre[:], in1=stab[:, 1:NCH + 1], op=ALU.mult), reads=['Sre', 'stab', 's2re'], writes=['xt1'])
                        P.op(v, lambda e: e.tensor_tensor(out=s2im[:], in0=Sim[:], in1=ctab[:, 1:NCH + 1], op=ALU.mult), reads=['Sim', 'ctab'], writes=['s2im'])
                        P.op(v, lambda e: e.tensor_tensor(out=s2im[:], in0=s2im[:], in1=xt1[:], op=ALU.subtract), reads=['s2im', 'xt1'], writes=['s2im'])
                        P.op(v, lambda e: e.tensor_tensor_scan(out=Zre[:, 1:NCH + 1], data0=rtab[:], data1=s2re[:], initial=0.0, op0=ALU.mult, op1=ALU.add), reads=['rtab', 's2re'], writes=['Zre'])
                        P.op(v, lambda e: e.tensor_tensor_scan(out=Zim[:, 1:NCH + 1], data0=rtab[:], data1=s2im[:], initial=0.0, op0=ALU.mult, op1=ALU.add), reads=['rtab', 's2im'], writes=['Zim'])
                        P.op(v, lambda e: e.tensor_tensor(out=xt1[:], in0=Zim[:, 0:NCH], in1=stab[:, 0:NCH], op=ALU.mult), reads=['Zim', 'stab', 's2im'], writes=['xt1'])
                        P.op(v, lambda e: e.tensor_tensor(out=s2re[:], in0=Zre[:, 0:NCH], in1=ctab[:, 0:NCH], op=ALU.mult), reads=['Zre', 'ctab'], writes=['s2re'])
                        P.op(v, lambda e: e.tensor_tensor(out=Xre[:], in0=s2re[:], in1=xt1[:], op=ALU.subtract), reads=['s2re', 'xt1'], writes=['Xre'])
                        P.op(v, lambda e: e.tensor_tensor(out=xt1[:], in0=Zre[:, 0:NCH], in1=stab[:, 0:NCH], op=ALU.mult), reads=['Zre', 'stab', 'Xre'], writes=['xt1'])
                        P.op(v, lambda e: e.tensor_tensor(out=s2im[:], in0=Zim[:, 0:NCH], in1=ctab[:, 0:NCH], op=ALU.mult), reads=['Zim', 'ctab'], writes=['s2im'])
                        P.op(v, lambda e: e.tensor_tensor(out=Xim[:], in0=s2im[:], in1=xt1[:], op=ALU.add), reads=['s2im', 'xt1'], writes=['Xim'])
                        for g2 in range(2):
                            r0, r1 = g2 * 64, (g2 + 1) * 64
                            gl = gp4 * 2 + g2
                            for hi_, (c0, c1) in enumerate(hs):
                                py = pY[hi_]
                                pk = f'PS:pY{hi_}'
                                P.op('tensor', lambda e, d=d, g2=g2, c0=c0, c1=c1, py=py: e.matmul(py[:, 0:c1 - c0], lhsT=Tin[:, g2, :], rhs=Usb[:, d, g2, c0:c1], start=True, stop=False),
                                     reads=[('Tin', g2), ('Usb', d, g2)], writes=[pk])
                                P.op('tensor', lambda e, r0=r0, r1=r1, c0=c0, c1=c1, py=py: e.matmul(py[:, 0:c1 - c0], lhsT=Wore[r0:r1, :], rhs=Xre[r0:r1, c0:c1], start=False, stop=False),
                                     reads=['Wore', 'Xre'], writes=[pk])
                                P.op('tensor', lambda e, r0=r0, r1=r1, c0=c0, c1=c1, py=py: e.matmul(py[:, 0:c1 - c0], lhsT=Woim[r0:r1, :], rhs=Xim[r0:r1, c0:c1], start=False, stop=True),
                                     reads=['Woim', 'Xim'], writes=[pk])
                                if hi_ == 0:
                                    P.op(a, lambda e, d=d, gl=gl, c0=c0, c1=c1, py=py: e.copy(out=Ysb[:, d, gl, c0:c1], in_=py[:, 0:c1 - c0]), reads=[pk], writes=[('Ysb', d, gl)])
                                else:
                                    P.op(v, lambda e, d=d, gl=gl, c0=c0, c1=c1, py=py: e.tensor_copy(out=Ysb[:, d, gl, c0:c1], in_=py[:, 0:c1 - c0]), reads=[pk], writes=[('Ysb', d, gl)])
                for d in range(2):
                    for i in range(8):
                        for hi_, (c0, c1) in enumerate(hs):
                            py = pY[hi_]
                            pk = f'PS:pY{hi_}'
                            for gl in range(8):
                                P.op('tensor', lambda e, d=d, gl=gl, i=i, c0=c0, c1=c1, py=py: e.matmul(py[:, 0:c1 - c0], lhsT=Sel[:, i, gl, :], rhs=Ysb[:, d, gl, c0:c1], start=(gl == 0), stop=(gl == 7)),
                                     reads=['Sel', ('Ysb', d, gl)], writes=[pk])
                            if hi_ == 0:
                                P.op(a, lambda e, d=d, i=i, c0=c0, c1=c1, py=py: e.copy(out=yT[:, d, c0 * 8 + i:c1 * 8:8], in_=py[:, 0:c1 - c0]), reads=[pk], writes=['yT5'])
                            else:
                                P.op(v, lambda e, d=d, i=i, c0=c0, c1=c1, py=py: e.tensor_copy(out=yT[:, d, c0 * 8 + i:c1 * 8:8], in_=py[:, 0:c1 - c0]), reads=[pk], writes=['yT5'])
                P.op(v, lambda e: e.tensor_tensor(out=yT[:, 0, 0:TC], in0=yT[:, 0, 0:TC], in1=yT[:, 1, TC - 1::-1], op=ALU.add), reads=['yT5'], writes=['yT5'])
                P.op(v, lambda e: e.tensor_tensor(out=yT[:, 0, TC:T], in0=yT[:, 0, TC:T], in1=yT[:, 1, T - 1:TC - 1:-1], op=ALU.add), reads=['yT5'], writes=['yT5'])
                P.op(v, lambda e, ft=ft: e.scalar_tensor_tensor(out=yT[:, 0, :], in0=uF[:], scalar=dsk[:, ft:ft + 1], in1=yT[:, 0, :], op0=ALU.mult, op1=ALU.add), reads=['uF', 'dsk', 'yT5'], writes=['yT5'])
                P.op(a, lambda e: e.activation(out=yT[:, 1, :], in_=yT[:, 0, :], func=AF.Square), reads=['yT5'], writes=['yT5b'])
                P.op(v, lambda e: e.tensor_scalar(out=yT[:, 1, :], in0=yT[:, 1, :], scalar1=0.044715, scalar2=1.0, op0=ALU.mult, op1=ALU.add), reads=['yT5b'], writes=['yT5b'])
                P.op(v, lambda e: e.tensor_tensor(out=yT[:, 1, :], in0=yT[:, 1, :], in1=yT[:, 0, :], op=ALU.mult), reads=['yT5b', 'yT5'], writes=['yT5b'])
                P.op(a, lambda e: e.activation(out=yT[:, 1, :], in_=yT[:, 1, :], func=AF.Sigmoid, scale=1.5957691216057308), reads=['yT5b'], writes=['yT5b'])
                P.op(v, lambda e, ft=ft: e.tensor_tensor(out=gy[:, ft, :], in0=yT[:, 1, :], in1=yT[:, 0, :], op=ALU.mult), reads=['yT5b', 'yT5'], writes=[('gy', ft)])
            it = 0
            for fo in range(4):
                for (tok0, nt, m) in self.blocks():
                    py = pY[it % 2]
                    pk = f'PS:pY{it % 2}'
                    it += 1
                    for fi in range(4):
                        P.op('tensor', lambda e, fo=fo, fi=fi, tok0=tok0, nt=nt, py=py: e.matmul(py[:, :nt], lhsT=gl_w[:, fi, fo * 128:(fo + 1) * 128], rhs=gy[:, fi, tok0:tok0 + nt], start=(fi == 0), stop=(fi == 3)),
                             reads=['gluw'] + [('gy', f) for f in range(4)], writes=[pk])
                    P.op(a, lambda e, nt=nt, py=py: e.activation(out=sgm[:, :nt], in_=py[:, :nt], func=AF.Sigmoid), reads=[pk], writes=['sgm'])
                    P.op(v, lambda e, fo=fo, tok0=tok0, nt=nt: e.tensor_tensor(out=og[:, :nt], in0=sgm[:, :nt], in1=gy[:, fo, tok0:tok0 + nt], op=ALU.mult), reads=['sgm', ('gy', fo)], writes=['og5'])
                    P.dma('gpsimd', dr['YT'][fo * 128:(fo + 1) * 128, tok0:tok0 + nt], og[:, :nt], reads=['og5'], writes=[('YT5', fo, tok0)])
            P.barrier()


B.phase_s5 = phase_s5


R_QM, R_KM, R_VM, R_OM, R_IG, R_FG = 4128, 4640, 5152, 6176, 7200, 7208


def phase_mlstm(self):
    P, nc, dr = self.P, self.nc, self.dr
    sb, ps = self.sb, self.ps
    PT = dr['PT']
    v, a, g_ = 'vector', 'scalar', 'gpsimd'
    with ExitStack() as es0:
        gt_tok = sb(es0, "m_gt", [128, 34, 40], F32)
        with ExitStack() as es:
            G = sb(es, "m_G", [128, T], F32)
            bcol = sb(es, "m_bcol", [128, 1], F32)
            ptr = ps(es, "m_ptr", [128, 512], F32)
            P.op(g_, lambda e: e.memset(G[:], 0.0), writes=['m_G'])
            P.op(g_, lambda e: e.memset(bcol[:], 0.0), writes=['m_bcol'])
            P.dma('sync', G[0:8, :], PT[R_IG:R_IG + 8, :], writes=['m_G'])
            P.dma('sync', G[32:40, :], PT[R_FG:R_FG + 8, :], writes=['m_G'])
            P.dma('sync', bcol[0:8, :], dr['ml_ib'], writes=['m_bcol'])
            P.dma('sync', bcol[32:40, :], dr['ml_fb'], writes=['m_bcol'])
            P.op(a, lambda e: e.activation(out=G[0:32, :], in_=G[0:32, :], func=AF.Exp, bias=bcol[0:32, :]), reads=['m_G', 'm_bcol'], writes=['m_G'])
            P.op(v, lambda e: e.tensor_scalar(out=bcol[32:64, :], in0=bcol[32:64, :], scalar1=-1.0, scalar2=None, op0=ALU.mult), reads=['m_bcol', 'm_G'], writes=['m_bcol'])
            P.op(a, lambda e: e.activation(out=G[32:64, :], in_=G[32:64, :], func=AF.Exp, scale=-1.0, bias=bcol[32:64, :]), reads=['m_G', 'm_bcol'], writes=['m_G'])
            P.op(a, lambda e: e.activation(out=G[32:64, :], in_=G[32:64, :], func=AF.Ln, bias=self.one_col[32:64, :]), reads=['m_G', 'one_col'], writes=['m_G'])
            P.op(v, lambda e: e.tensor_scalar(out=G[32:64, :], in0=G[32:64, :], scalar1=-1.0, scalar2=None, op0=ALU.mult), reads=['m_G'], writes=['m_G'])
            for c in range(34):
                P.op('tensor', lambda e, c=c: e.matmul(ptr[:, 0:40], lhsT=G[:, c * 128:(c + 1) * 128], rhs=self.ident_f[:, 0:40], start=True, stop=True), reads=['m_G', 'ident_f'], writes=['PS:m_ptr'])
                P.op(v, lambda e, c=c: e.tensor_copy(out=gt_tok[:, c, :], in_=ptr[:, 0:40]), reads=['PS:m_ptr'], writes=['m_gt'])
            P.barrier()
        with ExitStack() as es:
            xin = [sb(es, f"m_xin{i}", [128, 16, 512], F32) for i in range(2)]
            cs = sb(es, "m_cs", [128, 16, 512], BF16)
            tokst = [sb(es, f"m_tokst{i}", [128, 1536], BF16) for i in range(2)]
            ptA = ps(es, "m_ptA", [128, 8, 128], BF16)
            ptB = ps(es, "m_ptB", [128, 8, 128], BF16)
            ci = 0
            for bi, (tok0, nt, m) in enumerate(self.blocks()):
                x = xin[bi % 2]
                xk = f'm_xin{bi % 2}'
                P.dma('sync', x[:, 0:8, :nt], PT[R_QM:R_QM + 1024, :].rearrange("(f p) t -> p f t", p=128)[:, :, tok0:tok0 + nt], writes=[xk])
                P.dma('sync', x[:, 8:16, :nt], PT[R_VM:R_VM + 1024, :].rearrange("(f p) t -> p f t", p=128)[:, :, tok0:tok0 + nt], writes=[xk])
                P.op(a, lambda e, x=x, nt=nt: e.copy(out=cs[:, 0:4, :nt], in_=x[:, 0:4, :nt]), reads=[xk], writes=['m_cs'])
                P.op(a, lambda e, x=x, nt=nt: e.activation(out=cs[:, 4:8, :nt], in_=x[:, 4:8, :nt], func=AF.Copy, scale=128 ** -0.5), reads=[xk], writes=['m_cs'])
                P.op(v, lambda e, x=x, nt=nt: e.tensor_copy(out=cs[:, 8:16, :nt], in_=x[:, 8:16, :nt]), reads=[xk], writes=['m_cs'])
                P.dma('gpsimd', dr['BCT'][0:1024, :].rearrange("(f p) t -> p f t", p=128)[:, :, tok0:tok0 + nt], cs[:, 0:8, :nt], reads=['m_cs'], writes=[('BCT', tok0)])
                for ch in range(nt // 128):
                    tk = tokst[ci % 2]
                    tkk = f'm_tokst{ci % 2}'
                    ci += 1
                    for f in range(8):
                        P.op('tensor', lambda e, f=f, ch=ch: e.transpose(out=ptA[:, f, :], in_=cs[:, 8 + f, ch * 128:(ch + 1) * 128], identity=self.ident_b[:]), reads=['m_cs', 'ident_b'], writes=['PS:m_ptA'])
                    for f in range(4):
                        P.op('tensor', lambda e, f=f, ch=ch: e.transpose(out=ptB[:, f, :], in_=cs[:, 4 + f, ch * 128:(ch + 1) * 128], identity=self.ident_b[:]), reads=['m_cs', 'ident_b'], writes=['PS:m_ptB'])
                    P.op(v, lambda e, tk=tk: e.tensor_copy(out=tk[:, 0:1024], in_=ptA[:].rearrange("p a b -> p (a b)")), reads=['PS:m_ptA'], writes=[tkk])
                    P.op(a, lambda e, tk=tk: e.copy(out=tk[:, 1024:1536], in_=ptB[:, 0:4, :].rearrange("p a b -> p (a b)")), reads=['PS:m_ptB'], writes=[tkk])
                    P.dma('gpsimd', dr['TOK'][tok0 + ch * 128:tok0 + (ch + 1) * 128, 0:1536], tk[:], reads=[tkk], writes=[('TOK', tok0 + ch * 128)])
            P.barrier()
        with ExitStack() as es:
            S_f = sb(es, "m_Sf", [128, 4, 257], F32)
            S_b = sb(es, "m_Sb", [128, 4, 257], BF16)
            tok = [sb(es, f"m_tok{i}", [128, 1536], BF16) for i in range(2)]
            qk = [sb(es, f"m_qk{i}", [128, 8, 128], BF16) for i in range(2)]
            Lm = [sb(es, f"m_Lm{i}", [128, 128], F32) for i in range(2)]
            Dm = [sb(es, f"m_Dm{i}", [128, 128], F32) for i in range(2)]
            WT = [sb(es, f"m_WT{i}", [128, 128], BF16) for i in range(2)]
            v1 = [sb(es, f"m_v1_{i}", [128, 257], BF16) for i in range(2)]
            v2 = [sb(es, f"m_v2_{i}", [128, 257], BF16) for i in range(2)]
            yA = [sb(es, f"m_yA{i}", [128, 257], F32) for i in range(2)]
            num = [sb(es, f"m_num{i}", [128, 257], F32) for i in range(2)]
            den = [sb(es, f"m_den{i}", [128, 1], F32) for i in range(2)]
            hsb = [sb(es, f"m_hsb{i}", [128, 1024], F32) for i in range(2)]
            hfl = sb(es, "m_hfl", [128, 1024], F32)
            sqj = sb(es, "m_sqj", [128, 256], F32)
            ss = sb(es, "m_ss", [128, 4], F32)
            hn = sb(es, "m_hn", [128, 1024], BF16)
            ecum = sb(es, "m_ecum", [128, 4], F32)
            cdec = sb(es, "m_cdec", [128, 4], F32)
            nwB = sb(es, "m_nwB", [128, 1024], F32)
            ytb = sb(es, "m_ytb", [128, 8, 512], BF16)
            oT = sb(es, "m_oT", [128, 8, 512], F32)
            ot = sb(es, "m_ot", [128, 8, 512], BF16)
            pX = [ps(es, f"m_pX{i}", [128, 512], F32) for i in range(2)]
            pyA = [ps(es, f"m_pyA{i}", [128, 512], F32) for i in range(2)]
            pyB = [ps(es, f"m_pyB{i}", [128, 512], F32) for i in range(2)]
            pS = ps(es, "m_pS", [128, 512], F32)
            ptT = ps(es, "m_ptT", [128, 8, 128], BF16)
            P.dma('sync', nwB[:], dr['ml_nw'].partition_broadcast(128), writes=['m_nwB'])
            li = 0
            for d in range(2):
                MASK = self.MGT if d == 0 else self.MLT
                TRI = self.TLE if d == 0 else self.TGE
                mk, tk_ = ('MGT', 'TLE') if d == 0 else ('MLT', 'TGE')
                ecol = 127 if d == 0 else 0
                P.op(g_, lambda e: e.memset(S_f[:], 0.0), writes=[('m_Sf', h) for h in range(4)])
                P.op(g_, lambda e: e.memset(S_b[:], 0.0), writes=[('m_Sb', h) for h in range(4)])
                for c in chunk_order(d):
                    t0 = c * 128
                    tb, tbk = tok[li % 2], f'm_tok{li % 2}'
                    qb, qbk = qk[li % 2], f'm_qk{li % 2}'
                    hs_, hsk = hsb[li % 2], f'm_hsb{li % 2}'
                    li += 1
                    P.dma('sync', tb[:], dr['TOK'][t0:t0 + 128, 0:1536], writes=[tbk])
                    P.dma('sync', qb[:], dr['BCT'][0:1024, :].rearrange("(f p) t -> p f t", p=128)[:, :, t0:t0 + 128], writes=[qbk])
                    lac = gt_tok[:, c, 32 + d * 4:32 + (d + 1) * 4]
                    P.op('tensor', lambda e, lac=lac: e.matmul(pS[:, 300:304], lhsT=TRI[:], rhs=lac, start=True, stop=True), reads=[tk_, 'm_gt'], writes=['PS:m_pS'])
                    P.op('tensor', lambda e, lac=lac: e.matmul(pS[:, 320:324], lhsT=self.ones_f[:], rhs=lac, start=True, stop=True), reads=['ones_f', 'm_gt'], writes=['PS:m_pS'])
                    P.op(a, lambda e: e.activation(out=ecum[:], in_=pS[:, 300:304], func=AF.Exp), reads=['PS:m_pS'], writes=['m_ecum'])
                    P.op(a, lambda e: e.activation(out=cdec[:], in_=pS[:, 320:324], func=AF.Exp), reads=['PS:m_pS'], writes=['m_cdec'])
                    for h in range(4):
                        r = h % 2
                        X, Xk = pX[r], f'PS:m_pX{r}'
                        A_, Ak = pyA[r], f'PS:m_pyA{r}'
                        B_, Bk = pyB[r], f'PS:m_pyB{r}'
                        gcol = gt_tok[:, c, 32 + d * 4 + h:32 + d * 4 + h + 1]
                        icol = gt_tok[:, c, d * 4 + h:d * 4 + h + 1]
                        P.op(v, lambda e, r=r, gcol=gcol: e.tensor_scalar(out=Lm[r][:], in0=MASK[:], scalar1=gcol, scalar2=None, op0=ALU.mult), reads=[mk, 'm_gt'], writes=[f'm_Lm{r}'])
                        P.op('tensor', lambda e, r=r, X=X: e.matmul(X[:, 0:128], lhsT=Lm[r][:], rhs=TRI[:], start=True, stop=False), reads=[f'm_Lm{r}', tk_], writes=[Xk])
                        P.op('tensor', lambda e, r=r, X=X: e.matmul(X[:, 0:128], lhsT=self.negI[:], rhs=MASK[:], start=False, stop=True), reads=['negI', mk], writes=[Xk])
                        P.op('tensor', lambda e, h=h, X=X, qb=qb: e.matmul(X[:, 128:256], lhsT=qb[:, 4 + h, :], rhs=qb[:, h, :], start=True, stop=True), reads=[qbk], writes=[Xk])
                        P.op(a, lambda e, r=r, X=X: e.activation(out=Dm[r][:], in_=X[:, 0:128], func=AF.Exp), reads=[Xk], writes=[f'm_Dm{r}'])
                        P.op(v, lambda e, r=r, X=X: e.tensor_tensor(out=WT[r][:], in0=Dm[r][:], in1=X[:, 128:256], op=ALU.mult), reads=[f'm_Dm{r}', Xk], writes=[f'm_WT{r}'])
                        P.op(g_, lambda e, r=r, h=h, tb=tb, icol=icol: e.tensor_scalar(out=v1[r][:, 0:256], in0=tb[:, h * 256:(h + 1) * 256], scalar1=icol, scalar2=None, op0=ALU.mult), reads=[tbk, 'm_gt'], writes=[f'm_v1_{r}'])
                        P.op(g_, lambda e, r=r, icol=icol: e.tensor_copy(out=v1[r][:, 256:257], in_=icol), reads=['m_gt'], writes=[f'm_v1_{r}'])
                        P.op(g_, lambda e, r=r: e.tensor_scalar(out=v2[r][:], in0=v1[r][:], scalar1=Dm[r][:, ecol:ecol + 1], scalar2=None, op0=ALU.mult), reads=[f'm_v1_{r}', f'm_Dm{r}'], writes=[f'm_v2_{r}'])
                        P.op('tensor', lambda e, r=r, A_=A_: e.matmul(A_[:, 0:257], lhsT=WT[r][:], rhs=v1[r][:], start=True, stop=True), reads=[f'm_WT{r}', f'm_v1_{r}'], writes=[Ak])
                        P.op('tensor', lambda e, h=h, B_=B_, qb=qb: e.matmul(B_[:, 0:257], lhsT=qb[:, h, :], rhs=S_b[:, h, :], start=True, stop=True), reads=[qbk, ('m_Sb', h)], writes=[Bk])
                        P.op('tensor', lambda e, r=r, h=h, tb=tb: e.matmul(pS[:, 0:257], lhsT=tb[:, 1024 + h * 128:1024 + (h + 1) * 128], rhs=v2[r][:], start=True, stop=True), reads=[tbk, f'm_v2_{r}'], writes=['PS:m_pS'])
                        P.op(a, lambda e, r=r, A_=A_: e.copy(out=yA[r][:], in_=A_[:, 0:257]), reads=[Ak], writes=[f'm_yA{r}'])
                        P.op(v, lambda e, r=r, h=h, B_=B_: e.scalar_tensor_tensor(out=num[r][:], in0=B_[:, 0:257], scalar=ecum[:, h:h + 1], in1=yA[r][:], op0=ALU.mult, op1=ALU.add), reads=[Bk, 'm_ecum', f'm_yA{r}'], writes=[f'm_num{r}'])
                        P.op(v, lambda e, h=h: e.scalar_tensor_tensor(out=S_f[:, h, :], in0=S_f[:, h, :], scalar=cdec[:, h:h + 1], in1=pS[:, 0:257], op0=ALU.mult, op1=ALU.add), reads=[('m_Sf', h), 'm_cdec', 'PS:m_pS'], writes=[('m_Sf', h)])
                        P.op(a, lambda e, h=h: e.copy(out=S_b[:, h, :], in_=S_f[:, h, :]), reads=[('m_Sf', h)], writes=[('m_Sb', h)])
                        P.op(a, lambda e, r=r: e.activation(out=den[r][:], in_=num[r][:, 256:257], func=AF.Abs), reads=[f'm_num{r}'], writes=[f'm_den{r}'])
                        P.op(v, lambda e, r=r: e.tensor_scalar(out=den[r][:], in0=den[r][:], scalar1=1.0, scalar2=None, op0=ALU.max), reads=[f'm_den{r}'], writes=[f'm_den{r}'])
                        P.op(v, lambda e, r=r: e.reciprocal(out=den[r][:], in_=den[r][:]), reads=[f'm_den{r}'], writes=[f'm_den{r}'])
                        P.op(v, lambda e, r=r, h=h, hs_=hs_: e.tensor_scalar(out=hs_[:, h * 256:(h + 1) * 256], in0=num[r][:, 0:256], scalar1=den[r][:], scalar2=None, op0=ALU.mult), reads=[f'm_num{r}', f'm_den{r}'], writes=[hsk])
                    if d == 0:
                        P.dma('gpsimd', dr['YF'][t0:t0 + 128, 0:1024], hs_[:], reads=[hsk], writes=[('YF', c)])
                    else:
                        P.dma('sync', hfl[:], dr['YF'][t0:t0 + 128, 0:1024], reads=[('YF', c)], writes=['m_hfl'])
                        P.op(v, lambda e, hs_=hs_: e.tensor_tensor(out=hfl[:], in0=hfl[:], in1=hs_[:], op=ALU.add), reads=['m_hfl', hsk], writes=['m_hfl'])
                        for h in range(4):
                            P.op(a, lambda e, h=h: e.activation(out=sqj[:], in_=hfl[:, h * 256:(h + 1) * 256], func=AF.Square, accum_out=ss[:, h:h + 1]), reads=['m_hfl'], writes=['m_sqj', 'm_ss'])
                        P.op(a, lambda e: e.activation(out=ss[:], in_=ss[:], func=AF.Sqrt, scale=1.0 / 256, bias=self.eps_col[:]), reads=['m_ss', 'eps_col'], writes=['m_ss'])
                        P.op(v, lambda e: e.reciprocal(out=ss[:], in_=ss[:]), reads=['m_ss'], writes=['m_ss'])
                        for h in range(4):
                            P.op(v, lambda e, h=h: e.scalar_tensor_tensor(out=hn[:, h * 256:(h + 1) * 256], in0=hfl[:, h * 256:(h + 1) * 256], scalar=ss[:, h:h + 1], in1=nwB[:, h * 256:(h + 1) * 256], op0=ALU.mult, op1=ALU.mult),
                                 reads=['m_hfl', 'm_ss', 'm_nwB'], writes=['m_hn'])
                        if c < 2:
                            tok0, nt, ch = 0, TC, c
                        else:
                            tok0, nt, ch = TC + ((c - 2) // 4) * 512, 512, (c - 2) % 4
                        for f in range(8):
                            P.op('tensor', lambda e, f=f: e.transpose(out=ptT[:, f, :], in_=hn[:, f * 128:(f + 1) * 128], identity=self.ident_b[:]), reads=['m_hn', 'ident_b'], writes=['PS:m_ptT'])
                        P.op(v, lambda e, ch=ch: e.tensor_copy(out=ytb[:, :, ch * 128:(ch + 1) * 128], in_=ptT[:]), reads=['PS:m_ptT'], writes=['m_ytb'])
                        if ch == 0:
                            P.dma('sync', oT[:, :, :nt], PT[R_OM:R_OM + 1024, :].rearrange("(f p) t -> p f t", p=128)[:, :, tok0:tok0 + nt], writes=['m_oT'])
                            P.op(a, lambda e, nt=nt: e.activation(out=oT[:, :, :nt], in_=oT[:, :, :nt], func=AF.Sigmoid), reads=['m_oT'], writes=['m_oT'])
                            P.op(v, lambda e, nt=nt: e.tensor_tensor(out=ot[:, :, :nt], in0=ytb[:, :, :nt], in1=oT[:, :, :nt], op=ALU.mult), reads=['m_ytb', 'm_oT'], writes=['m_ot'])
                            P.dma('gpsimd', dr['YT'][1024:2048, :].rearrange("(f p) t -> p f t", p=128)[:, :, tok0:tok0 + nt], ot[:, :, :nt], reads=['m_ot'], writes=[('YT', 'ml', tok0)])
            P.barrier()


B.phase_mlstm = phase_mlstm


R_QKV, R_ZG, R_BETA, R_A = 0, 3072, 4096, 4112


def phase_gdn(self):
    P, nc, dr = self.P, self.nc, self.dr
    sb, ps = self.sb, self.ps
    PT = dr['PT']
    v, a, g_ = 'vector', 'scalar', 'gpsimd'
    with ExitStack() as es0:
        gt_tok = sb(es0, "g_gt", [128, 34, 48], F32)
        with ExitStack() as es:
            G = sb(es, "g_G", [128, T], F32)
            bcol = sb(es, "g_bcol", [128, 1], F32)
            acol = sb(es, "g_acol", [128, 1], F32)
            ptr = ps(es, "g_ptr", [128, 512], F32)
            P.op(g_, lambda e: e.memset(G[:], 0.0), writes=['g_G'])
            P.op(g_, lambda e: e.memset(bcol[:], 0.0), writes=['g_bcol'])
            P.op(g_, lambda e: e.memset(acol[:], 0.0), writes=['g_acol'])
            P.dma('sync', G[0:16, :], PT[R_BETA:R_BETA + 16, :], writes=['g_G'])
            P.dma('sync', G[32:48, :], PT[R_A:R_A + 16, :], writes=['g_G'])
            P.dma('sync', bcol[32:48, :], dr['gdn_dtb'], writes=['g_bcol'])
            P.dma('sync', acol[32:48, :], dr['gdn_alog'], writes=['g_acol'])
            P.op(a, lambda e: e.activation(out=G[0:32, :], in_=G[0:32, :], func=AF.Sigmoid), reads=['g_G'], writes=['g_G'])
            P.op(a, lambda e: e.activation(out=acol[32:64, :], in_=acol[32:64, :], func=AF.Exp), reads=['g_acol'], writes=['g_acol'])
            P.op(v, lambda e: e.tensor_scalar(out=acol[32:64, :], in0=acol[32:64, :], scalar1=-1.0, scalar2=None, op0=ALU.mult), reads=['g_acol'], writes=['g_acol'])
            P.op(a, lambda e: e.activation(out=G[32:64, :], in_=G[32:64, :], func=AF.Exp, bias=bcol[32:64, :]), reads=['g_G', 'g_bcol'], writes=['g_G'])
            P.op(a, lambda e: e.activation(out=G[32:64, :], in_=G[32:64, :], func=AF.Ln, bias=self.one_col[32:64, :]), reads=['g_G', 'one_col'], writes=['g_G'])
            P.op(v, lambda e: e.tensor_scalar(out=G[32:64, :], in0=G[32:64, :], scalar1=acol[32:64, :], scalar2=None, op0=ALU.mult), reads=['g_G', 'g_acol'], writes=['g_G'])
            for c in range(34):
                P.op('tensor', lambda e, c=c: e.matmul(ptr[:, 0:48], lhsT=G[:, c * 128:(c + 1) * 128], rhs=self.ident_f[:, 0:48], start=True, stop=True), reads=['g_G', 'ident_f'], writes=['PS:g_ptr'])
                P.op(v, lambda e, c=c: e.tensor_copy(out=gt_tok[:, c, :], in_=ptr[:, 0:48]), reads=['PS:g_ptr'], writes=['g_gt'])
            P.barrier()
        with ExitStack() as es:
            xin = [sb(es, f"g_xin{i}", [128, 24, 514], F32) for i in range(2)]
            cs = sb(es, "g_cs", [128, 24, 512], BF16)
            cf = sb(es, "g_cf", [128, 512], F32)
            sqb = sb(es, "g_sqb", [128, 512], BF16)
            rn = sb(es, "g_rn", [128, 512], F32)
            acc = [sb(es, f"g_cacc{i}", [128, 512], F32) for i in range(2)]
            cw = sb(es, "g_cw", [128, 24, 3], F32)
            tokst = [sb(es, f"g_tokst{i}", [128, 2048], BF16) for i in range(2)]
            ptA = ps(es, "g_ptA", [128, 8, 128], BF16)
            ptB = ps(es, "g_ptB", [128, 8, 128], BF16)
            pss = ps(es, "g_pss", [128, 512], F32)
            P.dma('sync', cw[:], dr['gdn_cw'], writes=['g_cw'])
            src = PT[0:3072, :].rearrange("(f p) t -> p f t", p=128)
            ci = 0
            for bi, (tok0, nt, m) in enumerate(self.blocks()):
                x = xin[bi % 2]
                xk = f'g_xin{bi % 2}'
                seq0, seq1 = (0, TC) if m == 1 else (TC, T)
                lo = max(tok0 - 1, seq0)
                hi = min(tok0 + nt + 1, seq1)
                P.dma('sync', x[:, 0:12, lo - (tok0 - 1):hi - (tok0 - 1)], src[:, 0:12, lo:hi], writes=[xk])
                P.dma('sync', x[:, 12:24, lo - (tok0 - 1):hi - (tok0 - 1)], src[:, 12:24, lo:hi], writes=[xk])
                if lo > tok0 - 1:
                    P.op(g_, lambda e, x=x: e.memset(x[:, :, 0:1], 0.0), writes=[xk])
                if hi < tok0 + nt + 1:
                    P.op(g_, lambda e, x=x, nt=nt: e.memset(x[:, :, nt + 1:nt + 2], 0.0), writes=[xk])
                for f in range(24):
                    ac = acc[f % 2]
                    ak = f'g_cacc{f % 2}'
                    P.op(v, lambda e, x=x, ac=ac, f=f, nt=nt: e.tensor_scalar(out=ac[:, :nt], in0=x[:, f, 0:nt], scalar1=cw[:, f, 0:1], scalar2=None, op0=ALU.mult), reads=[xk, 'g_cw'], writes=[ak])
                    P.op(v, lambda e, x=x, ac=ac, f=f, nt=nt: e.scalar_tensor_tensor(out=ac[:, :nt], in0=x[:, f, 1:nt + 1], scalar=cw[:, f, 1:2], in1=ac[:, :nt], op0=ALU.mult, op1=ALU.add), reads=[xk, 'g_cw', ak], writes=[ak])
                    P.op(v, lambda e, x=x, ac=ac, f=f, nt=nt: e.scalar_tensor_tensor(out=ac[:, :nt], in0=x[:, f, 2:nt + 2], scalar=cw[:, f, 2:3], in1=ac[:, :nt], op0=ALU.mult, op1=ALU.add), reads=[xk, 'g_cw', ak], writes=[ak])
                    if f >= 16:
                        P.op(a, lambda e, ac=ac, f=f, nt=nt: e.activation(out=cs[:, f, :nt], in_=ac[:, :nt], func=AF.Silu), reads=[ak], writes=[('g_cs', f)])
                    else:
                        P.op(a, lambda e, ac=ac, nt=nt: e.activation(out=cf[:, :nt], in_=ac[:, :nt], func=AF.Silu), reads=[ak], writes=['g_cf'])
                        P.op(a, lambda e, nt=nt: e.activation(out=sqb[:, :nt], in_=cf[:, :nt], func=AF.Square), reads=['g_cf'], writes=['g_sqb'])
                        P.op('tensor', lambda e, nt=nt: e.matmul(pss[:, :nt], lhsT=self.ones_b[:], rhs=sqb[:, :nt], start=True, stop=True), reads=['g_sqb', 'ones_b'], writes=['PS:g_pss'])
                        P.op(a, lambda e, nt=nt: e.activation(out=rn[:, :nt], in_=pss[:, :nt], func=AF.Sqrt, bias=self.eps_col[:]), reads=['PS:g_pss', 'eps_col'], writes=['g_rn'])
                        P.op(v, lambda e, nt=nt: e.reciprocal(out=rn[:, :nt], in_=rn[:, :nt]), reads=['g_rn'], writes=['g_rn'])
                        sc = 128 ** -0.5 if f < 8 else 1.0
                        P.op(v, lambda e, f=f, nt=nt, sc=sc: e.scalar_tensor_tensor(out=cs[:, f, :nt], in0=cf[:, :nt], scalar=sc, in1=rn[:, :nt], op0=ALU.mult, op1=ALU.mult), reads=['g_cf', 'g_rn'], writes=[('g_cs', f)])
                P.dma('gpsimd', dr['BCT'].rearrange("(f p) t -> p f t", p=128)[:, :, tok0:tok0 + nt], cs[:, 0:16, :nt],
                      reads=[('g_cs', f) for f in range(16)], writes=[('BCT', tok0)])
                for ch in range(nt // 128):
                    tk = tokst[ci % 2]
                    tkk = f'g_tokst{ci % 2}'
                    ci += 1
                    for f in range(8):
                        P.op('tensor', lambda e, f=f, ch=ch: e.transpose(out=ptA[:, f, :], in_=cs[:, 16 + f, ch * 128:(ch + 1) * 128], identity=self.ident_b[:]), reads=[('g_cs', 16 + f), 'ident_b'], writes=['PS:g_ptA'])
                    for f in range(8):
                        P.op('tensor', lambda e, f=f, ch=ch: e.transpose(out=ptB[:, f, :], in_=cs[:, 8 + f, ch * 128:(ch + 1) * 128], identity=self.ident_b[:]), reads=[('g_cs', 8 + f), 'ident_b'], writes=['PS:g_ptB'])
                    P.op(v, lambda e, tk=tk: e.tensor_copy(out=tk[:, 0:1024], in_=ptA[:].rearrange("p a b -> p (a b)")), reads=['PS:g_ptA'], writes=[tkk])
                    P.op(a, lambda e, tk=tk: e.copy(out=tk[:, 1024:2048], in_=ptB[:].rearrange("p a b -> p (a b)")), reads=['PS:g_ptB'], writes=[tkk])
                    P.dma('gpsimd', dr['TOK'][tok0 + ch * 128:tok0 + (ch + 1) * 128, :], tk[:], reads=[tkk], writes=[('TOK', tok0 + ch * 128)])
            P.barrier()
        with ExitStack() as es:
            S_f = sb(es, "g_Sf", [128, 8, 128], F32)
            S_b = sb(es, "g_Sb", [128, 8, 128], BF16)
            tok = [sb(es, f"g_tok{i}", [128, 2048], BF16) for i in range(2)]
            qk = [sb(es, f"g_qk{i}", [128, 16, 128], BF16) for i in range(2)]
            Lm = [sb(es, f"g_Lm{i}", [128, 128], F32) for i in range(2)]
            Dm = [sb(es, f"g_Dm{i}", [128, 128], F32) for i in range(2)]
            WT = [sb(es, f"g_WT{i}", [128, 128], BF16) for i in range(2)]
            tX = [sb(es, f"g_tX{i}", [128, 128], F32) for i in range(2)]
            Xp = [[sb(es, f"g_Xp{r}{i}", [128, 128], F32) for i in range(2)] for r in range(2)]
            Np = [[sb(es, f"g_Np{r}{i}", [128, 128], F32) for i in range(2)] for r in range(2)]
            rr = [[sb(es, f"g_rr{r}{i}", [128, 256], F32) for i in range(2)] for r in range(2)]
            wT = [sb(es, f"g_wT{i}", [128, 128], F32) for i in range(2)]
            vn = [sb(es, f"g_vn{i}", [128, 128], F32) for i in range(2)]
            v1 = [sb(es, f"g_v1{i}", [128, 128], BF16) for i in range(2)]
            v2 = [sb(es, f"g_v2{i}", [128, 128], BF16) for i in range(2)]
            yA = [sb(es, f"g_yA{i}", [128, 128], F32) for i in range(2)]
            osb = [sb(es, f"g_osb{i}", [128, 1024], F32) for i in range(2)]
            ofl = sb(es, "g_ofl", [128, 1024], F32)
            sqj = sb(es, "g_sqj", [128, 128], F32)
            ss = sb(es, "g_ss", [128, 8], F32)
            hn = sb(es, "g_hn", [128, 1024], BF16)
            ecum = sb(es, "g_ecum", [128, 8], F32)
            cdec = sb(es, "g_cdec", [128, 8], F32)
            nwB = sb(es, "g_nwB", [128, 128], F32)
            ytb = sb(es, "g_ytb", [128, 8, 512], BF16)
            zT = sb(es, "g_zT", [128, 8, 512], F32)
            ot = sb(es, "g_ot", [128, 8, 512], BF16)
            pX = [ps(es, f"g_pX{i}", [128, 512], F32) for i in range(2)]
            pN = [ps(es, f"g_pN{i}", [128, 512], F32) for i in range(2)]
            pA = [ps(es, f"g_pA{i}", [128, 512], F32) for i in range(2)]
            ptT = ps(es, "g_ptT", [128, 8, 128], BF16)
            P.dma('sync', nwB[:], dr['gdn_nw'].partition_broadcast(128), writes=['g_nwB'])
            li = 0
            for d in range(2):
                MASK = self.MGT if d == 0 else self.MLT
                TRI = self.TLE if d == 0 else self.TGE
                STR = self.MLT if d == 0 else self.MGT
                mk, tk_, sk_ = ('MGT', 'TLE', 'MLT') if d == 0 else ('MLT', 'TGE', 'MGT')
                ecol = 127 if d == 0 else 0
                P.op(g_, lambda e: e.memset(S_f[:], 0.0), writes=[('g_Sf', h) for h in range(8)])
                P.op(g_, lambda e: e.memset(S_b[:], 0.0), writes=[('g_Sb', h) for h in range(8)])
                for c in chunk_order(d):
                    t0 = c * 128
                    tb, tbk = tok[li % 2], f'g_tok{li % 2}'
                    qb, qbk = qk[li % 2], f'g_qk{li % 2}'
                    os_, osk = osb[li % 2], f'g_osb{li % 2}'
                    li += 1
                    P.dma('sync', tb[:], dr['TOK'][t0:t0 + 128, :], writes=[tbk])
                    P.dma('sync', qb[:], dr['BCT'].rearrange("(f p) t -> p f t", p=128)[:, :, t0:t0 + 128], writes=[qbk])
                    lac = gt_tok[:, c, 32 + d * 8:32 + (d + 1) * 8]
                    P.op('tensor', lambda e, lac=lac: e.matmul(pX[0][:, 400:408], lhsT=TRI[:], rhs=lac, start=True, stop=True), reads=[tk_, 'g_gt'], writes=['PS:g_pX0'])
                    P.op('tensor', lambda e, lac=lac: e.matmul(pX[0][:, 420:428], lhsT=self.ones_f[:], rhs=lac, start=True, stop=True), reads=['ones_f', 'g_gt'], writes=['PS:g_pX0'])
                    P.op(a, lambda e: e.activation(out=ecum[:], in_=pX[0][:, 400:408], func=AF.Exp), reads=['PS:g_pX0'], writes=['g_ecum'])
                    P.op(a, lambda e: e.activation(out=cdec[:], in_=pX[0][:, 420:428], func=AF.Exp), reads=['PS:g_pX0'], writes=['g_cdec'])
                    for hp in range(4):
                        hs = (2 * hp, 2 * hp + 1)
                        HV = {}
                        for r, h in enumerate(hs):
                            HV[r] = dict(
                                gcol=gt_tok[:, c, 32 + d * 8 + h:32 + d * 8 + h + 1],
                                bcol=gt_tok[:, c, d * 8 + h:d * 8 + h + 1],
                                kT=qb[:, 8 + h, :], qT=qb[:, h, :],
                                vtok=tb[:, h * 128:(h + 1) * 128], ktok=tb[:, 1024 + h * 128:1024 + (h + 1) * 128])
                        for r, h in enumerate(hs):
                            H = HV[r]
                            P.op(v, lambda e, r=r, H=H: e.tensor_scalar(out=Lm[r][:], in0=MASK[:], scalar1=H['gcol'], scalar2=None, op0=ALU.mult), reads=[mk, 'g_gt'], writes=[f'g_Lm{r}'])
                        for r, h in enumerate(hs):
                            H = HV[r]
                            P.op('tensor', lambda e, r=r: e.matmul(pX[r][:, 0:128], lhsT=Lm[r][:], rhs=TRI[:], start=True, stop=False), reads=[f'g_Lm{r}', tk_], writes=[f'PS:g_pX{r}'])
                            P.op('tensor', lambda e, r=r: e.matmul(pX[r][:, 0:128], lhsT=self.negI[:], rhs=MASK[:], start=False, stop=True), reads=['negI', mk], writes=[f'PS:g_pX{r}'])
                            P.op('tensor', lambda e, r=r, H=H: e.matmul(pX[r][:, 128:256], lhsT=H['kT'], rhs=H['kT'], start=True, stop=True), reads=[qbk], writes=[f'PS:g_pX{r}'])
                            P.op('tensor', lambda e, r=r, H=H: e.matmul(pX[r][:, 256:384], lhsT=H['kT'], rhs=H['qT'], start=True, stop=True), reads=[qbk], writes=[f'PS:g_pX{r}'])
                        for r, h in enumerate(hs):
                            H = HV[r]
                            P.op(a, lambda e, r=r: e.activation(out=Dm[r][:], in_=pX[r][:, 0:128], func=AF.Exp), reads=[f'PS:g_pX{r}'], writes=[f'g_Dm{r}'])
                            P.op(v, lambda e, r=r: e.tensor_tensor(out=tX[r][:], in0=Dm[r][:], in1=pX[r][:, 128:256], op=ALU.mult), reads=[f'g_Dm{r}', f'PS:g_pX{r}'], writes=[f'g_tX{r}'])
                            P.op(v, lambda e, r=r: e.tensor_tensor(out=WT[r][:], in0=Dm[r][:], in1=pX[r][:, 256:384], op=ALU.mult), reads=[f'g_Dm{r}', f'PS:g_pX{r}'], writes=[f'g_WT{r}'])
                            P.op(v, lambda e, r=r, H=H: e.scalar_tensor_tensor(out=Xp[r][0][:], in0=tX[r][:], scalar=H['bcol'], in1=STR[:], op0=ALU.mult, op1=ALU.mult), reads=[f'g_tX{r}', 'g_gt', sk_], writes=[f'g_Xp{r}0'])
                            P.op(g_, lambda e, r=r, H=H: e.tensor_copy(out=rr[r][0][:, 0:128], in_=H['vtok']), reads=[tbk], writes=[f'g_rr{r}0'])
                            P.op(g_, lambda e, r=r, H=H, h=h: e.tensor_scalar(out=rr[r][0][:, 128:256], in0=H['ktok'], scalar1=ecum[:, h:h + 1], scalar2=None, op0=ALU.mult), reads=[tbk, 'g_ecum'], writes=[f'g_rr{r}0'])
                        for r, h in enumerate(hs):
                            P.op('tensor', lambda e, r=r: e.matmul(pN[r][:, 0:128], lhsT=Xp[r][0][:], rhs=self.ident_f[:], start=True, stop=True), reads=[f'g_Xp{r}0', 'ident_f'], writes=[f'PS:g_pN{r}'])
                            P.op('tensor', lambda e, r=r: e.matmul(pX[r][:, 0:256], lhsT=Xp[r][0][:], rhs=rr[r][0][:], start=True, stop=True), reads=[f'g_Xp{r}0', f'g_rr{r}0'], writes=[f'PS:g_pX{r}'])
                        for r, h in enumerate(hs):
                            P.op(a, lambda e, r=r: e.copy(out=Np[r][0][:], in_=pN[r][:, 0:128]), reads=[f'PS:g_pN{r}'], writes=[f'g_Np{r}0'])
                            P.op(v, lambda e, r=r: e.tensor_tensor(out=rr[r][1][:], in0=rr[r][0][:], in1=pX[r][:, 0:256], op=ALU.subtract), reads=[f'g_rr{r}0', f'PS:g_pX{r}'], writes=[f'g_rr{r}1'])
                        cur = 1
                        xi = 0
                        for lev in range(6):
                            nx = 1 - xi
                            for r, h in enumerate(hs):
                                P.op('tensor', lambda e, r=r, xi=xi: e.matmul(pN[r][:, 0:128], lhsT=Np[r][xi][:], rhs=Xp[r][xi][:], start=True, stop=True), reads=[f'g_Np{r}{xi}', f'g_Xp{r}{xi}'], writes=[f'PS:g_pN{r}'])
                                if lev < 5:
                                    P.op('tensor', lambda e, r=r, xi=xi: e.matmul(pN[r][:, 128:256], lhsT=Xp[r][xi][:], rhs=Np[r][xi][:], start=True, stop=True), reads=[f'g_Np{r}{xi}', f'g_Xp{r}{xi}'], writes=[f'PS:g_pN{r}'])
                            for r, h in enumerate(hs):
                                P.op(a, lambda e, r=r, nx=nx: e.copy(out=Xp[r][nx][:], in_=pN[r][:, 0:128]), reads=[f'PS:g_pN{r}'], writes=[f'g_Xp{r}{nx}'])
                                if lev < 5:
                                    P.op(v, lambda e, r=r, nx=nx: e.tensor_copy(out=Np[r][nx][:], in_=pN[r][:, 128:256]), reads=[f'PS:g_pN{r}'], writes=[f'g_Np{r}{nx}'])
                            for r, h in enumerate(hs):
                                P.op('tensor', lambda e, r=r, nx=nx, cur=cur: e.matmul(pX[r][:, 0:256], lhsT=Xp[r][nx][:], rhs=rr[r][cur][:], start=True, stop=True), reads=[f'g_Xp{r}{nx}', f'g_rr{r}{cur}'], writes=[f'PS:g_pX{r}'])
                            for r, h in enumerate(hs):
                                P.op(v, lambda e, r=r, cur=cur: e.tensor_tensor(out=rr[r][1 - cur][:], in0=rr[r][cur][:], in1=pX[r][:, 0:256], op=ALU.add), reads=[f'g_rr{r}{cur}', f'PS:g_pX{r}'], writes=[f'g_rr{r}{1 - cur}'])
                            cur = 1 - cur
                            xi = nx
                        for r, h in enumerate(hs):
                            R_ = rr[r][cur]
                            Rk = f'g_rr{r}{cur}'
                            P.op('tensor', lambda e, r=r, R_=R_: e.matmul(pN[r][:, 256:384], lhsT=R_[:, 128:256], rhs=self.ident_f[:], start=True, stop=True), reads=[Rk, 'ident_f'], writes=[f'PS:g_pN{r}'])
                        for r, h in enumerate(hs):
                            P.op(a, lambda e, r=r: e.copy(out=wT[r][:], in_=pN[r][:, 256:384]), reads=[f'PS:g_pN{r}'], writes=[f'g_wT{r}'])
                        for r, h in enumerate(hs):
                            P.op('tensor', lambda e, r=r, h=h: e.matmul(pA[r][:, 128:256], lhsT=wT[r][:], rhs=S_f[:, h, :], start=True, stop=True), reads=[f'g_wT{r}', ('g_Sf', h)], writes=[f'PS:g_pA{r}'])
                        for r, h in enumerate(hs):
                            H = HV[r]
                            R_ = rr[r][cur]
                            Rk = f'g_rr{r}{cur}'
                            P.op(v, lambda e, r=r, R_=R_: e.scalar_tensor_tensor(out=vn[r][:], in0=pA[r][:, 128:256], scalar=-1.0, in1=R_[:, 0:128], op0=ALU.mult, op1=ALU.add), reads=[f'PS:g_pA{r}', Rk], writes=[f'g_vn{r}'])
                            P.op(v, lambda e, r=r, H=H: e.tensor_scalar(out=v1[r][:], in0=vn[r][:], scalar1=H['bcol'], scalar2=None, op0=ALU.mult), reads=[f'g_vn{r}', 'g_gt'], writes=[f'g_v1{r}'])
                            P.op(g_, lambda e, r=r: e.tensor_scalar(out=v2[r][:], in0=v1[r][:], scalar1=Dm[r][:, ecol:ecol + 1], scalar2=None, op0=ALU.mult), reads=[f'g_v1{r}', f'g_Dm{r}'], writes=[f'g_v2{r}'])
                        for r, h in enumerate(hs):
                            H = HV[r]
                            P.op('tensor', lambda e, r=r: e.matmul(pA[r][:, 0:128], lhsT=WT[r][:], rhs=v1[r][:], start=True, stop=True), reads=[f'g_WT{r}', f'g_v1{r}'], writes=[f'PS:g_pA{r}'])
                            P.op('tensor', lambda e, r=r, h=h, H=H: e.matmul(pA[r][:, 256:384], lhsT=H['qT'], rhs=S_b[:, h, :], start=True, stop=True), reads=[qbk, ('g_Sb', h)], writes=[f'PS:g_pA{r}'])
                            P.op('tensor', lambda e, r=r, H=H: e.matmul(pA[r][:, 384:512], lhsT=H['ktok'], rhs=v2[r][:], start=True, stop=True), reads=[tbk, f'g_v2{r}'], writes=[f'PS:g_pA{r}'])
                        for r, h in enumerate(hs):
                            P.op(a, lambda e, r=r: e.copy(out=yA[r][:], in_=pA[r][:, 0:128]), reads=[f'PS:g_pA{r}'], writes=[f'g_yA{r}'])
                            P.op(v, lambda e, r=r, h=h, os_=os_: e.scalar_tensor_tensor(out=os_[:, h * 128:(h + 1) * 128], in0=pA[r][:, 256:384], scalar=ecum[:, h:h + 1], in1=yA[r][:], op0=ALU.mult, op1=ALU.add), reads=[f'PS:g_pA{r}', 'g_ecum', f'g_yA{r}'], writes=[osk])
                            P.op(v, lambda e, r=r, h=h: e.scalar_tensor_tensor(out=S_f[:, h, :], in0=S_f[:, h, :], scalar=cdec[:, h:h + 1], in1=pA[r][:, 384:512], op0=ALU.mult, op1=ALU.add), reads=[('g_Sf', h), 'g_cdec', f'PS:g_pA{r}'], writes=[('g_Sf', h)])
                            P.op(a, lambda e, h=h: e.copy(out=S_b[:, h, :], in_=S_f[:, h, :]), reads=[('g_Sf', h)], writes=[('g_Sb', h)])
                    if d == 0:
                        P.dma('gpsimd', dr['YF'][t0:t0 + 128, 1024:2048], os_[:], reads=[osk], writes=[('YFg', c)])
                    else:
                        P.dma('sync', ofl[:], dr['YF'][t0:t0 + 128, 1024:2048], reads=[('YFg', c)], writes=['g_ofl'])
                        P.op(v, lambda e, os_=os_: e.tensor_tensor(out=ofl[:], in0=ofl[:], in1=os_[:], op=ALU.add), reads=['g_ofl', osk], writes=['g_ofl'])
                        for h in range(8):
                            P.op(a, lambda e, h=h: e.activation(out=sqj[:], in_=ofl[:, h * 128:(h + 1) * 128], func=AF.Square, accum_out=ss[:, h:h + 1]), reads=['g_ofl'], writes=['g_sqj', 'g_ss'])
                        P.op(a, lambda e: e.activation(out=ss[:], in_=ss[:], func=AF.Sqrt, scale=1.0 / 128, bias=self.eps_col[:]), reads=['g_ss', 'eps_col'], writes=['g_ss'])
                        P.op(v, lambda e: e.reciprocal(out=ss[:], in_=ss[:]), reads=['g_ss'], writes=['g_ss'])
                        for h in range(8):
                            P.op(v, lambda e, h=h: e.scalar_tensor_tensor(out=hn[:, h * 128:(h + 1) * 128], in0=ofl[:, h * 128:(h + 1) * 128], scalar=ss[:, h:h + 1], in1=nwB[:], op0=ALU.mult, op1=ALU.mult),
                                 reads=['g_ofl', 'g_ss', 'g_nwB'], writes=['g_hn'])
                        if c < 2:
                            tok0, nt, ch = 0, TC, c
                        else:
                            tok0, nt, ch = TC + ((c - 2) // 4) * 512, 512, (c - 2) % 4
                        for f in range(8):
                            P.op('tensor', lambda e, f=f: e.transpose(out=ptT[:, f, :], in_=hn[:, f * 128:(f + 1) * 128], identity=self.ident_b[:]), reads=['g_hn', 'ident_b'], writes=['PS:g_ptT'])
                        P.op(v, lambda e, ch=ch: e.tensor_copy(out=ytb[:, :, ch * 128:(ch + 1) * 128], in_=ptT[:]), reads=['PS:g_ptT'], writes=['g_ytb'])
                        if ch == 0:
                            P.dma('sync', zT[:, :, :nt], PT[R_ZG:R_ZG + 1024, :].rearrange("(f p) t -> p f t", p=128)[:, :, tok0:tok0 + nt], writes=['g_zT'])
                            P.op(a, lambda e, nt=nt: e.activation(out=zT[:, :, :nt], in_=zT[:, :, :nt], func=AF.Silu), reads=['g_zT'], writes=['g_zT'])
                            P.op(v, lambda e, nt=nt: e.tensor_tensor(out=ot[:, :, :nt], in0=ytb[:, :, :nt], in1=zT[:, :, :nt], op=ALU.mult), reads=['g_ytb', 'g_zT'], writes=['g_ot'])
                            P.dma('gpsimd', dr['YT'][0:1024, :].rearrange("(f p) t -> p f t", p=128)[:, :, tok0:tok0 + nt], ot[:, :, :nt], reads=['g_ot'], writes=[('YT', 'gdn', tok0)])
            P.barrier()


B.phase_gdn = phase_gdn


def phase_final(self, src):
    P, nc, dr = self.P, self.nc, self.dr
    sb, ps = self.sb, self.ps
    with ExitStack() as es:
        xt = [sb(es, f"fin_x{i}", [128, KC, 512], F32) for i in range(2)]
        sq = sb(es, "fin_sq", [128, KC, 512], BF16)
        rstd = sb(es, "fin_rstd", [128, 512], F32)
        fw = sb(es, "fin_w", [128, KC], F32)
        pss = ps(es, "fin_pss", [128, 512], F32)
        P.dma('sync', fw[:], dr['fnwT'], writes=['fin_w'])
        for bi, (tok0, nt, m) in enumerate([(TC + i * 512, 512, 0) for i in range(4)]):
            x, xk = xt[bi % 2], f'fin_x{bi % 2}'
            ts_ = self.dyn('sync', tok0)
            P.dma('sync', x[:, :, :nt], src.rearrange("(kc p) t -> p kc t", p=128)[:, :, bass.ds(ts_, nt)], writes=[xk])
            P.op('scalar', lambda e, x=x, nt=nt: e.activation(out=sq[:, :, :nt], in_=x[:, :, :nt], func=AF.Square), reads=[xk], writes=['fin_sq'])
            for kc in range(KC):
                P.op('tensor', lambda e, kc=kc, nt=nt: e.matmul(pss[:, :nt], lhsT=self.ones_b[:], rhs=sq[:, kc, :nt], start=(kc == 0), stop=(kc == KC - 1)), reads=['fin_sq', 'ones_b'], writes=['PS:fin_pss'])
            P.op('scalar', lambda e, nt=nt: e.activation(out=rstd[:, :nt], in_=pss[:, :nt], func=AF.Sqrt, scale=1.0 / D, bias=self.eps_col[:]), reads=['PS:fin_pss', 'eps_col'], writes=['fin_rstd'])
            P.op('vector', lambda e, nt=nt: e.reciprocal(out=rstd[:, :nt], in_=rstd[:, :nt]), reads=['fin_rstd'], writes=['fin_rstd'])
            for kc in range(KC):
                P.op('vector', lambda e, kc=kc, x=x, nt=nt: e.scalar_tensor_tensor(out=x[:, kc, :nt], in0=x[:, kc, :nt], scalar=fw[:, kc:kc + 1], in1=rstd[:, :nt], op0=ALU.mult, op1=ALU.mult),
                     reads=[xk, 'fin_w', 'fin_rstd'], writes=[xk])
            P.dma('gpsimd', dr['outT'].rearrange("(kc p) t -> p kc t", p=128)[:, :, tok0 - TC:tok0 - TC + nt], x[:, :, :nt], reads=[xk], writes=[('outT', tok0)])
        P.barrier()


B.phase_final = phase_final
```

```python
import numpy as np
from contextlib import ExitStack
import concourse.bass as bass
import concourse.mybir as mybir
from concourse.bass_utils import run_bass_kernel_spmd

F32 = mybir.dt.float32
BF16 = mybir.dt.bfloat16
I32 = mybir.dt.int32
AF = mybir.ActivationFunctionType
ALU = mybir.AluOpType
AX = mybir.AxisListType

D = 2048
KC = 16
TC = 256
TL = 4096
T = TC + TL
FFN = 5504
EPS = 1e-6
EVEN_IN = 4656
ODD_IN = 7216
NEG = -30000.0

SEM_LIMIT = 4000
DMA_SLOT_LIMIT = 1200
DMA_POOL = 6


class Prog:
    ENG = ('sync', 'scalar', 'vector', 'gpsimd', 'tensor')

    def __init__(self, nc, es):
        self.nc = nc
        self.es = es
        self.e = dict(sync=nc.sync, scalar=nc.scalar, vector=nc.vector,
                      gpsimd=nc.gpsimd, tensor=nc.tensor)
        self.semh = []
        self.cur = {}
        self.cnt = {}
        self.known = {e: {} for e in self.ENG}
        self.lastw = {}
        self.readers = {}
        self.pool = {}
        self.pidx = {}
        self.nins = {e: 0 for e in self.ENG}
        for e in self.ENG:
            self._fresh(e)

    def _newsem(self, name):
        h = self.es.enter_context(self.nc.semaphore(name))
        self.semh.append(h)
        return len(self.semh) - 1

    def _fresh(self, e):
        self.cur[e] = self._newsem(f"s_{e}_{len(self.semh)}")
        self.cnt[e] = 0

    def _deps(self, reads, writes):
        d = {}

        def add(tok):
            if tok is None:
                return
            sk, val, pe = tok
            if sk not in d or d[sk][0] < val:
                d[sk] = (val, pe)
        for k in reads:
            add(self.lastw.get(k))
        for k in writes:
            add(self.lastw.get(k))
            for t in self.readers.get(k, ()):
                add(t)
        return d

    def _update(self, reads, writes, tok):
        for k in reads:
            self.readers.setdefault(k, []).append(tok)
        for k in writes:
            self.lastw[k] = tok
            self.readers[k] = []

    def _wait(self, eng, sk, val):
        if self.known[eng].get(sk, 0) >= val:
            return
        self.e[eng].wait_ge(self.semh[sk], val)
        self.known[eng][sk] = val
        self.nins[eng] += 1

    def _waits(self, eng, deps):
        for sk, (val, pe) in deps.items():
            if pe == 'tensor' and eng == 'tensor':
                continue
            self._wait(eng, sk, val)

    @staticmethod
    def _excl(reads, writes):
        ex = [k for k in reads if isinstance(k, str) and k.startswith('PS:')]
        if ex:
            reads = [k for k in reads if k not in ex]
            writes = list(writes) + ex
        return reads, writes

    defer = None

    def op(self, eng, fn, reads=(), writes=()):
        if self.defer is not None:
            self.defer.append((eng, fn, reads, writes))
            return None
        reads, writes = self._excl(reads, writes)
        deps = self._deps(reads, writes)
        self._waits(eng, deps)
        ins = fn(self.e[eng])
        if self.cnt[eng] >= SEM_LIMIT:
            self._fresh(eng)
        self.cnt[eng] += 1
        ins.then_inc(self.semh[self.cur[eng]], 1)
        tok = (self.cur[eng], self.cnt[eng], eng)
        self._update(reads, writes, tok)
        self.nins[eng] += 1
        return tok

    def dma(self, q, out, in_, reads=(), writes=(), **kw):
        deps = self._deps(reads, writes)
        self._waits(q, deps)
        E = self.e[q]
        if q not in self.pool:
            self.pool[q] = [[self._newsem(f"d_{q}_{i}_{len(self.semh)}"), 0] for i in range(DMA_POOL)]
            self.pidx[q] = 0
        i = self.pidx[q] % DMA_POOL
        self.pidx[q] += 1
        slot = self.pool[q][i]
        if slot[1] >= DMA_SLOT_LIMIT:
            self._wait(q, slot[0], 16 * slot[1])
            slot[0] = self._newsem(f"d_{q}_{i}_{len(self.semh)}")
            slot[1] = 0
        sk, n = slot
        if n > 0:
            self._wait(q, sk, 16 * n)
        ins = E.dma_start(out=out, in_=in_, **kw)
        ins.then_inc(self.semh[sk], 16)
        slot[1] = n + 1
        tok = (sk, 16 * (n + 1), 'dma')
        self._update(reads, writes, tok)
        self.nins[q] += 1
        return tok

    def barrier(self):
        toks = []
        for q, slots in self.pool.items():
            for sk, n in slots:
                if n > 0:
                    toks.append((sk, 16 * n))
        for e in self.ENG:
            if self.cnt[e] > 0:
                toks.append((self.cur[e], self.cnt[e]))
        for e in self.ENG:
            for sk, val in toks:
                self._wait(e, sk, val)
        self.lastw.clear()
        self.readers.clear()

    def finish(self):
        self.barrier()


class B:
    def __init__(self, stage=99, dbg=(), sub=99):
        self.sub = sub
        self.stage = stage
        self.dbg = set(dbg)
        self.nc = bass.Bass("TRN2", target_bir_lowering=False)
        self.dr = {}

    def din(self, name, shape, dt=F32):
        self.dr[name] = self.nc.dram_tensor(name, list(shape), dt, kind="ExternalInput").ap()
        return self.dr[name]

    def dout(self, name, shape, dt=F32):
        self.dr[name] = self.nc.dram_tensor(name, list(shape), dt, kind="ExternalOutput").ap()
        return self.dr[name]

    def dscr(self, name, shape, dt=F32):
        kind = "ExternalOutput" if name in self.dbg else "Internal"
        self.dr[name] = self.nc.dram_tensor(name, list(shape), dt, kind=kind).ap()
        return self.dr[name]

    def _uniq(self, name):
        self._names = getattr(self, '_names', {})
        n = self._names.get(name, 0)
        self._names[name] = n + 1
        return name if n == 0 else f"{name}_u{n}"

    def sb(self, es, name, shape, dt=F32):
        return es.enter_context(self.nc.sbuf_tensor(self._uniq(name), list(shape), dt))

    def ps(self, es, name, shape, dt=F32):
        return es.enter_context(self.nc.psum_tensor(self._uniq(name), list(shape), dt))

    def build(self):
        nc = self.nc
        din = self.din
        din("xT", [D, T])
        din("ccT", [128, KC, 2])
        din("mod_w", [2, D, 6 * D])
        din("mod_bT", [2, 128, 96])
        din("nmwT", [2, 128, KC])
        din("nfwT", [2, 128, KC])
        din("ev_in_w", [D, EVEN_IN])
        din("od_in_w", [D, ODD_IN])
        self.dscr("XS", [D, T + 64])
        self.dscr("PT", [ODD_IN, T])
        self.dscr("WB", [D, ODD_IN], BF16)
        din("ssd_cw", [128, 20, 3])
        din("ssd_cb", [128, 20])
        din("ssd_dtb", [48, 1])
        din("ssd_alog", [48, 1])
        din("ssd_d", [1, 24])
        din("ssd_nw", [128, 12])
        self.dscr("TOK", [T, 2048], BF16)
        self.dscr("BCT", [2048, T], BF16)
        self.dscr("YF", [T, 2048])
        self.dscr("YT", [D, T + 64], BF16)
        for nm in ("s5_lre", "s5_lim", "s5_dlt"):
            din(nm, [128, 32])
        for nm in ("s5_bre", "s5_bim", "s5_cre", "s5_cim"):
            din(nm, [128, 32, 16])
        din("s5_dsk", [128, 4])
        din("s5_glu_w", [512, 512])
        din("fnwT", [128, KC])
        din("gdn_dtb", [16, 1])
        din("gdn_alog", [16, 1])
        din("gdn_cw", [128, 24, 3])
        din("gdn_nw", [1, 128])
        din("ml_ib", [8, 1])
        din("ml_fb", [8, 1])
        din("ml_nw", [1, 1024])
        din("ev_out_w", [D, D])
        din("od_out_w", [D, D])
        din("ffn_up_w", [2, D, 2 * FFN])
        din("ffn_down_w", [2, FFN, D])
        din("ffn_cw", [2, 128, 43, 9])
        self.dscr("WO", [D, D], BF16)
        self.dscr("WU", [D, 2 * FFN], BF16)
        self.dscr("WD", [FFN, D], BF16)
        self.dscr("XS2", [D, T + 64])
        self.dout("outT", [D, TL // 2])
        din("halo_mask", [128, 2])
        if 'YTin' in self.dbg:
            din('YTin', [D, T])
        if 'XSin' in self.dbg:
            din('XSin', [D, T])
        dshapes = {'dbg_mod': [128, 96, 2], 'dbg_h': [D, T], 'dbg_gates': [2, 128, 34 * 48]}
        for k in self.dbg:
            if k in dshapes:
                self.dout(k, dshapes[k])
        with ExitStack() as es:
            self.P = Prog(nc, es)
            self.consts(es)
            for layer in range(2):
                if 'XSin' in self.dbg:
                    if layer == 0:
                        for r in range(0, D, 512):
                            self.P.dma('gpsimd', self.dr['XS'][r:r + 512, 0:T], self.dr['XSin'][r:r + 512, :], writes=['xsin'])
                        self.P.barrier()
                        continue
                self.layer(layer)
                if self.stage <= layer * 10 + 9:
                    break
            self.P.finish()
        return nc

    def consts(self, es):
        P = self.P
        sb = self.sb
        self.ident_b = sb(es, "ident_b", [128, 128], BF16)
        self.ident_f = sb(es, "ident_f", [128, 128], F32)
        self.ones_b = sb(es, "ones_b", [128, 128], BF16)
        self.ones_f = sb(es, "ones_f", [128, 128], F32)
        self.negI = sb(es, "negI", [128, 128], F32)
        self.MGT = sb(es, "MGT", [128, 128], F32)
        self.MLT = sb(es, "MLT", [128, 128], F32)
        self.TLE = sb(es, "TLE", [128, 128], F32)
        self.TGE = sb(es, "TGE", [128, 128], F32)
        g = 'gpsimd'
        P.op(g, lambda e: e.memset(self.ones_f[:], 1.0), writes=['ones_f'])
        P.op(g, lambda e: e.memset(self.ones_b[:], 1.0), writes=['ones_b'])

        def sel(out, cm, step, cmp, key, fill=0.0, src=None):
            src = self.ones_f if src is None else src
            P.op(g, lambda e: e.affine_select(out=out[:], in_=src[:], pattern=[[step, 128]], base=0,
                                              channel_multiplier=cm, compare_op=cmp, fill=fill),
                 reads=['ones_f'], writes=[key])
        sel(self.ident_f, 1, -1, ALU.is_equal, 'ident_f')
        sel(self.MGT, 1, -1, ALU.is_gt, 'MGT')
        sel(self.MLT, -1, 1, ALU.is_gt, 'MLT')
        sel(self.TLE, -1, 1, ALU.is_ge, 'TLE')
        sel(self.TGE, 1, -1, ALU.is_ge, 'TGE')
        P.op(g, lambda e: e.tensor_copy(out=self.ident_b[:], in_=self.ident_f[:]), reads=['ident_f'], writes=['ident_b'])
        P.op(g, lambda e: e.tensor_scalar(out=self.negI[:], in0=self.ident_f[:], scalar1=NEG, scalar2=0.0, op0=ALU.mult, op1=ALU.add),
             reads=['ident_f'], writes=['negI'])
        self.modT = sb(es, "modT", [128, 96, 2], F32)
        self.s1 = sb(es, "s1", [128, KC, 2], F32)
        self.s2 = sb(es, "s2", [128, KC, 2], F32)
        self.eps_col = sb(es, "eps_col", [128, 1], F32)
        P.op(g, lambda e: e.memset(self.eps_col[:], EPS), writes=['eps_col'])
        self.one_col = sb(es, "one_col", [128, 1], F32)
        P.op(g, lambda e: e.memset(self.one_col[:], 1.0), writes=['one_col'])
        self.half = {'sync': self.nc.sync.partition_id() // 4, 'gpsimd': self.nc.gpsimd.partition_id() // 4}
        self.hmask = sb(es, "hmask", [128, 2], F32)
        P.dma('sync', self.hmask[:], self.dr['halo_mask'], writes=['hmask'])
        zpad = sb(es, "zpad", [128, KC, 64], F32)
        zpadb = sb(es, "zpadb", [128, KC, 64], BF16)
        P.op(g, lambda e: e.memset(zpad[:], 0.0), writes=['zpad'])
        P.op(g, lambda e: e.memset(zpadb[:], 0.0), writes=['zpadb'])
        for nm in ('XS', 'XS2'):
            P.dma('sync', self.dr[nm].rearrange("(kc p) t -> p kc t", p=128)[:, :, T:T + 64], zpad[:], reads=['zpad'], writes=[nm + 'pad'])
        P.dma('sync', self.dr['YT'].rearrange("(kc p) t -> p kc t", p=128)[:, :, T:T + 64], zpadb[:], reads=['zpadb'], writes=['YTpad'])
        P.barrier()

    def dyn(self, q, static):
        return self.half[q] * 2048 + static

    def layer(self, layer):
        import os
        if os.environ.get('SKIP12'):
            self.phase_ssd()
            return
        self.phase_mod(layer)
        if self.stage <= layer * 10 + 1:
            return
        self.phase_inproj(layer)
        if self.stage <= layer * 10 + 2:
            return
        if layer == 0 and 'YTin' not in self.dbg:
            import os
            if not os.environ.get('NOS5'):
                self.phase_s5()
            if self.stage <= layer * 10 + 2 or os.environ.get('NOSSD'):
                return
            self.phase_ssd()
            if self.stage <= layer * 10 + 3:
                return
        if layer == 1 and 'YTin' not in self.dbg:
            import os
            if not os.environ.get('NOGDN'):
                self.phase_gdn()
            if not os.environ.get('NOML'):
                self.phase_mlstm()
            if self.stage <= layer * 10 + 3:
                return
        if 'YTin' in self.dbg:
            self.P.dma('gpsimd', self.dr['YT'], self.dr['YTin'], writes=['ytin'])
            self.P.barrier()
        src = self.dr['xT'] if layer == 0 else self.dr['XS']
        self.phase_outproj(layer, src, self.dr['XS2'])
        if self.stage <= layer * 10 + 4:
            return
        self.phase_ffn(layer, self.dr['XS2'], self.dr['XS'])
        if layer == 1:
            self.phase_final(self.dr['XS'])

    def phase_mod(self, layer):
        P, nc = self.P, self.nc
        dr = self.dr
        with ExitStack() as es:
            sb, ps = self.sb, self.ps
            PW = 768
            wt = [sb(es, f"modw{i}", [128, KC, PW], F32) for i in range(2)]
            cc = sb(es, "cc", [128, KC, 2], F32)
            scc = sb(es, "scc", [128, KC, 2], F32)
            mb = sb(es, "mb", [128, 96], F32)
            nmw = sb(es, "nmw", [128, KC], F32)
            nfw = sb(es, "nfw", [128, KC], F32)
            pm = ps(es, "pm", [128, 96, 2], F32)
            P.dma('sync', cc[:], dr["ccT"], writes=['cc'])
            P.dma('sync', mb[:], dr["mod_bT"][layer], writes=['mb'])
            P.dma('sync', nmw[:], dr["nmwT"][layer], writes=['nmw'])
            P.dma('sync', nfw[:], dr["nfwT"][layer], writes=['nfw'])
            P.op('scalar', lambda e: e.activation(out=scc[:], in_=cc[:], func=AF.Silu), reads=['cc'], writes=['scc'])
            wsrc = dr["mod_w"][layer].rearrange("(kc p) n -> p kc n", p=128)
            for pn in range(16):
                w = wt[pn % 2]
                wk = f'modw{pn % 2}'
                P.dma('sync' if pn % 2 == 0 else 'gpsimd', w[:], wsrc[:, :, pn * PW:(pn + 1) * PW], writes=[wk])
                for jj in range(6):
                    j = pn * 6 + jj
                    for kc in range(KC):
                        P.op('tensor', lambda e, w=w, jj=jj, kc=kc, j=j: e.matmul(
                            pm[:, j, :], lhsT=w[:, kc, jj * 128:(jj + 1) * 128], rhs=scc[:, kc, :],
                            start=(kc == 0), stop=(kc == KC - 1)), reads=[wk, 'scc'], writes=['PS:pm'])
            for m in range(2):
                P.op('vector', lambda e, m=m: e.tensor_tensor(out=self.modT[:, :, m], in0=pm[:, :, m], in1=mb[:], op=ALU.add),
                     reads=['PS:pm', 'mb'], writes=['modT'])
            for m in range(2):
                P.op('vector', lambda e, m=m: e.scalar_tensor_tensor(
                    out=self.s1[:, :, m], in0=self.modT[:, 16:32, m], scalar=1.0, in1=nmw[:], op0=ALU.add, op1=ALU.mult),
                    reads=['modT', 'nmw'], writes=['s1'])
                P.op('vector', lambda e, m=m: e.scalar_tensor_tensor(
                    out=self.s2[:, :, m], in0=self.modT[:, 64:80, m], scalar=1.0, in1=nfw[:], op0=ALU.add, op1=ALU.mult),
                    reads=['modT', 'nfw'], writes=['s2'])
            if 'dbg_mod' in self.dbg:
                P.dma('sync', dr['dbg_mod'], self.modT[:], reads=['modT'], writes=['dbg_mod'])
            P.barrier()

    def blocks(self):
        return [(0, TC, 1)] + [(TC + i * 512, 512, 0) for i in range(8)]

    def norm_block(self, src, tok0, nt, m, scale_t, shift_lo, bufs, keys):
        P = self.P
        xt, sq, hT, rstd, tmp, pss = bufs['xt'], bufs['sq'], bufs['hT'], bufs['rstd'], bufs['tmp'], bufs['pss']
        kx, ksq, kh, kr, kt, kp = keys
        P.dma('sync', xt[:, :, :nt], src.rearrange("(kc p) t -> p kc t", p=128)[:, :, tok0:tok0 + nt], writes=[kx])
        P.op('scalar', lambda e: e.activation(out=sq[:, :, :nt], in_=xt[:, :, :nt], func=AF.Square), reads=[kx], writes=[ksq])
        for kc in range(KC):
            P.op('tensor', lambda e, kc=kc: e.matmul(pss[:, :nt], lhsT=self.ones_b[:], rhs=sq[:, kc, :nt],
                                                     start=(kc == 0), stop=(kc == KC - 1)), reads=[ksq, 'ones_b'], writes=[kp])
        P.op('scalar', lambda e: e.activation(out=rstd[:, :nt], in_=pss[:, :nt], func=AF.Sqrt, scale=1.0 / D, bias=self.eps_col[:]),
             reads=[kp, 'eps_col'], writes=[kr])
        P.op('vector', lambda e: e.reciprocal(out=rstd[:, :nt], in_=rstd[:, :nt]), reads=[kr], writes=[kr])
        for kc in range(KC):
            tk = kt + str(kc % 2)
            t = tmp[kc % 2]
            P.op('vector', lambda e, kc=kc, t=t: e.tensor_tensor(out=t[:, :nt], in0=xt[:, kc, :nt], in1=rstd[:, :nt], op=ALU.mult),
                 reads=[kx, kr], writes=[tk])
            P.op('scalar', lambda e, kc=kc, t=t: e.activation(out=hT[:, kc, :nt], in_=t[:, :nt], func=AF.Identity,
                                                              scale=scale_t[:, kc, m:m + 1], bias=self.modT[:, shift_lo + kc, m:m + 1]),
                 reads=[tk, 'modT', 's1', 's2'], writes=[kh + str(kc)])

    def phase_inproj(self, layer):
        P, nc, dr = self.P, self.nc, self.dr
        nin = EVEN_IN if layer == 0 else ODD_IN
        wsrc = dr["ev_in_w"] if layer == 0 else dr["od_in_w"]
        WB = dr["WB"]
        for r in range(0, D, 256):
            P.dma('gpsimd', WB[r:r + 256, :nin], wsrc[r:r + 256, :], writes=[('WB', r)])
        src = dr["xT"] if layer == 0 else dr["XS"]
        ntile = [(c0, min(128, nin - c0)) for c0 in range(0, nin, 128)]
        PWT = 4
        with ExitStack() as es:
            sb, ps = self.sb, self.ps
            bufs = dict(xt=sb(es, "xt", [128, KC, 512], F32), sq=sb(es, "sq", [128, KC, 512], BF16),
                        hT=sb(es, "hT", [128, KC, 512], BF16), rstd=sb(es, "rstd", [128, 512], F32),
                        tmp=[sb(es, f"ntmp{i}", [128, 512], F32) for i in range(2)],
                        pss=ps(es, "pss", [128, 512], F32))
            wb = [sb(es, f"wb{i}", [128, KC, PWT * 128], BF16) for i in range(2)]
            stg = [sb(es, f"stg{i}", [128, 512], F32) for i in range(3)]
            pacc = [ps(es, f"pacc{i}", [128, 512], F32) for i in range(4)]
            wv = WB.rearrange("(kc p) n -> p kc n", p=128)
            hkeys = ['hT' + str(kc) for kc in range(KC)]
            it = 0
            pi = 0
            for (tok0, nt, m) in self.blocks():
                self.norm_block(src, tok0, nt, m, self.s1, 0, bufs, ('xt', 'sq', 'hT', 'rstd', 'ntmp', 'PS:pss'))
                if 'dbg_h' in self.dbg:
                    hf = bufs['xt']
                    P.op('vector', lambda e: e.tensor_copy(out=hf[:, :, :nt], in_=bufs['hT'][:, :, :nt]), reads=hkeys + ['xt'], writes=['xt'])
                    P.dma('sync', dr['dbg_h'].rearrange("(kc p) t -> p kc t", p=128)[:, :, tok0:tok0 + nt], hf[:, :, :nt], reads=['xt'], writes=['dbg_h'])
                for p0 in range(0, len(ntile), PWT):
                    tiles = ntile[p0:p0 + PWT]
                    c0 = tiles[0][0]
                    cw = sum(t[1] for t in tiles)
                    w = wb[pi % 2]
                    wk = f'wb{pi % 2}'
                    pi += 1
                    P.dma('sync', w[:, :, :cw], wv[:, :, c0:c0 + cw], reads=[('WB', r) for r in range(0, D, 256)], writes=[wk])
                    for (tc0, tw) in tiles:
                        pa = pacc[it % 4]
                        pk = f'PS:pacc{it % 4}'
                        st = stg[it % 3]
                        sk = f'stg{it % 3}'
                        for kc in range(KC):
                            P.op('tensor', lambda e, kc=kc, pa=pa, w=w, tc0=tc0, tw=tw, c0=c0: e.matmul(
                                pa[:tw, :nt], lhsT=w[:, kc, tc0 - c0:tc0 - c0 + tw], rhs=bufs['hT'][:, kc, :nt],
                                start=(kc == 0), stop=(kc == KC - 1)), reads=[wk, hkeys[kc]], writes=[pk])
                        eng = 'scalar' if it % 2 == 0 else 'vector'
                        if eng == 'scalar':
                            P.op('scalar', lambda e, pa=pa, st=st, tw=tw: e.copy(out=st[:tw, :nt], in_=pa[:tw, :nt]), reads=[pk], writes=[sk])
                        else:
                            P.op('vector', lambda e, pa=pa, st=st, tw=tw: e.tensor_copy(out=st[:tw, :nt], in_=pa[:tw, :nt]), reads=[pk], writes=[sk])
                        P.dma('gpsimd', dr['PT'][tc0:tc0 + tw, tok0:tok0 + nt], st[:tw, :nt], reads=[sk], writes=[('PT', tc0, tok0)])
                        it += 1
            P.barrier()


def _ssd_methods():
    pass


def build_program(stage=99, dbg=()):
    b = B(stage, dbg)
    for name in dbg:
        pass
    return b


def make_inputs_small(inputs, b):
    f = np.float32
    x = np.asarray(inputs['x'], f)
    ctx = np.asarray(inputs['ctx'], f)
    c = np.asarray(inputs['c'], f)
    c_ctx = np.asarray(inputs['c_ctx'], f)
    xT = np.ascontiguousarray(np.concatenate([ctx[b], x[b]], axis=0).T)
    cc = np.stack([c[b], c_ctx], axis=0)
    ccT = np.ascontiguousarray(cc.reshape(2, KC, 128).transpose(2, 1, 0))
    return {"xT": xT, "ccT": ccT}


def halo_mask(core):
    top, bot = (0.0, 1.0) if core < 4 else (1.0, 0.0)
    return np.ascontiguousarray(np.tile(np.array([[top, bot]], np.float32), (128, 1)))


def make_inputs(inputs, b):
    f = np.float32
    x = np.asarray(inputs['x'], f)
    ctx = np.asarray(inputs['ctx'], f)
    c = np.asarray(inputs['c'], f)
    c_ctx = np.asarray(inputs['c_ctx'], f)
    xT = np.ascontiguousarray(np.concatenate([ctx[b], x[b]], axis=0).T)
    cc = np.stack([c[b], c_ctx], axis=0)
    ccT = np.ascontiguousarray(cc.reshape(2, KC, 128).transpose(2, 1, 0))
    mod_b = np.asarray(inputs['mod_b'], f)
    m = {
        "xT": xT, "ccT": ccT, "halo_mask": halo_mask(0),
        "mod_w": np.ascontiguousarray(np.asarray(inputs['mod_w'], f)),
        "mod_bT": np.ascontiguousarray(mod_b.reshape(2, 96, 128).transpose(0, 2, 1)),
        "nmwT": np.ascontiguousarray(np.asarray(inputs['norm_mix_w'], f).reshape(2, KC, 128).transpose(0, 2, 1)),
        "nfwT": np.ascontiguousarray(np.asarray(inputs['norm_ffn_w'], f).reshape(2, KC, 128).transpose(0, 2, 1)),
        "ev_in_w": np.ascontiguousarray(np.asarray(inputs['ev_in_w'], f)[0]),
        "od_in_w": np.ascontiguousarray(np.asarray(inputs['od_in_w'], f)[0]),
    }
    m["ev_out_w"] = np.ascontiguousarray(np.asarray(inputs['ev_out_w'], f)[0])
    m["od_out_w"] = np.ascontiguousarray(np.asarray(inputs['od_out_w'], f)[0])
    m["ffn_up_w"] = np.ascontiguousarray(np.asarray(inputs['ffn_up_w'], f))
    m["ffn_down_w"] = np.ascontiguousarray(np.asarray(inputs['ffn_down_w'], f))
    fcw = np.asarray(inputs['ffn_conv_w'], f)
    fcw = np.concatenate([fcw.reshape(2, 9, FFN), np.zeros((2, 9, 43 * 128 - FFN), f)], axis=2)
    m["ffn_cw"] = np.ascontiguousarray(fcw.reshape(2, 9, 43, 128).transpose(0, 3, 2, 1))
    m["fnwT"] = np.ascontiguousarray(np.asarray(inputs['final_norm_w'], f).reshape(KC, 128).T)
    m["gdn_dtb"] = np.ascontiguousarray(np.asarray(inputs['gdn_dt_bias'], f)[0].reshape(16, 1))
    m["gdn_alog"] = np.ascontiguousarray(np.asarray(inputs['gdn_a_log'], f)[0].reshape(16, 1))
    gcw = np.asarray(inputs['gdn_conv_w'], f)[0]
    m["gdn_cw"] = np.ascontiguousarray(gcw.reshape(3, 24, 128).transpose(2, 1, 0))
    m["gdn_nw"] = np.ascontiguousarray(np.asarray(inputs['gdn_norm_w'], f)[0].reshape(1, 128))
    m["ml_ib"] = np.ascontiguousarray(np.asarray(inputs['mlstm_igate_b'], f)[0].reshape(8, 1))
    m["ml_fb"] = np.ascontiguousarray(np.asarray(inputs['mlstm_fgate_b'], f)[0].reshape(8, 1))
    m["ml_nw"] = np.ascontiguousarray(np.asarray(inputs['mlstm_norm_w'], f)[0].reshape(1, 1024))
    def pair(x):
        sh = x.shape[3:]
        x = x.reshape((2, 16, 2, 64) + sh)
        x = np.moveaxis(x, (2, 3), (0, 1))
        return np.ascontiguousarray(x.reshape((128, 32) + sh))
    m["s5_lre"] = pair(np.asarray(inputs['s5_lam_re'], f)[0])
    m["s5_lim"] = pair(np.asarray(inputs['s5_lam_im'], f)[0])
    m["s5_dlt"] = pair(np.repeat(np.asarray(inputs['s5_log_step'], f)[0][:, :, None], 64, axis=2))
    m["s5_bre"] = pair(np.asarray(inputs['s5_b_re'], f)[0])
    m["s5_bim"] = pair(np.asarray(inputs['s5_b_im'], f)[0])
    m["s5_cre"] = pair(np.asarray(inputs['s5_c_re'], f)[0].transpose(0, 1, 3, 2))
    m["s5_cim"] = pair(np.asarray(inputs['s5_c_im'], f)[0].transpose(0, 1, 3, 2))
    m["s5_dsk"] = np.ascontiguousarray(np.asarray(inputs['s5_d'], f)[0].reshape(4, 128).T)
    m["s5_glu_w"] = np.ascontiguousarray(np.asarray(inputs['s5_glu_w'], f)[0])
    cw = np.asarray(inputs['ssd_conv_w'], f)[0]
    m["ssd_cw"] = np.ascontiguousarray(cw.reshape(3, 20, 128).transpose(2, 1, 0))
    m["ssd_cb"] = np.ascontiguousarray(np.asarray(inputs['ssd_conv_b'], f)[0].reshape(20, 128).T)
    m["ssd_dtb"] = np.ascontiguousarray(np.asarray(inputs['ssd_dt_bias'], f)[0].reshape(48, 1))
    m["ssd_alog"] = np.ascontiguousarray(np.asarray(inputs['ssd_a_log'], f)[0].reshape(48, 1))
    m["ssd_d"] = np.ascontiguousarray(np.asarray(inputs['ssd_d'], f)[0].reshape(1, 24))
    m["ssd_nw"] = np.ascontiguousarray(np.asarray(inputs['ssd_norm_w'], f)[0].reshape(12, 128).T)
    return m


def kernel(**inputs):
    b = B()
    nc = b.build()
    n = 8
    shared = make_inputs(inputs, 0)
    in_maps = []
    for i in range(n):
        mi = dict(shared)
        if i % 4 != 0:
            pi = make_inputs_small(inputs, i % 4)
            mi.update(pi)
        mi["halo_mask"] = halo_mask(i)
        in_maps.append(mi)
    res = run_bass_kernel_spmd(nc, in_maps, core_ids=list(range(n)))
    out = np.stack([np.concatenate([res.results[i]["outT"].T, res.results[i + 4]["outT"].T], axis=0) for i in range(4)], axis=0)
    return out.astype(np.float32)


def chunk_order(d):
    if d == 0:
        return list(range(34))
    return [1, 0] + list(range(33, 1, -1))


def phase_ssd(self):
    import os
    P, nc, dr = self.P, self.nc, self.dr
    sb, ps = self.sb, self.ps
    PT = dr['PT']
    with ExitStack() as es0:
        dt_tok = sb(es0, "dt_tok", [128, 34, 48], F32)
        la_tok = sb(es0, "la_tok", [128, 34, 48], F32)
        with ExitStack() as es:
            dtr = sb(es, "dtr", [128, T], F32)
            laT = sb(es, "laT", [128, T], F32)
            P.op('gpsimd', lambda e: e.memset(dtr[:], 0.0), writes=['dtr'])
            P.op('gpsimd', lambda e: e.memset(laT[:], 0.0), writes=['laT'])
            dtb = sb(es, "dtb", [48, 1], F32)
            alog = sb(es, "alog", [48, 1], F32)
            nega = sb(es, "nega", [48, 1], F32)
            ptr = ps(es, "ptr_g", [128, 512], F32)
            if os.environ.get('SKIP12') or os.environ.get('DTRMEM'):
                P.op('gpsimd', lambda e: e.memset(dtr[:48, :], 0.5), writes=['dtr'])
            else:
                P.dma('sync', dtr[:48, :], PT[4608:4656, :], writes=['dtr'])
            P.dma('sync', dtb[:], dr['ssd_dtb'], writes=['dtb'])
            P.dma('sync', alog[:], dr['ssd_alog'], writes=['alog'])
            P.op('scalar', lambda e: e.activation(out=nega[:], in_=alog[:], func=AF.Exp), reads=['alog'], writes=['nega'])
            P.op('vector', lambda e: e.tensor_scalar(out=nega[:], in0=nega[:], scalar1=-1.0, scalar2=None, op0=ALU.mult), reads=['nega'], writes=['nega'])
            P.op('scalar', lambda e: e.activation(out=dtr[:48, :], in_=dtr[:48, :], func=AF.Exp, bias=dtb[:]), reads=['dtr', 'dtb'], writes=['dtr'])
            P.op('scalar', lambda e: e.activation(out=dtr[:48, :], in_=dtr[:48, :], func=AF.Ln, bias=self.one_col[:48, :]), reads=['dtr', 'one_col'], writes=['dtr'])
            P.op('vector', lambda e: e.tensor_scalar(out=laT[:48, :], in0=dtr[:48, :], scalar1=nega[:], scalar2=None, op0=ALU.mult), reads=['dtr', 'nega'], writes=['laT'])
            import os
            CUT = int(os.environ.get('CUT', '99'))
            if CUT <= 2:
                P.op('gpsimd', lambda e: e.memset(dt_tok[:], 0.05), writes=['dt_tok'])
                P.op('gpsimd', lambda e: e.memset(la_tok[:], -0.01), writes=['la_tok'])
            VV = os.environ.get('VV', 'ABCD')
            for c in range((34 if CUT > 3 else 1) if CUT > 2 else 0):
                if 'A' in VV:
                    P.op('tensor', lambda e, c=c: e.matmul(ptr[:, 0:48], lhsT=dtr[:, c * 128:(c + 1) * 128], rhs=self.ident_f[:, :48], start=True, stop=True),
                         reads=['dtr', 'ident_f'], writes=['PS:ptr_g'])
                if 'B' in VV:
                    P.op('tensor', lambda e, c=c: e.matmul(ptr[:, 64:112], lhsT=laT[:, c * 128:(c + 1) * 128], rhs=self.ident_f[:, :48], start=True, stop=True),
                         reads=['laT', 'ident_f'], writes=['PS:ptr_g'])
                if 'C' in VV:
                    P.op('vector', lambda e, c=c: e.tensor_copy(out=dt_tok[:, c, :], in_=ptr[:, 0:48]), reads=['PS:ptr_g'], writes=['dt_tok'])
                if 'D' in VV:
                    P.op('scalar', lambda e, c=c: e.copy(out=la_tok[:, c, :], in_=ptr[:, 64:112]), reads=['PS:ptr_g'], writes=['la_tok'])
            P.barrier()
        if self.sub <= 1:
            return
        with ExitStack() as es:
            xin = [sb(es, f"xin{i}", [128, 20, 514], F32) for i in range(2)]
            cs = sb(es, "cs", [128, 20, 512], BF16)
            acc = [sb(es, f"cacc{i}", [128, 512], F32) for i in range(2)]
            cw = sb(es, "cw", [128, 20, 3], F32)
            cb = sb(es, "cb", [128, 20], F32)
            tokst = [sb(es, f"tokst{i}", [128, 2048], BF16) for i in range(2)]
            ptA = ps(es, "ptA", [128, 8, 128], BF16)
            ptB = ps(es, "ptB", [128, 8, 128], BF16)
            P.dma('sync', cw[:], dr['ssd_cw'], writes=['cw'])
            P.dma('sync', cb[:], dr['ssd_cb'], writes=['cb'])
            src = PT[2048:4608, :].rearrange("(f p) t -> p f t", p=128)
            ci = 0
            for bi, (tok0, nt, m) in enumerate(self.blocks()):
                x = xin[bi % 2]
                xk = f'xin{bi % 2}'
                seq0, seq1 = (0, TC) if m == 1 else (TC, T)
                lo = max(tok0 - 1, seq0)
                hi = min(tok0 + nt + 1, seq1)
                P.dma('sync', x[:, :, lo - (tok0 - 1):hi - (tok0 - 1)], src[:, :, lo:hi], writes=[xk])
                if lo > tok0 - 1:
                    P.op('gpsimd', lambda e, x=x: e.memset(x[:, :, 0:1], 0.0), writes=[xk])
                if hi < tok0 + nt + 1:
                    P.op('gpsimd', lambda e, x=x, nt=nt: e.memset(x[:, :, nt + 1:nt + 2], 0.0), writes=[xk])
                for f in range(20):
                    a = acc[f % 2]
                    ak = f'cacc{f % 2}'
                    P.op('vector', lambda e, x=x, a=a, f=f, nt=nt: e.tensor_scalar(out=a[:, :nt], in0=x[:, f, 0:nt], scalar1=cw[:, f, 0:1], scalar2=None, op0=ALU.mult),
                         reads=[xk, 'cw'], writes=[ak])
                    P.op('vector', lambda e, x=x, a=a, f=f, nt=nt: e.scalar_tensor_tensor(out=a[:, :nt], in0=x[:, f, 1:nt + 1], scalar=cw[:, f, 1:2], in1=a[:, :nt], op0=ALU.mult, op1=ALU.add),
                         reads=[xk, 'cw', ak], writes=[ak])
                    P.op('vector', lambda e, x=x, a=a, f=f, nt=nt: e.scalar_tensor_tensor(out=a[:, :nt], in0=x[:, f, 2:nt + 2], scalar=cw[:, f, 2:3], in1=a[:, :nt], op0=ALU.mult, op1=ALU.add),
                         reads=[xk, 'cw', ak], writes=[ak])
                    P.op('scalar', lambda e, a=a, f=f, nt=nt: e.activation(out=cs[:, f, :nt], in_=a[:, :nt], func=AF.Silu, bias=cb[:, f:f + 1]),
                         reads=[ak, 'cb'], writes=[('cs', f)])
                P.dma('gpsimd', dr['BCT'][0:1024, :].rearrange("(f p) t -> p f t", p=128)[:, :, tok0:tok0 + nt], cs[:, 12:20, :nt],
                      reads=[('cs', f) for f in range(12, 20)], writes=[('BCT', tok0)])
                for ch in range(nt // 128):
                    tk = tokst[ci % 2]
                    tkk = f'tokst{ci % 2}'
                    ci += 1
                    for f in range(16):
                        pt_ = ptA if f < 8 else ptB
                        P.op('tensor', lambda e, f=f, ch=ch, pt_=pt_: e.transpose(out=pt_[:, f % 8, :], in_=cs[:, f, ch * 128:(ch + 1) * 128], identity=self.ident_b[:]),
                             reads=[('cs', f), 'ident_b'], writes=['PS:ptA' if f < 8 else 'PS:ptB'])
                    P.op('vector', lambda e, tk=tk: e.tensor_copy(out=tk[:, 0:1024], in_=ptA[:].rearrange("p a b -> p (a b)")), reads=['PS:ptA'], writes=[tkk])
                    P.op('scalar', lambda e, tk=tk: e.copy(out=tk[:, 1024:2048], in_=ptB[:].rearrange("p a b -> p (a b)")), reads=['PS:ptB'], writes=[tkk])
                    P.dma('gpsimd', dr['TOK'][tok0 + ch * 128:tok0 + (ch + 1) * 128, :], tk[:], reads=[tkk], writes=[('TOK', tok0 + ch * 128)])
            P.barrier()
        if 'dbg_gates' in self.dbg:
            P.dma('sync', dr['dbg_gates'][0], dt_tok[:].rearrange("p a b -> p (a b)"), reads=[], writes=['dbg_gates'])
            P.dma('sync', dr['dbg_gates'][1], la_tok[:].rearrange("p a b -> p (a b)"), reads=[], writes=['dbg_gates'])
            P.barrier()
        if self.sub <= 2:
            return
        with ExitStack() as es:
            S_f = sb(es, "S_f", [128, 24, 64], F32)
            S_b = sb(es, "S_b", [128, 24, 64], BF16)
            tok = [sb(es, f"tok{i}", [128, 2048], BF16) for i in range(2)]
            bct = [sb(es, f"bct{i}", [128, 8, 128], BF16) for i in range(2)]
            Lm = [sb(es, f"Lm{i}", [128, 128], F32) for i in range(8)]
            DmAll = sb(es, "DmAll", [128, 2, 24, 128], F32)
            WT = [sb(es, f"WT{i}", [128, 128], BF16) for i in range(6)]
            v1 = [sb(es, f"v1_{i}", [128, 64], BF16) for i in range(6)]
            v2 = [sb(es, f"v2_{i}", [128, 64], BF16) for i in range(6)]
            yA = sb(es, "yA", [128, 384], F32)
            ysb = [sb(es, f"ysb{i}", [128, 1536], F32) for i in range(2)]
            yfl = sb(es, "yfl", [128, 1536], F32)
            ytk = sb(es, "ytk", [128, 1536], BF16)
            ecum = sb(es, "ecum", [128, 24], F32)
            cdec = sb(es, "cdec", [128, 24], F32)
            dB = sb(es, "dB", [128, 24], F32)
            nw = sb(es, "ssd_nw_sb", [128, 12], F32)
            ytb = sb(es, "ytb", [128, 12, 512], BF16)
            zT = sb(es, "zT", [128, 12, 512], F32)
            yg = sb(es, "yg", [128, 12, 512], F32)
            sqg = sb(es, "sqg", [128, 12, 512], BF16)
            rstd = sb(es, "rstd_g", [128, 512], F32)
            ot = sb(es, "ot", [128, 12, 512], BF16)
            psegA = ps(es, "psegA", [128, 4, 128], F32)
            psegB = ps(es, "psegB", [128, 4, 128], F32)
            pyA = ps(es, "pyA", [128, 512], F32)
            pyB = ps(es, "pyB", [128, 512], F32)
            pS = ps(es, "pS", [128, 512], F32)
            ptT = ps(es, "ptT", [128, 8, 128], BF16)
            ptU = ps(es, "ptU", [128, 8, 128], BF16)
            pss = ps(es, "pss_g", [128, 512], F32)
            P.dma('sync', dB[:], dr['ssd_d'].partition_broadcast(128), writes=['dB'])
            P.dma('sync', nw[:], dr['ssd_nw'], writes=['ssd_nw'])
            seg = lambda r: (psegA[:, r, :] if r < 4 else psegB[:, r - 4, :])
            segk = lambda r: ('PS:psegA' if r < 4 else 'PS:psegB')
            li = 0
            for d in range(2):
                MASK = self.MGT if d == 0 else self.MLT
                TRI = self.TLE if d == 0 else self.TGE
                mk, tk_ = ('MGT', 'TLE') if d == 0 else ('MLT', 'TGE')
                ecol = 127 if d == 0 else 0
                P.op('gpsimd', lambda e: e.memset(S_f[:], 0.0), writes=[('S_f', h) for h in range(24)])
                P.op('gpsimd', lambda e: e.memset(S_b[:], 0.0), writes=[('S_b', h) for h in range(24)])
                order = chunk_order(d)

                def stageA(c, par):
                    t0 = c * 128
                    P.dma('sync', tok[par][:], dr['TOK'][t0:t0 + 128, :], writes=[f'tok{par}'])
                    P.dma('sync', bct[par][:], dr['BCT'][0:1024, :].rearrange("(f p) t -> p f t", p=128)[:, :, t0:t0 + 128], writes=[f'bct{par}'])
                    for b in range(6):
                        bank, bk = (psegA, 'PS:psegA') if b % 2 == 0 else (psegB, 'PS:psegB')
                        for r4 in range(4):
                            h = b * 4 + r4
                            L = Lm[(b % 2) * 4 + r4]
                            Lk = f'Lm{(b % 2) * 4 + r4}'
                            P.op('vector', lambda e, L=L, h=h, c=c: e.tensor_scalar(out=L[:], in0=MASK[:], scalar1=la_tok[:, c, d * 24 + h:d * 24 + h + 1], scalar2=None, op0=ALU.mult),
                                 reads=[mk, 'la_tok'], writes=[Lk])
                        for r4 in range(4):
                            L = Lm[(b % 2) * 4 + r4]
                            Lk = f'Lm{(b % 2) * 4 + r4}'
                            P.op('tensor', lambda e, L=L, bank=bank, r4=r4: e.matmul(bank[:, r4, :], lhsT=L[:], rhs=TRI[:], start=True, stop=False), reads=[Lk, tk_], writes=[bk])
                            P.op('tensor', lambda e, bank=bank, r4=r4: e.matmul(bank[:, r4, :], lhsT=self.negI[:], rhs=MASK[:], start=False, stop=True), reads=['negI', mk], writes=[bk])
                        for r4 in range(4):
                            h = b * 4 + r4
                            P.op('scalar', lambda e, bank=bank, r4=r4, h=h, par=par: e.activation(out=DmAll[:, par, h, :], in_=bank[:, r4, :], func=AF.Exp), reads=[bk], writes=[('DmAll', par, h)])

                stageA(order[0], li % 2)
                for oi, c in enumerate(order):
                    t0 = c * 128
                    par = li % 2
                    tb = tok[par]
                    tbk = f'tok{par}'
                    bc = bct[par]
                    bck = f'bct{par}'
                    ys = ysb[par]
                    ysk = f'ysb{par}'
                    li += 1
                    if oi + 1 < len(order):
                        stageA(order[oi + 1], li % 2)
                    lac = la_tok[:, c, d * 24:(d + 1) * 24]
                    P.op('tensor', lambda e, lac=lac: e.matmul(pss[:, 128:152], lhsT=TRI[:], rhs=lac, start=True, stop=True),
                         reads=[tk_, 'la_tok'], writes=['PS:pss_g'])
                    P.op('tensor', lambda e, lac=lac: e.matmul(pss[:, 160:184], lhsT=self.ones_f[:], rhs=lac, start=True, stop=True),
                         reads=['ones_f', 'la_tok'], writes=['PS:pss_g'])
                    P.op('scalar', lambda e: e.activation(out=ecum[:], in_=pss[:, 128:152], func=AF.Exp), reads=['PS:pss_g'], writes=['ecum'])
                    P.op('scalar', lambda e: e.activation(out=cdec[:], in_=pss[:, 160:184], func=AF.Exp), reads=['PS:pss_g'], writes=['cdec'])
                    for g in range(4):
                        hs = [g * 6 + r for r in range(6)]
                        P.op('tensor', lambda e, g=g, bc=bc: e.matmul(pss[:, 0:128], lhsT=bc[:, g, :], rhs=bc[:, 4 + g, :], start=True, stop=True),
                             reads=[bck], writes=['PS:pss_g'])
                        for r, h in enumerate(hs):
                            P.op('vector', lambda e, r=r, h=h, par=par: e.tensor_tensor(out=WT[r][:], in0=DmAll[:, par, h, :], in1=pss[:, 0:128], op=ALU.mult),
                                 reads=[('DmAll', par, h), 'PS:pss_g'], writes=[f'WT{r}'])
                            P.op('gpsimd', lambda e, r=r, h=h, c=c, tb=tb: e.tensor_scalar(out=v1[r][:], in0=tb[:, h * 64:(h + 1) * 64], scalar1=dt_tok[:, c, d * 24 + h:d * 24 + h + 1], scalar2=0.0, op0=ALU.mult, op1=ALU.add),
                                 reads=[tbk, 'dt_tok'], writes=[f'v1_{r}'])
                            P.op('gpsimd', lambda e, r=r, h=h, par=par: e.tensor_scalar(out=v2[r][:], in0=v1[r][:], scalar1=DmAll[:, par, h, ecol:ecol + 1], scalar2=0.0, op0=ALU.mult, op1=ALU.add),
                                 reads=[f'v1_{r}', ('DmAll', par, h)], writes=[f'v2_{r}'])
                        for r, h in enumerate(hs):
                            P.op('tensor', lambda e, r=r: e.matmul(pyA[:, r * 64:(r + 1) * 64], lhsT=WT[r][:], rhs=v1[r][:], start=True, stop=True),
                                 reads=[f'WT{r}', f'v1_{r}'], writes=['PS:pyA'])
                            P.op('tensor', lambda e, r=r, h=h, g=g, bc=bc: e.matmul(pyB[:, r * 64:(r + 1) * 64], lhsT=bc[:, 4 + g, :], rhs=S_b[:, h, :], start=True, stop=True),
                                 reads=[bck, ('S_b', h)], writes=['PS:pyB'])
                            P.op('tensor', lambda e, r=r, g=g, tb=tb: e.matmul(pS[:, r * 64:(r + 1) * 64], lhsT=tb[:, 1536 + g * 128:1536 + (g + 1) * 128], rhs=v2[r][:], start=True, stop=True),
                                 reads=[tbk, f'v2_{r}'], writes=['PS:pS'])
                        P.op('scalar', lambda e: e.copy(out=yA[:], in_=pyA[:, 0:384]), reads=['PS:pyA'], writes=['yA'])
                        for r, h in enumerate(hs):
                            P.op('vector', lambda e, r=r, h=h, ys=ys: e.scalar_tensor_tensor(out=ys[:, h * 64:(h + 1) * 64], in0=pyB[:, r * 64:(r + 1) * 64], scalar=ecum[:, h:h + 1], in1=yA[:, r * 64:(r + 1) * 64], op0=ALU.mult, op1=ALU.add),
                                 reads=['PS:pyB', 'ecum', 'yA'], writes=[ysk])
                            P.op('vector', lambda e, r=r, h=h: e.scalar_tensor_tensor(out=S_f[:, h, :], in0=S_f[:, h, :], scalar=cdec[:, h:h + 1], in1=pS[:, r * 64:(r + 1) * 64], op0=ALU.mult, op1=ALU.add),
                                 reads=[('S_f', h), 'cdec', 'PS:pS'], writes=[('S_f', h)])
                            P.op('scalar', lambda e, h=h: e.copy(out=S_b[:, h, :], in_=S_f[:, h, :]), reads=[('S_f', h)], writes=[('S_b', h)])
                    if d == 0:
                        P.dma('gpsimd', dr['YF'][t0:t0 + 128, 0:1536], ys[:], reads=[ysk], writes=[('YF', c)])
                    else:
                        P.dma('sync', yfl[:], dr['YF'][t0:t0 + 128, 0:1536], reads=[('YF', c)], writes=['yfl'])
                        P.op('vector', lambda e, ys=ys: e.tensor_tensor(out=yfl[:], in0=yfl[:], in1=ys[:], op=ALU.add), reads=['yfl', ysk], writes=['yfl'])
                        for h in range(24):
                            P.op('vector', lambda e, h=h, tb=tb: e.scalar_tensor_tensor(out=ytk[:, h * 64:(h + 1) * 64], in0=tb[:, h * 64:(h + 1) * 64], scalar=dB[:, h:h + 1], in1=yfl[:, h * 64:(h + 1) * 64], op0=ALU.mult, op1=ALU.add),
                                 reads=[tbk, 'dB', 'yfl'], writes=['ytk'])
                        if c < 2:
                            tok0, nt, ch = 0, TC, c
                        else:
                            tok0, nt, ch = TC + ((c - 2) // 4) * 512, 512, (c - 2) % 4
                        for f in range(12):
                            pt_ = ptT if f < 8 else ptU
                            P.op('tensor', lambda e, f=f, pt_=pt_: e.transpose(out=pt_[:, f % 8, :], in_=ytk[:, f * 128:(f + 1) * 128], identity=self.ident_b[:]),
                                 reads=['ytk', 'ident_b'], writes=['PS:ptT' if f < 8 else 'PS:ptU'])
                        P.op('vector', lambda e, ch=ch: e.tensor_copy(out=ytb[:, 0:8, ch * 128:(ch + 1) * 128], in_=ptT[:]), reads=['PS:ptT'], writes=['ytb'])
                        P.op('scalar', lambda e, ch=ch: e.copy(out=ytb[:, 8:12, ch * 128:(ch + 1) * 128], in_=ptU[:, 0:4, :]), reads=['PS:ptU'], writes=['ytb'])
                        if ch == 0:
                            P.dma('sync', zT[:, :, :nt], PT[512:2048, :].rearrange("(f p) t -> p f t", p=128)[:, :, tok0:tok0 + nt], writes=['zT'])
                            P.op('scalar', lambda e, nt=nt: e.activation(out=zT[:, :, :nt], in_=zT[:, :, :nt], func=AF.Silu), reads=['zT'], writes=['zT'])
                            P.op('vector', lambda e, nt=nt: e.tensor_tensor(out=yg[:, :, :nt], in0=ytb[:, :, :nt], in1=zT[:, :, :nt], op=ALU.mult), reads=['ytb', 'zT'], writes=['yg'])
                            P.op('scalar', lambda e, nt=nt: e.activation(out=sqg[:, :, :nt], in_=yg[:, :, :nt], func=AF.Square), reads=['yg'], writes=['sqg'])
                            for gq in range(4):
                                for i3 in range(3):
                                    P.op('tensor', lambda e, gq=gq, i3=i3, nt=nt: e.matmul(pss[:, :nt], lhsT=self.ones_b[:], rhs=sqg[:, gq * 3 + i3, :nt], start=(i3 == 0), stop=(i3 == 2)),
                                         reads=['sqg', 'ones_b'], writes=['PS:pss_g'])
                                P.op('scalar', lambda e, nt=nt: e.activation(out=rstd[:, :nt], in_=pss[:, :nt], func=AF.Sqrt, scale=1.0 / 384, bias=self.eps_col[:]),
                                     reads=['PS:pss_g', 'eps_col'], writes=['rstd_g'])
                                P.op('vector', lambda e, nt=nt: e.reciprocal(out=rstd[:, :nt], in_=rstd[:, :nt]), reads=['rstd_g'], writes=['rstd_g'])
                                for i3 in range(3):
                                    f = gq * 3 + i3
                                    P.op('vector', lambda e, f=f, nt=nt: e.scalar_tensor_tensor(out=ot[:, f, :nt], in0=yg[:, f, :nt], scalar=nw[:, f:f + 1], in1=rstd[:, :nt], op0=ALU.mult, op1=ALU.mult),
                                         reads=['yg', 'ssd_nw', 'rstd_g'], writes=['ot'])
                            P.dma('gpsimd', dr['YT'][512:2048, :].rearrange("(f p) t -> p f t", p=128)[:, :, tok0:tok0 + nt], ot[:, :, :nt], reads=['ot'], writes=[('YT', 'ssd', tok0)])
            P.barrier()


B.phase_ssd = phase_ssd


def norm_cols(self, xt, sq, hT, rstd, tmp, pss, c0, n, m, scale_t, shift_lo, tag):
    P = self.P
    kx, ksq, kr, kp = tag + 'xt', tag + 'sq', tag + 'rstd', 'PS:' + tag + 'pss'
    P.op('scalar', lambda e: e.activation(out=sq[:, :, c0:c0 + n], in_=xt[:, :, c0:c0 + n], func=AF.Square), reads=[kx], writes=[ksq])
    for kc in range(KC):
        P.op('tensor', lambda e, kc=kc: e.matmul(pss[:, :n], lhsT=self.ones_b[:], rhs=sq[:, kc, c0:c0 + n],
                                                 start=(kc == 0), stop=(kc == KC - 1)), reads=[ksq, 'ones_b'], writes=[kp])
    P.op('scalar', lambda e: e.activation(out=rstd[:, :n], in_=pss[:, :n], func=AF.Sqrt, scale=1.0 / D, bias=self.eps_col[:]),
         reads=[kp, 'eps_col'], writes=[kr])
    P.op('vector', lambda e: e.reciprocal(out=rstd[:, :n], in_=rstd[:, :n]), reads=[kr], writes=[kr])
    for kc in range(KC):
        tk = tag + 'ntmp' + str(kc % 2)
        t = tmp[kc % 2]
        P.op('vector', lambda e, kc=kc, t=t: e.tensor_tensor(out=t[:, :n], in0=xt[:, kc, c0:c0 + n], in1=rstd[:, :n], op=ALU.mult),
             reads=[kx, kr], writes=[tk])
        P.op('scalar', lambda e, kc=kc, t=t: e.activation(out=hT[:, kc, c0:c0 + n], in_=t[:, :n], func=AF.Identity,
                                                          scale=scale_t[:, kc, m:m + 1], bias=self.modT[:, shift_lo + kc, m:m + 1]),
             reads=[tk, 'modT', 's1', 's2'], writes=[tag + 'hT'])


def phase_outproj(self, layer, src, dst):
    P, nc, dr = self.P, self.nc, self.dr
    sb, ps = self.sb, self.ps
    wsrc = dr['ev_out_w'] if layer == 0 else dr['od_out_w']
    WO = dr['WO']
    for r in range(0, D, 512):
        P.dma('gpsimd', WO[r:r + 512, :], wsrc[r:r + 512, :], writes=[('WO', r)])
    split = (layer == 1)
    if split:
        blocks = [(TC - 64 + i * 512, 512, 0) for i in range(4)] + [(TC - 64 + 2048, 128, 0)]
    else:
        blocks = self.blocks()
    with ExitStack() as es:
        W = sb(es, "wo_sb", [128, KC, D], BF16)
        xt = [sb(es, f"ox{i}", [128, KC, 512], F32) for i in range(2)]
        yt = [sb(es, f"oy{i}", [128, KC, 512], BF16) for i in range(2)]
        pacc = [ps(es, f"opacc{i}", [128, 512], F32) for i in range(4)]
        P.dma('sync', W[:], WO.rearrange("(kc p) n -> p kc n", p=128), reads=[('WO', r) for r in range(0, D, 512)], writes=['wo_sb'])
        it = 0
        for bi, (tok0, nt, m) in enumerate(blocks):
            x, xk = xt[bi % 2], f'ox{bi % 2}'
            y, yk = yt[bi % 2], f'oy{bi % 2}'
            ts_ = self.dyn('sync', tok0) if split else tok0
            tg_ = self.dyn('gpsimd', tok0) if split else tok0
            P.dma('sync', x[:, :, :nt], src.rearrange("(kc p) t -> p kc t", p=128)[:, :, (bass.ds(ts_, nt) if split else slice(ts_, ts_ + nt))], writes=[xk])
            P.dma('sync', y[:, :, :nt], dr['YT'].rearrange("(kc p) t -> p kc t", p=128)[:, :, (bass.ds(ts_, nt) if split else slice(ts_, ts_ + nt))], writes=[yk])
            for d in range(KC):
                pa, pk = pacc[it % 4], f'PS:opacc{it % 4}'
                it += 1
                for kc in range(KC):
                    P.op('tensor', lambda e, kc=kc, d=d, pa=pa, y=y, nt=nt: e.matmul(pa[:, :nt], lhsT=W[:, kc, d * 128:(d + 1) * 128], rhs=y[:, kc, :nt],
                                                                                   start=(kc == 0), stop=(kc == KC - 1)), reads=['wo_sb', yk], writes=[pk])
                P.op('vector', lambda e, d=d, pa=pa, x=x, nt=nt, m=m: e.scalar_tensor_tensor(out=x[:, d, :nt], in0=pa[:, :nt], scalar=self.modT[:, 32 + d, m:m + 1], in1=x[:, d, :nt], op0=ALU.mult, op1=ALU.add),
                     reads=[pk, 'modT', xk], writes=[xk])
            P.dma('gpsimd', dst.rearrange("(kc p) t -> p kc t", p=128)[:, :, (bass.ds(tg_, nt) if split else slice(tg_, tg_ + nt))], x[:, :, :nt], reads=[xk], writes=[('X1', tok0)])
        P.barrier()


def phase_ffn(self, layer, src, dst):
    P, nc, dr = self.P, self.nc, self.dr
    sb, ps = self.sb, self.ps
    WU, WD = dr['WU'], dr['WD']
    for r in range(0, D, 128):
        P.dma('gpsimd', WU[r:r + 128, :], dr['ffn_up_w'][layer][r:r + 128, :], writes=[('WU', r)])
    for r in range(0, FFN, 512):
        r1 = min(r + 512, FFN)
        P.dma('gpsimd', WD[r:r1, :], dr['ffn_down_w'][layer][r:r1, :], writes=[('WD', r)])
    wu_keys = [('WU', r) for r in range(0, D, 128)]
    wd_keys = [('WD', r) for r in range(0, FFN, 512)]
    NF = 43
    split = (layer == 1)
    blocks = [(TC + i * 512, 512, 0) for i in range(4)] if split else self.blocks()
    with ExitStack() as es0:
        gT = sb(es0, "gT", [128, NF, 512], BF16)
        xt = sb(es0, "fx", [128, KC, 640], F32)
        cwt = sb(es0, "fcw", [128, NF, 9], F32)
        P.dma('sync', cwt[:], dr['ffn_cw'][layer], writes=['fcw'])
        for bi, (tok0, nt, m) in enumerate(blocks):
            if split:
                lo, hi, off = tok0 - 64, tok0 + nt + 64, 0
            elif m == 1:
                lo, hi, off = 0, TC, 0
            else:
                lo = max(tok0 - 64, TC)
                hi = min(tok0 + nt + 64, T)
                off = lo - (tok0 - 64)
            nw = hi - lo
            with ExitStack() as es:
                sq = sb(es, "fsq", [128, KC, 640], BF16)
                hT = sb(es, "fh", [128, KC, 640], BF16)
                rstd = sb(es, "frstd", [128, 512], F32)
                tmp = [sb(es, f"ftmp{i}", [128, 512], F32) for i in range(2)]
                wa = [sb(es, f"fwa{i}", [128, KC, 256], BF16) for i in range(2)]
                wv = [sb(es, f"fwv{i}", [128, KC, 256], BF16) for i in range(2)]
                asb = [sb(es, f"fasb{i}", [128, 10, 66], F32) for i in range(2)]
                acc = [sb(es, f"facc{i}", [128, 8, 64], F32) for i in range(2)]
                sg = [sb(es, f"fsg{i}", [128, 512], F32) for i in range(2)]
                pss = ps(es, "fpss", [128, 512], F32)
                pa0 = [ps(es, f"fpa0_{i}", [128, 512], F32) for i in range(2)]
                pa1 = [ps(es, f"fpa1_{i}", [128, 512], F32) for i in range(2)]
                pv = [ps(es, f"fpv{i}", [128, 512], F32) for i in range(2)]
                lo_ = self.dyn('sync', lo) if split else lo
                P.dma('sync', xt[:, :, off:off + nw], src.rearrange("(kc p) t -> p kc t", p=128)[:, :, (bass.ds(lo_, nw) if split else slice(lo_, lo_ + nw))], writes=['fxt'])
                c = off
                while c < off + nw:
                    n = min(320, off + nw - c)
                    norm_cols(self, xt, sq, hT, rstd, tmp, pss, c, n, m, self.s2, 48, 'f')
                    c += n
                for i in range(2):
                    P.op('gpsimd', lambda e, i=i: e.memset(asb[i][:], 0.0), writes=[f'fasb{i}'])
                wuv = WU.rearrange("(kc p) n -> p kc n", p=128)
                for fp in range(0, NF, 2):
                    nf = min(2, NF - fp)
                    wi = (fp // 2) % 2
                    P.dma('sync', wa[wi][:, :, :nf * 128], wuv[:, :, fp * 128:(fp + nf) * 128], reads=wu_keys, writes=[f'fwa{wi}'])
                    P.dma('sync', wv[wi][:, :, :nf * 128], wuv[:, :, FFN + fp * 128:FFN + (fp + nf) * 128], reads=wu_keys, writes=[f'fwv{wi}'])
                    for ff in range(nf):
                        f = fp + ff
                        b2 = f % 2
                        A, Ak = asb[b2], f'fasb{b2}'
                        if m == 1:
                            P0, P0k = pa0[b2], f'PS:fpa0_{b2}'
                            for kc in range(KC):
                                P.op('tensor', lambda e, kc=kc, ff=ff, wi=wi, P0=P0: e.matmul(P0[:, :256], lhsT=wa[wi][:, kc, ff * 128:(ff + 1) * 128], rhs=hT[:, kc, 0:256], start=(kc == 0), stop=(kc == KC - 1)),
                                     reads=[f'fwa{wi}', 'fhT'], writes=[P0k])
                            Av = A[:].rearrange("p a b -> p (a b)")
                            P.op('scalar', lambda e, Av=Av, P0=P0: e.copy(out=Av[:, 1:257], in_=P0[:, :256]), reads=[P0k], writes=[Ak])
                            ac, ack = acc[b2][:].rearrange("p a b -> p (a b)"), f'facc{b2}'
                            P.op('vector', lambda e, Av=Av, ac=ac, f=f: e.tensor_scalar(out=ac[:, :256], in0=Av[:, 0:256], scalar1=cwt[:, f, 3:4], scalar2=None, op0=ALU.mult), reads=[Ak, 'fcw'], writes=[ack])
                            for dx in (1, 2):
                                P.op('vector', lambda e, Av=Av, ac=ac, f=f, dx=dx: e.scalar_tensor_tensor(out=ac[:, :256], in0=Av[:, dx:dx + 256], scalar=cwt[:, f, 3 + dx:4 + dx], in1=ac[:, :256], op0=ALU.mult, op1=ALU.add),
                                     reads=[Ak, 'fcw', ack], writes=[ack])
                            vlo = 0
                        else:
                            P0, P0k = pa0[b2], f'PS:fpa0_{b2}'
                            P1, P1k = pa1[b2], f'PS:fpa1_{b2}'
                            h0 = min(320, nw)
                            h1 = nw - h0
                            for kc in range(KC):
                                P.op('tensor', lambda e, kc=kc, ff=ff, wi=wi, P0=P0, h0=h0: e.matmul(P0[:, :h0], lhsT=wa[wi][:, kc, ff * 128:(ff + 1) * 128], rhs=hT[:, kc, off:off + h0], start=(kc == 0), stop=(kc == KC - 1)),
                                     reads=[f'fwa{wi}', 'fhT'], writes=[P0k])
                            for kc in range(KC):
                                P.op('tensor', lambda e, kc=kc, ff=ff, wi=wi, P1=P1, h0=h0, h1=h1: e.matmul(P1[:, :h1], lhsT=wa[wi][:, kc, ff * 128:(ff + 1) * 128], rhs=hT[:, kc, off + h0:off + h0 + h1], start=(kc == 0), stop=(kc == KC - 1)),
                                     reads=[f'fwa{wi}', 'fhT'], writes=[P1k])
                            r_off = off // 64
                            P.op('scalar', lambda e, A=A, P0=P0, h0=h0, r_off=r_off: e.copy(out=A[:, r_off:r_off + h0 // 64, 1:65], in_=P0[:, :h0].rearrange("p (a b) -> p a b", b=64)), reads=[P0k], writes=[Ak])
                            P.op('scalar', lambda e, A=A, P1=P1, h0=h0, h1=h1, r_off=r_off: e.copy(out=A[:, r_off + h0 // 64:r_off + (h0 + h1) // 64, 1:65], in_=P1[:, :h1].rearrange("p (a b) -> p a b", b=64)), reads=[P1k], writes=[Ak])
                            if split and bi == 0:
                                P.op('gpsimd', lambda e, A=A: e.tensor_scalar(out=A[:, 0, :], in0=A[:, 0, :], scalar1=self.hmask[:, 0:1], scalar2=0.0, op0=ALU.mult, op1=ALU.add), reads=[Ak, 'hmask'], writes=[Ak])
                            if split and bi == 3:
                                P.op('gpsimd', lambda e, A=A: e.tensor_scalar(out=A[:, 9, :], in0=A[:, 9, :], scalar1=self.hmask[:, 1:2], scalar2=0.0, op0=ALU.mult, op1=ALU.add), reads=[Ak, 'hmask'], writes=[Ak])
                            ac3, ack = acc[b2], f'facc{b2}'
                            first = True
                            for dy in range(3):
                                for dx in range(3):
                                    tap = dy * 3 + dx
                                    if first:
                                        P.op('vector', lambda e, A=A, ac3=ac3, f=f, dy=dy, dx=dx, tap=tap: e.tensor_scalar(out=ac3[:], in0=A[:, dy:dy + 8, dx:dx + 64], scalar1=cwt[:, f, tap:tap + 1], scalar2=None, op0=ALU.mult),
                                             reads=[Ak, 'fcw'], writes=[ack])
                                        first = False
                                    else:
                                        P.op('vector', lambda e, A=A, ac3=ac3, f=f, dy=dy, dx=dx, tap=tap: e.scalar_tensor_tensor(out=ac3[:], in0=A[:, dy:dy + 8, dx:dx + 64], scalar=cwt[:, f, tap:tap + 1], in1=ac3[:], op0=ALU.mult, op1=ALU.add),
                                             reads=[Ak, 'fcw', ack], writes=[ack])
                            ac = ac3[:].rearrange("p a b -> p (a b)")
                            vlo = 64
                        PV, PVk = pv[b2], f'PS:fpv{b2}'
                        for kc in range(KC):
                            P.op('tensor', lambda e, kc=kc, ff=ff, wi=wi, PV=PV, vlo=vlo, nt=nt: e.matmul(PV[:, :nt], lhsT=wv[wi][:, kc, ff * 128:(ff + 1) * 128], rhs=hT[:, kc, vlo:vlo + nt], start=(kc == 0), stop=(kc == KC - 1)),
                                 reads=[f'fwv{wi}', 'fhT'], writes=[PVk])
                        S, Sk = sg[b2], f'fsg{b2}'
                        P.op('scalar', lambda e, S=S, ac=ac, nt=nt: e.activation(out=S[:, :nt], in_=ac[:, :nt], func=AF.Silu), reads=[ack], writes=[Sk])
                        P.op('vector', lambda e, S=S, PV=PV, f=f, nt=nt: e.tensor_tensor(out=gT[:, f, :nt], in0=S[:, :nt], in1=PV[:, :nt], op=ALU.mult), reads=[Sk, PVk], writes=[('gT', f)])
                P.barrier()
            with ExitStack() as es:
                wd = [sb(es, f"fwd{i}", [128, 1024], BF16) for i in range(3)]
                pacc = [ps(es, f"fdacc{i}", [128, 512], F32) for i in range(8)]
                vlo = 0 if m == 1 else 64
                for half in range(2):
                    for f in range(NF):
                        w, wk = wd[f % 3], f'fwd{f % 3}'
                        P.dma('sync', w[:], WD[f * 128:(f + 1) * 128, half * 1024:(half + 1) * 1024], reads=wd_keys, writes=[wk])
                        for d in range(8):
                            P.op('tensor', lambda e, d=d, f=f, w=w, nt=nt: e.matmul(pacc[d][:, :nt], lhsT=w[:, d * 128:(d + 1) * 128], rhs=gT[:, f, :nt], start=(f == 0), stop=(f == NF - 1)),
                                 reads=[wk, ('gT', f)], writes=[f'PS:fdacc{d}'])
                    for d in range(8):
                        dd = half * 8 + d
                        P.op('vector', lambda e, d=d, dd=dd, nt=nt, m=m, vlo=vlo: e.scalar_tensor_tensor(out=xt[:, dd, vlo:vlo + nt], in0=pacc[d][:, :nt], scalar=self.modT[:, 80 + dd, m:m + 1], in1=xt[:, dd, vlo:vlo + nt], op0=ALU.mult, op1=ALU.add),
                             reads=[f'PS:fdacc{d}', 'modT', 'fxt'], writes=['fxt'])
                tg_ = self.dyn('gpsimd', tok0) if split else tok0
                P.dma('gpsimd', dst.rearrange("(kc p) t -> p kc t", p=128)[:, :, (bass.ds(tg_, nt) if split else slice(tg_, tg_ + nt))], xt[:, :, vlo:vlo + nt], reads=['fxt'], writes=[('X2', tok0)])
                P.barrier()


B.phase_outproj = phase_outproj
B.phase_ffn = phase_ffn


NCH = T // 8
HALF = NCH // 2
TWO_PI = 6.283185307179586
PI = 3.141592653589793


def reduce_angle(self, t, n, ki, kf, msk, tag):
    P = self.P
    v = 'vector'
    P.op(v, lambda e: e.tensor_scalar(out=ki[:, :n], in0=t, scalar1=1.0 / TWO_PI, scalar2=None, op0=ALU.mult), reads=[tag], writes=[tag + 'ki'])
    P.op(v, lambda e: e.tensor_copy(out=kf[:, :n], in_=ki[:, :n]), reads=[tag + 'ki'], writes=[tag + 'kf'])
    P.op(v, lambda e: e.scalar_tensor_tensor(out=t, in0=kf[:, :n], scalar=-TWO_PI, in1=t, op0=ALU.mult, op1=ALU.add), reads=[tag + 'kf', tag], writes=[tag])
    P.op(v, lambda e: e.tensor_single_scalar(out=msk[:, :n], in_=t, scalar=PI, op=ALU.is_gt), reads=[tag], writes=[tag + 'm'])
    P.op(v, lambda e: e.scalar_tensor_tensor(out=t, in0=msk[:, :n], scalar=-TWO_PI, in1=t, op0=ALU.mult, op1=ALU.add), reads=[tag + 'm', tag], writes=[tag])
    P.op(v, lambda e: e.tensor_single_scalar(out=msk[:, :n], in_=t, scalar=-PI, op=ALU.is_lt), reads=[tag], writes=[tag + 'm'])
    P.op(v, lambda e: e.scalar_tensor_tensor(out=t, in0=msk[:, :n], scalar=TWO_PI, in1=t, op0=ALU.mult, op1=ALU.add), reads=[tag + 'm', tag], writes=[tag])
    P.op(v, lambda e: e.tensor_scalar(out=t, in0=t, scalar1=PI, scalar2=-PI, op0=ALU.min, op1=ALU.max), reads=[tag], writes=[tag])


def cmul_cols(self, o_re, o_im, a_re, a_im, s_re, s_im, tmp, rk, wk):
    P = self.P
    v = 'vector'
    P.op(v, lambda e: e.tensor_scalar(out=tmp, in0=a_im, scalar1=s_im, scalar2=None, op0=ALU.mult), reads=rk, writes=[wk + 't'])
    P.op(v, lambda e: e.scalar_tensor_tensor(out=o_re, in0=a_re, scalar=s_re, in1=tmp, op0=ALU.mult, op1=ALU.subtract), reads=rk + [wk + 't'], writes=[wk + 're'])
    P.op(v, lambda e: e.tensor_scalar(out=tmp, in0=a_re, scalar1=s_im, scalar2=None, op0=ALU.mult), reads=rk + [wk + 're'], writes=[wk + 't'])
    P.op(v, lambda e: e.scalar_tensor_tensor(out=o_im, in0=a_im, scalar=s_re, in1=tmp, op0=ALU.mult, op1=ALU.add), reads=rk + [wk + 't'], writes=[wk + 'im'])


def phase_s5(self):
    P, nc, dr = self.P, self.nc, self.dr
    sb, ps = self.sb, self.ps
    PT = dr['PT']
    v, a, g_ = 'vector', 'scalar', 'gpsimd'
    with ExitStack() as es0:
        Sel = sb(es0, "Sel", [128, 8, 8, 128], BF16)
        bmask = sb(es0, "bmask", [128, 8, 16], F32)
        krow = sb(es0, "krow", [128, 16], F32)
        crow = sb(es0, "crow", [128, NCH + 1], F32)
        lre = sb(es0, "lre", [128, 32], F32)
        lim = sb(es0, "lim", [128, 32], F32)
        dlt = sb(es0, "dlt", [128, 32], F32)
        reD = sb(es0, "reD", [128, 32], F32)
        imD = sb(es0, "imD", [128, 32], F32)
        bre = sb(es0, "bre", [128, 32, 16], F32)
        bim = sb(es0, "bim", [128, 32, 16], F32)
        cre = sb(es0, "cre", [128, 32, 16], F32)
        cim = sb(es0, "cim", [128, 32, 16], F32)
        dsk = sb(es0, "dsk", [128, 4], F32)
        gy = sb(es0, "gy", [128, 4, T], BF16)
        uT = sb(es0, "uT", [128, 2, T], BF16)
        uF = sb(es0, "uF", [128, T], F32)
        ki = sb(es0, "ki", [128, NCH + 1], I32)
        kf = sb(es0, "kf", [128, NCH + 1], F32)
        msk = sb(es0, "msk", [128, NCH + 1], F32)
        P.op(g_, lambda e: e.memset(Sel[:], 0.0), writes=['Sel'])
        for a_ in range(8):
            for b_ in range(8):
                P.op(g_ if (a_ + b_) % 2 else v, lambda e, a_=a_, b_=b_: e.tensor_copy(out=Sel[:, a_, b_, b_ * 16:(b_ + 1) * 16], in_=self.ident_f[:, a_ * 16:(a_ + 1) * 16]),
                     reads=['ident_f', 'Sel'], writes=['Sel'])
        P.op(g_, lambda e: e.memset(bmask[:], 1.0), writes=['bmask'])
        P.op(g_, lambda e: e.affine_select(out=bmask[:], in_=bmask[:], pattern=[[16, 8], [0, 16]], base=15, channel_multiplier=-1, compare_op=ALU.is_ge, fill=0.0),
             reads=['bmask'], writes=['bmask'])
        P.op(g_, lambda e: e.iota(krow[:], pattern=[[1, 16]], base=-7, channel_multiplier=0, allow_small_or_imprecise_dtypes=True), writes=['krow'])
        P.op(g_, lambda e: e.iota(crow[:], pattern=[[1, NCH + 1]], base=0, channel_multiplier=0, allow_small_or_imprecise_dtypes=True), writes=['crow'])
        for nm, t_ in (('s5_lre', lre), ('s5_lim', lim), ('s5_dlt', dlt)):
            P.dma('sync', t_[:], dr[nm], writes=[nm])
        for nm, t_ in (('s5_bre', bre), ('s5_bim', bim), ('s5_cre', cre), ('s5_cim', cim)):
            P.dma('sync', t_[:], dr[nm], writes=[nm])
        P.dma('sync', dsk[:], dr['s5_dsk'], writes=['dsk'])
        P.op(a, lambda e: e.activation(out=dlt[:], in_=dlt[:], func=AF.Exp), reads=['s5_dlt'], writes=['s5_dlt'])
        P.op(v, lambda e: e.tensor_tensor(out=reD[:], in0=lre[:], in1=dlt[:], op=ALU.mult), reads=['s5_lre', 's5_dlt'], writes=['reD'])
        P.op(v, lambda e: e.tensor_tensor(out=imD[:], in0=lim[:], in1=dlt[:], op=ALU.mult), reads=['s5_lim', 's5_dlt'], writes=['imD'])
        self.cfre = sb(es0, "cfre", [128, 32], F32)
        self.cfim = sb(es0, "cfim", [128, 32], F32)
        with ExitStack() as es:
            mg = sb(es, "c_mg", [128, 32], F32)
            an = sb(es, "c_an", [128, 32], F32)
            an2 = sb(es, "c_an2", [128, 32], F32)
            nre = sb(es, "c_nre", [128, 32], F32)
            nim = sb(es, "c_nim", [128, 32], F32)
            den = sb(es, "c_den", [128, 32], F32)
            t1 = sb(es, "c_t1", [128, 32], F32)
            cfre, cfim = self.cfre, self.cfim
            P.op(a, lambda e: e.activation(out=mg[:], in_=reD[:], func=AF.Exp), reads=['reD'], writes=['c_mg'])
            P.op(v, lambda e: e.tensor_copy(out=an[:], in_=imD[:]), reads=['imD'], writes=['c_an'])
            reduce_angle(self, an[:], 32, ki, kf, msk, 'c_an')
            P.op(v, lambda e: e.tensor_scalar(out=an2[:], in0=imD[:], scalar1=PI / 2, scalar2=None, op0=ALU.add), reads=['imD'], writes=['c_an2'])
            reduce_angle(self, an2[:], 32, ki, kf, msk, 'c_an2')
            P.op(a, lambda e: e.activation(out=an[:], in_=an[:], func=AF.Sin), reads=['c_an'], writes=['c_an'])
            P.op(a, lambda e: e.activation(out=an2[:], in_=an2[:], func=AF.Sin), reads=['c_an2'], writes=['c_an2'])
            P.op(v, lambda e: e.tensor_tensor(out=nre[:], in0=mg[:], in1=an2[:], op=ALU.mult), reads=['c_mg', 'c_an2'], writes=['c_nre'])
            P.op(v, lambda e: e.tensor_scalar(out=nre[:], in0=nre[:], scalar1=-1.0, scalar2=None, op0=ALU.add), reads=['c_nre'], writes=['c_nre'])
            P.op(v, lambda e: e.tensor_tensor(out=nim[:], in0=mg[:], in1=an[:], op=ALU.mult), reads=['c_mg', 'c_an'], writes=['c_nim'])
            P.op(v, lambda e: e.tensor_tensor(out=den[:], in0=lre[:], in1=lre[:], op=ALU.mult), reads=['s5_lre'], writes=['c_den'])
            P.op(v, lambda e: e.tensor_tensor(out=t1[:], in0=lim[:], in1=lim[:], op=ALU.mult), reads=['s5_lim'], writes=['c_t1'])
            P.op(v, lambda e: e.tensor_tensor(out=den[:], in0=den[:], in1=t1[:], op=ALU.add), reads=['c_den', 'c_t1'], writes=['c_den'])
            P.op(v, lambda e: e.reciprocal(out=den[:], in_=den[:]), reads=['c_den'], writes=['c_den'])
            P.op(v, lambda e: e.tensor_tensor(out=cfre[:], in0=nre[:], in1=lre[:], op=ALU.mult), reads=['c_nre', 's5_lre'], writes=['cfre'])
            P.op(v, lambda e: e.tensor_tensor(out=t1[:], in0=nim[:], in1=lim[:], op=ALU.mult), reads=['c_nim', 's5_lim', 'c_den'], writes=['c_t1'])
            P.op(v, lambda e: e.tensor_tensor(out=cfre[:], in0=cfre[:], in1=t1[:], op=ALU.add), reads=['cfre', 'c_t1'], writes=['cfre'])
            P.op(v, lambda e: e.tensor_tensor(out=cfre[:], in0=cfre[:], in1=den[:], op=ALU.mult), reads=['cfre', 'c_den'], writes=['cfre'])
            P.op(v, lambda e: e.tensor_tensor(out=cfim[:], in0=nim[:], in1=lre[:], op=ALU.mult), reads=['c_nim', 's5_lre'], writes=['cfim'])
            P.op(v, lambda e: e.tensor_tensor(out=t1[:], in0=nre[:], in1=lim[:], op=ALU.mult), reads=['c_nre', 's5_lim', 'cfre'], writes=['c_t1'])
            P.op(v, lambda e: e.tensor_tensor(out=cfim[:], in0=cfim[:], in1=t1[:], op=ALU.subtract), reads=['cfim', 'c_t1'], writes=['cfim'])
            P.op(v, lambda e: e.tensor_tensor(out=cfim[:], in0=cfim[:], in1=den[:], op=ALU.mult), reads=['cfim', 'c_den'], writes=['cfim'])
            P.barrier()
        cfre, cfim = self.cfre, self.cfim
        with ExitStack() as es:
            mgk = sb(es, "mgk", [128, 16], F32)
            ank = sb(es, "ank", [128, 16], F32)
            ank2 = sb(es, "ank2", [128, 16], F32)
            LPre = sb(es, "LPre", [128, 16], F32)
            LPim = sb(es, "LPim", [128, 16], F32)
            bbre = sb(es, "bbre", [128, 16], F32)
            bbim = sb(es, "bbim", [128, 16], F32)
            ctmp = sb(es, "ctmp", [128, 128], F32)
            Bmre = sb(es, "Bmre", [128, 8, 16], F32)
            Bmim = sb(es, "Bmim", [128, 8, 16], F32)
            Cmre = sb(es, "Cmre", [128, 8, 16], F32)
            Cmim = sb(es, "Cmim", [128, 8, 16], F32)
            WiTre = sb(es, "WiTre", [128, 128], F32)
            WiTim = sb(es, "WiTim", [128, 128], F32)
            Wore = sb(es, "Wore", [128, 128], BF16)
            Woim = sb(es, "Woim", [128, 128], BF16)
            WoFre = sb(es, "WoFre", [128, 128], F32)
            WoFim = sb(es, "WoFim", [128, 128], F32)
            Tin = sb(es, "Tin", [128, 2, 128], BF16)
            Win = sb(es, "Win", [128, 2, 2, 64], BF16)
            th = sb(es, "th", [128, 1], F32)
            rr = sb(es, "rr", [128, 1], F32)
            rtab = sb(es, "rtab", [128, NCH], F32)
            ctab = sb(es, "ctab", [128, NCH + 1], F32)
            stab = sb(es, "stab", [128, NCH + 1], F32)
            Usb = sb(es, "Usb", [128, 2, 2, NCH], BF16)
            Sre = sb(es, "Sre", [128, NCH], F32)
            Sim = sb(es, "Sim", [128, NCH], F32)
            s2re = sb(es, "s2re", [128, NCH], F32)
            s2im = sb(es, "s2im", [128, NCH], F32)
            Zre = sb(es, "Zre", [128, NCH + 1], F32)
            Zim = sb(es, "Zim", [128, NCH + 1], F32)
            Xre = sb(es, "Xre", [128, NCH], BF16)
            Xim = sb(es, "Xim", [128, NCH], BF16)
            xt1 = sb(es, "xt1", [128, NCH], F32)
            Ysb = sb(es, "Ysb", [128, 2, 8, NCH], BF16)
            yT = sb(es, "yT5", [128, 2, T], F32)
            gl_w = sb(es, "gluw", [128, 4, 512], BF16)
            sgm = sb(es, "sgm", [128, 512], F32)
            og = sb(es, "og5", [128, 512], BF16)
            pU = [ps(es, f"pU{i}", [128, 512], F32) for i in range(2)]
            pSr = ps(es, "pSr", [128, 512], F32)
            pSi = ps(es, "pSi", [128, 512], F32)
            pY = [ps(es, f"pY{i}", [128, 512], F32) for i in range(2)]
            pB = ps(es, "pB5", [128, 512], F32)
            P.dma('gpsimd', gl_w[:], dr['s5_glu_w'].rearrange("(kt p) n -> p kt n", p=128), writes=['gluw'])
            P.op(g_, lambda e: e.memset(Zre[:, 0:1], 0.0), writes=['Zre'])
            P.op(g_, lambda e: e.memset(Zim[:, 0:1], 0.0), writes=['Zim'])
            hs = [(0, HALF), (HALF, NCH)]
            for ft in range(4):
                P.dma('sync', uF[:], PT[ft * 128:(ft + 1) * 128, :], writes=['uF'])
                P.op(a, lambda e: e.copy(out=uT[:, 0, :], in_=uF[:]), reads=['uF'], writes=['uT'])
                P.op(v, lambda e: e.tensor_copy(out=uT[:, 1, 0:TC], in_=uF[:, TC - 1::-1]), reads=['uF'], writes=['uT'])
                P.op(v, lambda e: e.tensor_copy(out=uT[:, 1, TC:T], in_=uF[:, T - 1:TC - 1:-1]), reads=['uF'], writes=['uT'])
                for gp4 in range(4):
                    gp = ft * 4 + gp4
                    for d in range(2):
                        for g2 in range(2):
                            gl = gp4 * 2 + g2
                            for hi_, (c0, c1) in enumerate(hs):
                                for j in range(8):
                                    P.op('tensor', lambda e, d=d, gl=gl, j=j, c0=c0, c1=c1, hi_=hi_: e.matmul(
                                        pU[hi_][:, :c1 - c0], lhsT=Sel[:, gl, j, :], rhs=uT[:, d, c0 * 8 + j:c1 * 8:8], start=(j == 0), stop=(j == 7)),
                                        reads=['Sel', 'uT'], writes=[f'PS:pU{hi_}'])
                                eng = a if hi_ == 0 else v
                                if eng == a:
                                    P.op(a, lambda e, d=d, g2=g2, c0=c0, c1=c1, hi_=hi_: e.copy(out=Usb[:, d, g2, c0:c1], in_=pU[hi_][:, :c1 - c0]), reads=[f'PS:pU{hi_}'], writes=[('Usb', d, g2)])
                                else:
                                    P.op(v, lambda e, d=d, g2=g2, c0=c0, c1=c1, hi_=hi_: e.tensor_copy(out=Usb[:, d, g2, c0:c1], in_=pU[hi_][:, :c1 - c0]), reads=[f'PS:pU{hi_}'], writes=[('Usb', d, g2)])
                    for d in range(2):
                        col = d * 16 + gp
                        P.op(v, lambda e, col=col: e.tensor_scalar(out=mgk[:], in0=krow[:], scalar1=reD[:, col:col + 1], scalar2=None, op0=ALU.mult), reads=['krow', 'reD'], writes=['mgk'])
                        P.op(a, lambda e: e.activation(out=mgk[:], in_=mgk[:], func=AF.Exp), reads=['mgk'], writes=['mgk'])
                        P.op(v, lambda e, col=col: e.tensor_scalar(out=ank[:], in0=krow[:], scalar1=imD[:, col:col + 1], scalar2=None, op0=ALU.mult), reads=['krow', 'imD'], writes=['ank'])
                        P.op(v, lambda e: e.tensor_scalar(out=ank2[:], in0=ank[:], scalar1=PI / 2, scalar2=None, op0=ALU.add), reads=['ank'], writes=['ank2'])
                        reduce_angle(self, ank[:], 16, ki, kf, msk, 'ank')
                        reduce_angle(self, ank2[:], 16, ki, kf, msk, 'ank2')
                        P.op(a, lambda e: e.activation(out=ank[:], in_=ank[:], func=AF.Sin), reads=['ank'], writes=['ank'])
                        P.op(a, lambda e: e.activation(out=ank2[:], in_=ank2[:], func=AF.Sin), reads=['ank2'], writes=['ank2'])
                        P.op(v, lambda e: e.tensor_tensor(out=LPre[:], in0=mgk[:], in1=ank2[:], op=ALU.mult), reads=['mgk', 'ank2'], writes=['LPre'])
                        P.op(v, lambda e: e.tensor_tensor(out=LPim[:], in0=mgk[:], in1=ank[:], op=ALU.mult), reads=['mgk', 'ank'], writes=['LPim'])
                        LP = ['LPre', 'LPim']
                        cmul_cols(self, bbre[:], bbim[:], bre[:, col, :], bim[:, col, :], cfre[:, col:col + 1], cfim[:, col:col + 1], ctmp[:, 0:16], ['s5_bre', 's5_bim', 'cfre', 'cfim'], 'bb')
                        for j in range(8):
                            kk = 7 - j
                            cmul_cols(self, Bmre[:, j, :], Bmim[:, j, :], bbre[:], bbim[:], LPre[:, kk:kk + 1], LPim[:, kk:kk + 1], ctmp[:, 0:16], ['bbre', 'bbim'] + LP, 'Bm')
                            kk = 7 + j
                            cmul_cols(self, Cmre[:, j, :], Cmim[:, j, :], cre[:, col, :], cim[:, col, :], LPre[:, kk:kk + 1], LPim[:, kk:kk + 1], ctmp[:, 0:16], ['s5_cre', 's5_cim'] + LP, 'Cm')
                        P.op(v, lambda e: e.tensor_scalar(out=Cmim[:], in0=Cmim[:], scalar1=-1.0, scalar2=None, op0=ALU.mult), reads=['Cmim'], writes=['Cmim'])
                        B2re, B2im = Bmre[:].rearrange("p a b -> p (a b)"), Bmim[:].rearrange("p a b -> p (a b)")
                        C2re, C2im = Cmre[:].rearrange("p a b -> p (a b)"), Cmim[:].rearrange("p a b -> p (a b)")
                        cmul_cols(self, WiTre[:], WiTim[:], B2re, B2im, LPre[:, 14:15], LPim[:, 14:15], ctmp[:], ['Bmre', 'Bmim'] + LP, 'WiT')
                        P.op(v, lambda e: e.tensor_scalar(out=ctmp[:], in0=C2im, scalar1=LPim[:, 8:9], scalar2=None, op0=ALU.mult), reads=['Cmim', 'LPim'], writes=['Wot'])
                        P.op(v, lambda e: e.scalar_tensor_tensor(out=WoFre[:], in0=C2re, scalar=LPre[:, 8:9], in1=ctmp[:], op0=ALU.mult, op1=ALU.add), reads=['Cmre', 'LPre', 'Wot'], writes=['WoFre'])
                        P.op(v, lambda e: e.tensor_scalar(out=ctmp[:], in0=C2re, scalar1=LPim[:, 8:9], scalar2=None, op0=ALU.mult), reads=['Cmre', 'LPim', 'WoFre'], writes=['Wot'])
                        P.op(v, lambda e: e.scalar_tensor_tensor(out=WoFim[:], in0=C2im, scalar=LPre[:, 8:9], in1=ctmp[:], op0=ALU.mult, op1=ALU.subtract), reads=['Cmim', 'LPre', 'Wot'], writes=['WoFim'])
                        P.op(a, lambda e: e.copy(out=Wore[:], in_=WoFre[:]), reads=['WoFre'], writes=['Wore'])
                        P.op(a, lambda e: e.copy(out=Woim[:], in_=WoFim[:]), reads=['WoFim'], writes=['Woim'])
                        for g2 in range(2):
                            r0, r1 = g2 * 64, (g2 + 1) * 64
                            P.op('tensor', lambda e, r0=r0, r1=r1: e.matmul(pB[:, 0:128], lhsT=B2re[r0:r1, :], rhs=C2re[r0:r1, :], start=True, stop=False), reads=['Bmre', 'Cmre'], writes=['PS:pB5'])
                            P.op('tensor', lambda e, r0=r0, r1=r1: e.matmul(pB[:, 0:128], lhsT=B2im[r0:r1, :], rhs=C2im[r0:r1, :], start=False, stop=True), reads=['Bmim', 'Cmim'], writes=['PS:pB5'])
                            P.op(v, lambda e, g2=g2: e.tensor_tensor(out=Tin[:, g2, :], in0=pB[:, 0:128], in1=bmask[:].rearrange("p a b -> p (a b)"), op=ALU.mult), reads=['PS:pB5', 'bmask'], writes=[('Tin', g2)])
                            P.op('tensor', lambda e, r0=r0, r1=r1: e.matmul(pB[:, 128:192], lhsT=WiTre[r0:r1, :], rhs=self.ident_f[r0:r1, r0:r1], start=True, stop=True), reads=['WiTre', 'ident_f'], writes=['PS:pB5'])
                            P.op('tensor', lambda e, r0=r0, r1=r1: e.matmul(pB[:, 192:256], lhsT=WiTim[r0:r1, :], rhs=self.ident_f[r0:r1, r0:r1], start=True, stop=True), reads=['WiTim', 'ident_f'], writes=['PS:pB5'])
                            P.op(a, lambda e, g2=g2: e.copy(out=Win[:, g2, :, :].rearrange("p a b -> p (a b)"), in_=pB[:, 128:256]), reads=['PS:pB5'], writes=[('Win', g2)])
                        P.op(v, lambda e, col=col: e.tensor_scalar(out=th[:], in0=imD[:, col:col + 1], scalar1=8.0, scalar2=None, op0=ALU.mult), reads=['imD'], writes=['th'])
                        reduce_angle(self, th[:], 1, ki, kf, msk, 'th')
                        P.op(a, lambda e, col=col: e.activation(out=rr[:], in_=reD[:, col:col + 1], func=AF.Exp, scale=8.0), reads=['reD'], writes=['rr'])
                        P.op(v, lambda e: e.tensor_scalar(out=rtab[:], in0=crow[:, 0:NCH], scalar1=0.0, scalar2=rr[:], op0=ALU.mult, op1=ALU.add), reads=['crow', 'rr'], writes=['rtab'])
                        P.op(v, lambda e: e.tensor_scalar(out=stab[:], in0=crow[:], scalar1=th[:], scalar2=None, op0=ALU.mult), reads=['crow', 'th'], writes=['stab'])
                        P.op(v, lambda e: e.tensor_scalar(out=ctab[:], in0=stab[:], scalar1=PI / 2, scalar2=None, op0=ALU.add), reads=['stab'], writes=['ctab'])
                        reduce_angle(self, stab[:], NCH + 1, ki, kf, msk, 'stab')
                        reduce_angle(self, ctab[:], NCH + 1, ki, kf, msk, 'ctab')
                        P.op(a, lambda e: e.activation(out=stab[:], in_=stab[:], func=AF.Sin), reads=['stab'], writes=['stab'])
                        P.op(a, lambda e: e.activation(out=ctab[:], in_=ctab[:], func=AF.Sin), reads=['ctab'], writes=['ctab'])
                        for g2 in range(2):
                            r0, r1 = g2 * 64, (g2 + 1) * 64
                            for hi_, (c0, c1) in enumerate(hs):
                                w_ = 272 * hi_
                                P.op('tensor', lambda e, d=d, g2=g2, r0=r0, r1=r1, c0=c0, c1=c1: e.matmul(pSr[r0:r1, 0:c1 - c0], lhsT=Win[:, g2, 0, :], rhs=Usb[:, d, g2, c0:c1], start=True, stop=True),
                                     reads=[('Win', g2), ('Usb', d, g2)], writes=['PS:pSr'])
                                P.op('tensor', lambda e, d=d, g2=g2, r0=r0, r1=r1, c0=c0, c1=c1: e.matmul(pSi[r0:r1, 0:c1 - c0], lhsT=Win[:, g2, 1, :], rhs=Usb[:, d, g2, c0:c1], start=True, stop=True),
                                     reads=[('Win', g2), ('Usb', d, g2)], writes=['PS:pSi'])
                                P.op(a, lambda e, r0=r0, r1=r1, c0=c0, c1=c1: e.copy(out=Sre[r0:r1, c0:c1], in_=pSr[r0:r1, 0:c1 - c0]), reads=['PS:pSr'], writes=['Sre'])
                                P.op(v, lambda e, r0=r0, r1=r1, c0=c0, c1=c1: e.tensor_copy(out=Sim[r0:r1, c0:c1], in_=pSi[r0:r1, 0:c1 - c0]), reads=['PS:pSi'], writes=['Sim'])
                        P.op(v, lambda e: e.tensor_tensor(out=xt1[:], in0=Sim[:], in1=stab[:, 1:NCH + 1], op=ALU.mult), reads=['Sim', 'stab'], writes=['xt1'])
                        P.op(v, lambda e: e.tensor_tensor(out=s2re[:], in0=Sre[:], in1=ctab[:, 1:NCH + 1], op=ALU.mult), reads=['Sre', 'ctab'], writes=['s2re'])
                        P.op(v, lambda e: e.tensor_tensor(out=s2re[:], in0=s2re[:], in1=xt1[:], op=ALU.add), reads=['s2re', 'xt1'], writes=['s2re'])
                        P.op(v, lambda e: e.tensor_tensor(out=xt1[:], in0=Sre[:], in1=stab[:, 1:NCH + 1], op=ALU.mult), reads=['Sre', 'stab', 's2re'], writes=['xt1'])
                        P.op(v, lambda e: e.tensor_tensor(out=s2im[:], in0=Sim[:], in1=ctab[:, 1:NCH + 1], op=ALU.mult), reads=['Sim', 'ctab'], writes=['s2im'])
                        P.op(v, lambda e: e.tensor_tensor(out=s2im[:], in0=s2im[:], in1=xt1[:], op=ALU.subtract), reads=['s2im', 'xt1'], writes=['s2im'])
                        P.op(v, lambda e: e.tensor_tensor_scan(out=Zre[:, 1:NCH + 1], data0=rtab[:], data1=s2re[:], initial=0.0, op0=ALU.mult, op1=ALU.add), reads=['rtab', 's2re'], writes=['Zre'])
                        P.op(v, lambda e: e.tensor_tensor_scan(out=Zim[:, 1:NCH + 1], data0=rtab[:], data1=s2im[:], initial=0.0, op0=ALU.mult, op1=ALU.add), reads=['rtab', 's2im'], writes=['Zim'])
                        P.op(v, lambda e: e.tensor_tensor(out=xt1[:], in0=Zim[:, 0:NCH], in1=stab[:, 0:NCH], op=ALU.mult), reads=['Zim', 'stab', 's2im'], writes=['xt1'])
                        P.op(v, lambda e: e.tensor_tensor(out=s2re[:], in0=Zre[:, 0:NCH], in1=ctab[:, 0:NCH], op=ALU.mult), reads=['Zre', 'ctab'], writes=['s2re'])
                        P.op(v, lambda e: e.tensor_tensor(out=Xre[:], in0=s2re[:], in1=xt1[:], op=ALU.subtract), reads=['s2re', 'xt1'], writes=['Xre'])
                        P.op(v, lambda e: e.tensor_tensor(out=xt1[:], in0=Zre[:, 0:NCH], in1=stab[:, 0:NCH], op=ALU.mult), reads=['Zre', 'stab', 'Xre'], writes=['xt1'])
                        P.op(v, lambda e: e.tensor_tensor(out=s2im[:], in0=Zim[:, 0:NCH], in1=ctab[:, 0:NCH], op=ALU.mult), reads=['Zim', 'ctab'], writes=['s2im'])
                        P.op(v, lambda e: e.tensor_tensor(out=Xim[:], in0=s2im[:], in1=xt1[:], op=ALU.add), reads=['s2im', 'xt1'], writes=['Xim'])
                        for g2 in range(2):
                            r0, r1 = g2 * 64, (g2 + 1) * 64
                            gl = gp4 * 2 + g2
                            for hi_, (c0, c1) in enumerate(hs):
                                py = pY[hi_]
                                pk = f'PS:pY{hi_}'
                                P.op('tensor', lambda e, d=d, g2=g2, c0=c0, c1=c1, py=py: e.matmul(py[:, 0:c1 - c0], lhsT=Tin[:, g2, :], rhs=Usb[:, d, g2, c0:c1], start=True, stop=False),
                                     reads=[('Tin', g2), ('Usb', d, g2)], writes=[pk])
                                P.op('tensor', lambda e, r0=r0, r1=r1, c0=c0, c1=c1, py=py: e.matmul(py[:, 0:c1 - c0], lhsT=Wore[r0:r1, :], rhs=Xre[r0:r1, c0:c1], start=False, stop=False),
                                     reads=['Wore', 'Xre'], writes=[pk])
                                P.op('tensor', lambda e, r0=r0, r1=r1, c0=c0, c1=c1, py=py: e.matmul(py[:, 0:c1 - c0], lhsT=Woim[r0:r1, :], rhs=Xim[r0:r1, c0:c1], start=False, stop=True),
                                     reads=['Woim', 'Xim'], writes=[pk])
                                if hi_ == 0:
                                    P.op(a, lambda e, d=d, gl=gl, c0=c0, c1=c1, py=py: e.copy(out=Ysb[:, d, gl, c0:c1], in_=py[:, 0:c1 - c0]), reads=[pk], writes=[('Ysb', d, gl)])
                                else:
                                    P.op(v, lambda e, d=d, gl=gl, c0=c0, c1=c1, py=py: e.tensor_copy(out=Ysb[:, d, gl, c0:c1], in_=py[:, 0:c1 - c0]), reads=[pk], writes=[('Ysb', d, gl)])
                for d in range(2):
                    for i in range(8):
                        for hi_, (c0, c1) in enumerate(hs):
                            py = pY[hi_]
                            pk = f'PS:pY{hi_}'
                            for gl in range(8):
                                P.op('tensor', lambda e, d=d, gl=gl, i=i, c0=c0, c1=c1, py=py: e.matmul(py[:, 0:c1 - c0], lhsT=Sel[:, i, gl, :], rhs=Ysb[:, d, gl, c0:c1], start=(gl == 0), stop=(gl == 7)),
                                     reads=['Sel', ('Ysb', d, gl)], writes=[pk])
                            if hi_ == 0:
                                P.op(a, lambda e, d=d, i=i, c0=c0, c1=c1, py=py: e.copy(out=yT[:, d, c0 * 8 + i:c1 * 8:8], in_=py[:, 0:c1 - c0]), reads=[pk], writes=['yT5'])
                            else:
                                P.op(v, lambda e, d=d, i=i, c0=c0, c1=c1, py=py: e.tensor_copy(out=yT[:, d, c0 * 8 + i:c1 * 8:8], in_=py[:, 0:c1 - c0]), reads=[pk], writes=['yT5'])
                P.op(v, lambda e: e.tensor_tensor(out=yT[:, 0, 0:TC], in0=yT[:, 0, 0:TC], in1=yT[:, 1, TC - 1::-1], op=ALU.add), reads=['yT5'], writes=['yT5'])
                P.op(v, lambda e: e.tensor_tensor(out=yT[:, 0, TC:T], in0=yT[:, 0, TC:T], in1=yT[:, 1, T - 1:TC - 1:-1], op=ALU.add), reads=['yT5'], writes=['yT5'])
                P.op(v, lambda e, ft=ft: e.scalar_tensor_tensor(out=yT[:, 0, :], in0=uF[:], scalar=dsk[:, ft:ft + 1], in1=yT[:, 0, :], op0=ALU.mult, op1=ALU.add), reads=['uF', 'dsk', 'yT5'], writes=['yT5'])
                P.op(a, lambda e: e.activation(out=yT[:, 1, :], in_=yT[:, 0, :], func=AF.Square), reads=['yT5'], writes=['yT5b'])
                P.op(v, lambda e: e.tensor_scalar(out=yT[:, 1, :], in0=yT[:, 1, :], scalar1=0.044715, scalar2=1.0, op0=ALU.mult, op1=ALU.add), reads=['yT5b'], writes=['yT5b'])
                P.op(v, lambda e: e.tensor_tensor(out=yT[:, 1, :], in0=yT[:, 1, :], in1=yT[:, 0, :], op=ALU.mult), reads=['yT5b', 'yT5'], writes=['yT5b'])
                P.op(a, lambda e: e.activation(out=yT[:, 1, :], in_=yT[:, 1, :], func=AF.Sigmoid, scale=1.5957691216057308), reads=['yT5b'], writes=['yT5b'])
                P.op(v, lambda e, ft=ft: e.tensor_tensor(out=gy[:, ft, :], in0=yT[:, 1, :], in1=yT[:, 0, :], op=ALU.mult), reads=['yT5b', 'yT5'], writes=[('gy', ft)])
            it = 0
            for fo in range(4):
                for (tok0, nt, m) in self.blocks():
                    py = pY[it % 2]
                    pk = f'PS:pY{it % 2}'
                    it += 1
                    for fi in range(4):
                        P.op('tensor', lambda e, fo=fo, fi=fi, tok0=tok0, nt=nt, py=py: e.matmul(py[:, :nt], lhsT=gl_w[:, fi, fo * 128:(fo + 1) * 128], rhs=gy[:, fi, tok0:tok0 + nt], start=(fi == 0), stop=(fi == 3)),
                             reads=['gluw'] + [('gy', f) for f in range(4)], writes=[pk])
                    P.op(a, lambda e, nt=nt, py=py: e.activation(out=sgm[:, :nt], in_=py[:, :nt], func=AF.Sigmoid), reads=[pk], writes=['sgm'])
                    P.op(v, lambda e, fo=fo, tok0=tok0, nt=nt: e.tensor_tensor(out=og[:, :nt], in0=sgm[:, :nt], in1=gy[:, fo, tok0:tok0 + nt], op=ALU.mult), reads=['sgm', ('gy', fo)], writes=['og5'])
                    P.dma('gpsimd', dr['YT'][fo * 128:(fo + 1) * 128, tok0:tok0 + nt], og[:, :nt], reads=['og5'], writes=[('YT5', fo, tok0)])
            P.barrier()


B.phase_s5 = phase_s5


R_QM, R_KM, R_VM, R_OM, R_IG, R_FG = 4128, 4640, 5152, 6176, 7200, 7208


def phase_mlstm(self):
    P, nc, dr = self.P, self.nc, self.dr
    sb, ps = self.sb, self.ps
    PT = dr['PT']
    v, a, g_ = 'vector', 'scalar', 'gpsimd'
    with ExitStack() as es0:
        gt_tok = sb(es0, "m_gt", [128, 34, 40], F32)
        with ExitStack() as es:
            G = sb(es, "m_G", [128, T], F32)
            bcol = sb(es, "m_bcol", [128, 1], F32)
            ptr = ps(es, "m_ptr", [128, 512], F32)
            P.op(g_, lambda e: e.memset(G[:], 0.0), writes=['m_G'])
            P.op(g_, lambda e: e.memset(bcol[:], 0.0), writes=['m_bcol'])
            P.dma('sync', G[0:8, :], PT[R_IG:R_IG + 8, :], writes=['m_G'])
            P.dma('sync', G[32:40, :], PT[R_FG:R_FG + 8, :], writes=['m_G'])
            P.dma('sync', bcol[0:8, :], dr['ml_ib'], writes=['m_bcol'])
            P.dma('sync', bcol[32:40, :], dr['ml_fb'], writes=['m_bcol'])
            P.op(a, lambda e: e.activation(out=G[0:32, :], in_=G[0:32, :], func=AF.Exp, bias=bcol[0:32, :]), reads=['m_G', 'm_bcol'], writes=['m_G'])
            P.op(v, lambda e: e.tensor_scalar(out=bcol[32:64, :], in0=bcol[32:64, :], scalar1=-1.0, scalar2=None, op0=ALU.mult), reads=['m_bcol', 'm_G'], writes=['m_bcol'])
            P.op(a, lambda e: e.activation(out=G[32:64, :], in_=G[32:64, :], func=AF.Exp, scale=-1.0, bias=bcol[32:64, :]), reads=['m_G', 'm_bcol'], writes=['m_G'])
            P.op(a, lambda e: e.activation(out=G[32:64, :], in_=G[32:64, :], func=AF.Ln, bias=self.one_col[32:64, :]), reads=['m_G', 'one_col'], writes=['m_G'])
            P.op(v, lambda e: e.tensor_scalar(out=G[32:64, :], in0=G[32:64, :], scalar1=-1.0, scalar2=None, op0=ALU.mult), reads=['m_G'], writes=['m_G'])
            for c in range(34):
                P.op('tensor', lambda e, c=c: e.matmul(ptr[:, 0:40], lhsT=G[:, c * 128:(c + 1) * 128], rhs=self.ident_f[:, 0:40], start=True, stop=True), reads=['m_G', 'ident_f'], writes=['PS:m_ptr'])
                P.op(v, lambda e, c=c: e.tensor_copy(out=gt_tok[:, c, :], in_=ptr[:, 0:40]), reads=['PS:m_ptr'], writes=['m_gt'])
            P.barrier()
        with ExitStack() as es:
            xin = [sb(es, f"m_xin{i}", [128, 16, 512], F32) for i in range(2)]
            cs = sb(es, "m_cs", [128, 16, 512], BF16)
            tokst = [sb(es, f"m_tokst{i}", [128, 1536], BF16) for i in range(2)]
            ptA = ps(es, "m_ptA", [128, 8, 128], BF16)
            ptB = ps(es, "m_ptB", [128, 8, 128], BF16)
            ci = 0
            for bi, (tok0, nt, m) in enumerate(self.blocks()):
                x = xin[bi % 2]
                xk = f'm_xin{bi % 2}'
                P.dma('sync', x[:, 0:8, :nt], PT[R_QM:R_QM + 1024, :].rearrange("(f p) t -> p f t", p=128)[:, :, tok0:tok0 + nt], writes=[xk])
                P.dma('sync', x[:, 8:16, :nt], PT[R_VM:R_VM + 1024, :].rearrange("(f p) t -> p f t", p=128)[:, :, tok0:tok0 + nt], writes=[xk])
                P.op(a, lambda e, x=x, nt=nt: e.copy(out=cs[:, 0:4, :nt], in_=x[:, 0:4, :nt]), reads=[xk], writes=['m_cs'])
                P.op(a, lambda e, x=x, nt=nt: e.activation(out=cs[:, 4:8, :nt], in_=x[:, 4:8, :nt], func=AF.Copy, scale=128 ** -0.5), reads=[xk], writes=['m_cs'])
                P.op(v, lambda e, x=x, nt=nt: e.tensor_copy(out=cs[:, 8:16, :nt], in_=x[:, 8:16, :nt]), reads=[xk], writes=['m_cs'])
                P.dma('gpsimd', dr['BCT'][0:1024, :].rearrange("(f p) t -> p f t", p=128)[:, :, tok0:tok0 + nt], cs[:, 0:8, :nt], reads=['m_cs'], writes=[('BCT', tok0)])
                for ch in range(nt // 128):
                    tk = tokst[ci % 2]
                    tkk = f'm_tokst{ci % 2}'
                    ci += 1
                    for f in range(8):
                        P.op('tensor', lambda e, f=f, ch=ch: e.transpose(out=ptA[:, f, :], in_=cs[:, 8 + f, ch * 128:(ch + 1) * 128], identity=self.ident_b[:]), reads=['m_cs', 'ident_b'], writes=['PS:m_ptA'])
                    for f in range(4):
                        P.op('tensor', lambda e, f=f, ch=ch: e.transpose(out=ptB[:, f, :], in_=cs[:, 4 + f, ch * 128:(ch + 1) * 128], identity=self.ident_b[:]), reads=['m_cs', 'ident_b'], writes=['PS:m_ptB'])
                    P.op(v, lambda e, tk=tk: e.tensor_copy(out=tk[:, 0:1024], in_=ptA[:].rearrange("p a b -> p (a b)")), reads=['PS:m_ptA'], writes=[tkk])
                    P.op(a, lambda e, tk=tk: e.copy(out=tk[:, 1024:1536], in_=ptB[:, 0:4, :].rearrange("p a b -> p (a b)")), reads=['PS:m_ptB'], writes=[tkk])
                    P.dma('gpsimd', dr['TOK'][tok0 + ch * 128:tok0 + (ch + 1) * 128, 0:1536], tk[:], reads=[tkk], writes=[('TOK', tok0 + ch * 128)])
            P.barrier()
        with ExitStack() as es:
            S_f = sb(es, "m_Sf", [128, 4, 257], F32)
            S_b = sb(es, "m_Sb", [128, 4, 257], BF16)
            tok = [sb(es, f"m_tok{i}", [128, 1536], BF16) for i in range(2)]
            qk = [sb(es, f"m_qk{i}", [128, 8, 128], BF16) for i in range(2)]
            Lm = [sb(es, f"m_Lm{i}", [128, 128], F32) for i in range(2)]
            Dm = [sb(es, f"m_Dm{i}", [128, 128], F32) for i in range(2)]
            WT = [sb(es, f"m_WT{i}", [128, 128], BF16) for i in range(2)]
            v1 = [sb(es, f"m_v1_{i}", [128, 257], BF16) for i in range(2)]
            v2 = [sb(es, f"m_v2_{i}", [128, 257], BF16) for i in range(2)]
            yA = [sb(es, f"m_yA{i}", [128, 257], F32) for i in range(2)]
            num = [sb(es, f"m_num{i}", [128, 257], F32) for i in range(2)]
            den = [sb(es, f"m_den{i}", [128, 1], F32) for i in range(2)]
            hsb = [sb(es, f"m_hsb{i}", [128, 1024], F32) for i in range(2)]
            hfl = sb(es, "m_hfl", [128, 1024], F32)
            sqj = sb(es, "m_sqj", [128, 256], F32)
            ss = sb(es, "m_ss", [128, 4], F32)
            hn = sb(es, "m_hn", [128, 1024], BF16)
            ecum = sb(es, "m_ecum", [128, 4], F32)
            cdec = sb(es, "m_cdec", [128, 4], F32)
            nwB = sb(es, "m_nwB", [128, 1024], F32)
            ytb = sb(es, "m_ytb", [128, 8, 512], BF16)
            oT = sb(es, "m_oT", [128, 8, 512], F32)
            ot = sb(es, "m_ot", [128, 8, 512], BF16)
            pX = [ps(es, f"m_pX{i}", [128, 512], F32) for i in range(2)]
            pyA = [ps(es, f"m_pyA{i}", [128, 512], F32) for i in range(2)]
            pyB = [ps(es, f"m_pyB{i}", [128, 512], F32) for i in range(2)]
            pS = ps(es, "m_pS", [128, 512], F32)
            ptT = ps(es, "m_ptT", [128, 8, 128], BF16)
            P.dma('sync', nwB[:], dr['ml_nw'].partition_broadcast(128), writes=['m_nwB'])
            li = 0
            for d in range(2):
                MASK = self.MGT if d == 0 else self.MLT
                TRI = self.TLE if d == 0 else self.TGE
                mk, tk_ = ('MGT', 'TLE') if d == 0 else ('MLT', 'TGE')
                ecol = 127 if d == 0 else 0
                P.op(g_, lambda e: e.memset(S_f[:], 0.0), writes=[('m_Sf', h) for h in range(4)])
                P.op(g_, lambda e: e.memset(S_b[:], 0.0), writes=[('m_Sb', h) for h in range(4)])
                for c in chunk_order(d):
                    t0 = c * 128
                    tb, tbk = tok[li % 2], f'm_tok{li % 2}'
                    qb, qbk = qk[li % 2], f'm_qk{li % 2}'
                    hs_, hsk = hsb[li % 2], f'm_hsb{li % 2}'
                    li += 1
                    P.dma('sync', tb[:], dr['TOK'][t0:t0 + 128, 0:1536], writes=[tbk])
                    P.dma('sync', qb[:], dr['BCT'][0:1024, :].rearrange("(f p) t -> p f t", p=128)[:, :, t0:t0 + 128], writes=[qbk])
                    lac = gt_tok[:, c, 32 + d * 4:32 + (d + 1) * 4]
                    P.op('tensor', lambda e, lac=lac: e.matmul(pS[:, 300:304], lhsT=TRI[:], rhs=lac, start=True, stop=True), reads=[tk_, 'm_gt'], writes=['PS:m_pS'])
                    P.op('tensor', lambda e, lac=lac: e.matmul(pS[:, 320:324], lhsT=self.ones_f[:], rhs=lac, start=True, stop=True), reads=['ones_f', 'm_gt'], writes=['PS:m_pS'])
                    P.op(a, lambda e: e.activation(out=ecum[:], in_=pS[:, 300:304], func=AF.Exp), reads=['PS:m_pS'], writes=['m_ecum'])
                    P.op(a, lambda e: e.activation(out=cdec[:], in_=pS[:, 320:324], func=AF.Exp), reads=['PS:m_pS'], writes=['m_cdec'])
                    def head_body(h):
                        r = h % 2
                        X, Xk = pX[r], f'PS:m_pX{r}'
                        A_, Ak = pyA[r], f'PS:m_pyA{r}'
                        B_, Bk = pyB[r], f'PS:m_pyB{r}'
                        gcol = gt_tok[:, c, 32 + d * 4 + h:32 + d * 4 + h + 1]
                        icol = gt_tok[:, c, d * 4 + h:d * 4 + h + 1]
                        P.op(v, lambda e, r=r, gcol=gcol: e.tensor_scalar(out=Lm[r][:], in0=MASK[:], scalar1=gcol, scalar2=None, op0=ALU.mult), reads=[mk, 'm_gt'], writes=[f'm_Lm{r}'])
                        P.op('tensor', lambda e, r=r, X=X: e.matmul(X[:, 0:128], lhsT=Lm[r][:], rhs=TRI[:], start=True, stop=False), reads=[f'm_Lm{r}', tk_], writes=[Xk])
                        P.op('tensor', lambda e, r=r, X=X: e.matmul(X[:, 0:128], lhsT=self.negI[:], rhs=MASK[:], start=False, stop=True), reads=['negI', mk], writes=[Xk])
                        P.op('tensor', lambda e, h=h, X=X, qb=qb: e.matmul(X[:, 128:256], lhsT=qb[:, 4 + h, :], rhs=qb[:, h, :], start=True, stop=True), reads=[qbk], writes=[Xk])
                        P.op(a, lambda e, r=r, X=X: e.activation(out=Dm[r][:], in_=X[:, 0:128], func=AF.Exp), reads=[Xk], writes=[f'm_Dm{r}'])
                        P.op(v, lambda e, r=r, X=X: e.tensor_tensor(out=WT[r][:], in0=Dm[r][:], in1=X[:, 128:256], op=ALU.mult), reads=[f'm_Dm{r}', Xk], writes=[f'm_WT{r}'])
                        P.op(g_, lambda e, r=r, h=h, tb=tb, icol=icol: e.tensor_scalar(out=v1[r][:, 0:256], in0=tb[:, h * 256:(h + 1) * 256], scalar1=icol, scalar2=0.0, op0=ALU.mult, op1=ALU.add), reads=[tbk, 'm_gt'], writes=[f'm_v1_{r}'])
                        P.op(g_, lambda e, r=r, icol=icol: e.tensor_copy(out=v1[r][:, 256:257], in_=icol), reads=['m_gt'], writes=[f'm_v1_{r}'])
                        P.op(g_, lambda e, r=r: e.tensor_scalar(out=v2[r][:], in0=v1[r][:], scalar1=Dm[r][:, ecol:ecol + 1], scalar2=0.0, op0=ALU.mult, op1=ALU.add), reads=[f'm_v1_{r}', f'm_Dm{r}'], writes=[f'm_v2_{r}'])
                        P.op('tensor', lambda e, r=r, A_=A_: e.matmul(A_[:, 0:257], lhsT=WT[r][:], rhs=v1[r][:], start=True, stop=True), reads=[f'm_WT{r}', f'm_v1_{r}'], writes=[Ak])
                        P.op('tensor', lambda e, h=h, B_=B_, qb=qb: e.matmul(B_[:, 0:257], lhsT=qb[:, h, :], rhs=S_b[:, h, :], start=True, stop=True), reads=[qbk, ('m_Sb', h)], writes=[Bk])
                        P.op('tensor', lambda e, r=r, h=h, tb=tb, X=X: e.matmul(X[:, 256:512], lhsT=tb[:, 1024 + h * 128:1024 + (h + 1) * 128], rhs=v2[r][:, 0:256], start=True, stop=True), reads=[tbk, f'm_v2_{r}'], writes=[Xk])
                        P.op('tensor', lambda e, r=r, h=h, tb=tb, A_=A_: e.matmul(A_[:, 300:301], lhsT=tb[:, 1024 + h * 128:1024 + (h + 1) * 128], rhs=v2[r][:, 256:257], start=True, stop=True), reads=[tbk, f'm_v2_{r}'], writes=[Ak])
                        P.op(a, lambda e, r=r, A_=A_: e.copy(out=yA[r][:], in_=A_[:, 0:257]), reads=[Ak], writes=[f'm_yA{r}'])
                        P.op(v, lambda e, r=r, h=h, B_=B_: e.scalar_tensor_tensor(out=num[r][:], in0=B_[:, 0:257], scalar=ecum[:, h:h + 1], in1=yA[r][:], op0=ALU.mult, op1=ALU.add), reads=[Bk, 'm_ecum', f'm_yA{r}'], writes=[f'm_num{r}'])
                        P.op(v, lambda e, h=h, X=X: e.scalar_tensor_tensor(out=S_f[:, h, 0:256], in0=S_f[:, h, 0:256], scalar=cdec[:, h:h + 1], in1=X[:, 256:512], op0=ALU.mult, op1=ALU.add), reads=[('m_Sf', h), 'm_cdec', Xk], writes=[('m_Sf', h)])
                        P.op(v, lambda e, h=h, A_=A_: e.scalar_tensor_tensor(out=S_f[:, h, 256:257], in0=S_f[:, h, 256:257], scalar=cdec[:, h:h + 1], in1=A_[:, 300:301], op0=ALU.mult, op1=ALU.add), reads=[('m_Sf', h), 'm_cdec', Ak], writes=[('m_Sf', h)])
                        P.op(a, lambda e, h=h: e.copy(out=S_b[:, h, :], in_=S_f[:, h, :]), reads=[('m_Sf', h)], writes=[('m_Sb', h)])
                        P.op(a, lambda e, r=r: e.activation(out=den[r][:], in_=num[r][:, 256:257], func=AF.Abs), reads=[f'm_num{r}'], writes=[f'm_den{r}'])
                        P.op(v, lambda e, r=r: e.tensor_scalar(out=den[r][:], in0=den[r][:], scalar1=1.0, scalar2=None, op0=ALU.max), reads=[f'm_den{r}'], writes=[f'm_den{r}'])
                        P.op(v, lambda e, r=r: e.reciprocal(out=den[r][:], in_=den[r][:]), reads=[f'm_den{r}'], writes=[f'm_den{r}'])
                        P.op(v, lambda e, r=r, h=h, hs_=hs_: e.tensor_scalar(out=hs_[:, h * 256:(h + 1) * 256], in0=num[r][:, 0:256], scalar1=den[r][:], scalar2=None, op0=ALU.mult), reads=[f'm_num{r}', f'm_den{r}'], writes=[hsk])
                    for hp in range(2):
                        lists = []
                        for h in (2 * hp, 2 * hp + 1):
                            P.defer = []
                            head_body(h)
                            lists.append(P.defer)
                        P.defer = None
                        for i_ in range(max(len(l_) for l_ in lists)):
                            for l_ in lists:
                                if i_ < len(l_):
                                    P.op(*l_[i_])
                    if d == 0:
                        P.dma('gpsimd', dr['YF'][t0:t0 + 128, 0:1024], hs_[:], reads=[hsk], writes=[('YF', c)])
                    else:
                        P.dma('sync', hfl[:], dr['YF'][t0:t0 + 128, 0:1024], reads=[('YF', c)], writes=['m_hfl'])
                        P.op(v, lambda e, hs_=hs_: e.tensor_tensor(out=hfl[:], in0=hfl[:], in1=hs_[:], op=ALU.add), reads=['m_hfl', hsk], writes=['m_hfl'])
                        for h in range(4):
                            P.op(a, lambda e, h=h: e.activation(out=sqj[:], in_=hfl[:, h * 256:(h + 1) * 256], func=AF.Square, accum_out=ss[:, h:h + 1]), reads=['m_hfl'], writes=['m_sqj', 'm_ss'])
                        P.op(a, lambda e: e.activation(out=ss[:], in_=ss[:], func=AF.Sqrt, scale=1.0 / 256, bias=self.eps_col[:]), reads=['m_ss', 'eps_col'], writes=['m_ss'])
                        P.op(v, lambda e: e.reciprocal(out=ss[:], in_=ss[:]), reads=['m_ss'], writes=['m_ss'])
                        for h in range(4):
                            P.op(v, lambda e, h=h: e.scalar_tensor_tensor(out=hn[:, h * 256:(h + 1) * 256], in0=hfl[:, h * 256:(h + 1) * 256], scalar=ss[:, h:h + 1], in1=nwB[:, h * 256:(h + 1) * 256], op0=ALU.mult, op1=ALU.mult),
                                 reads=['m_hfl', 'm_ss', 'm_nwB'], writes=['m_hn'])
                        if c < 2:
                            tok0, nt, ch = 0, TC, c
                        else:
                            tok0, nt, ch = TC + ((c - 2) // 4) * 512, 512, (c - 2) % 4
                        for f in range(8):
                            P.op('tensor', lambda e, f=f: e.transpose(out=ptT[:, f, :], in_=hn[:, f * 128:(f + 1) * 128], identity=self.ident_b[:]), reads=['m_hn', 'ident_b'], writes=['PS:m_ptT'])
                        P.op(v, lambda e, ch=ch: e.tensor_copy(out=ytb[:, :, ch * 128:(ch + 1) * 128], in_=ptT[:]), reads=['PS:m_ptT'], writes=['m_ytb'])
                        if ch == 0:
                            P.dma('sync', oT[:, :, :nt], PT[R_OM:R_OM + 1024, :].rearrange("(f p) t -> p f t", p=128)[:, :, tok0:tok0 + nt], writes=['m_oT'])
                            P.op(a, lambda e, nt=nt: e.activation(out=oT[:, :, :nt], in_=oT[:, :, :nt], func=AF.Sigmoid), reads=['m_oT'], writes=['m_oT'])
                            P.op(v, lambda e, nt=nt: e.tensor_tensor(out=ot[:, :, :nt], in0=ytb[:, :, :nt], in1=oT[:, :, :nt], op=ALU.mult), reads=['m_ytb', 'm_oT'], writes=['m_ot'])
                            P.dma('gpsimd', dr['YT'][1024:2048, :].rearrange("(f p) t -> p f t", p=128)[:, :, tok0:tok0 + nt], ot[:, :, :nt], reads=['m_ot'], writes=[('YT', 'ml', tok0)])
            P.barrier()


B.phase_mlstm = phase_mlstm


R_QKV, R_ZG, R_BETA, R_A = 0, 3072, 4096, 4112


def phase_gdn(self):
    P, nc, dr = self.P, self.nc, self.dr
    sb, ps = self.sb, self.ps
    PT = dr['PT']
    v, a, g_ = 'vector', 'scalar', 'gpsimd'
    with ExitStack() as es0:
        gt_tok = sb(es0, "g_gt", [128, 34, 48], F32)
        with ExitStack() as es:
            G = sb(es, "g_G", [128, T], F32)
            bcol = sb(es, "g_bcol", [128, 1], F32)
            acol = sb(es, "g_acol", [128, 1], F32)
            ptr = ps(es, "g_ptr", [128, 512], F32)
            P.op(g_, lambda e: e.memset(G[:], 0.0), writes=['g_G'])
            P.op(g_, lambda e: e.memset(bcol[:], 0.0), writes=['g_bcol'])
            P.op(g_, lambda e: e.memset(acol[:], 0.0), writes=['g_acol'])
            P.dma('sync', G[0:16, :], PT[R_BETA:R_BETA + 16, :], writes=['g_G'])
            P.dma('sync', G[32:48, :], PT[R_A:R_A + 16, :], writes=['g_G'])
            P.dma('sync', bcol[32:48, :], dr['gdn_dtb'], writes=['g_bcol'])
            P.dma('sync', acol[32:48, :], dr['gdn_alog'], writes=['g_acol'])
            P.op(a, lambda e: e.activation(out=G[0:32, :], in_=G[0:32, :], func=AF.Sigmoid), reads=['g_G'], writes=['g_G'])
            P.op(a, lambda e: e.activation(out=acol[32:64, :], in_=acol[32:64, :], func=AF.Exp), reads=['g_acol'], writes=['g_acol'])
            P.op(v, lambda e: e.tensor_scalar(out=acol[32:64, :], in0=acol[32:64, :], scalar1=-1.0, scalar2=None, op0=ALU.mult), reads=['g_acol'], writes=['g_acol'])
            P.op(a, lambda e: e.activation(out=G[32:64, :], in_=G[32:64, :], func=AF.Exp, bias=bcol[32:64, :]), reads=['g_G', 'g_bcol'], writes=['g_G'])
            P.op(a, lambda e: e.activation(out=G[32:64, :], in_=G[32:64, :], func=AF.Ln, bias=self.one_col[32:64, :]), reads=['g_G', 'one_col'], writes=['g_G'])
            P.op(v, lambda e: e.tensor_scalar(out=G[32:64, :], in0=G[32:64, :], scalar1=acol[32:64, :], scalar2=None, op0=ALU.mult), reads=['g_G', 'g_acol'], writes=['g_G'])
            for c in range(34):
                P.op('tensor', lambda e, c=c: e.matmul(ptr[:, 0:48], lhsT=G[:, c * 128:(c + 1) * 128], rhs=self.ident_f[:, 0:48], start=True, stop=True), reads=['g_G', 'ident_f'], writes=['PS:g_ptr'])
                P.op(v, lambda e, c=c: e.tensor_copy(out=gt_tok[:, c, :], in_=ptr[:, 0:48]), reads=['PS:g_ptr'], writes=['g_gt'])
            P.barrier()
        with ExitStack() as es:
            xin = [sb(es, f"g_xin{i}", [128, 24, 514], F32) for i in range(2)]
            cs = sb(es, "g_cs", [128, 24, 512], BF16)
            cf = sb(es, "g_cf", [128, 512], F32)
            sqb = sb(es, "g_sqb", [128, 512], BF16)
            rn = sb(es, "g_rn", [128, 512], F32)
            acc = [sb(es, f"g_cacc{i}", [128, 512], F32) for i in range(2)]
            cw = sb(es, "g_cw", [128, 24, 3], F32)
            tokst = [sb(es, f"g_tokst{i}", [128, 2048], BF16) for i in range(2)]
            ptA = ps(es, "g_ptA", [128, 8, 128], BF16)
            ptB = ps(es, "g_ptB", [128, 8, 128], BF16)
            pss = ps(es, "g_pss", [128, 512], F32)
            P.dma('sync', cw[:], dr['gdn_cw'], writes=['g_cw'])
            src = PT[0:3072, :].rearrange("(f p) t -> p f t", p=128)
            ci = 0
            for bi, (tok0, nt, m) in enumerate(self.blocks()):
                x = xin[bi % 2]
                xk = f'g_xin{bi % 2}'
                seq0, seq1 = (0, TC) if m == 1 else (TC, T)
                lo = max(tok0 - 1, seq0)
                hi = min(tok0 + nt + 1, seq1)
                P.dma('sync', x[:, 0:12, lo - (tok0 - 1):hi - (tok0 - 1)], src[:, 0:12, lo:hi], writes=[xk])
                P.dma('sync', x[:, 12:24, lo - (tok0 - 1):hi - (tok0 - 1)], src[:, 12:24, lo:hi], writes=[xk])
                if lo > tok0 - 1:
                    P.op(g_, lambda e, x=x: e.memset(x[:, :, 0:1], 0.0), writes=[xk])
                if hi < tok0 + nt + 1:
                    P.op(g_, lambda e, x=x, nt=nt: e.memset(x[:, :, nt + 1:nt + 2], 0.0), writes=[xk])
                for f in range(24):
                    ac = acc[f % 2]
                    ak = f'g_cacc{f % 2}'
                    P.op(v, lambda e, x=x, ac=ac, f=f, nt=nt: e.tensor_scalar(out=ac[:, :nt], in0=x[:, f, 0:nt], scalar1=cw[:, f, 0:1], scalar2=None, op0=ALU.mult), reads=[xk, 'g_cw'], writes=[ak])
                    P.op(v, lambda e, x=x, ac=ac, f=f, nt=nt: e.scalar_tensor_tensor(out=ac[:, :nt], in0=x[:, f, 1:nt + 1], scalar=cw[:, f, 1:2], in1=ac[:, :nt], op0=ALU.mult, op1=ALU.add), reads=[xk, 'g_cw', ak], writes=[ak])
                    P.op(v, lambda e, x=x, ac=ac, f=f, nt=nt: e.scalar_tensor_tensor(out=ac[:, :nt], in0=x[:, f, 2:nt + 2], scalar=cw[:, f, 2:3], in1=ac[:, :nt], op0=ALU.mult, op1=ALU.add), reads=[xk, 'g_cw', ak], writes=[ak])
                    if f >= 16:
                        P.op(a, lambda e, ac=ac, f=f, nt=nt: e.activation(out=cs[:, f, :nt], in_=ac[:, :nt], func=AF.Silu), reads=[ak], writes=[('g_cs', f)])
                    else:
                        P.op(a, lambda e, ac=ac, nt=nt: e.activation(out=cf[:, :nt], in_=ac[:, :nt], func=AF.Silu), reads=[ak], writes=['g_cf'])
                        P.op(a, lambda e, nt=nt: e.activation(out=sqb[:, :nt], in_=cf[:, :nt], func=AF.Square), reads=['g_cf'], writes=['g_sqb'])
                        P.op('tensor', lambda e, nt=nt: e.matmul(pss[:, :nt], lhsT=self.ones_b[:], rhs=sqb[:, :nt], start=True, stop=True), reads=['g_sqb', 'ones_b'], writes=['PS:g_pss'])
                        P.op(a, lambda e, nt=nt: e.activation(out=rn[:, :nt], in_=pss[:, :nt], func=AF.Sqrt, bias=self.eps_col[:]), reads=['PS:g_pss', 'eps_col'], writes=['g_rn'])
                        P.op(v, lambda e, nt=nt: e.reciprocal(out=rn[:, :nt], in_=rn[:, :nt]), reads=['g_rn'], writes=['g_rn'])
                        sc = 128 ** -0.5 if f < 8 else 1.0
                        P.op(v, lambda e, f=f, nt=nt, sc=sc: e.scalar_tensor_tensor(out=cs[:, f, :nt], in0=cf[:, :nt], scalar=sc, in1=rn[:, :nt], op0=ALU.mult, op1=ALU.mult), reads=['g_cf', 'g_rn'], writes=[('g_cs', f)])
                P.dma('gpsimd', dr['BCT'].rearrange("(f p) t -> p f t", p=128)[:, :, tok0:tok0 + nt], cs[:, 0:16, :nt],
                      reads=[('g_cs', f) for f in range(16)], writes=[('BCT', tok0)])
                for ch in range(nt // 128):
                    tk = tokst[ci % 2]
                    tkk = f'g_tokst{ci % 2}'
                    ci += 1
                    for f in range(8):
                        P.op('tensor', lambda e, f=f, ch=ch: e.transpose(out=ptA[:, f, :], in_=cs[:, 16 + f, ch * 128:(ch + 1) * 128], identity=self.ident_b[:]), reads=[('g_cs', 16 + f), 'ident_b'], writes=['PS:g_ptA'])
                    for f in range(8):
                        P.op('tensor', lambda e, f=f, ch=ch: e.transpose(out=ptB[:, f, :], in_=cs[:, 8 + f, ch * 128:(ch + 1) * 128], identity=self.ident_b[:]), reads=[('g_cs', 8 + f), 'ident_b'], writes=['PS:g_ptB'])
                    P.op(v, lambda e, tk=tk: e.tensor_copy(out=tk[:, 0:1024], in_=ptA[:].rearrange("p a b -> p (a b)")), reads=['PS:g_ptA'], writes=[tkk])
                    P.op(a, lambda e, tk=tk: e.copy(out=tk[:, 1024:2048], in_=ptB[:].rearrange("p a b -> p (a b)")), reads=['PS:g_ptB'], writes=[tkk])
                    P.dma('gpsimd', dr['TOK'][tok0 + ch * 128:tok0 + (ch + 1) * 128, :], tk[:], reads=[tkk], writes=[('TOK', tok0 + ch * 128)])
            P.barrier()
        with ExitStack() as es:
            S_f = sb(es, "g_Sf", [128, 8, 128], F32)
            S_b = sb(es, "g_Sb", [128, 8, 128], BF16)
            tok = [sb(es, f"g_tok{i}", [128, 2048], BF16) for i in range(2)]
            qk = [sb(es, f"g_qk{i}", [128, 16, 128], BF16) for i in range(2)]
            Lm = [sb(es, f"g_Lm{i}", [128, 128], F32) for i in range(2)]
            Dm = [sb(es, f"g_Dm{i}", [128, 128], F32) for i in range(2)]
            WT = [sb(es, f"g_WT{i}", [128, 128], BF16) for i in range(2)]
            tX = [sb(es, f"g_tX{i}", [128, 128], F32) for i in range(2)]
            Xp = [[sb(es, f"g_Xp{r}{i}", [128, 128], F32) for i in range(2)] for r in range(2)]
            Np = [[sb(es, f"g_Np{r}{i}", [128, 128], F32) for i in range(2)] for r in range(2)]
            rr = [[sb(es, f"g_rr{r}{i}", [128, 256], F32) for i in range(2)] for r in range(2)]
            wT = [sb(es, f"g_wT{i}", [128, 128], F32) for i in range(2)]
            vn = [sb(es, f"g_vn{i}", [128, 128], F32) for i in range(2)]
            v1 = [sb(es, f"g_v1{i}", [128, 128], BF16) for i in range(2)]
            v2 = [sb(es, f"g_v2{i}", [128, 128], BF16) for i in range(2)]
            yA = [sb(es, f"g_yA{i}", [128, 128], F32) for i in range(2)]
            osb = [sb(es, f"g_osb{i}", [128, 1024], F32) for i in range(2)]
            ofl = sb(es, "g_ofl", [128, 1024], F32)
            sqj = sb(es, "g_sqj", [128, 128], F32)
            ss = sb(es, "g_ss", [128, 8], F32)
            hn = sb(es, "g_hn", [128, 1024], BF16)
            ecum = sb(es, "g_ecum", [128, 8], F32)
            cdec = sb(es, "g_cdec", [128, 8], F32)
            nwB = sb(es, "g_nwB", [128, 128], F32)
            ytb = sb(es, "g_ytb", [128, 8, 512], BF16)
            zT = sb(es, "g_zT", [128, 8, 512], F32)
            ot = sb(es, "g_ot", [128, 8, 512], BF16)
            pX = [ps(es, f"g_pX{i}", [128, 512], F32) for i in range(2)]
            pN = [ps(es, f"g_pN{i}", [128, 512], F32) for i in range(2)]
            pA = [ps(es, f"g_pA{i}", [128, 512], F32) for i in range(2)]
            ptT = ps(es, "g_ptT", [128, 8, 128], BF16)
            P.dma('sync', nwB[:], dr['gdn_nw'].partition_broadcast(128), writes=['g_nwB'])
            li = 0
            for d in range(2):
                MASK = self.MGT if d == 0 else self.MLT
                TRI = self.TLE if d == 0 else self.TGE
                STR = self.MLT if d == 0 else self.MGT
                mk, tk_, sk_ = ('MGT', 'TLE', 'MLT') if d == 0 else ('MLT', 'TGE', 'MGT')
                ecol = 127 if d == 0 else 0
                P.op(g_, lambda e: e.memset(S_f[:], 0.0), writes=[('g_Sf', h) for h in range(8)])
                P.op(g_, lambda e: e.memset(S_b[:], 0.0), writes=[('g_Sb', h) for h in range(8)])
                for c in chunk_order(d):
                    t0 = c * 128
                    tb, tbk = tok[li % 2], f'g_tok{li % 2}'
                    qb, qbk = qk[li % 2], f'g_qk{li % 2}'
                    os_, osk = osb[li % 2], f'g_osb{li % 2}'
                    li += 1
                    P.dma('sync', tb[:], dr['TOK'][t0:t0 + 128, :], writes=[tbk])
                    P.dma('sync', qb[:], dr['BCT'].rearrange("(f p) t -> p f t", p=128)[:, :, t0:t0 + 128], writes=[qbk])
                    lac = gt_tok[:, c, 32 + d * 8:32 + (d + 1) * 8]
                    P.op('tensor', lambda e, lac=lac: e.matmul(pX[0][:, 400:408], lhsT=TRI[:], rhs=lac, start=True, stop=True), reads=[tk_, 'g_gt'], writes=['PS:g_pX0'])
                    P.op('tensor', lambda e, lac=lac: e.matmul(pX[0][:, 420:428], lhsT=self.ones_f[:], rhs=lac, start=True, stop=True), reads=['ones_f', 'g_gt'], writes=['PS:g_pX0'])
                    P.op(a, lambda e: e.activation(out=ecum[:], in_=pX[0][:, 400:408], func=AF.Exp), reads=['PS:g_pX0'], writes=['g_ecum'])
                    P.op(a, lambda e: e.activation(out=cdec[:], in_=pX[0][:, 420:428], func=AF.Exp), reads=['PS:g_pX0'], writes=['g_cdec'])
                    for hp in range(4):
                        hs = (2 * hp, 2 * hp + 1)
                        HV = {}
                        for r, h in enumerate(hs):
                            HV[r] = dict(
                                gcol=gt_tok[:, c, 32 + d * 8 + h:32 + d * 8 + h + 1],
                                bcol=gt_tok[:, c, d * 8 + h:d * 8 + h + 1],
                                kT=qb[:, 8 + h, :], qT=qb[:, h, :],
                                vtok=tb[:, h * 128:(h + 1) * 128], ktok=tb[:, 1024 + h * 128:1024 + (h + 1) * 128])
                        for r, h in enumerate(hs):
                            H = HV[r]
                            P.op(v, lambda e, r=r, H=H: e.tensor_scalar(out=Lm[r][:], in0=MASK[:], scalar1=H['gcol'], scalar2=None, op0=ALU.mult), reads=[mk, 'g_gt'], writes=[f'g_Lm{r}'])
                        for r, h in enumerate(hs):
                            H = HV[r]
                            P.op('tensor', lambda e, r=r: e.matmul(pX[r][:, 0:128], lhsT=Lm[r][:], rhs=TRI[:], start=True, stop=False), reads=[f'g_Lm{r}', tk_], writes=[f'PS:g_pX{r}'])
                            P.op('tensor', lambda e, r=r: e.matmul(pX[r][:, 0:128], lhsT=self.negI[:], rhs=MASK[:], start=False, stop=True), reads=['negI', mk], writes=[f'PS:g_pX{r}'])
                            P.op('tensor', lambda e, r=r, H=H: e.matmul(pX[r][:, 128:256], lhsT=H['kT'], rhs=H['kT'], start=True, stop=True), reads=[qbk], writes=[f'PS:g_pX{r}'])
                            P.op('tensor', lambda e, r=r, H=H: e.matmul(pX[r][:, 256:384], lhsT=H['kT'], rhs=H['qT'], start=True, stop=True), reads=[qbk], writes=[f'PS:g_pX{r}'])
                        for r, h in enumerate(hs):
                            H = HV[r]
                            P.op(a, lambda e, r=r: e.activation(out=Dm[r][:], in_=pX[r][:, 0:128], func=AF.Exp), reads=[f'PS:g_pX{r}'], writes=[f'g_Dm{r}'])
                            P.op(v, lambda e, r=r: e.tensor_tensor(out=tX[r][:], in0=Dm[r][:], in1=pX[r][:, 128:256], op=ALU.mult), reads=[f'g_Dm{r}', f'PS:g_pX{r}'], writes=[f'g_tX{r}'])
                            P.op(v, lambda e, r=r: e.tensor_tensor(out=WT[r][:], in0=Dm[r][:], in1=pX[r][:, 256:384], op=ALU.mult), reads=[f'g_Dm{r}', f'PS:g_pX{r}'], writes=[f'g_WT{r}'])
                            P.op(v, lambda e, r=r, H=H: e.scalar_tensor_tensor(out=Xp[r][0][:], in0=tX[r][:], scalar=H['bcol'], in1=STR[:], op0=ALU.mult, op1=ALU.mult), reads=[f'g_tX{r}', 'g_gt', sk_], writes=[f'g_Xp{r}0'])
                            P.op(g_, lambda e, r=r, H=H: e.tensor_copy(out=rr[r][0][:, 0:128], in_=H['vtok']), reads=[tbk], writes=[f'g_rr{r}0'])
                            P.op(g_, lambda e, r=r, H=H, h=h: e.tensor_scalar(out=rr[r][0][:, 128:256], in0=H['ktok'], scalar1=ecum[:, h:h + 1], scalar2=0.0, op0=ALU.mult, op1=ALU.add), reads=[tbk, 'g_ecum'], writes=[f'g_rr{r}0'])
                        for r, h in enumerate(hs):
                            P.op('tensor', lambda e, r=r: e.matmul(pN[r][:, 0:128], lhsT=Xp[r][0][:], rhs=self.ident_f[:], start=True, stop=True), reads=[f'g_Xp{r}0', 'ident_f'], writes=[f'PS:g_pN{r}'])
                            P.op('tensor', lambda e, r=r: e.matmul(pX[r][:, 0:256], lhsT=Xp[r][0][:], rhs=rr[r][0][:], start=True, stop=True), reads=[f'g_Xp{r}0', f'g_rr{r}0'], writes=[f'PS:g_pX{r}'])
                        for r, h in enumerate(hs):
                            P.op(a, lambda e, r=r: e.copy(out=Np[r][0][:], in_=pN[r][:, 0:128]), reads=[f'PS:g_pN{r}'], writes=[f'g_Np{r}0'])
                            P.op(v, lambda e, r=r: e.tensor_tensor(out=rr[r][1][:], in0=rr[r][0][:], in1=pX[r][:, 0:256], op=ALU.subtract), reads=[f'g_rr{r}0', f'PS:g_pX{r}'], writes=[f'g_rr{r}1'])
                        cur = 1
                        xi = 0
                        for lev in range(6):
                            nx = 1 - xi
                            for r, h in enumerate(hs):
                                P.op('tensor', lambda e, r=r, xi=xi: e.matmul(pN[r][:, 0:128], lhsT=Np[r][xi][:], rhs=Xp[r][xi][:], start=True, stop=True), reads=[f'g_Np{r}{xi}', f'g_Xp{r}{xi}'], writes=[f'PS:g_pN{r}'])
                                if lev < 5:
                                    P.op('tensor', lambda e, r=r, xi=xi: e.matmul(pN[r][:, 128:256], lhsT=Xp[r][xi][:], rhs=Np[r][xi][:], start=True, stop=True), reads=[f'g_Np{r}{xi}', f'g_Xp{r}{xi}'], writes=[f'PS:g_pN{r}'])
                            for r, h in enumerate(hs):
                                P.op(a, lambda e, r=r, nx=nx: e.copy(out=Xp[r][nx][:], in_=pN[r][:, 0:128]), reads=[f'PS:g_pN{r}'], writes=[f'g_Xp{r}{nx}'])
                                if lev < 5:
                                    P.op(v, lambda e, r=r, nx=nx: e.tensor_copy(out=Np[r][nx][:], in_=pN[r][:, 128:256]), reads=[f'PS:g_pN{r}'], writes=[f'g_Np{r}{nx}'])
                            for r, h in enumerate(hs):
                                P.op('tensor', lambda e, r=r, nx=nx, cur=cur: e.matmul(pX[r][:, 0:256], lhsT=Xp[r][nx][:], rhs=rr[r][cur][:], start=True, stop=True), reads=[f'g_Xp{r}{nx}', f'g_rr{r}{cur}'], writes=[f'PS:g_pX{r}'])
                            for r, h in enumerate(hs):
                                P.op(v, lambda e, r=r, cur=cur: e.tensor_tensor(out=rr[r][1 - cur][:], in0=rr[r][cur][:], in1=pX[r][:, 0:256], op=ALU.add), reads=[f'g_rr{r}{cur}', f'PS:g_pX{r}'], writes=[f'g_rr{r}{1 - cur}'])
                            cur = 1 - cur
                            xi = nx
                        for r, h in enumerate(hs):
                            R_ = rr[r][cur]
                            Rk = f'g_rr{r}{cur}'
                            P.op('tensor', lambda e, r=r, R_=R_: e.matmul(pN[r][:, 256:384], lhsT=R_[:, 128:256], rhs=self.ident_f[:], start=True, stop=True), reads=[Rk, 'ident_f'], writes=[f'PS:g_pN{r}'])
                        for r, h in enumerate(hs):
                            P.op(a, lambda e, r=r: e.copy(out=wT[r][:], in_=pN[r][:, 256:384]), reads=[f'PS:g_pN{r}'], writes=[f'g_wT{r}'])
                        for r, h in enumerate(hs):
                            P.op('tensor', lambda e, r=r, h=h: e.matmul(pA[r][:, 128:256], lhsT=wT[r][:], rhs=S_f[:, h, :], start=True, stop=True), reads=[f'g_wT{r}', ('g_Sf', h)], writes=[f'PS:g_pA{r}'])
                        for r, h in enumerate(hs):
                            H = HV[r]
                            R_ = rr[r][cur]
                            Rk = f'g_rr{r}{cur}'
                            P.op(v, lambda e, r=r, R_=R_: e.scalar_tensor_tensor(out=vn[r][:], in0=pA[r][:, 128:256], scalar=-1.0, in1=R_[:, 0:128], op0=ALU.mult, op1=ALU.add), reads=[f'PS:g_pA{r}', Rk], writes=[f'g_vn{r}'])
                            P.op(v, lambda e, r=r, H=H: e.tensor_scalar(out=v1[r][:], in0=vn[r][:], scalar1=H['bcol'], scalar2=None, op0=ALU.mult), reads=[f'g_vn{r}', 'g_gt'], writes=[f'g_v1{r}'])
                            P.op(g_, lambda e, r=r: e.tensor_scalar(out=v2[r][:], in0=v1[r][:], scalar1=Dm[r][:, ecol:ecol + 1], scalar2=0.0, op0=ALU.mult, op1=ALU.add), reads=[f'g_v1{r}', f'g_Dm{r}'], writes=[f'g_v2{r}'])
                        for r, h in enumerate(hs):
                            H = HV[r]
                            P.op('tensor', lambda e, r=r: e.matmul(pA[r][:, 0:128], lhsT=WT[r][:], rhs=v1[r][:], start=True, stop=True), reads=[f'g_WT{r}', f'g_v1{r}'], writes=[f'PS:g_pA{r}'])
                            P.op('tensor', lambda e, r=r, h=h, H=H: e.matmul(pA[r][:, 256:384], lhsT=H['qT'], rhs=S_b[:, h, :], start=True, stop=True), reads=[qbk, ('g_Sb', h)], writes=[f'PS:g_pA{r}'])
                            P.op('tensor', lambda e, r=r, H=H: e.matmul(pA[r][:, 384:512], lhsT=H['ktok'], rhs=v2[r][:], start=True, stop=True), reads=[tbk, f'g_v2{r}'], writes=[f'PS:g_pA{r}'])
                        for r, h in enumerate(hs):
                            P.op(a, lambda e, r=r: e.copy(out=yA[r][:], in_=pA[r][:, 0:128]), reads=[f'PS:g_pA{r}'], writes=[f'g_yA{r}'])
                            P.op(v, lambda e, r=r, h=h, os_=os_: e.scalar_tensor_tensor(out=os_[:, h * 128:(h + 1) * 128], in0=pA[r][:, 256:384], scalar=ecum[:, h:h + 1], in1=yA[r][:], op0=ALU.mult, op1=ALU.add), reads=[f'PS:g_pA{r}', 'g_ecum', f'g_yA{r}'], writes=[osk])
                            P.op(v, lambda e, r=r, h=h: e.scalar_tensor_tensor(out=S_f[:, h, :], in0=S_f[:, h, :], scalar=cdec[:, h:h + 1], in1=pA[r][:, 384:512], op0=ALU.mult, op1=ALU.add), reads=[('g_Sf', h), 'g_cdec', f'PS:g_pA{r}'], writes=[('g_Sf', h)])
                            P.op(a, lambda e, h=h: e.copy(out=S_b[:, h, :], in_=S_f[:, h, :]), reads=[('g_Sf', h)], writes=[('g_Sb', h)])
                    if d == 0:
                        P.dma('gpsimd', dr['YF'][t0:t0 + 128, 1024:2048], os_[:], reads=[osk], writes=[('YFg', c)])
                    else:
                        P.dma('sync', ofl[:], dr['YF'][t0:t0 + 128, 1024:2048], reads=[('YFg', c)], writes=['g_ofl'])
                        P.op(v, lambda e, os_=os_: e.tensor_tensor(out=ofl[:], in0=ofl[:], in1=os_[:], op=ALU.add), reads=['g_ofl', osk], writes=['g_ofl'])
                        for h in range(8):
                            P.op(a, lambda e, h=h: e.activation(out=sqj[:], in_=ofl[:, h * 128:(h + 1) * 128], func=AF.Square, accum_out=ss[:, h:h + 1]), reads=['g_ofl'], writes=['g_sqj', 'g_ss'])
                        P.op(a, lambda e: e.activation(out=ss[:], in_=ss[:], func=AF.Sqrt, scale=1.0 / 128, bias=self.eps_col[:]), reads=['g_ss', 'eps_col'], writes=['g_ss'])
                        P.op(v, lambda e: e.reciprocal(out=ss[:], in_=ss[:]), reads=['g_ss'], writes=['g_ss'])
                        for h in range(8):
                            P.op(v, lambda e, h=h: e.scalar_tensor_tensor(out=hn[:, h * 128:(h + 1) * 128], in0=ofl[:, h * 128:(h + 1) * 128], scalar=ss[:, h:h + 1], in1=nwB[:], op0=ALU.mult, op1=ALU.mult),
                                 reads=['g_ofl', 'g_ss', 'g_nwB'], writes=['g_hn'])
                        if c < 2:
                            tok0, nt, ch = 0, TC, c
                        else:
                            tok0, nt, ch = TC + ((c - 2) // 4) * 512, 512, (c - 2) % 4
                        for f in range(8):
                            P.op('tensor', lambda e, f=f: e.transpose(out=ptT[:, f, :], in_=hn[:, f * 128:(f + 1) * 128], identity=self.ident_b[:]), reads=['g_hn', 'ident_b'], writes=['PS:g_ptT'])
                        P.op(v, lambda e, ch=ch: e.tensor_copy(out=ytb[:, :, ch * 128:(ch + 1) * 128], in_=ptT[:]), reads=['PS:g_ptT'], writes=['g_ytb'])
                        if ch == 0:
                            P.dma('sync', zT[:, :, :nt], PT[R_ZG:R_ZG + 1024, :].rearrange("(f p) t -> p f t", p=128)[:, :, tok0:tok0 + nt], writes=['g_zT'])
                            P.op(a, lambda e, nt=nt: e.activation(out=zT[:, :, :nt], in_=zT[:, :, :nt], func=AF.Silu), reads=['g_zT'], writes=['g_zT'])
                            P.op(v, lambda e, nt=nt: e.tensor_tensor(out=ot[:, :, :nt], in0=ytb[:, :, :nt], in1=zT[:, :, :nt], op=ALU.mult), reads=['g_ytb', 'g_zT'], writes=['g_ot'])
                            P.dma('gpsimd', dr['YT'][0:1024, :].rearrange("(f p) t -> p f t", p=128)[:, :, tok0:tok0 + nt], ot[:, :, :nt], reads=['g_ot'], writes=[('YT', 'gdn', tok0)])
            P.barrier()


B.phase_gdn = phase_gdn


def phase_final(self, src):
    P, nc, dr = self.P, self.nc, self.dr
    sb, ps = self.sb, self.ps
    with ExitStack() as es:
        xt = [sb(es, f"fin_x{i}", [128, KC, 512], F32) for i in range(2)]
        sq = sb(es, "fin_sq", [128, KC, 512], BF16)
        rstd = sb(es, "fin_rstd", [128, 512], F32)
        fw = sb(es, "fin_w", [128, KC], F32)
        pss = ps(es, "fin_pss", [128, 512], F32)
        P.dma('sync', fw[:], dr['fnwT'], writes=['fin_w'])
        for bi, (tok0, nt, m) in enumerate([(TC + i * 512, 512, 0) for i in range(4)]):
            x, xk = xt[bi % 2], f'fin_x{bi % 2}'
            ts_ = self.dyn('sync', tok0)
            P.dma('sync', x[:, :, :nt], src.rearrange("(kc p) t -> p kc t", p=128)[:, :, bass.ds(ts_, nt)], writes=[xk])
            P.op('scalar', lambda e, x=x, nt=nt: e.activation(out=sq[:, :, :nt], in_=x[:, :, :nt], func=AF.Square), reads=[xk], writes=['fin_sq'])
            for kc in range(KC):
                P.op('tensor', lambda e, kc=kc, nt=nt: e.matmul(pss[:, :nt], lhsT=self.ones_b[:], rhs=sq[:, kc, :nt], start=(kc == 0), stop=(kc == KC - 1)), reads=['fin_sq', 'ones_b'], writes=['PS:fin_pss'])
            P.op('scalar', lambda e, nt=nt: e.activation(out=rstd[:, :nt], in_=pss[:, :nt], func=AF.Sqrt, scale=1.0 / D, bias=self.eps_col[:]), reads=['PS:fin_pss', 'eps_col'], writes=['fin_rstd'])
            P.op('vector', lambda e, nt=nt: e.reciprocal(out=rstd[:, :nt], in_=rstd[:, :nt]), reads=['fin_rstd'], writes=['fin_rstd'])
            for kc in range(KC):
                P.op('vector', lambda e, kc=kc, x=x, nt=nt: e.scalar_tensor_tensor(out=x[:, kc, :nt], in0=x[:, kc, :nt], scalar=fw[:, kc:kc + 1], in1=rstd[:, :nt], op0=ALU.mult, op1=ALU.mult),
                     reads=[xk, 'fin_w', 'fin_rstd'], writes=[xk])
            P.dma('gpsimd', dr['outT'].rearrange("(kc p) t -> p kc t", p=128)[:, :, tok0 - TC:tok0 - TC + nt], x[:, :, :nt], reads=[xk], writes=[('outT', tok0)])
        P.barrier()


B.phase_final = phase_final
```
